# Optimizing a Trainium2 kernel written in Bass

```python
import jax
import jax.numpy as jnp
from jax import lax
import numpy as np

D_MODEL = 1024
BATCH = 32
SEQ = 2048
DEPTH = 4
DEC_BATCH = 4
DEC_SEQ = 8192
PAST_LEN = 128

N_MIXERS = 2
N_RWKV_LAYERS = (DEPTH + 1) // 2
N_NA_LAYERS = DEPTH // 2
HEAD_DIM = 64
N_HEADS = D_MODEL // HEAD_DIM
D_FF = 4 * D_MODEL
D_DECAY_LORA = 64
D_ICLR_LORA = 64
D_VRES_LORA = 32
D_GATE_LORA = 160
GRID_W = 64
WIN_ROWS_MAX = 8
WIN_COLS = 16
STRIP_COLS = 2 * WIN_COLS
N_COL_BLOCKS = GRID_W // WIN_COLS
RMS_EPS = 1e-6
GN_EPS = 64e-5
L2_EPS = 1e-24

kernel_name = 'hybrid_rwkv7_natten_encoder'


def _rms_norm(x, g):
    xf = x.astype(jnp.float32)
    y = xf * lax.rsqrt(jnp.mean(jnp.square(xf), axis=-1, keepdims=True) + RMS_EPS)
    return y.astype(x.dtype) * g


def _heads(z):
    return z.reshape(z.shape[:-1] + (N_HEADS, HEAD_DIM))


def _sq_relu_mlp(h, w_up, w_down):
    return jnp.square(jax.nn.relu(h @ w_up)) @ w_down


def _wkv_scan(r, w, k, v, a, b, reverse):
    B, T, H, N = r.shape

    def step(S, inp):
        r_t, w_t, k_t, v_t, a_t, b_t = inp
        sa = jnp.einsum('bhvk,bhk->bhv', S, a_t)
        S = S * w_t[:, :, None, :] + sa[..., None] * b_t[:, :, None, :] + v_t[..., None] * k_t[:, :, None, :]
        return S, jnp.einsum('bhvk,bhk->bhv', S, r_t)

    seq = tuple(jnp.moveaxis(z, 1, 0) for z in (r, w, k, v, a, b))
    s0 = jnp.zeros((B, H, N, N), jnp.float32)
    _, y = lax.scan(step, s0, seq, reverse=reverse)
    return jnp.moveaxis(y, 0, 1)


def _rwkv7_mixer(h, v_first, vres, mix, w_rkv, w0, w1, w2, a0, a1, a2, g1, g2, k_k, k_a, r_k, lnx_g, lnx_b, w_o):
    B, T, C = h.shape
    f32 = jnp.float32
    zero = jnp.zeros_like(h[:, :1])
    xx = 0.5 * (jnp.concatenate([zero, h[:, :-1]], axis=1) + jnp.concatenate([h[:, 1:], zero], axis=1)) - h
    r, k, v = jnp.einsum('sbtc,scd->sbtd', h[None] + xx[None] * mix[:3, None, None, :], w_rkv)
    xw = h + xx * mix[3]
    xa = h + xx * mix[4]
    xg = h + xx * mix[5]
    if vres is None:
        v_first = v
    else:
        v0, v1, v2 = vres
        xv = h + xx * mix[2]
        v = v + (v_first - v) * jax.nn.sigmoid(v0 + (xv @ v1) @ v2)
    g = jax.nn.sigmoid(xg @ g1) @ g2
    rf, kf, vf = _heads(r.astype(f32)), _heads(k.astype(f32)), _heads(v.astype(f32))
    kk = _heads((k * k_k).astype(f32))
    kk = kk * lax.rsqrt(jnp.maximum(jnp.sum(jnp.square(kk), axis=-1, keepdims=True), L2_EPS))
    k_a_h = _heads(k_a.astype(f32))
    y = jnp.zeros_like(rf)
    bonus = jnp.zeros_like(rf[..., :1])
    for d in range(2):
        w_lora = jnp.tanh(xw @ w1[d]) @ w2[d]
        decay = jnp.exp(-jnp.exp(-jax.nn.softplus(-(w0[d] + w_lora).astype(f32)) - 0.5))
        iclr = _heads(jax.nn.sigmoid((a0[d] + (xa @ a1[d]) @ a2[d]).astype(f32)))
        k_d = kf * (1.0 + (iclr - 1.0) * k_a_h)
        y = y + _wkv_scan(rf, _heads(decay), k_d, vf, -kk, kk * iclr, reverse=(d == 1))
        bonus = bonus + jnp.sum(rf * k_d * r_k, axis=-1, keepdims=True)
    mu = jnp.mean(y, axis=-1, keepdims=True)
    var = jnp.mean(jnp.square(y - mu), axis=-1, keepdims=True)
    y = ((y - mu) * lax.rsqrt(var + GN_EPS)).reshape(B, T, C) * lnx_g + lnx_b
    y = y + (bonus * vf).reshape(B, T, C)
    out = (y.astype(h.dtype) * g) @ w_o
    return out, v_first


def _column_pattern():
    q_col = np.arange(GRID_W).reshape(N_COL_BLOCKS, WIN_COLS)
    win_start = np.clip(q_col - WIN_COLS // 2, 0, GRID_W - WIN_COLS)
    strip_start = np.clip(np.arange(N_COL_BLOCKS) * WIN_COLS - WIN_COLS // 2, 0, GRID_W - STRIP_COLS)
    strip_cols = strip_start[:, None] + np.arange(STRIP_COLS)[None, :]
    key_col = strip_cols[:, None, :]
    col_mask = (key_col >= win_start[..., None]) & (key_col < win_start[..., None] + WIN_COLS)
    dc_idx = np.clip(key_col - q_col[..., None] + WIN_COLS - 1, 0, 2 * WIN_COLS - 2)
    return strip_cols, col_mask, dc_idx


def _neighbourhood_attention(h, w_qkv, q_g, k_g, rpb, w_o):
    B, T, C = h.shape
    rows = T // GRID_W
    kr = min(WIN_ROWS_MAX, rows)
    strip_cols, col_mask, dc_idx = _column_pattern()
    qkv = (h @ w_qkv).reshape(B, rows, GRID_W, 3, N_HEADS, HEAD_DIM)
    q = _rms_norm(qkv[:, :, :, 0], q_g) * (HEAD_DIM ** -0.5)
    k = _rms_norm(qkv[:, :, :, 1], k_g)
    v = qkv[:, :, :, 2]
    q, k, v = (jnp.transpose(z, (0, 3, 1, 2, 4)) for z in (q, k, v))
    row_off = jnp.arange(kr)

    def row_block(i):
        r0 = jnp.clip(i - kr // 2, 0, rows - kr)
        q_i = lax.dynamic_index_in_dim(q, i, axis=2, keepdims=False)
        q_i = q_i.reshape(B, N_HEADS, N_COL_BLOCKS, WIN_COLS, HEAD_DIM)
        k_s = lax.dynamic_slice_in_dim(k, r0, kr, axis=2)[:, :, :, strip_cols]
        v_s = lax.dynamic_slice_in_dim(v, r0, kr, axis=2)[:, :, :, strip_cols]
        s = jnp.einsum('bhcqd,bhrcsd->bhcqrs', q_i, k_s).astype(jnp.float32)
        dr_idx = r0 + row_off - i + WIN_ROWS_MAX - 1
        bias = jnp.take(rpb, dr_idx, axis=1)[:, :, dc_idx]
        s = s + jnp.transpose(bias, (0, 2, 3, 1, 4)).astype(jnp.float32)
        s = jnp.where(col_mask[:, :, None, :], s, -jnp.inf)
        p = jax.nn.softmax(s, axis=(-2, -1)).astype(v_s.dtype)
        o = jnp.einsum('bhcqrs,bhrcsd->bhcqd', p, v_s)
        return o.reshape(B, N_HEADS, GRID_W, HEAD_DIM)

    o = lax.map(row_block, jnp.arange(rows))
    o = jnp.transpose(o, (1, 0, 3, 2, 4)).reshape(B, T, C)
    return o @ w_o


def _trunk(x, p):
    v_first = None
    for layer in range(DEPTH):
        j = layer // N_MIXERS
        h = _rms_norm(x, p['norm_mix_g'][layer])
        if layer % N_MIXERS == 0:
            vres = None if j == 0 else (p['rw_v0'][j - 1], p['rw_v1'][j - 1], p['rw_v2'][j - 1])
            y, v_first = _rwkv7_mixer(
                h, v_first, vres, p['rw_mix'][j], p['rw_rkv'][j],
                p['rw_w0'][j], p['rw_w1'][j], p['rw_w2'][j],
                p['rw_a0'][j], p['rw_a1'][j], p['rw_a2'][j],
                p['rw_g1'][j], p['rw_g2'][j], p['rw_kk'][j], p['rw_ka'][j], p['rw_rk'][j],
                p['rw_lnx_g'][j], p['rw_lnx_b'][j], p['rw_o'][j])
        else:
            y = _neighbourhood_attention(h, p['na_qkv'][j], p['na_q_g'][j], p['na_k_g'][j],
                                         p['na_rpb'][j], p['na_o'][j])
        x = x + y
        x = x + _sq_relu_mlp(_rms_norm(x, p['norm_mlp_g'][layer]), p['w_up'][layer], p['w_down'][layer])
    return x


def _normal(k, shape, scale):
    return scale * jax.random.normal(k, shape, jnp.float32)


def setup_inputs(seed: int = 0) -> dict:
    key = jax.random.key(seed)
    k = jax.random.split(key, 32)
    C = D_MODEL
    inv = C ** -0.5
    NA_, NB_ = N_RWKV_LAYERS, N_NA_LAYERS
    return {
        'x_prompt': _normal(k[0], (BATCH, SEQ, C), 1.0),
        'x_sample': _normal(k[1], (DEC_BATCH, DEC_SEQ, C), 1.0),
        'norm_mix_g': 1.0 + _normal(k[2], (DEPTH, C), 0.05),
        'norm_mlp_g': 1.0 + _normal(k[3], (DEPTH, C), 0.05),
        'w_up': _normal(k[4], (DEPTH, C, D_FF), inv),
        'w_down': _normal(k[5], (DEPTH, D_FF, C), D_FF ** -0.5),
        'rw_mix': jax.random.uniform(k[6], (NA_, 6, C), jnp.float32),
        'rw_rkv': _normal(k[7], (NA_, 3, C, C), inv),
        'rw_w0': jax.random.uniform(k[8], (NA_, 2, C), jnp.float32, -6.0, 1.0),
        'rw_w1': _normal(k[9], (NA_, 2, C, D_DECAY_LORA), inv),
        'rw_w2': _normal(k[10], (NA_, 2, D_DECAY_LORA, C), 0.1 * D_DECAY_LORA ** -0.5),
        'rw_a0': _normal(k[11], (NA_, 2, C), 0.5),
        'rw_a1': _normal(k[12], (NA_, 2, C, D_ICLR_LORA), inv),
        'rw_a2': _normal(k[13], (NA_, 2, D_ICLR_LORA, C), D_ICLR_LORA ** -0.5),
        'rw_v0': _normal(k[14], (NA_ - 1, C), 0.5),
        'rw_v1': _normal(k[15], (NA_ - 1, C, D_VRES_LORA), inv),
        'rw_v2': _normal(k[16], (NA_ - 1, D_VRES_LORA, C), D_VRES_LORA ** -0.5),
        'rw_g1': _normal(k[17], (NA_, C, D_GATE_LORA), inv),
        'rw_g2': _normal(k[18], (NA_, D_GATE_LORA, C), D_GATE_LORA ** -0.5),
        'rw_kk': 0.85 + _normal(k[19], (NA_, C), 0.05),
        'rw_ka': 1.0 + _normal(k[20], (NA_, C), 0.05),
        'rw_rk': _normal(k[21], (NA_, N_HEADS, HEAD_DIM), 0.1),
        'rw_lnx_g': 1.0 + _normal(k[22], (NA_, C), 0.05),
        'rw_lnx_b': _normal(k[23], (NA_, C), 0.02),
        'rw_o': _normal(k[24], (NA_, C, C), inv),
        'na_qkv': _normal(k[25], (NB_, C, 3 * C), inv),
        'na_q_g': 1.0 + _normal(k[26], (NB_, HEAD_DIM), 0.05),
        'na_k_g': 1.0 + _normal(k[27], (NB_, HEAD_DIM), 0.05),
        'na_rpb': _normal(k[28], (NB_, N_HEADS, 2 * WIN_ROWS_MAX - 1, 2 * WIN_COLS - 1), 0.1),
        'na_o': _normal(k[29], (NB_, C, C), inv),
    }


def reference(x_prompt, x_sample, norm_mix_g, norm_mlp_g, w_up, w_down, rw_mix, rw_rkv, rw_w0, rw_w1, rw_w2,
              rw_a0, rw_a1, rw_a2, rw_v0, rw_v1, rw_v2, rw_g1, rw_g2, rw_kk, rw_ka, rw_rk, rw_lnx_g, rw_lnx_b,
              rw_o, na_qkv, na_q_g, na_k_g, na_rpb, na_o):
    p = dict(norm_mix_g=norm_mix_g, norm_mlp_g=norm_mlp_g, w_up=w_up, w_down=w_down,
             rw_mix=rw_mix, rw_rkv=rw_rkv, rw_w0=rw_w0, rw_w1=rw_w1, rw_w2=rw_w2,
             rw_a0=rw_a0, rw_a1=rw_a1, rw_a2=rw_a2, rw_v0=rw_v0, rw_v1=rw_v1, rw_v2=rw_v2,
             rw_g1=rw_g1, rw_g2=rw_g2, rw_kk=rw_kk, rw_ka=rw_ka, rw_rk=rw_rk,
             rw_lnx_g=rw_lnx_g, rw_lnx_b=rw_lnx_b, rw_o=rw_o,
             na_qkv=na_qkv, na_q_g=na_q_g, na_k_g=na_k_g, na_rpb=na_rpb, na_o=na_o)
    y_prompt = _trunk(x_prompt, p)
    y_sample = _trunk(x_sample, p)
    return (y_prompt, y_sample)
```

```python
from contextlib import ExitStack
import numpy as np
import concourse.bass as bass
import concourse.mybir as mybir
from concourse.bass_utils import run_bass_kernel_spmd

F32 = mybir.dt.float32
BF16 = mybir.dt.bfloat16
ALU = mybir.AluOpType
AF = mybir.ActivationFunctionType
AX = mybir.AxisListType

D = 1024
NCH = 8
DFF = 4096
NSEG = 6
SEG = 2048
NCORES = 8
RMS_EPS = 1e-6

NSLOT = 12
EPOCH = 15000
NEPOCH = 16


class Buf:
    __slots__ = ("w", "r")

    def __init__(self):
        self.w = None
        self.r = []


class Op:
    __slots__ = ("eng", "seq", "fn", "waits", "dma", "sigidx", "slot", "slotval", "slotprev")


class Prog:
    ENGS = ("pe", "act", "dve", "pool", "sp")
    COMPUTE = ("pe", "act", "dve", "pool")

    def __init__(self):
        self.ops = {e: [] for e in self.ENGS}
        self.known = {f: {e: -1 for e in self.ENGS} for f in self.ENGS}
        self.known_dma = {f: set() for f in self.ENGS}
        self.last_compute = {e: -1 for e in self.ENGS}
        self.slot_uses = {q: [0] * NSLOT for q in ("sp", "pool", "act")}
        self.slot_last = {q: [None] * NSLOT for q in ("sp", "pool", "act")}
        self.dma_n = {q: 0 for q in ("sp", "pool", "act")}
        self.fence = {e: -1 for e in self.ENGS}

    def add(self, eng, fn, reads=(), writes=(), dma=False):
        ops = self.ops[eng]
        seq = len(ops)
        deps = set()
        for b in reads:
            if b.w is not None:
                deps.add(b.w)
        for b in writes:
            if b.w is not None:
                deps.add(b.w)
            deps.update(b.r)
        waits = []
        kn = self.known[eng]
        kd = self.known_dma[eng]
        for d in sorted(deps):
            E, s, isdma = d
            if s <= self.fence[E]:
                continue
            if isdma:
                if (E, s) in kd:
                    continue
                kd.add((E, s))
                waits.append(d)
            else:
                if E == eng and not dma:
                    if eng == "pe":
                        continue
                    if seq - s > 3:
                        continue
                if kn[E] >= s:
                    continue
                kn[E] = s
                waits.append(d)
        op = Op()
        op.eng, op.seq, op.fn, op.waits, op.dma, op.sigidx = eng, seq, fn, waits, dma, 0
        op.slot = op.slotval = op.slotprev = None
        if dma:
            n = self.dma_n[eng]
            self.dma_n[eng] = n + 1
            sl = n % NSLOT
            op.slot = sl
            op.slotprev = self.slot_last[eng][sl]
            self.slot_uses[eng][sl] += 1
            op.slotval = 16 * self.slot_uses[eng][sl]
            self.slot_last[eng][sl] = (eng, seq, True)
            if op.slotprev is not None:
                kd.add(op.slotprev[:2])
        else:
            self.last_compute[eng] = seq
        tok = (eng, seq, dma)
        for b in reads:
            b.r.append(tok)
        for b in writes:
            b.w = tok
            b.r = []
        ops.append(op)
        return op

    def barrier(self):
        lasts = dict(self.last_compute)
        dmas = []
        for q in self.slot_last:
            for t in self.slot_last[q]:
                if t is not None:
                    dmas.append(t)
        for F in self.ENGS:
            waits = []
            for E in self.COMPUTE:
                if E != F and lasts[E] >= 0 and self.known[F][E] < lasts[E]:
                    waits.append((E, lasts[E], False))
                    self.known[F][E] = lasts[E]
            for t in dmas:
                if t[:2] not in self.known_dma[F]:
                    self.known_dma[F].add(t[:2])
                    waits.append(t)
            op = Op()
            op.eng, op.seq, op.fn, op.waits, op.dma, op.sigidx = F, len(self.ops[F]), None, waits, False, 0
            op.slot = op.slotval = op.slotprev = None
            self.ops[F].append(op)
        for E in self.ENGS:
            self.fence[E] = len(self.ops[E]) - 1

    def emit(self, nc, st):
        sig = {e: set() for e in self.ENGS}
        for F in self.ENGS:
            for op in self.ops[F]:
                for (E, s, isdma) in op.waits:
                    if not isdma:
                        sig[E].add(s)
        nsig = {}
        for E in self.COMPUTE:
            c = 0
            for op in self.ops[E]:
                if op.seq in sig[E]:
                    c += 1
                    op.sigidx = c
            nsig[E] = c
            assert c <= EPOCH * NEPOCH, (E, c)
        csem = {E: [st.enter_context(nc.semaphore(f"c_{E}_{k}")) for k in range((nsig[E] + EPOCH - 1) // EPOCH)]
                for E in self.COMPUTE}
        dsem = {q: [st.enter_context(nc.semaphore(f"d_{q}_{k}")) for k in range(NSLOT)]
                for q in self.slot_last if self.dma_n[q] > 0}
        allops = self.ops

        def run(F, eng):
            for op in allops[F]:
                for (E, s, isdma) in op.waits:
                    t = allops[E][s]
                    if isdma:
                        eng.wait_ge(dsem[E][t.slot], t.slotval)
                    else:
                        i = t.sigidx - 1
                        eng.wait_ge(csem[E][i // EPOCH], i % EPOCH + 1)
                if op.dma and op.slotprev is not None:
                    t = allops[op.slotprev[0]][op.slotprev[1]]
                    eng.wait_ge(dsem[F][t.slot], t.slotval)
                if op.fn is None:
                    continue
                ins = op.fn(eng)
                if op.dma:
                    ins.then_inc(dsem[F][op.slot], 16)
                elif op.sigidx:
                    i = op.sigidx - 1
                    ins.then_inc(csem[F][i // EPOCH], 1)

        block = st.enter_context(nc.Block())

        @block.tensor
        def _(eng):
            run("pe", eng)

        @block.scalar
        def _(eng):
            run("act", eng)

        @block.vector
        def _(eng):
            run("dve", eng)

        @block.gpsimd
        def _(eng):
            run("pool", eng)

        @block.sync
        def _(eng):
            run("sp", eng)


class SB:
    def __init__(self, nc):
        self.nc = nc
        self.base = nc.sbuf_base + 64
        self.top = nc.sbuf_top
        self.ptr = self.base
        self.n = 0

    def alloc(self, shape, dtype, name="t"):
        esz = 2 if dtype == BF16 else 4
        per = esz
        for s in shape[1:]:
            per *= s
        off = (self.ptr + 63) // 64 * 64
        assert off + per <= self.top, (name, off, per, self.top)
        self.ptr = off + per
        self.n += 1
        return self.nc.alloc_sbuf_tensor_at(f"{name}_{self.n}", list(shape), dtype, offset=off)

    def mark(self):
        return self.ptr

    def release(self, m):
        self.ptr = m


class Builder:
    def __init__(self, T):
        self.T = T
        self.nc = bass.Bass("TRN2", target_bir_lowering=False)
        self.p = Prog()
        self.sb = SB(self.nc)
        nc = self.nc
        self.ps = [nc.alloc_psum_tensor(f"psb{i}", [128, 512], F32) for i in range(8)]
        self.psB = [Buf() for _ in range(8)]
        self.ones_bf = self.sb.alloc([128, 128], BF16, "ones")
        self.onesB = Buf()
        self.p.add("pool", lambda e: e.memset(self.ones_bf[:, :], 1.0), writes=[self.onesB])
        self.perm_mark = None

    def din(self, name, shape, dtype=F32):
        return self.nc.dram_tensor(name, list(shape), dtype, kind="ExternalInput")

    def dout(self, name, shape, dtype=F32):
        return self.nc.dram_tensor(name, list(shape), dtype, kind="ExternalOutput")

    def dscr(self, name, shape, dtype=F32):
        return self.nc.dram_tensor(name, list(shape), dtype)

    def dma(self, q, out, in_, reads=(), writes=(), slow=False):
        if slow:
            self.p.add(q, lambda e, o=out, i=in_: e.dma_start(out=o, in_=i, allow_slow_non_contiguous=True),
                       reads, writes, dma=True)
        else:
            self.p.add(q, lambda e, o=out, i=in_: e.dma_start(out=o, in_=i), reads, writes, dma=True)

    def cast(self, k, out, in_, reads, writes):
        eng = ("dve", "pool", "act")[k % 3]
        if eng == "act":
            self.p.add("act", lambda e, o=out, i=in_: e.activation(out=o, in_=i, func=AF.Copy), reads, writes)
        else:
            self.p.add(eng, lambda e, o=out, i=in_: e.tensor_copy(out=o, in_=i), reads, writes)


def mlp_stage(B, xin, xout, w_up, w_down, vecs_sb, gcol):
    nc, p, sb = B.nc, B.p, B.sb
    T = B.T
    TT = 512
    NT = T // TT
    m = sb.mark()
    wup = sb.alloc([128, 8, DFF], BF16, "wup")
    wdn = sb.alloc([128, 32, D], BF16, "wdn")
    wupB = [Buf() for _ in range(8)]
    wdnB = [Buf() for _ in range(8)]
    xs = [sb.alloc([128, 8, TT], F32, "x") for _ in range(2)]
    xoff = []
    xB = [Buf() for _ in range(2)]
    hn = sb.alloc([128, 8, TT], BF16, "hn")
    hnB = Buf()
    a = sb.alloc([128, 16, TT], BF16, "a")
    aB = [Buf() for _ in range(16)]
    r = [sb.alloc([128, TT], F32, "r") for _ in range(2)]
    rB = [Buf() for _ in range(2)]
    rs = sb.alloc([128, TT], F32, "rs")
    rsB = Buf()
    rstd = sb.alloc([128, TT], F32, "rstd")
    rstdB = Buf()
    epsb = sb.alloc([128, 1], F32, "eps")
    epsB = Buf()
    p.add("pool", lambda e: e.memset(epsb[:, :], RMS_EPS), writes=[epsB])
    stg = [xs[0], xs[1]]
    k = 0
    for kc in range(8):
        s = stg[k % 2]
        sv = s[:, :, :].rearrange("p a b -> p (a b)")
        B.dma("sp", sv, w_up[kc * 128:(kc + 1) * 128, :], writes=[xB[k % 2]])
        for h in range(2):
            B.cast(2 * k + h, wup[:, kc, h * 2048:(h + 1) * 2048], sv[:, h * 2048:(h + 1) * 2048],
                   [xB[k % 2]], [wupB[kc]])
        k += 1
    wdv = w_down.rearrange("(c p) n -> p c n", p=128)
    for g4 in range(8):
        s = stg[k % 2]
        sv = s[:, :, :].rearrange("p a b -> p (a b)").rearrange("p (g c) -> p g c", c=D)
        B.dma("sp", sv, wdv[:, g4 * 4:(g4 + 1) * 4, :], writes=[xB[k % 2]])
        for h in range(2):
            B.cast(2 * k + h, wdn[:, g4 * 4 + 2 * h:g4 * 4 + 2 * h + 2, :], sv[:, 2 * h:2 * h + 2, :],
                   [xB[k % 2]], [wdnB[g4]])
        k += 1
    xiv = xin.rearrange("(c p) t -> p c t", p=128)
    xov = xout.rearrange("(c p) t -> p c t", p=128)
    PS_SS, PS_UP, PS_DN = 0, (1, 2), (3, 4, 5, 6)
    dn_i = 0
    for t in range(NT):
        t0 = t * TT
        b = t % 2
        x = xs[b]
        B.dma("sp", x[:, :, :], xiv[:, :, t0:t0 + TT], writes=[xB[b]])
        sq = a
        for h in range(2):
            p.add("act", lambda e, o=sq[:, 4 * h:4 * h + 4, :], i=x[:, 4 * h:4 * h + 4, :]:
                  e.activation(out=o, in_=i, func=AF.Square), [xB[b]], aB[4 * h:4 * h + 4])
        for c in range(8):
            p.add("pe", lambda e, c=c: e.matmul(B.ps[PS_SS][:, :], B.ones_bf[:, :], sq[:, c, :],
                                                 start=(c == 0), stop=(c == 7)),
                  [B.onesB, aB[c]], [B.psB[PS_SS]])
        p.add("act", lambda e: e.activation(out=rs[:, :], in_=B.ps[PS_SS][:, :], func=AF.Sqrt,
                                            bias=epsb[:, 0:1], scale=1.0 / D),
              [B.psB[PS_SS], epsB], [rsB])
        p.add("dve", lambda e: e.reciprocal(out=rstd[:, :], in_=rs[:, :]), [rsB], [rstdB])
        for c in range(8):
            p.add("dve", lambda e, c=c, x=x: e.scalar_tensor_tensor(
                out=hn[:, c, :], in0=x[:, c, :], scalar=vecs_sb[:, gcol + c:gcol + c + 1], in1=rstd[:, :],
                op0=ALU.mult, op1=ALU.mult), [xB[b], rstdB], [hnB])
        for half in range(2):
            for jj in range(16):
                j = half * 16 + jj
                pu = PS_UP[j % 2]
                for kc in range(8):
                    p.add("pe", lambda e, kc=kc, j=j, pu=pu: e.matmul(
                        B.ps[pu][:, :], wup[:, kc, j * 128:(j + 1) * 128], hn[:, kc, :],
                        start=(kc == 0), stop=(kc == 7)), [wupB[kc], hnB], [B.psB[pu]])
                p.add("act", lambda e, j=j, pu=pu: e.activation(out=r[j % 2][:, :], in_=B.ps[pu][:, :],
                                                                 func=AF.Relu),
                      [B.psB[pu]], [rB[j % 2]])
                p.add("pool", lambda e, j=j, jj=jj: e.tensor_tensor(out=a[:, jj, :], in0=r[j % 2][:, :],
                                                                     in1=r[j % 2][:, :], op=ALU.mult),
                      [rB[j % 2]], [aB[jj]])
            for o in range(8):
                pd = PS_DN[dn_i % 4]
                dn_i += 1
                for jj in range(16):
                    j = half * 16 + jj
                    p.add("pe", lambda e, o=o, j=j, jj=jj, pd=pd: e.matmul(
                        B.ps[pd][:, :], wdn[:, j, o * 128:(o + 1) * 128], a[:, jj, :],
                        start=(jj == 0), stop=(jj == 15)), [wdnB[j // 4], aB[jj]], [B.psB[pd]])
                p.add("dve", lambda e, o=o, pd=pd, x=x: e.tensor_tensor(
                    out=x[:, o, :], in0=x[:, o, :], in1=B.ps[pd][:, :], op=ALU.add),
                    [xB[b], B.psB[pd]], [xB[b]])
        B.dma("pool", xov[:, :, t0:t0 + TT], x[:, :, :], reads=[xB[b]])
    p.barrier()
    sb.release(m)


class VecPack:
    def __init__(self):
        self.cols = {}
        self.n = 0
        self.data = []

    def add(self, name, v):
        v = np.asarray(v, np.float32).reshape(-1)
        assert v.size % 128 == 0
        nc_ = v.size // 128
        self.cols[name] = self.n
        self.n += nc_
        self.data.append(np.ascontiguousarray(v.reshape(nc_, 128).T))
        return self.cols[name]

    def array(self):
        return np.ascontiguousarray(np.concatenate(self.data, axis=1))


def build_mlp_only(T, ncols, gcol):
    B = Builder(T)
    nc = B.nc
    xT = B.din("xT", [D, T]).ap()
    vecs = B.din("vecs", [128, ncols]).ap()
    w_up = B.din("w_up", [D, DFF]).ap()
    w_down = B.din("w_down", [DFF, D]).ap()
    yT = B.dout("yT", [D, T]).ap()
    vecs_sb = B.sb.alloc([128, ncols], F32, "vecs")
    vB = Buf()
    B.dma("sp", vecs_sb[:, :], vecs[:, :], writes=[vB])
    B.p.barrier()
    mlp_stage(B, xT, yT, w_up, w_down, vecs_sb, gcol)
    with ExitStack() as st:
        B.p.emit(nc, st)
    return B


class Stager:
    def __init__(self, B, tiles, bufs):
        self.B, self.tiles, self.bufs, self.k = B, tiles, bufs, 0

    def load(self, dst_ap, src_ap, stage_view, dstB):
        i = self.k % len(self.tiles)
        sv = stage_view(self.tiles[i])
        self.B.dma("sp", sv, src_ap, writes=[self.bufs[i]])
        self.B.cast(self.k, dst_ap, sv, [self.bufs[i]], [dstB])
        self.k += 1


def bank(B):
    i = B.bank_i % 8
    B.bank_i += 1
    return B.ps[i], B.psB[i]


def act(B, out, in_, func, reads, writes, bias=None, scale=None):
    kw = {}
    if bias is not None:
        kw["bias"] = bias
    if scale is not None:
        kw["scale"] = scale
    B.p.add("act", lambda e: e.activation(out=out, in_=in_, func=func, **kw), reads, writes)


def tt(B, eng, out, in0, in1, op, reads, writes):
    B.p.add(eng, lambda e: e.tensor_tensor(out=out, in0=in0, in1=in1, op=op), reads, writes)


def ts(B, eng, out, in0, s1, s2, op0, op1, reads, writes):
    if op1 is None:
        B.p.add(eng, lambda e: e.tensor_scalar(out=out, in0=in0, scalar1=s1, scalar2=None, op0=op0), reads, writes)
    else:
        B.p.add(eng, lambda e: e.tensor_scalar(out=out, in0=in0, scalar1=s1, scalar2=s2, op0=op0, op1=op1),
                reads, writes)


def stt(B, out, in0, scalar, in1, op0, op1, reads, writes):
    B.p.add("dve", lambda e: e.scalar_tensor_tensor(out=out, in0=in0, scalar=scalar, in1=in1, op0=op0, op1=op1),
            reads, writes)


def mm(B, out, lhsT, rhs, start, stop, reads, writes):
    B.p.add("pe", lambda e: e.matmul(out, lhsT, rhs, start=start, stop=stop), reads, writes)


def rwkv_r1(B, j, layer, xin, W, V, S):
    nc, p, sb = B.nc, B.p, B.sb
    T = B.T
    TT = 512
    NT = T // TT
    vs = B.vecs_sb
    has_vres = j > 0
    m0 = sb.mark()
    wrkv = [sb.alloc([128, 8, D], BF16, f"w{n}") for n in "rkv"]
    wrkvB = [[Buf() for _ in range(2)] for _ in range(3)]
    w1c = sb.alloc([128, 8, 128], BF16, "w1c")
    a1c = sb.alloc([128, 8, 128], BF16, "a1c")
    g1a = sb.alloc([128, 8, 128], BF16, "g1a")
    g1b = sb.alloc([128, 8, 32], BF16, "g1b")
    w2c = sb.alloc([128, D], BF16, "w2c")
    a2c = sb.alloc([128, D], BF16, "a2c")
    g2a = sb.alloc([128, D], BF16, "g2a")
    g2b = sb.alloc([32, D], BF16, "g2b")
    if has_vres:
        v1s = sb.alloc([128, 8, 32], BF16, "v1s")
        v2s = sb.alloc([32, D], BF16, "v2s")
    wsB = Buf()
    xh = sb.alloc([128, 8, TT + 2], F32, "xh")
    xhB = Buf()
    xx = sb.alloc([128, 8, TT], F32, "xx")
    xxB = Buf()
    sq = sb.alloc([128, 8, TT + 2], BF16, "sq")
    sqB = Buf()
    rs = sb.alloc([128, TT + 2], F32, "rs")
    rsB = Buf()
    rstd = sb.alloc([128, TT + 2], F32, "rstd")
    rstdB = Buf()
    epsb = sb.alloc([128, 1], F32, "eps")
    epsB = Buf()
    p.add("pool", lambda e: e.memset(epsb[:, :], RMS_EPS), writes=[epsB])
    xm_off = (sb.ptr + 63) // 64 * 64
    xm = [sb.alloc([128, 8, TT], BF16, f"xm{i}") for i in range(6)]
    xmB = [Buf() for _ in range(6)]
    stg_t = [nc.alloc_sbuf_tensor_at(f"r1stg{j}_{i}", [128, 4096], F32, offset=xm_off + i * 16384) for i in range(2)]
    stgB = [Buf(), Buf()]
    stg = Stager(B, stg_t, stgB)
    names = ["tw", "ta", "tg0", "tg1", "tv"]
    lo = {n: sb.alloc([128, TT], BF16, n) for n in names}
    loB = {n: Buf() for n in names}
    tmpn = ["r", "k", "v", "g", "kk", "rn", "lw0", "lw1", "ic0", "ic1", "t1", "kd0", "kd1", "b0", "b1", "bv",
            "vg", "vf"]
    tm = {n: sb.alloc([128, TT], F32, n) for n in tmpn}
    tmB = {n: Buf() for n in tmpn}
    sqk = sb.alloc([128, TT], BF16, "sqk")
    sqkB = Buf()
    rk = sb.alloc([128, TT], BF16, "rk")
    rkB = Buf()
    vt = sb.alloc([128, 4, 128], F32, "vt")
    vtB = Buf()

    rkv = W["rw_rkv"]
    for mI in range(3):
        src = rkv[j, mI].rearrange("(c p) n -> p c n", p=128)
        for hI in range(2):
            stg.load(wrkv[mI][:, 4 * hI:4 * hI + 4, :], src[:, 4 * hI:4 * hI + 4, :],
                     lambda t: t[:, :].rearrange("p (c n) -> p c n", n=D), wsB)
    for dI in range(2):
        stg.load(w1c[:, :, 64 * dI:64 * dI + 64], W["rw_w1"][j, dI].rearrange("(c p) n -> p c n", p=128),
                 lambda t: t[:, 0:512].rearrange("p (c n) -> p c n", n=64), wsB)
        stg.load(a1c[:, :, 64 * dI:64 * dI + 64], W["rw_a1"][j, dI].rearrange("(c p) n -> p c n", p=128),
                 lambda t: t[:, 0:512].rearrange("p (c n) -> p c n", n=64), wsB)
        stg.load(w2c[64 * dI:64 * dI + 64, :], W["rw_w2"][j, dI], lambda t, dI=dI: t[64 * dI:64 * dI + 64, 0:D], wsB)
        stg.load(a2c[64 * dI:64 * dI + 64, :], W["rw_a2"][j, dI], lambda t, dI=dI: t[64 * dI:64 * dI + 64, 0:D], wsB)
    g1v = W["rw_g1"][j].rearrange("(c p) n -> p c n", p=128)
    stg.load(g1a[:, :, :], g1v[:, :, 0:128], lambda t: t[:, 0:1024].rearrange("p (c n) -> p c n", n=128), wsB)
    stg.load(g1b[:, :, :], g1v[:, :, 128:160], lambda t: t[:, 0:256].rearrange("p (c n) -> p c n", n=32), wsB)
    stg.load(g2a[:, :], W["rw_g2"][j, 0:128, :], lambda t: t[:, 0:D], wsB)
    stg.load(g2b[:, :], W["rw_g2"][j, 128:160, :], lambda t: t[0:32, 0:D], wsB)
    if has_vres:
        stg.load(v1s[:, :, :], W["rw_v1"][j - 1].rearrange("(c p) n -> p c n", p=128),
                 lambda t: t[:, 0:256].rearrange("p (c n) -> p c n", n=32), wsB)
        stg.load(v2s[:, :], W["rw_v2"][j - 1], lambda t: t[0:32, 0:D], wsB)
    p.barrier()

    xiv = xin.rearrange("(c p) t -> p c t", p=128)
    cst = B.consts_sb
    ident = cst[:, B.C["ident"]:B.C["ident"] + 128]
    bones = B.bones_bf
    for t in range(NT):
        t0 = t * TT
        seg = t0 // SEG
        first = (t0 % SEG == 0)
        last = ((t0 + TT) % SEG == 0)
        B.dma("sp", xh[:, :, 1:TT + 1], xiv[:, :, t0:t0 + TT], writes=[xhB])
        if t0 > 0:
            B.dma("sp", xh[:, :, 0:1], xiv[:, :, t0 - 1:t0], writes=[xhB], slow=True)
        else:
            p.add("pool", lambda e: e.memset(xh[:, :, 0:1], 0.0), writes=[xhB])
        if t0 + TT < T:
            B.dma("sp", xh[:, :, TT + 1:TT + 2], xiv[:, :, t0 + TT:t0 + TT + 1], writes=[xhB], slow=True)
        else:
            p.add("pool", lambda e: e.memset(xh[:, :, TT + 1:TT + 2], 0.0), writes=[xhB])
        for hI in range(2):
            act(B, sq[:, 4 * hI:4 * hI + 4, :], xh[:, 4 * hI:4 * hI + 4, :], AF.Square, [xhB], [sqB])
        psA, psAB = bank(B)
        for c in range(8):
            mm(B, psA[:, :], B.ones_bf[:, :], sq[:, c, 0:TT], c == 0, c == 7, [B.onesB, sqB], [psAB])
        psH, psHB = bank(B)
        for c in range(8):
            mm(B, psH[:, 0:2], B.ones_bf[:, :], sq[:, c, TT:TT + 2], c == 0, c == 7, [B.onesB, sqB], [psHB])
        act(B, rs[:, 0:TT], psA[:, :], AF.Sqrt, [psAB, epsB], [rsB], bias=epsb[:, 0:1], scale=1.0 / D)
        act(B, rs[:, TT:TT + 2], psH[:, 0:2], AF.Sqrt, [psHB, epsB], [rsB], bias=epsb[:, 0:1], scale=1.0 / D)
        p.add("dve", lambda e: e.reciprocal(out=rstd[:, :], in_=rs[:, :]), [rsB], [rstdB])
        gc = V["norm_mix_g"][layer]
        for c in range(8):
            stt(B, xh[:, c, :], xh[:, c, :], vs[:, gc + c:gc + c + 1], rstd[:, :], ALU.mult, ALU.mult,
                [xhB, rstdB], [xhB])
        if first and seg > 0:
            ts(B, "dve", xh[:, :, 0:1], xh[:, :, 0:1], B.flags_sb[:, seg:seg + 1], None, ALU.mult, None,
               [xhB], [xhB])
        if last and seg < (T // SEG) - 1:
            ts(B, "dve", xh[:, :, TT + 1:TT + 2], xh[:, :, TT + 1:TT + 2], B.flags_sb[:, seg + 1:seg + 2], None,
               ALU.mult, None, [xhB], [xhB])
        tt(B, "pool", xx[:, :, :], xh[:, :, 0:TT], xh[:, :, 2:TT + 2], ALU.add, [xhB], [xxB])
        stt(B, xx[:, :, :], xx[:, :, :], 0.5, xh[:, :, 1:TT + 1], ALU.mult, ALU.subtract, [xxB, xhB], [xxB])
        mc = V["rw_mix"][j]
        for mI in range(6):
            for c in range(8):
                stt(B, xm[mI][:, c, :], xx[:, c, :], vs[:, mc + 8 * mI + c:mc + 8 * mI + c + 1], xh[:, c, 1:TT + 1],
                    ALU.mult, ALU.add, [xxB, xhB], [xmB[mI]])
        XR, XK, XV, XW, XA, XG = range(6)
        ps_, psB_ = bank(B)
        for c in range(8):
            mm(B, ps_[:, :], w1c[:, c, :], xm[XW][:, c, :], c == 0, c == 7, [xmB[XW]], [psB_])
        act(B, lo["tw"][:, :], ps_[:, :], AF.Tanh, [psB_], [loB["tw"]])
        ps_, psB_ = bank(B)
        for c in range(8):
            mm(B, ps_[:, :], a1c[:, c, :], xm[XA][:, c, :], c == 0, c == 7, [xmB[XA]], [psB_])
        act(B, lo["ta"][:, :], ps_[:, :], AF.Copy, [psB_], [loB["ta"]])
        ps_, psB_ = bank(B)
        for c in range(8):
            mm(B, ps_[:, :], g1a[:, c, :], xm[XG][:, c, :], c == 0, c == 7, [xmB[XG]], [psB_])
        act(B, lo["tg0"][:, :], ps_[:, :], AF.Sigmoid, [psB_], [loB["tg0"]])
        ps_, psB_ = bank(B)
        for c in range(8):
            mm(B, ps_[0:32, :], g1b[:, c, :], xm[XG][:, c, :], c == 0, c == 7, [xmB[XG]], [psB_])
        act(B, lo["tg1"][0:32, :], ps_[0:32, :], AF.Sigmoid, [psB_], [loB["tg1"]])
        if has_vres:
            ps_, psB_ = bank(B)
            for c in range(8):
                mm(B, ps_[0:32, :], v1s[:, c, :], xm[XV][:, c, :], c == 0, c == 7, [xmB[XV]], [psB_])
            act(B, lo["tv"][0:32, :], ps_[0:32, :], AF.Copy, [psB_], [loB["tv"]])
        for oc in range(8):
            osl = slice(oc * 128, (oc + 1) * 128)
            for mI, nm in enumerate("rkv"):
                ps_, psB_ = bank(B)
                for c in range(8):
                    mm(B, ps_[:, :], wrkv[mI][:, c, osl], xm[mI][:, c, :], c == 0, c == 7, [xmB[mI]], [psB_])
                if mI == 1:
                    p.add("dve", lambda e, ps_=ps_: e.tensor_copy(out=tm["k"][:, :], in_=ps_[:, :]),
                          [psB_], [tmB["k"]])
                else:
                    act(B, tm[nm][:, :], ps_[:, :], AF.Copy, [psB_], [tmB[nm]])
            for dI in range(2):
                dsl = slice(64 * dI, 64 * dI + 64)
                ps_, psB_ = bank(B)
                mm(B, ps_[:, :], w2c[dsl, osl], lo["tw"][dsl, :], True, True, [loB["tw"]], [psB_])
                w0c = V["rw_w0"][j][dI] + oc
                act(B, tm[f"lw{dI}"][:, :], ps_[:, :], AF.Sigmoid, [psB_], [tmB[f"lw{dI}"]],
                    bias=vs[:, w0c:w0c + 1], scale=1.0)
                ts(B, "pool", tm[f"lw{dI}"][:, :], tm[f"lw{dI}"][:, :], -0.6065306597126334, None, ALU.mult, None,
                   [tmB[f"lw{dI}"]], [tmB[f"lw{dI}"]])
                ps_, psB_ = bank(B)
                mm(B, ps_[:, :], a2c[dsl, osl], lo["ta"][dsl, :], True, True, [loB["ta"]], [psB_])
                a0c = V["rw_a0"][j][dI] + oc
                act(B, tm[f"ic{dI}"][:, :], ps_[:, :], AF.Sigmoid, [psB_], [tmB[f"ic{dI}"]],
                    bias=vs[:, a0c:a0c + 1], scale=1.0)
            ps_, psB_ = bank(B)
            mm(B, ps_[:, :], g2a[:, osl], lo["tg0"][:, :], True, False, [loB["tg0"]], [psB_])
            mm(B, ps_[:, :], g2b[0:32, osl], lo["tg1"][0:32, :], False, True, [loB["tg1"]], [psB_])
            act(B, tm["g"][:, :], ps_[:, :], AF.Copy, [psB_], [tmB["g"]])
            if has_vres:
                ps_, psB_ = bank(B)
                mm(B, ps_[:, :], v2s[0:32, osl], lo["tv"][0:32, :], True, True, [loB["tv"]], [psB_])
                v0c = V["rw_v0"][j - 1] + oc
                act(B, tm["vg"][:, :], ps_[:, :], AF.Sigmoid, [psB_], [tmB["vg"]], bias=vs[:, v0c:v0c + 1],
                    scale=1.0)
                B.dma("sp", tm["vf"][:, :], S["vfirst"][osl, t0:t0 + TT], writes=[tmB["vf"]])
                tt(B, "pool", tm["vf"][:, :], tm["vf"][:, :], tm["v"][:, :], ALU.subtract,
                   [tmB["vf"], tmB["v"]], [tmB["vf"]])
                tt(B, "pool", tm["vf"][:, :], tm["vf"][:, :], tm["vg"][:, :], ALU.mult,
                   [tmB["vf"], tmB["vg"]], [tmB["vf"]])
                tt(B, "pool", tm["v"][:, :], tm["v"][:, :], tm["vf"][:, :], ALU.add,
                   [tmB["vf"], tmB["v"]], [tmB["v"]])
            else:
                B.dma("sp", S["vfirst"][osl, t0:t0 + TT], tm["v"][:, :], reads=[tmB["v"]])
            kkc = V["rw_kk"][j] + oc
            ts(B, "pool", tm["kk"][:, :], tm["k"][:, :], vs[:, kkc:kkc + 1], None, ALU.mult, None,
               [tmB["k"]], [tmB["kk"]])
            act(B, sqk[:, :], tm["kk"][:, :], AF.Square, [tmB["kk"]], [sqkB])
            ps_, psB_ = bank(B)
            mm(B, ps_[:, :], bones[:, :], sqk[:, :], True, True, [sqkB, B.bonesB], [psB_])
            act(B, tm["rn"][:, :], ps_[:, :], AF.Sqrt, [psB_], [tmB["rn"]])
            ts(B, "dve", tm["rn"][:, :], tm["rn"][:, :], 1e-12, None, ALU.max, None, [tmB["rn"]], [tmB["rn"]])
            p.add("dve", lambda e: e.reciprocal(out=tm["rn"][:, :], in_=tm["rn"][:, :]), [tmB["rn"]], [tmB["rn"]])
            tt(B, "dve", tm["kk"][:, :], tm["kk"][:, :], tm["rn"][:, :], ALU.mult, [tmB["kk"], tmB["rn"]],
               [tmB["kk"]])
            kac = V["rw_ka"][j] + oc
            for dI in range(2):
                ic, kd, bb = tm[f"ic{dI}"], tm[f"kd{dI}"], tm[f"b{dI}"]
                icB, kdB, bbB = tmB[f"ic{dI}"], tmB[f"kd{dI}"], tmB[f"b{dI}"]
                ts(B, "pool", tm["t1"][:, :], ic[:, :], 1.0, vs[:, kac:kac + 1], ALU.subtract, ALU.mult,
                   [icB], [tmB["t1"]])
                stt(B, kd[:, :], tm["t1"][:, :], 1.0, tm["k"][:, :], ALU.add, ALU.mult, [tmB["t1"], tmB["k"]], [kdB])
                tt(B, "pool", bb[:, :], tm["kk"][:, :], ic[:, :], ALU.mult, [tmB["kk"], icB], [bbB])
            tt(B, "pool", tm["t1"][:, :], tm["kd0"][:, :], tm["kd1"][:, :], ALU.add, [tmB["kd0"], tmB["kd1"]],
               [tmB["t1"]])
            rkc = V["rw_rk"][j] + oc
            stt(B, rk[:, :], tm["t1"][:, :], vs[:, rkc:rkc + 1], tm["r"][:, :], ALU.mult, ALU.mult,
                [tmB["t1"], tmB["r"]], [rkB])
            ps_, psB_ = bank(B)
            mm(B, ps_[:, :], bones[:, :], rk[:, :], True, True, [rkB, B.bonesB], [psB_])
            tt(B, "dve", tm["bv"][:, :], ps_[:, :], tm["v"][:, :], ALU.mult, [psB_, tmB["v"]], [tmB["bv"]])
            ps_, psB_ = bank(B)
            for s4 in range(4):
                p.add("pe", lambda e, ps_=ps_, s4=s4: e.transpose(ps_[:, s4 * 128:(s4 + 1) * 128],
                                                                 tm["v"][:, s4 * 128:(s4 + 1) * 128], ident),
                      [tmB["v"]], [psB_])
            act(B, vt[:, :, :], ps_[:, :].rearrange("p (s c) -> p s c", c=128), AF.Copy, [psB_], [vtB])
            B.dma("sp", S["vtok"][t0:t0 + TT, osl].rearrange("(s p) c -> p s c", p=128), vt[:, :, :], reads=[vtB])
            for nm in ("r", "kk", "g", "bv", "lw0", "lw1", "kd0", "kd1", "b0", "b1"):
                B.dma("sp", S[nm][osl, t0:t0 + TT], tm[nm][:, :], reads=[tmB[nm]])
    p.barrier()
    sb.release(m0)


def rwkv_r2(B, S):
    nc, p, sb = B.nc, B.p, B.sb
    T = B.T
    TT, L, NQ = 512, 64, 8
    NT = T // TT
    m0 = sb.mark()
    cst = B.consts_sb
    C = B.C
    MASK = {k: cst[:, C[k]:C[k] + 128] for k in ("LT", "LE", "GT", "GE")}
    identbf = B.ident_bf
    blk_sb = sb.alloc([128, 768], F32, "blk")
    B.dma("sp", blk_sb[:, :], B.blkm_dram[:, :], writes=[Buf()])
    BLK = [blk_sb[:, 128 * li:128 * li + 128] for li in range(6)]
    onesf = sb.alloc([128, L], F32, "onesf")
    p.add("pool", lambda e: e.memset(onesf[:, :], 1.0))
    nseg = T // SEG

    class CS:
        pass

    def mk(tag):
        c_ = CS()
        inn = ["r", "kk", "lw", "kd", "b"]
        c_.inp = {n: sb.alloc([128, NQ, L], F32, n + tag) for n in inn}
        c_.inpB = {n: Buf() for n in inn}
        c_.Vf = sb.alloc([128, NQ, L], F32, "Vf" + tag)
        c_.VfB = Buf()
        c_.Vs = sb.alloc([128, NQ, L], BF16, "Vs" + tag)
        c_.VsB = Buf()
        f32n = ["P", "E", "Sx", "Si", "epos", "eneg", "egm", "er"]
        c_.ft = {n: sb.alloc([128, NQ, L], F32, n + tag) for n in f32n}
        c_.ftB = {n: Buf() for n in f32n}
        c_.wl = sb.alloc([128, NQ], F32, "wl" + tag)
        c_.wlB = Buf()
        bdn = ["bdr", "bda", "bdb", "bdk", "bdbw", "bdkw"]
        c_.bd = {n: sb.alloc([128, NQ, 128], BF16, n + tag) for n in bdn}
        c_.bdB = {n: Buf() for n in bdn}
        for n in bdn:
            p.add("pool", lambda e, t_=c_.bd[n]: e.memset(t_[:, :, :], 0.0), writes=[c_.bdB[n]])
        c_.X0 = sb.alloc([128, NQ, 128], BF16, "X0" + tag)
        c_.Y0 = sb.alloc([128, NQ, 128], BF16, "Y0" + tag)
        c_.X0B, c_.Y0B = Buf(), Buf()
        c_.xo = [sb.alloc([128, NQ, 128], BF16, f"xo{i}" + tag) for i in range(2)]
        c_.ao = [sb.alloc([128, NQ, 128], BF16, f"ao{i}" + tag) for i in range(2)]
        c_.xoB = [Buf(), Buf()]
        c_.aoB = [Buf(), Buf()]
        c_.Et = [sb.alloc([128, NQ, 128], BF16, f"E{i}" + tag) for i in range(2)]
        c_.Dt = [sb.alloc([128, NQ, 128], BF16, f"D{i}" + tag) for i in range(2)]
        c_.EtB = [Buf(), Buf()]
        c_.DtB = [Buf(), Buf()]
        c_.Qt = sb.alloc([128, NQ, 128], BF16, "Qt" + tag)
        c_.Rt = sb.alloc([128, NQ, 128], BF16, "Rt" + tag)
        c_.QtB, c_.RtB = Buf(), Buf()
        amn = ["ArbT", "AakT", "ArkT", "bWT", "kWT"]
        c_.am = {n: sb.alloc([128, NQ, 128], BF16, n + tag) for n in amn}
        c_.amB = {n: Buf() for n in amn}
        c_.ST = sb.alloc([128, L], F32, "ST" + tag)
        c_.STb = sb.alloc([128, L], BF16, "STb" + tag)
        c_.STB, c_.STbB = Buf(), Buf()
        c_.RHS = sb.alloc([128, L], BF16, "RHS" + tag)
        c_.U = sb.alloc([128, L], BF16, "U" + tag)
        c_.RHSB, c_.UB = Buf(), Buf()
        c_.yt = sb.alloc([128, NQ, L], F32, "yt" + tag)
        c_.ytB = Buf()
        return c_

    chains = [mk("f"), mk("b")]
    p.barrier()

    def unit(cs, d, c, t):
        fwd = (d == 0)
        M_strict, M_incl, M_strictT = (MASK["LT"], MASK["LE"], MASK["GT"]) if fwd else \
            (MASK["GT"], MASK["GE"], MASK["LT"])
        csl = slice(c * 128, (c + 1) * 128)
        t0 = t * TT
        seg = t0 // SEG
        I_, IB, ft, ftB, bd, bdB, am, amB = cs.inp, cs.inpB, cs.ft, cs.ftB, cs.bd, cs.bdB, cs.am, cs.amB
        ST, STb, STB, STbB = cs.ST, cs.STb, cs.STB, cs.STbB
        fl = None
        if fwd and t0 % SEG == 0 and seg > 0:
            fl = seg
        if (not fwd) and (t0 + TT) % SEG == 0 and seg < nseg - 1:
            fl = seg + 1
        if fl is not None:
            ts(B, "dve", ST[:, :], ST[:, :], B.flags_sb[:, fl:fl + 1], None, ALU.mult, None, [STB], [STB])
            ts(B, "dve", STb[:, :], STb[:, :], B.flags_sb[:, fl:fl + 1], None, ALU.mult, None, [STbB], [STbB])
        for n, key in (("r", "r"), ("kk", "kk"), ("lw", f"lw{d}"), ("kd", f"kd{d}"), ("b", f"b{d}")):
            B.dma("sp", I_[n][:, :, :].rearrange("p q l -> p (q l)"), S[key][csl, t0:t0 + TT], writes=[IB[n]])
        for h in range(2):
            col = (2 * c + h) * 64
            B.dma("sp", cs.Vf[h * 64:(h + 1) * 64, :, :],
                  S["vtok"][t0:t0 + TT, col:col + 64].rearrange("(q j) v -> j q v", j=L), writes=[cs.VfB])
        yield
        act(B, cs.Vs[:, :, :], cs.Vf[:, :, :], AF.Copy, [cs.VfB], [cs.VsB])
        lw = I_["lw"]
        for q in range(NQ):
            p.add("dve", lambda e, q=q: e.tensor_tensor_scan(
                out=ft["P"][:, q, :], data0=onesf[:, :], data1=lw[:, q, :], initial=0.0,
                op0=ALU.mult, op1=ALU.add), [IB["lw"]], [ftB["P"]])
        yield
        tot = ft["P"][:, :, L - 1:L]
        tt(B, "pool", ft["E"][:, :, :], ft["P"][:, :, :], lw[:, :, :], ALU.subtract, [ftB["P"], IB["lw"]], [ftB["E"]])
        tt(B, "dve", ft["Sx"][:, :, :], tot.to_broadcast([128, NQ, L]), ft["P"][:, :, :], ALU.subtract,
           [ftB["P"]], [ftB["Sx"]])
        if fwd:
            G, GB, Gm, GmB, R, RB = ft["P"], ftB["P"], ft["E"], ftB["E"], ft["Sx"], ftB["Sx"]
        else:
            tt(B, "pool", ft["Si"][:, :, :], ft["Sx"][:, :, :], lw[:, :, :], ALU.add, [ftB["Sx"], IB["lw"]],
               [ftB["Si"]])
            G, GB, Gm, GmB, R, RB = ft["Si"], ftB["Si"], ft["Sx"], ftB["Sx"], ft["E"], ftB["E"]
        yield
        act(B, ft["epos"][:, :, :], G[:, :, :], AF.Exp, [GB], [ftB["epos"]])
        act(B, ft["eneg"][:, :, :], G[:, :, :], AF.Exp, [GB], [ftB["eneg"]], scale=-1.0)
        yield
        act(B, ft["egm"][:, :, :], Gm[:, :, :], AF.Exp, [GmB], [ftB["egm"]])
        act(B, ft["er"][:, :, :], R[:, :, :], AF.Exp, [RB], [ftB["er"]])
        act(B, cs.wl[:, :], ft["P"][:, :, L - 1], AF.Exp, [ftB["P"]], [cs.wlB])
        yield
        k2 = 0
        for h in range(2):
            hs = slice(h * 64, (h + 1) * 64)
            stt(B, bd["bda"][hs, :, hs], I_["kk"][hs, :, :], -1.0, ft["egm"][hs, :, :], ALU.mult, ALU.mult,
                [IB["kk"], ftB["egm"]], [bdB["bda"]])
            for dst, a_, e_ in (("bdr", "r", "epos"), ("bdb", "b", "eneg"), ("bdk", "kd", "eneg"),
                                ("bdbw", "b", "er"), ("bdkw", "kd", "er")):
                eng = "dve" if k2 % 2 == 0 else "pool"
                k2 += 1
                tt(B, eng, bd[dst][hs, :, hs], I_[a_][hs, :, :], ft[e_][hs, :, :], ALU.mult,
                   [IB[a_], ftB[e_]], [bdB[dst]])
            yield

        def grp(lhs, lhsB, rhs, rhsB, evac, g):
            ps_, psB_ = bank(B)
            for qq in range(4):
                q = g * 4 + qq
                rr_ = rhs if rhs is identbf else None
                mm(B, ps_[:, qq * 128:(qq + 1) * 128], lhs[:, q, :],
                   (identbf[:, :] if rhs is identbf else rhs[:, q, :]), True, True,
                   [lhsB] + ([] if rhs is identbf else [rhsB]), [psB_])
            evac(g, ps_[:, :].rearrange("p (q c) -> p q c", c=128), psB_)

        def ev_mask(dst, dstB, mask):
            return lambda g, pv, pB: tt(B, "dve", dst[:, g * 4:g * 4 + 4, :], pv,
                                        mask.unsqueeze(1).to_broadcast([128, 4, 128]), ALU.mult, [pB], [dstB])

        def ev_act(dst, dstB):
            return lambda g, pv, pB: act(B, dst[:, g * 4:g * 4 + 4, :], pv, AF.Copy, [pB], [dstB])

        def ev_dve(dst, dstB):
            return lambda g, pv, pB: p.add("dve", lambda e: e.tensor_copy(out=dst[:, g * 4:g * 4 + 4, :], in_=pv),
                                           [pB], [dstB])

        def ev_add(dst, dstB, old, oldB):
            return lambda g, pv, pB: tt(B, "dve", dst[:, g * 4:g * 4 + 4, :], pv, old[:, g * 4:g * 4 + 4, :],
                                        ALU.add, [pB, oldB], [dstB])

        for (lh, rh, mk_, dst, dstB) in (
                ("bdb", "bda", M_strict, cs.X0, cs.X0B), ("bda", "bdb", M_strictT, cs.Y0, cs.Y0B),
                ("bdb", "bdr", M_incl, am["ArbT"], amB["ArbT"]), ("bdk", "bda", M_strict, am["AakT"], amB["AakT"]),
                ("bdk", "bdr", M_incl, am["ArkT"], amB["ArkT"])):
            for g in range(2):
                grp(bd[lh], bdB[lh], bd[rh], bdB[rh], ev_mask(dst, dstB, mk_), g)
                yield
        for nm, src_ in (("bWT", "bdbw"), ("kWT", "bdkw")):
            for g in range(2):
                grp(bd[src_], bdB[src_], identbf, None, ev_act(am[nm], amB[nm]), g)
                yield
        idb = identbf[:, :].unsqueeze(1).to_broadcast([128, NQ, 128])

        def offs(li, slot):
            mk2 = BLK[li].unsqueeze(1).to_broadcast([128, NQ, 128])
            tt(B, "pool", cs.xo[slot][:, :, :], cs.X0[:, :, :], mk2, ALU.mult, [cs.X0B], [cs.xoB[slot]])
            tt(B, "pool", cs.ao[slot][:, :, :], cs.Y0[:, :, :], mk2, ALU.mult, [cs.Y0B], [cs.aoB[slot]])

        offs(0, 0)
        cur = 0
        tt(B, "pool", cs.Et[0][:, :, :], cs.xo[0][:, :, :], idb, ALU.add, [cs.xoB[0]], [cs.EtB[0]])
        tt(B, "pool", cs.Dt[0][:, :, :], cs.ao[0][:, :, :], idb, ALU.add, [cs.aoB[0]], [cs.DtB[0]])
        offs(1, 1)
        yield
        for li in range(1, 6):
            lastl = (li == 5)
            nxt = 1 - cur
            sl_ = li % 2
            xo, xoB, ao, aoB = cs.xo[sl_], cs.xoB[sl_], cs.ao[sl_], cs.aoB[sl_]
            E_, EB_, D_, DB_ = cs.Et[cur], cs.EtB[cur], cs.Dt[cur], cs.DtB[cur]
            for g in range(2):
                grp(ao, aoB, E_, EB_, ev_act(cs.Qt, cs.QtB), g)
                yield
            if not lastl:
                for g in range(2):
                    grp(xo, xoB, D_, DB_, ev_dve(cs.Rt, cs.RtB), g)
                    yield
            for g in range(2):
                grp(D_, DB_, cs.Qt, cs.QtB, ev_add(cs.Et[nxt], cs.EtB[nxt], E_, EB_), g)
                yield
            if not lastl:
                for g in range(2):
                    grp(E_, EB_, cs.Rt, cs.RtB, ev_add(cs.Dt[nxt], cs.DtB[nxt], D_, DB_), g)
                    yield
                offs(li + 1, (li + 1) % 2)
            cur = nxt
        Z, ZB = cs.Et[cur], cs.EtB[cur]
        Vs, VsB, RHS, U, RHSB, UB, wl, wlB = cs.Vs, cs.VsB, cs.RHS, cs.U, cs.RHSB, cs.UB, cs.wl, cs.wlB
        qs = list(range(NQ)) if fwd else list(range(NQ - 1, -1, -1))
        for q in qs:
            ps1, ps1B = bank(B)
            mm(B, ps1[:, 0:L], bd["bda"][:, q, :], STb[:, :], True, False, [bdB["bda"], STbB], [ps1B])
            mm(B, ps1[:, 0:L], am["AakT"][:, q, :], Vs[:, q, :], False, True, [amB["AakT"], VsB], [ps1B])
            act(B, RHS[:, :], ps1[:, 0:L], AF.Copy, [ps1B], [RHSB])
            yield
            ps2, ps2B = bank(B)
            mm(B, ps2[:, 0:L], Z[:, q, :], RHS[:, :], True, True, [ZB, RHSB], [ps2B])
            p.add("dve", lambda e, ps2=ps2: e.tensor_copy(out=U[:, :], in_=ps2[:, 0:L]), [ps2B], [UB])
            yield
            ps3, ps3B = bank(B)
            mm(B, ps3[:, 0:L], bd["bdr"][:, q, :], STb[:, :], True, False, [bdB["bdr"], STbB], [ps3B])
            mm(B, ps3[:, 0:L], am["ArbT"][:, q, :], U[:, :], False, False, [amB["ArbT"], UB], [ps3B])
            mm(B, ps3[:, 0:L], am["ArkT"][:, q, :], Vs[:, q, :], False, True, [amB["ArkT"], VsB], [ps3B])
            act(B, cs.yt[:, q, :], ps3[:, 0:L], AF.Copy, [ps3B], [cs.ytB])
            ps4, ps4B = bank(B)
            mm(B, ps4[:, 0:L], am["bWT"][:, q, :], U[:, :], True, False, [amB["bWT"], UB], [ps4B])
            mm(B, ps4[:, 0:L], am["kWT"][:, q, :], Vs[:, q, :], False, True, [amB["kWT"], VsB], [ps4B])
            stt(B, STb[:, :], ST[:, :], wl[:, q:q + 1], ps4[:, 0:L], ALU.mult, ALU.add, [STB, wlB, ps4B], [STbB])
            stt(B, ST[:, :], ST[:, :], wl[:, q:q + 1], ps4[:, 0:L], ALU.mult, ALU.add, [STB, wlB, ps4B], [STB])
            yield
        for h in range(2):
            col = (2 * c + h) * 64
            B.dma("pool", S[f"ytok{d}"][t0:t0 + TT, col:col + 64].rearrange("(q t) v -> t q v", t=L),
                  cs.yt[h * 64:(h + 1) * 64, :, :], reads=[cs.ytB])
        yield

    for c in range(8):
        for cs in chains:
            p.add("pool", lambda e, cs=cs: e.memset(cs.ST[:, :], 0.0), writes=[cs.STB])
            p.add("pool", lambda e, cs=cs: e.memset(cs.STb[:, :], 0.0), writes=[cs.STbB])
        for k in range(NT):
            gens = [unit(chains[0], 0, c, k), unit(chains[1], 1, c, NT - 1 - k)]
            live = [True, True]
            while any(live):
                for gi, g_ in enumerate(gens):
                    if live[gi]:
                        try:
                            next(g_)
                        except StopIteration:
                            live[gi] = False
    p.barrier()
    sb.release(m0)


GN_EPS = 64e-5


def rwkv_r3(B, j, xin, xout, W, V, S):
    nc, p, sb = B.nc, B.p, B.sb
    T = B.T
    TT = 512
    NT = T // TT
    vs = B.vecs_sb
    m0 = sb.mark()
    cst = B.consts_sb
    ident = cst[:, B.C["ident"]:B.C["ident"] + 128]
    wo = sb.alloc([128, 8, D], BF16, "wo")
    woB = Buf()
    stg_t = [sb.alloc([128, 4096], F32, "stg") for _ in range(2)]
    stg = Stager(B, stg_t, [Buf(), Buf()])
    src = W["rw_o"][j].rearrange("(c p) n -> p c n", p=128)
    for hI in range(2):
        stg.load(wo[:, 4 * hI:4 * hI + 4, :], src[:, 4 * hI:4 * hI + 4, :],
                 lambda t: t[:, :].rearrange("p (c n) -> p c n", n=D), woB)
    yin = [[sb.alloc([128, 16, 64], F32, f"y{d}") for d in range(2)] for _ in range(2)]
    yinB = [[Buf(), Buf()] for _ in range(2)]
    ys = sb.alloc([128, 16, 64], F32, "ys")
    ysB = Buf()
    sqc = sb.alloc([128, 16, 64], F32, "sqc")
    sqcB = Buf()
    yn = sb.alloc([128, 16, 64], F32, "yn")
    ynB = Buf()
    st1 = sb.alloc([128, 16], F32, "st1")
    st2 = sb.alloc([128, 16], F32, "st2")
    st1B, st2B = Buf(), Buf()
    gne = sb.alloc([128, 1], F32, "gne")
    gneB = Buf()
    p.add("pool", lambda e: e.memset(gne[:, :], GN_EPS), writes=[gneB])
    zt = sb.alloc([128, 8, TT], F32, "zt")
    ztB = [Buf() for _ in range(8)]
    zb = sb.alloc([128, 8, TT], BF16, "zb")
    zbB = [Buf() for _ in range(8)]
    bvt = [sb.alloc([128, TT], F32, "bvt") for _ in range(2)]
    gt = [sb.alloc([128, TT], F32, "gt") for _ in range(2)]
    bvB = [Buf(), Buf()]
    gB = [Buf(), Buf()]
    xt = sb.alloc([128, 8, TT], F32, "xt")
    xtB = Buf()
    p.barrier()
    xiv = xin.rearrange("(c p) t -> p c t", p=128)
    xov = xout.rearrange("(c p) t -> p c t", p=128)
    lg, lb = V["rw_lnx_g"][j], V["rw_lnx_b"][j]
    k = 0
    for t in range(NT):
        t0 = t * TT
        B.dma("sp", xt[:, :, :], xiv[:, :, t0:t0 + TT], writes=[xtB])
        for s4 in range(4):
            bi = k % 2
            k += 1
            r0 = t0 + s4 * 128
            for d in range(2):
                B.dma("sp", yin[bi][d][:, :, :].rearrange("p h v -> p (h v)"), S[f"ytok{d}"][r0:r0 + 128, :],
                      writes=[yinB[bi][d]])
            tt(B, "pool", ys[:, :, :], yin[bi][0][:, :, :], yin[bi][1][:, :, :], ALU.add,
               [yinB[bi][0], yinB[bi][1]], [ysB])
            p.add("dve", lambda e: e.tensor_reduce(out=st1[:, :], in_=ys[:, :, :], axis=AX.X, op=ALU.add),
                  [ysB], [st1B])
            ts(B, "pool", st1[:, :], st1[:, :], -1.0 / 64, None, ALU.mult, None, [st1B], [st1B])
            tt(B, "dve", ys[:, :, :], ys[:, :, :], st1[:, :].unsqueeze(2).to_broadcast([128, 16, 64]), ALU.add,
               [ysB, st1B], [ysB])
            act(B, sqc[:, :, :], ys[:, :, :], AF.Square, [ysB], [sqcB])
            p.add("dve", lambda e: e.tensor_reduce(out=st2[:, :], in_=sqc[:, :, :], axis=AX.X, op=ALU.add),
                  [sqcB], [st2B])
            act(B, st2[:, :], st2[:, :], AF.Sqrt, [st2B, gneB], [st2B], bias=gne[:, 0:1], scale=1.0 / 64)
            p.add("dve", lambda e: e.reciprocal(out=st2[:, :], in_=st2[:, :]), [st2B], [st2B])
            tt(B, "dve", yn[:, :, :], ys[:, :, :], st2[:, :].unsqueeze(2).to_broadcast([128, 16, 64]), ALU.mult,
               [ysB, st2B], [ynB])
            ynf = yn[:, :, :].rearrange("p h v -> p (h v)")
            for g2 in range(2):
                ps_, psB_ = bank(B)
                for o4 in range(4):
                    oc = g2 * 4 + o4
                    p.add("pe", lambda e, ps_=ps_, o4=o4, oc=oc: e.transpose(
                        ps_[:, o4 * 128:(o4 + 1) * 128], ynf[:, oc * 128:(oc + 1) * 128], ident), [ynB], [psB_])
                for o4 in range(4):
                    oc = g2 * 4 + o4
                    act(B, zt[:, oc, s4 * 128:(s4 + 1) * 128], ps_[:, o4 * 128:(o4 + 1) * 128], AF.Identity,
                        [psB_], [ztB[oc]], bias=vs[:, lb + oc:lb + oc + 1], scale=vs[:, lg + oc:lg + oc + 1])
        for oc in range(8):
            osl = slice(oc * 128, (oc + 1) * 128)
            bi = oc % 2
            B.dma("sp", bvt[bi][:, :], S["bv"][osl, t0:t0 + TT], writes=[bvB[bi]])
            B.dma("sp", gt[bi][:, :], S["g"][osl, t0:t0 + TT], writes=[gB[bi]])
            tt(B, "pool", zt[:, oc, :], zt[:, oc, :], bvt[bi][:, :], ALU.add, [ztB[oc], bvB[bi]], [ztB[oc]])
            tt(B, "dve", zb[:, oc, :], zt[:, oc, :], gt[bi][:, :], ALU.mult, [ztB[oc], gB[bi]], [zbB[oc]])
        for oc in range(8):
            ps_, psB_ = bank(B)
            for c in range(8):
                mm(B, ps_[:, :], wo[:, c, oc * 128:(oc + 1) * 128], zb[:, c, :], c == 0, c == 7, [woB, zbB[c]],
                   [psB_])
            tt(B, "dve", xt[:, oc, :], xt[:, oc, :], ps_[:, :], ALU.add, [xtB, psB_], [xtB])
        B.dma("pool", xov[:, :, t0:t0 + TT], xt[:, :, :], reads=[xtB])
    p.barrier()
    sb.release(m0)


CONST_LAYOUT = {"ident": 0, "LT": 128, "LE": 256, "GT": 384, "GE": 512, "bones": 640}
NCONST = 768


def host_consts():
    pp = np.arange(128)[:, None]
    ff = np.arange(128)[None, :]
    out = np.zeros((128, NCONST), np.float32)
    out[:, 0:128] = (pp == ff)
    out[:, 128:256] = (pp % 64 < ff % 64)
    out[:, 256:384] = (pp % 64 <= ff % 64)
    out[:, 384:512] = (pp % 64 > ff % 64)
    out[:, 512:640] = (pp % 64 >= ff % 64)
    out[:, 640:768] = (pp // 64 == ff // 64)
    return out


def host_blk_masks():
    pp = np.arange(128)[:, None]
    ff = np.arange(128)[None, :]
    out = np.zeros((128, 768), np.float32)
    for li, s in enumerate((1, 2, 4, 8, 16, 32)):
        out[:, 128 * li:128 * li + 128] = ((pp // 64 == ff // 64) & (pp // (2 * s) == ff // (2 * s))
                                           & (pp // s != ff // s))
    return out


def setup_common(B, ncols):
    nc, p, sb = B.nc, B.p, B.sb
    B.bank_i = 0
    B.C = CONST_LAYOUT
    consts = B.din("consts", [128, NCONST]).ap()
    flags = B.din("flags", [128, 8]).ap()
    B.blkm_dram = B.din("blkm", [128, 768]).ap()
    vecs = B.din("vecs", [128, ncols]).ap()
    B.consts_sb = sb.alloc([128, NCONST], F32, "consts")
    B.flags_sb = sb.alloc([128, 8], F32, "flags")
    B.vecs_sb = sb.alloc([128, ncols], F32, "vecs")
    B.ident_bf = sb.alloc([128, 128], BF16, "identbf")
    B.bones_bf = sb.alloc([128, 128], BF16, "bonesbf")
    B.bonesB = Buf()
    cB = Buf()
    B.dma("sp", B.consts_sb[:, :], consts[:, :], writes=[cB])
    B.dma("sp", B.flags_sb[:, :], flags[:, :], writes=[Buf()])
    B.dma("sp", B.vecs_sb[:, :], vecs[:, :], writes=[Buf()])
    p.add("dve", lambda e: e.tensor_copy(out=B.ident_bf[:, :], in_=B.consts_sb[:, 0:128]), [cB], [Buf()])
    p.add("dve", lambda e: e.tensor_copy(out=B.bones_bf[:, :], in_=B.consts_sb[:, 640:768]), [cB], [B.bonesB])
    p.barrier()


RW_SCRATCH_F = ["r", "kk", "g", "bv", "lw0", "lw1", "kd0", "kd1", "b0", "b1", "vfirst"]


def alloc_rwkv_scratch(B):
    T = B.T
    S = {n: B.dscr("s_" + n, [D, T]).ap() for n in RW_SCRATCH_F}
    for n in ("vtok", "ytok0", "ytok1"):
        S[n] = B.dscr("s_" + n, [T, D]).ap()
    return S


def pack_vectors(inp):
    vp = VecPack()
    V = {}
    V["norm_mix_g"] = [vp.add(f"nmg{l}", inp["norm_mix_g"][l]) for l in range(inp["norm_mix_g"].shape[0])]
    V["norm_mlp_g"] = [vp.add(f"nlg{l}", inp["norm_mlp_g"][l]) for l in range(inp["norm_mlp_g"].shape[0])]
    nrw = inp["rw_mix"].shape[0]
    V["rw_mix"] = [vp.add(f"mix{j}", inp["rw_mix"][j]) for j in range(nrw)]
    V["rw_w0"] = [[vp.add(f"w0{j}{d}", inp["rw_w0"][j, d]) for d in range(2)] for j in range(nrw)]
    V["rw_a0"] = [[vp.add(f"a0{j}{d}", inp["rw_a0"][j, d]) for d in range(2)] for j in range(nrw)]
    V["rw_v0"] = [vp.add(f"v0{j}", inp["rw_v0"][j]) for j in range(inp["rw_v0"].shape[0])]
    for nm in ("rw_kk", "rw_ka", "rw_rk", "rw_lnx_g", "rw_lnx_b"):
        V[nm] = [vp.add(f"{nm}{j}", inp[nm][j]) for j in range(nrw)]
    if "na_q_g" in inp:
        nna = inp["na_q_g"].shape[0]
        V["na_q_g"] = [vp.add(f"qg{j}", np.tile(inp["na_q_g"][j], 2)) for j in range(nna)]
        V["na_k_g"] = [vp.add(f"kg{j}", np.tile(inp["na_k_g"][j], 2)) for j in range(nna)]
    return vp, V


RW_WEIGHTS = ["rw_rkv", "rw_w1", "rw_w2", "rw_a1", "rw_a2", "rw_v1", "rw_v2", "rw_g1", "rw_g2", "rw_o"]


def build_rwkv_probe(T, ncols, V, shapes, j, layer):
    B = Builder(T)
    nc = B.nc
    setup_common(B, ncols)
    xT = B.din("xT", [D, T]).ap()
    W = {n: B.din(n, list(shapes[n])).ap() for n in RW_WEIGHTS}
    yT = B.dout("yT", [D, T]).ap()
    S = alloc_rwkv_scratch(B)
    if j > 0:
        vf_in = B.din("vfirst_in", [D, T]).ap()
        S["vfirst"] = vf_in
    rwkv_r1(B, j, layer, xT, W, V, S)
    rwkv_r2(B, S)
    rwkv_r3(B, j, xT, yT, W, V, S)
    with ExitStack() as st:
        B.p.emit(nc, st)
    return B


GRID_W = 64
ROWS_SEG = SEG // GRID_W
NEG = -30000.0


def na_window(i, kind, nseg_sample=4):
    seg = i // ROWS_SEG
    if kind == "S" and seg < nseg_sample:
        rows = nseg_sample * ROWS_SEG
        return int(np.clip(i - 4, 0, rows - 8))
    li = i % ROWS_SEG
    return seg * ROWS_SEG + int(np.clip(li - 4, 0, ROWS_SEG - 8))


def na_slots(T):
    nrows = T // GRID_W
    nseg = T // SEG
    nss = min(4, nseg)
    out = []
    for i in range(nrows):
        lo = min(na_window(i, "P"), na_window(i, "S", nss))
        hi = max(na_window(i, "P"), na_window(i, "S", nss)) + 8
        out.append(list(range(lo // 2, (hi - 1) // 2 + 1)))
    return out


def host_na_nbias(T, kind):
    slots = na_slots(T)
    nss = min(4, T // SEG)
    cols = []
    for i, ms in enumerate(slots):
        lo = na_window(i, kind, nss)
        for m in ms:
            col = np.zeros(128, np.float32)
            for hf in range(2):
                r = 2 * m + hf
                if not (lo <= r < lo + 8):
                    col[hf * 64:(hf + 1) * 64] = NEG
            cols.append(col)
    return np.ascontiguousarray(np.stack(cols, axis=1))


def host_na_bias_table(rpb):
    qc = np.arange(64)
    kc = np.arange(64)
    ws = np.clip(qc - 8, 0, 48)
    cm = (kc[:, None] >= ws[None, :]) & (kc[:, None] < ws[None, :] + 16)
    dc = np.clip(kc[:, None] - qc[None, :] + 15, 0, 30)
    out = np.full((16, 128, 16, 64), NEG, np.float32)
    for e in range(16):
        for hf in range(2):
            dr = e - 8 + hf
            if abs(dr) > 7:
                continue
            g = rpb[:, dr + 7, :][:, dc]
            g = np.where(cm[None], g, np.float32(NEG))
            out[e, hf * 64:(hf + 1) * 64] = np.transpose(g, (1, 0, 2))
    return out


def na_n1(B, jn, layer, xin, W, V, S):
    nc, p, sb = B.nc, B.p, B.sb
    T = B.T
    TT = 512
    NT = T // TT
    vs = B.vecs_sb
    m0 = sb.mark()
    wq = sb.alloc([128, 8, 3 * D], BF16, "wqkv")
    wqB = Buf()
    stg_t = [sb.alloc([128, 4096], F32, "stg") for _ in range(2)]
    stg = Stager(B, stg_t, [Buf(), Buf()])
    src = W["na_qkv"][jn].rearrange("(c p) n -> p c n", p=128)
    for c in range(8):
        for h3 in range(3):
            if h3 < 2:
                stg.load(wq[:, c, h3 * 1024:(h3 + 1) * 1024], src[:, c, h3 * 1024:(h3 + 1) * 1024],
                         lambda t: t[:, 0:1024], wqB)
            else:
                stg.load(wq[:, c, 2048:3072], src[:, c, 2048:3072], lambda t: t[:, 0:1024], wqB)
    xs = [sb.alloc([128, 8, TT], F32, "x") for _ in range(2)]
    xB = [Buf(), Buf()]
    sq = sb.alloc([128, 8, TT], BF16, "sq")
    sqB = Buf()
    hn = sb.alloc([128, 8, TT], BF16, "hn")
    hnB = Buf()
    rs = sb.alloc([128, TT], F32, "rs")
    rstd = sb.alloc([128, TT], F32, "rstd")
    rsB, rstdB = Buf(), Buf()
    epsb = sb.alloc([128, 2], F32, "eps")
    epsB = Buf()
    p.add("pool", lambda e: e.memset(epsb[:, 0:1], RMS_EPS), writes=[epsB])
    p.add("pool", lambda e: e.memset(epsb[:, 1:2], 64 * RMS_EPS), writes=[epsB])
    tq = [sb.alloc([128, TT], F32, "tq") for _ in range(2)]
    tqB = [Buf(), Buf()]
    sqq = [sb.alloc([128, TT], BF16, "sqq") for _ in range(2)]
    sqqB = [Buf(), Buf()]
    rq = [sb.alloc([128, TT], F32, "rq") for _ in range(2)]
    rqB = [Buf(), Buf()]
    qo = [sb.alloc([128, TT], BF16, "qo") for _ in range(2)]
    qoB = [Buf(), Buf()]
    vtl = [sb.alloc([128, D], BF16, "vtl") for _ in range(2)]
    vtlB = [Buf(), Buf()]
    p.barrier()
    xiv = xin.rearrange("(c p) t -> p c t", p=128)
    gc = V["norm_mix_g"][layer]
    k2 = 0
    for t in range(NT):
        t0 = t * TT
        b = t % 2
        x = xs[b]
        B.dma("sp", x[:, :, :], xiv[:, :, t0:t0 + TT], writes=[xB[b]])
        for hI in range(2):
            act(B, sq[:, 4 * hI:4 * hI + 4, :], x[:, 4 * hI:4 * hI + 4, :], AF.Square, [xB[b]], [sqB])
        psA, psAB = bank(B)
        for c in range(8):
            mm(B, psA[:, :], B.ones_bf[:, :], sq[:, c, :], c == 0, c == 7, [B.onesB, sqB], [psAB])
        act(B, rs[:, :], psA[:, :], AF.Sqrt, [psAB, epsB], [rsB], bias=epsb[:, 0:1], scale=1.0 / D)
        p.add("dve", lambda e: e.reciprocal(out=rstd[:, :], in_=rs[:, :]), [rsB], [rstdB])
        for c in range(8):
            stt(B, hn[:, c, :], x[:, c, :], vs[:, gc + c:gc + c + 1], rstd[:, :], ALU.mult, ALU.mult,
                [xB[b], rstdB], [hnB])
        for mI in range(2):
            gcol = V["na_q_g"][jn] if mI == 0 else V["na_k_g"][jn]
            for oc in range(8):
                bi = k2 % 2
                k2 += 1
                ps_, psB_ = bank(B)
                for c in range(8):
                    mm(B, ps_[:, :], wq[:, c, mI * 1024 + oc * 128:mI * 1024 + (oc + 1) * 128], hn[:, c, :],
                       c == 0, c == 7, [hnB], [psB_])
                act(B, tq[bi][:, :], ps_[:, :], AF.Copy, [psB_], [tqB[bi]])
                tt(B, "pool", sqq[bi][:, :], tq[bi][:, :], tq[bi][:, :], ALU.mult, [tqB[bi]], [sqqB[bi]])
                ps2, ps2B = bank(B)
                mm(B, ps2[:, :], B.bones_bf[:, :], sqq[bi][:, :], True, True, [sqqB[bi], B.bonesB], [ps2B])
                if mI == 0:
                    act(B, rq[bi][:, :], ps2[:, :], AF.Sqrt, [ps2B, epsB], [rqB[bi]], bias=epsb[:, 1:2], scale=1.0)
                else:
                    act(B, rq[bi][:, :], ps2[:, :], AF.Sqrt, [ps2B, epsB], [rqB[bi]], bias=epsb[:, 0:1],
                        scale=1.0 / 64)
                p.add("dve", lambda e, bi=bi: e.reciprocal(out=rq[bi][:, :], in_=rq[bi][:, :]), [rqB[bi]], [rqB[bi]])
                stt(B, qo[bi][:, :], tq[bi][:, :], vs[:, gcol:gcol + 1], rq[bi][:, :], ALU.mult, ALU.mult,
                    [tqB[bi], rqB[bi]], [qoB[bi]])
                dst = S["qT"] if mI == 0 else S["kT"]
                B.dma("sp", dst[oc * 128:(oc + 1) * 128, t0:t0 + TT], qo[bi][:, :], reads=[qoB[bi]])
        for tb in range(4):
            bi = tb % 2
            for hf in range(2):
                ps_, psB_ = bank(B)
                for c in range(8):
                    mm(B, ps_[:, :], hn[:, c, tb * 128:(tb + 1) * 128], wq[:, c, 2048 + hf * 512:2048 + (hf + 1) * 512],
                       c == 0, c == 7, [hnB], [psB_])
                if hf == 0:
                    act(B, vtl[bi][:, 0:512], ps_[:, :], AF.Copy, [psB_], [vtlB[bi]])
                else:
                    p.add("dve", lambda e, ps_=ps_, bi=bi: e.tensor_copy(out=vtl[bi][:, 512:1024], in_=ps_[:, :]),
                          [psB_], [vtlB[bi]])
            B.dma("sp", S["vtokb"][t0 + tb * 128:t0 + (tb + 1) * 128, :], vtl[bi][:, :], reads=[vtlB[bi]])
    p.barrier()
    sb.release(m0)


def na_n2(B, jn, xin, xout, W, S, nbias_dram, btab_dram, dbg=9):
    nc, p, sb = B.nc, B.p, B.sb
    T = B.T
    TT = 512
    NT = T // TT
    nrows = T // GRID_W
    slots = na_slots(T)
    nslot_tot = sum(len(s) for s in slots)
    m0 = sb.mark()
    cst = B.consts_sb
    ident = cst[:, B.C["ident"]:B.C["ident"] + 128]
    wo = sb.alloc([128, 8, D], BF16, "wo")
    woB = Buf()
    btab = sb.alloc([128, 16, 16, 64], BF16, "btab")
    btB = Buf()
    nb = sb.alloc([128, nslot_tot], F32, "nbias")
    m1 = sb.mark()
    stg_t = [sb.alloc([128, 4096], F32, "stg") for _ in range(2)]
    stgB = [Buf(), Buf()]
    stg = Stager(B, stg_t, stgB)
    src = W["na_o"][jn].rearrange("(c p) n -> p c n", p=128)
    for hI in range(2):
        stg.load(wo[:, 4 * hI:4 * hI + 4, :], src[:, 4 * hI:4 * hI + 4, :],
                 lambda t: t[:, :].rearrange("p (c n) -> p c n", n=D), woB)
    for e in range(16):
        stg.load(btab[:, e, :, :], btab_dram[e].rearrange("p (h q) -> p h q", q=64),
                 lambda t: t[:, 0:1024].rearrange("p (h q) -> p h q", q=64), btB)
    B.dma("sp", nb[:, :], nbias_dram[:, :], writes=[Buf()])
    p.barrier()
    sb.release(m1)
    NKR = 24
    KT = sb.alloc([128, 8, NKR * 64], BF16, "KT")
    KTB = Buf()
    QT = sb.alloc([128, 8, TT], BF16, "QT")
    QTB = Buf()
    Vraw = sb.alloc([128, NKR // 2, D], BF16, "Vraw")
    VrawB = Buf()
    Vaug = sb.alloc([128, NKR // 2, 16, 68], BF16, "Vaug")
    VaugB = Buf()
    p.add("pool", lambda e: e.memset(Vaug[:, :, :, 64:68], 0.0), writes=[VaugB])
    p.add("pool", lambda e: e.memset(Vaug[:, :, :, 64:65], 1.0), writes=[VaugB])
    NPT = 8
    PT = [sb.alloc([128, 16, 64], BF16, f"PT{i}") for i in range(NPT)]
    PTB = [Buf() for _ in range(NPT)]
    tmp = [sb.alloc([128, 8, 64], F32, "tmp") for _ in range(2)]
    tmpB = [Buf(), Buf()]
    rc = sb.alloc([64, 16], F32, "rc")
    rcB = Buf()
    o = sb.alloc([64, 16, 64], F32, "o")
    oB = Buf()
    oT = sb.alloc([128, 8, TT], BF16, "oT")
    oTB = [Buf() for _ in range(8)]
    xt = sb.alloc([128, 8, TT], F32, "xt")
    xtB = Buf()
    p.barrier()
    xiv = xin.rearrange("(c p) t -> p c t", p=128)
    xov = xout.rearrange("(c p) t -> p c t", p=128)
    qv = S["qT"].rearrange("(c p) t -> p c t", p=128)
    kv = S["kT"].rearrange("(c p) t -> p c t", p=128)
    scol = 0
    k2 = 0
    for t in range(NT):
        t0 = t * TT
        i0 = t0 // GRID_W
        klo = max(0, i0 - 8)
        khi = min(nrows, i0 + 16)
        nk = khi - klo
        B.dma("sp", xt[:, :, :], xiv[:, :, t0:t0 + TT], writes=[xtB])
        B.dma("sp", QT[:, :, :], qv[:, :, t0:t0 + TT], writes=[QTB])
        B.dma("sp", KT[:, :, 0:nk * 64], kv[:, :, klo * 64:khi * 64], writes=[KTB])
        B.dma("sp", Vraw[:, 0:nk // 2, :], S["vtokb"][klo * 64:khi * 64, :].rearrange("(m p) c -> p m c", p=128),
              writes=[VrawB])
        p.add("pool", lambda e, nk=nk: e.tensor_copy(
            out=Vaug[:, 0:nk // 2, :, 0:64], in_=Vraw[:, 0:nk // 2, :].rearrange("p m (h v) -> p m h v", v=64)),
            [VrawB], [VaugB])
        for rr in range(8):
            i = i0 + rr
            ms = slots[i]
            assert len(ms) <= NPT
            for si, m in enumerate(ms if dbg >= 2 else []):
                pl = m - klo // 2
                e_ = 2 * m - i + 8
                assert 0 <= pl < nk // 2 and 0 <= e_ < 16, (i, m, pl, e_)
                pss = [bank(B), bank(B)]
                for h in range(16):
                    hs = slice((h % 2) * 64, (h % 2) * 64 + 64)
                    mm(B, pss[h % 2][0][:, (h // 2) * 64:(h // 2 + 1) * 64], KT[hs, h // 2, pl * 128:(pl + 1) * 128],
                       QT[hs, h // 2, rr * 64:(rr + 1) * 64], True, True, [KTB, QTB], [pss[h % 2][1]])
                for g in range(2):
                    ps_, psB_ = pss[g]
                    tb_ = k2 % 2
                    k2 += 1
                    import os
                    sub = int(os.environ.get("NA_SUB", "9"))
                    if sub >= 2:
                        tt(B, "dve", tmp[tb_][:, :, :], ps_[:, :].rearrange("p (h q) -> p h q", q=64),
                           btab[:, e_, g:16:2, :], ALU.add, [psB_, btB], [tmpB[tb_]])
                    if sub >= 3:
                        act(B, PT[si][:, g:16:2, :], tmp[tb_][:, :, :], AF.Exp, [tmpB[tb_]], [PTB[si]],
                            bias=nb[:, scol:scol + 1], scale=1.0)
                scol += 1
            if dbg < 2:
                scol += len(ms)
            pvb = [bank(B) for _ in range(4)]
            for h in range(16 if dbg >= 3 else 0):
                ps_, psB_ = pvb[h // 4]
                for si, m in enumerate(ms):
                    pl = m - klo // 2
                    mm(B, ps_[0:64, (h % 4) * 128:(h % 4) * 128 + 66], PT[si][:, h, :], Vaug[:, pl, h, 0:66],
                       si == 0, si == len(ms) - 1, [PTB[si], VaugB], [psB_])
            for b4 in range(4 if dbg >= 4 else 0):
                ps_, psB_ = pvb[b4]
                pv3 = ps_[0:64, :].rearrange("p (h c) -> p h c", c=128)
                p.add("dve", lambda e, pv3=pv3, b4=b4: e.reciprocal(out=rc[:, b4 * 4:(b4 + 1) * 4], in_=pv3[:, :, 64]),
                      [psB_], [rcB])
                tt(B, "dve", o[:, b4 * 4:(b4 + 1) * 4, :], pv3[:, :, 0:64],
                   rc[:, b4 * 4:(b4 + 1) * 4].unsqueeze(2).to_broadcast([64, 4, 64]), ALU.mult, [psB_, rcB], [oB])
            of = o[:, :, :].rearrange("p h v -> p (h v)")
            ps_, psB_ = bank(B)
            for oc in range(8 if dbg >= 5 else 0):
                p.add("pe", lambda e, ps_=ps_, oc=oc: e.transpose(ps_[:, oc * 64:(oc + 1) * 64],
                                                                 of[:, oc * 128:(oc + 1) * 128], ident[0:64, 0:64]),
                      [oB], [psB_])
            if dbg >= 5:
                act(B, oT[:, :, rr * 64:(rr + 1) * 64], ps_[:, :].rearrange("p (c q) -> p c q", q=64), AF.Copy,
                    [psB_], oTB)
        for oc in range(8 if dbg >= 6 else 0):
            ps_, psB_ = bank(B)
            for c in range(8):
                mm(B, ps_[:, :], wo[:, c, oc * 128:(oc + 1) * 128], oT[:, c, :], c == 0, c == 7, [woB] + oTB, [psB_])
            tt(B, "dve", xt[:, oc, :], xt[:, oc, :], ps_[:, :], ALU.add, [xtB, psB_], [xtB])
        B.dma("pool", xov[:, :, t0:t0 + TT], xt[:, :, :], reads=[xtB])
    assert scol == nslot_tot
    p.barrier()
    sb.release(m0)


def alloc_na_scratch(B):
    T = B.T
    S = {"qT": B.dscr("s_qT", [D, T], BF16).ap(), "kT": B.dscr("s_kT", [D, T], BF16).ap(),
         "vtokb": B.dscr("s_vtokb", [T, D], BF16).ap()}
    return S


def build_na_probe(T, ncols, V, shapes, jn, layer, mode="full"):
    B = Builder(T)
    nc = B.nc
    setup_common(B, ncols)
    xT = B.din("xT", [D, T]).ap()
    W = {n: B.din(n, list(shapes[n])).ap() for n in ("na_qkv", "na_o")}
    nslot_tot = sum(len(s) for s in na_slots(T))
    nbias = B.din("nbias", [128, nslot_tot]).ap()
    btab = B.din("btab", [16, 128, 1024]).ap()
    yT = B.dout("yT", [D, T]).ap()
    S = alloc_na_scratch(B)
    na_n1(B, jn, layer, xT, W, V, S)
    if mode == "n1":
        B.dma("sp", yT[:, :], xT[:, :])
        B.p.barrier()
    else:
        na_n2(B, jn, xT, yT, W, S, nbias, btab, dbg=int(mode) if mode.isdigit() else 9)
    with ExitStack() as st:
        B.p.emit(nc, st)
    return B


T_CORE = NSEG * SEG
DEPTH = 4
W_NAMES = ["w_up", "w_down", "rw_rkv", "rw_w1", "rw_w2", "rw_a1", "rw_a2", "rw_v1", "rw_v2", "rw_g1", "rw_g2",
           "rw_o", "na_qkv", "na_o"]
_CACHE = {}


def build_full(ncols, V, shapes, T=None, depth=DEPTH, skip_last_mlp=False):
    T = T or T_CORE
    B = Builder(T)
    nc = B.nc
    setup_common(B, ncols)
    xT = B.din("xT", [D, T]).ap()
    W = {n: B.din(n, list(shapes[n])).ap() for n in W_NAMES}
    nslot_tot = sum(len(s) for s in na_slots(T))
    nbias = B.din("nbias", [128, nslot_tot]).ap()
    btabs = [B.din(f"btab{j}", [16, 128, 1024]).ap() for j in range(2)]
    yT = B.dout("yT", [D, T]).ap()
    xA = B.dscr("xA", [D, T]).ap()
    xB = B.dscr("xB", [D, T]).ap()
    SR = alloc_rwkv_scratch(B)
    SN = alloc_na_scratch(B)
    cur = xT
    for layer in range(depth):
        j = layer // 2
        lastl = (layer == depth - 1)
        mdst = yT if (lastl and skip_last_mlp) else xA
        if layer % 2 == 0:
            rwkv_r1(B, j, layer, cur, W, V, SR)
            rwkv_r2(B, SR)
            rwkv_r3(B, j, cur, mdst, W, V, SR)
        else:
            na_n1(B, j, layer, cur, W, V, SN)
            na_n2(B, j, cur, mdst, W, SN, nbias, btabs[j])
        if lastl and skip_last_mlp:
            break
        dst = yT if lastl else xB
        mlp_stage(B, xA, dst, W["w_up"][layer], W["w_down"][layer], B.vecs_sb, V["norm_mlp_g"][layer])
        cur = xB
    with ExitStack() as st:
        B.p.emit(nc, st)
    return B


def kernel(**inputs):
    inp = {k: np.asarray(v) for k, v in inputs.items()}
    xp = inp["x_prompt"]
    xs = inp["x_sample"]
    vp, V = pack_vectors(inp)
    vecs = vp.array()
    shapes = {n: inp[n].shape for n in W_NAMES}
    key = (vp.n,)
    if key not in _CACHE:
        _CACHE[key] = build_full(vp.n, V, shapes)
    B = _CACHE[key]
    consts = host_consts()
    blkm = host_blk_masks()
    btabs = [np.ascontiguousarray(host_na_bias_table(inp["na_rpb"][j]).reshape(16, 128, 1024)) for j in range(2)]
    nb = {"S": host_na_nbias(T_CORE, "S"), "P": host_na_nbias(T_CORE, "P")}
    prompt_ids = []
    in_maps = []
    for c in range(NCORES):
        if c < 4:
            ids = [2 * c, 2 * c + 1]
            xc = np.concatenate([xs[c], xp[ids[0]], xp[ids[1]]], axis=0)
        else:
            ids = list(range(8 + 6 * (c - 4), 8 + 6 * (c - 4) + 6))
            xc = np.concatenate([xp[i] for i in ids], axis=0)
        prompt_ids.append(ids)
        fl = np.zeros((128, 8), np.float32)
        if c < 4:
            fl[:, 1:4] = 1.0
        m = {"xT": np.ascontiguousarray(xc.T), "vecs": vecs, "consts": consts, "flags": fl, "blkm": blkm,
             "nbias": nb["S" if c < 4 else "P"], "btab0": btabs[0], "btab1": btabs[1]}
        for n in W_NAMES:
            m[n] = inp[n]
        in_maps.append(m)
    res = run_bass_kernel_spmd(B.nc, in_maps, core_ids=list(range(NCORES)))
    y_prompt = np.empty_like(xp)
    y_sample = np.empty_like(xs)
    for c in range(NCORES):
        y = res.results[c]["yT"].T
        if c < 4:
            y_sample[c] = y[0:8192]
            for k, i in enumerate(prompt_ids[c]):
                y_prompt[i] = y[8192 + k * SEG:8192 + (k + 1) * SEG]
        else:
            for k, i in enumerate(prompt_ids[c]):
                y_prompt[i] = y[k * SEG:(k + 1) * SEG]
    return (y_prompt, y_sample)
```

```python
from contextlib import ExitStack
import numpy as np
import concourse.bass as bass
import concourse.mybir as mybir
from concourse.bass_utils import run_bass_kernel_spmd

F32 = mybir.dt.float32
BF16 = mybir.dt.bfloat16
ALU = mybir.AluOpType
AF = mybir.ActivationFunctionType
AX = mybir.AxisListType

D = 1024
NCH = 8
DFF = 4096
NSEG = 6
SEG = 2048
NCORES = 8
RMS_EPS = 1e-6

NSLOT = 12
EPOCH = 15000
NEPOCH = 16


class Buf:
    __slots__ = ("w", "r", "parent", "kids")

    def __init__(self, parent=None):
        self.w = None
        self.r = []
        self.parent = parent
        self.kids = []
        if parent is not None:
            parent.kids.append(self)


class Op:
    __slots__ = ("eng", "seq", "fn", "waits", "dma", "sigidx", "slot", "slotval", "slotprev")


class Prog:
    ENGS = ("pe", "act", "dve", "pool", "sp")
    COMPUTE = ("pe", "act", "dve", "pool")

    def __init__(self):
        self.ops = {e: [] for e in self.ENGS}
        self.known = {f: {e: -1 for e in self.ENGS} for f in self.ENGS}
        self.known_dma = {f: set() for f in self.ENGS}
        self.last_compute = {e: -1 for e in self.ENGS}
        self.slot_uses = {q: [0] * NSLOT for q in ("sp", "pool", "act")}
        self.slot_last = {q: [None] * NSLOT for q in ("sp", "pool", "act")}
        self.dma_n = {q: 0 for q in ("sp", "pool", "act")}
        self.fence = {e: -1 for e in self.ENGS}

    def add(self, eng, fn, reads=(), writes=(), dma=False):
        ops = self.ops[eng]
        seq = len(ops)
        deps = set()
        for b in reads:
            if b.w is not None:
                deps.add(b.w)
            if b.parent is not None and b.parent.w is not None:
                deps.add(b.parent.w)
            for kb in b.kids:
                if kb.w is not None:
                    deps.add(kb.w)
        for b in writes:
            if b.w is not None:
                deps.add(b.w)
            deps.update(b.r)
            if b.parent is not None:
                if b.parent.w is not None:
                    deps.add(b.parent.w)
                deps.update(b.parent.r)
            for kb in b.kids:
                if kb.w is not None:
                    deps.add(kb.w)
                deps.update(kb.r)
        waits = []
        kn = self.known[eng]
        kd = self.known_dma[eng]
        for d in sorted(deps):
            E, s, isdma = d
            if s <= self.fence[E]:
                continue
            if isdma:
                if (E, s) in kd:
                    continue
                kd.add((E, s))
                waits.append(d)
            else:
                if E == eng and not dma:
                    if eng == "pe":
                        continue
                    if seq - s > 3:
                        continue
                if kn[E] >= s:
                    continue
                kn[E] = s
                waits.append(d)
        op = Op()
        op.eng, op.seq, op.fn, op.waits, op.dma, op.sigidx = eng, seq, fn, waits, dma, 0
        op.slot = op.slotval = op.slotprev = None
        if dma:
            n = self.dma_n[eng]
            self.dma_n[eng] = n + 1
            sl = n % NSLOT
            op.slot = sl
            op.slotprev = self.slot_last[eng][sl]
            self.slot_uses[eng][sl] += 1
            op.slotval = 16 * self.slot_uses[eng][sl]
            self.slot_last[eng][sl] = (eng, seq, True)
            if op.slotprev is not None:
                kd.add(op.slotprev[:2])
        else:
            self.last_compute[eng] = seq
        tok = (eng, seq, dma)
        for b in reads:
            b.r.append(tok)
        for b in writes:
            b.w = tok
            b.r = []
        ops.append(op)
        return op

    def barrier(self):
        lasts = dict(self.last_compute)
        dmas = []
        for q in self.slot_last:
            for t in self.slot_last[q]:
                if t is not None:
                    dmas.append(t)
        for F in self.ENGS:
            waits = []
            for E in self.COMPUTE:
                if E != F and lasts[E] >= 0 and self.known[F][E] < lasts[E]:
                    waits.append((E, lasts[E], False))
                    self.known[F][E] = lasts[E]
            for t in dmas:
                if t[:2] not in self.known_dma[F]:
                    self.known_dma[F].add(t[:2])
                    waits.append(t)
            op = Op()
            op.eng, op.seq, op.fn, op.waits, op.dma, op.sigidx = F, len(self.ops[F]), None, waits, False, 0
            op.slot = op.slotval = op.slotprev = None
            self.ops[F].append(op)
        for E in self.ENGS:
            self.fence[E] = len(self.ops[E]) - 1

    def emit(self, nc, st):
        sig = {e: set() for e in self.ENGS}
        for F in self.ENGS:
            for op in self.ops[F]:
                for (E, s, isdma) in op.waits:
                    if not isdma:
                        sig[E].add(s)
        nsig = {}
        for E in self.COMPUTE:
            c = 0
            for op in self.ops[E]:
                if op.seq in sig[E]:
                    c += 1
                    op.sigidx = c
            nsig[E] = c
            assert c <= EPOCH * NEPOCH, (E, c)
        csem = {E: [st.enter_context(nc.semaphore(f"c_{E}_{k}")) for k in range((nsig[E] + EPOCH - 1) // EPOCH)]
                for E in self.COMPUTE}
        dsem = {q: [st.enter_context(nc.semaphore(f"d_{q}_{k}")) for k in range(NSLOT)]
                for q in self.slot_last if self.dma_n[q] > 0}
        allops = self.ops

        def run(F, eng):
            for op in allops[F]:
                for (E, s, isdma) in op.waits:
                    t = allops[E][s]
                    if isdma:
                        eng.wait_ge(dsem[E][t.slot], t.slotval)
                    else:
                        i = t.sigidx - 1
                        eng.wait_ge(csem[E][i // EPOCH], i % EPOCH + 1)
                if op.dma and op.slotprev is not None:
                    t = allops[op.slotprev[0]][op.slotprev[1]]
                    eng.wait_ge(dsem[F][t.slot], t.slotval)
                if op.fn is None:
                    continue
                ins = op.fn(eng)
                if op.dma:
                    ins.then_inc(dsem[F][op.slot], 16)
                elif op.sigidx:
                    i = op.sigidx - 1
                    ins.then_inc(csem[F][i // EPOCH], 1)

        block = st.enter_context(nc.Block())

        @block.tensor
        def _(eng):
            run("pe", eng)

        @block.scalar
        def _(eng):
            run("act", eng)

        @block.vector
        def _(eng):
            run("dve", eng)

        @block.gpsimd
        def _(eng):
            run("pool", eng)

        @block.sync
        def _(eng):
            run("sp", eng)


class SB:
    def __init__(self, nc):
        self.nc = nc
        self.base = nc.sbuf_base + 64
        self.top = nc.sbuf_top
        self.ptr = self.base
        self.n = 0

    def alloc(self, shape, dtype, name="t"):
        esz = 2 if dtype == BF16 else 4
        per = esz
        for s in shape[1:]:
            per *= s
        off = (self.ptr + 63) // 64 * 64
        assert off + per <= self.top, (name, off, per, self.top)
        self.ptr = off + per
        self.n += 1
        return self.nc.alloc_sbuf_tensor_at(f"{name}_{self.n}", list(shape), dtype, offset=off)

    def mark(self):
        return self.ptr

    def release(self, m):
        self.ptr = m


class Builder:
    def __init__(self, T):
        self.T = T
        self.nc = bass.Bass("TRN2", target_bir_lowering=False)
        self.p = Prog()
        self.sb = SB(self.nc)
        nc = self.nc
        self.ps = [nc.alloc_psum_tensor(f"psb{i}", [128, 512], F32) for i in range(8)]
        self.psB = [Buf() for _ in range(8)]
        self.ones_bf = self.sb.alloc([128, 128], BF16, "ones")
        self.onesB = Buf()
        self.p.add("pool", lambda e: e.memset(self.ones_bf[:, :], 1.0), writes=[self.onesB])
        self.perm_mark = None

    def din(self, name, shape, dtype=F32):
        return self.nc.dram_tensor(name, list(shape), dtype, kind="ExternalInput")

    def dout(self, name, shape, dtype=F32):
        return self.nc.dram_tensor(name, list(shape), dtype, kind="ExternalOutput")

    def dscr(self, name, shape, dtype=F32):
        return self.nc.dram_tensor(name, list(shape), dtype)

    def dma(self, q, out, in_, reads=(), writes=(), slow=False):
        if slow:
            self.p.add(q, lambda e, o=out, i=in_: e.dma_start(out=o, in_=i, allow_slow_non_contiguous=True),
                       reads, writes, dma=True)
        else:
            self.p.add(q, lambda e, o=out, i=in_: e.dma_start(out=o, in_=i), reads, writes, dma=True)

    def cast(self, k, out, in_, reads, writes):
        eng = ("dve", "pool", "act")[k % 3]
        if eng == "act":
            self.p.add("act", lambda e, o=out, i=in_: e.activation(out=o, in_=i, func=AF.Copy), reads, writes)
        else:
            self.p.add(eng, lambda e, o=out, i=in_: e.tensor_copy(out=o, in_=i), reads, writes)


def mlp_stage(B, xin, xout, w_up, w_down, vecs_sb, gcol):
    nc, p, sb = B.nc, B.p, B.sb
    T = B.T
    TT = 512
    NT = T // TT
    m = sb.mark()
    wup = sb.alloc([128, 8, DFF], BF16, "wup")
    wdn = sb.alloc([128, 32, D], BF16, "wdn")
    wupB = [Buf() for _ in range(8)]
    wdnB = [Buf() for _ in range(8)]
    xs = [sb.alloc([128, 8, TT], F32, "x") for _ in range(2)]
    xoff = []
    xB = [Buf() for _ in range(2)]
    hn = sb.alloc([128, 8, TT], BF16, "hn")
    hnB = Buf()
    a = sb.alloc([128, 16, TT], BF16, "a")
    aB = [Buf() for _ in range(16)]
    r = [sb.alloc([128, TT], F32, "r") for _ in range(2)]
    rB = [Buf() for _ in range(2)]
    rs = sb.alloc([128, TT], F32, "rs")
    rsB = Buf()
    rstd = sb.alloc([128, TT], F32, "rstd")
    rstdB = Buf()
    epsb = sb.alloc([128, 1], F32, "eps")
    epsB = Buf()
    p.add("pool", lambda e: e.memset(epsb[:, :], RMS_EPS), writes=[epsB])
    stg = [xs[0], xs[1]]
    k = 0
    for kc in range(8):
        s = stg[k % 2]
        sv = s[:, :, :].rearrange("p a b -> p (a b)")
        B.dma("sp", sv, w_up[kc * 128:(kc + 1) * 128, :], writes=[xB[k % 2]])
        for h in range(2):
            B.cast(2 * k + h, wup[:, kc, h * 2048:(h + 1) * 2048], sv[:, h * 2048:(h + 1) * 2048],
                   [xB[k % 2]], [wupB[kc]])
        k += 1
    wdv = w_down.rearrange("(c p) n -> p c n", p=128)
    for g4 in range(8):
        s = stg[k % 2]
        sv = s[:, :, :].rearrange("p a b -> p (a b)").rearrange("p (g c) -> p g c", c=D)
        B.dma("sp", sv, wdv[:, g4 * 4:(g4 + 1) * 4, :], writes=[xB[k % 2]])
        for h in range(2):
            B.cast(2 * k + h, wdn[:, g4 * 4 + 2 * h:g4 * 4 + 2 * h + 2, :], sv[:, 2 * h:2 * h + 2, :],
                   [xB[k % 2]], [wdnB[g4]])
        k += 1
    xiv = xin.rearrange("(c p) t -> p c t", p=128)
    xov = xout.rearrange("(c p) t -> p c t", p=128)
    PS_SS, PS_UP, PS_DN = 0, (1, 2), (3, 4, 5, 6)
    dn_i = 0
    for t in range(NT):
        t0 = t * TT
        b = t % 2
        x = xs[b]
        B.dma("sp", x[:, :, :], xiv[:, :, t0:t0 + TT], writes=[xB[b]])
        sq = a
        for h in range(2):
            p.add("act", lambda e, o=sq[:, 4 * h:4 * h + 4, :], i=x[:, 4 * h:4 * h + 4, :]:
                  e.activation(out=o, in_=i, func=AF.Square), [xB[b]], aB[4 * h:4 * h + 4])
        for c in range(8):
            p.add("pe", lambda e, c=c: e.matmul(B.ps[PS_SS][:, :], B.ones_bf[:, :], sq[:, c, :],
                                                 start=(c == 0), stop=(c == 7)),
                  [B.onesB, aB[c]], [B.psB[PS_SS]])
        p.add("act", lambda e: e.activation(out=rs[:, :], in_=B.ps[PS_SS][:, :], func=AF.Sqrt,
                                            bias=epsb[:, 0:1], scale=1.0 / D),
              [B.psB[PS_SS], epsB], [rsB])
        p.add("dve", lambda e: e.reciprocal(out=rstd[:, :], in_=rs[:, :]), [rsB], [rstdB])
        for c in range(8):
            p.add("dve", lambda e, c=c, x=x: e.scalar_tensor_tensor(
                out=hn[:, c, :], in0=x[:, c, :], scalar=vecs_sb[:, gcol + c:gcol + c + 1], in1=rstd[:, :],
                op0=ALU.mult, op1=ALU.mult), [xB[b], rstdB], [hnB])
        for half in range(2):
            for jj in range(16):
                j = half * 16 + jj
                pu = PS_UP[j % 2]
                for kc in range(8):
                    p.add("pe", lambda e, kc=kc, j=j, pu=pu: e.matmul(
                        B.ps[pu][:, :], wup[:, kc, j * 128:(j + 1) * 128], hn[:, kc, :],
                        start=(kc == 0), stop=(kc == 7)), [wupB[kc], hnB], [B.psB[pu]])
                p.add("act", lambda e, j=j, pu=pu: e.activation(out=r[j % 2][:, :], in_=B.ps[pu][:, :],
                                                                 func=AF.Relu),
                      [B.psB[pu]], [rB[j % 2]])
                p.add("pool", lambda e, j=j, jj=jj: e.tensor_tensor(out=a[:, jj, :], in0=r[j % 2][:, :],
                                                                     in1=r[j % 2][:, :], op=ALU.mult),
                      [rB[j % 2]], [aB[jj]])
            for o in range(8):
                pd = PS_DN[dn_i % 4]
                dn_i += 1
                for jj in range(16):
                    j = half * 16 + jj
                    p.add("pe", lambda e, o=o, j=j, jj=jj, pd=pd: e.matmul(
                        B.ps[pd][:, :], wdn[:, j, o * 128:(o + 1) * 128], a[:, jj, :],
                        start=(jj == 0), stop=(jj == 15)), [wdnB[j // 4], aB[jj]], [B.psB[pd]])
                p.add("dve", lambda e, o=o, pd=pd, x=x: e.tensor_tensor(
                    out=x[:, o, :], in0=x[:, o, :], in1=B.ps[pd][:, :], op=ALU.add),
                    [xB[b], B.psB[pd]], [xB[b]])
        B.dma("pool", xov[:, :, t0:t0 + TT], x[:, :, :], reads=[xB[b]])
    p.barrier()
    sb.release(m)


class VecPack:
    def __init__(self):
        self.cols = {}
        self.n = 0
        self.data = []

    def add(self, name, v):
        v = np.asarray(v, np.float32).reshape(-1)
        assert v.size % 128 == 0
        nc_ = v.size // 128
        self.cols[name] = self.n
        self.n += nc_
        self.data.append(np.ascontiguousarray(v.reshape(nc_, 128).T))
        return self.cols[name]

    def array(self):
        return np.ascontiguousarray(np.concatenate(self.data, axis=1))


def build_mlp_only(T, ncols, gcol):
    B = Builder(T)
    nc = B.nc
    xT = B.din("xT", [D, T]).ap()
    vecs = B.din("vecs", [128, ncols]).ap()
    w_up = B.din("w_up", [D, DFF]).ap()
    w_down = B.din("w_down", [DFF, D]).ap()
    yT = B.dout("yT", [D, T]).ap()
    vecs_sb = B.sb.alloc([128, ncols], F32, "vecs")
    vB = Buf()
    B.dma("sp", vecs_sb[:, :], vecs[:, :], writes=[vB])
    B.p.barrier()
    mlp_stage(B, xT, yT, w_up, w_down, vecs_sb, gcol)
    with ExitStack() as st:
        B.p.emit(nc, st)
    return B


class Stager:
    def __init__(self, B, tiles, bufs):
        self.B, self.tiles, self.bufs, self.k = B, tiles, bufs, 0

    def load(self, dst_ap, src_ap, stage_view, dstB):
        i = self.k % len(self.tiles)
        sv = stage_view(self.tiles[i])
        self.B.dma("sp", sv, src_ap, writes=[self.bufs[i]])
        self.B.cast(self.k, dst_ap, sv, [self.bufs[i]], [dstB])
        self.k += 1


def bank(B):
    i = B.bank_i % 8
    B.bank_i += 1
    return B.ps[i], B.psB[i]


def act(B, out, in_, func, reads, writes, bias=None, scale=None):
    kw = {}
    if bias is not None:
        kw["bias"] = bias
    if scale is not None:
        kw["scale"] = scale
    B.p.add("act", lambda e: e.activation(out=out, in_=in_, func=func, **kw), reads, writes)


def tt(B, eng, out, in0, in1, op, reads, writes):
    B.p.add(eng, lambda e: e.tensor_tensor(out=out, in0=in0, in1=in1, op=op), reads, writes)


def ts(B, eng, out, in0, s1, s2, op0, op1, reads, writes):
    if op1 is None:
        B.p.add(eng, lambda e: e.tensor_scalar(out=out, in0=in0, scalar1=s1, scalar2=None, op0=op0), reads, writes)
    else:
        B.p.add(eng, lambda e: e.tensor_scalar(out=out, in0=in0, scalar1=s1, scalar2=s2, op0=op0, op1=op1),
                reads, writes)


def stt(B, out, in0, scalar, in1, op0, op1, reads, writes):
    B.p.add("dve", lambda e: e.scalar_tensor_tensor(out=out, in0=in0, scalar=scalar, in1=in1, op0=op0, op1=op1),
            reads, writes)


def mm(B, out, lhsT, rhs, start, stop, reads, writes):
    B.p.add("pe", lambda e: e.matmul(out, lhsT, rhs, start=start, stop=stop), reads, writes)


def rwkv_r1(B, j, layer, xin, W, V, S):
    nc, p, sb = B.nc, B.p, B.sb
    T = B.T
    TT = 512
    NT = T // TT
    vs = B.vecs_sb
    has_vres = j > 0
    m0 = sb.mark()
    wrkv = [sb.alloc([128, 8, D], BF16, f"w{n}") for n in "rkv"]
    wrkvB = [[Buf() for _ in range(2)] for _ in range(3)]
    w1c = sb.alloc([128, 8, 128], BF16, "w1c")
    a1c = sb.alloc([128, 8, 128], BF16, "a1c")
    g1a = sb.alloc([128, 8, 128], BF16, "g1a")
    g1b = sb.alloc([128, 8, 32], BF16, "g1b")
    w2c = sb.alloc([128, D], BF16, "w2c")
    a2c = sb.alloc([128, D], BF16, "a2c")
    g2a = sb.alloc([128, D], BF16, "g2a")
    g2b = sb.alloc([32, D], BF16, "g2b")
    if has_vres:
        v1s = sb.alloc([128, 8, 32], BF16, "v1s")
        v2s = sb.alloc([32, D], BF16, "v2s")
    wsB = Buf()
    B.sb_off = {}
    B.sb_off["xh"] = (sb.ptr + 63) // 64 * 64
    xh = sb.alloc([128, 8, TT + 2], F32, "xh")
    xhB = Buf()
    B.sb_off["xx"] = (sb.ptr + 63) // 64 * 64
    xx = sb.alloc([128, 8, TT], F32, "xx")
    xxB = Buf()
    rs = sb.alloc([128, TT + 2], F32, "rs")
    rsB = Buf()
    rstd = sb.alloc([128, TT + 2], F32, "rstd")
    rstdB = Buf()
    epsb = sb.alloc([128, 1], F32, "eps")
    epsB = Buf()
    p.add("pool", lambda e: e.memset(epsb[:, :], RMS_EPS), writes=[epsB])
    xm_off = (sb.ptr + 63) // 64 * 64
    xm = [sb.alloc([128, 8, TT], BF16, f"xm{i}") for i in range(6)]
    xmB = [Buf() for _ in range(6)]
    sq = nc.alloc_sbuf_tensor_at(f"r1sq{j}", [128, 8, TT + 2], BF16, offset=xm_off + 4 * 8192)
    sqB = Buf()
    SQW = [sqB, xmB[4], xmB[5]]
    stg_t = [nc.alloc_sbuf_tensor_at(f"r1stg{j}_{i}", [128, 4096], F32, offset=xm_off + i * 16384) for i in range(2)]
    stgB = [Buf(), Buf()]
    stg = Stager(B, stg_t, stgB)
    names = ["tw", "ta", "tg0", "tg1", "tv"]
    lo = {n: sb.alloc([128, TT], BF16, n) for n in names}
    loB = {n: Buf() for n in names}
    tmpn = ["r", "k", "v", "g", "kk", "rn", "lw0", "lw1", "ic0", "ic1", "t1", "kd0", "kd1", "b0", "b1", "bv",
            "vg", "vf"]
    TMs = [{n: sb.alloc([128, TT], F32, n) for n in tmpn}]
    TMBs = [{n: Buf() for n in tmpn}]
    SQKs = [sb.alloc([128, TT], BF16, "sqk")]
    RKs = [sb.alloc([128, TT], BF16, "rk")]
    VTs = [sb.alloc([128, 4, 128], F32, "vt")]
    SQKBs, RKBs, VTBs = [Buf()], [Buf()], [Buf()]
    xh_off = B.sb_off["xh"]
    xx_off = B.sb_off["xx"]
    slots = [(xh_off + i * 2048, xhB) for i in range(8)] + [(xx_off + i * 2048, xxB) for i in range(8)]
    tm1, tmB1 = {}, {}
    si_ = 0
    for n in tmpn:
        if si_ < 16 and n not in ("t1", "rn"):
            off_, par_ = slots[si_]
            si_ += 1
            tm1[n] = nc.alloc_sbuf_tensor_at(f"r1t1{j}_{n}", [128, TT], F32, offset=off_)
            tmB1[n] = Buf(parent=par_)
        else:
            tm1[n] = sb.alloc([128, TT], F32, n + "1")
            tmB1[n] = Buf()
    TMs.append(tm1)
    TMBs.append(tmB1)
    SQKs.append(sb.alloc([128, TT], BF16, "sqk1"))
    RKs.append(sb.alloc([128, TT], BF16, "rk1"))
    VTs.append(sb.alloc([128, 4, 128], F32, "vt1"))
    SQKBs.append(Buf())
    RKBs.append(Buf())
    VTBs.append(Buf())

    rkv = W["rw_rkv"]
    for mI in range(3):
        src = rkv[j, mI].rearrange("(c p) n -> p c n", p=128)
        for hI in range(2):
            stg.load(wrkv[mI][:, 4 * hI:4 * hI + 4, :], src[:, 4 * hI:4 * hI + 4, :],
                     lambda t: t[:, :].rearrange("p (c n) -> p c n", n=D), wsB)
    for dI in range(2):
        stg.load(w1c[:, :, 64 * dI:64 * dI + 64], W["rw_w1"][j, dI].rearrange("(c p) n -> p c n", p=128),
                 lambda t: t[:, 0:512].rearrange("p (c n) -> p c n", n=64), wsB)
        stg.load(a1c[:, :, 64 * dI:64 * dI + 64], W["rw_a1"][j, dI].rearrange("(c p) n -> p c n", p=128),
                 lambda t: t[:, 0:512].rearrange("p (c n) -> p c n", n=64), wsB)
        stg.load(w2c[64 * dI:64 * dI + 64, :], W["rw_w2"][j, dI], lambda t, dI=dI: t[64 * dI:64 * dI + 64, 0:D], wsB)
        stg.load(a2c[64 * dI:64 * dI + 64, :], W["rw_a2"][j, dI], lambda t, dI=dI: t[64 * dI:64 * dI + 64, 0:D], wsB)
    g1v = W["rw_g1"][j].rearrange("(c p) n -> p c n", p=128)
    stg.load(g1a[:, :, :], g1v[:, :, 0:128], lambda t: t[:, 0:1024].rearrange("p (c n) -> p c n", n=128), wsB)
    stg.load(g1b[:, :, :], g1v[:, :, 128:160], lambda t: t[:, 0:256].rearrange("p (c n) -> p c n", n=32), wsB)
    stg.load(g2a[:, :], W["rw_g2"][j, 0:128, :], lambda t: t[:, 0:D], wsB)
    stg.load(g2b[:, :], W["rw_g2"][j, 128:160, :], lambda t: t[0:32, 0:D], wsB)
    if has_vres:
        stg.load(v1s[:, :, :], W["rw_v1"][j - 1].rearrange("(c p) n -> p c n", p=128),
                 lambda t: t[:, 0:256].rearrange("p (c n) -> p c n", n=32), wsB)
        stg.load(v2s[:, :], W["rw_v2"][j - 1], lambda t: t[0:32, 0:D], wsB)
    p.barrier()

    xiv = xin.rearrange("(c p) t -> p c t", p=128)
    cst = B.consts_sb
    ident = cst[:, B.C["ident"]:B.C["ident"] + 128]
    bones = B.bones_bf
    for t in range(NT):
        t0 = t * TT
        seg = t0 // SEG
        first = (t0 % SEG == 0)
        last = ((t0 + TT) % SEG == 0)
        B.dma("sp", xh[:, :, 1:TT + 1], xiv[:, :, t0:t0 + TT], writes=[xhB])
        if t0 > 0:
            B.dma("sp", xh[:, :, 0:1], xiv[:, :, t0 - 1:t0], writes=[xhB], slow=True)
        else:
            p.add("pool", lambda e: e.memset(xh[:, :, 0:1], 0.0), writes=[xhB])
        if t0 + TT < T:
            B.dma("sp", xh[:, :, TT + 1:TT + 2], xiv[:, :, t0 + TT:t0 + TT + 1], writes=[xhB], slow=True)
        else:
            p.add("pool", lambda e: e.memset(xh[:, :, TT + 1:TT + 2], 0.0), writes=[xhB])
        for hI in range(2):
            act(B, sq[:, 4 * hI:4 * hI + 4, :], xh[:, 4 * hI:4 * hI + 4, :], AF.Square, [xhB], SQW)
        psA, psAB = bank(B)
        for c in range(8):
            mm(B, psA[:, :], B.ones_bf[:, :], sq[:, c, 0:TT], c == 0, c == 7, [B.onesB] + SQW, [psAB])
        psH, psHB = bank(B)
        for c in range(8):
            mm(B, psH[:, 0:2], B.ones_bf[:, :], sq[:, c, TT:TT + 2], c == 0, c == 7, [B.onesB] + SQW, [psHB])
        act(B, rs[:, 0:TT], psA[:, :], AF.Sqrt, [psAB, epsB], [rsB], bias=epsb[:, 0:1], scale=1.0 / D)
        act(B, rs[:, TT:TT + 2], psH[:, 0:2], AF.Sqrt, [psHB, epsB], [rsB], bias=epsb[:, 0:1], scale=1.0 / D)
        p.add("dve", lambda e: e.reciprocal(out=rstd[:, :], in_=rs[:, :]), [rsB], [rstdB])
        gc = V["norm_mix_g"][layer]
        for c in range(8):
            stt(B, xh[:, c, :], xh[:, c, :], vs[:, gc + c:gc + c + 1], rstd[:, :], ALU.mult, ALU.mult,
                [xhB, rstdB], [xhB])
        if first and seg > 0:
            ts(B, "dve", xh[:, :, 0:1], xh[:, :, 0:1], B.flags_sb[:, seg:seg + 1], None, ALU.mult, None,
               [xhB], [xhB])
        if last and seg < (T // SEG) - 1:
            ts(B, "dve", xh[:, :, TT + 1:TT + 2], xh[:, :, TT + 1:TT + 2], B.flags_sb[:, seg + 1:seg + 2], None,
               ALU.mult, None, [xhB], [xhB])
        tt(B, "pool", xx[:, :, :], xh[:, :, 0:TT], xh[:, :, 2:TT + 2], ALU.add, [xhB], [xxB])
        stt(B, xx[:, :, :], xx[:, :, :], 0.5, xh[:, :, 1:TT + 1], ALU.mult, ALU.subtract, [xxB, xhB], [xxB])
        mc = V["rw_mix"][j]
        for mI in range(6):
            for c in range(8):
                stt(B, xm[mI][:, c, :], xx[:, c, :], vs[:, mc + 8 * mI + c:mc + 8 * mI + c + 1], xh[:, c, 1:TT + 1],
                    ALU.mult, ALU.add, [xxB, xhB], [xmB[mI]])
        XR, XK, XV, XW, XA, XG = range(6)
        ps_, psB_ = bank(B)
        for c in range(8):
            mm(B, ps_[:, :], w1c[:, c, :], xm[XW][:, c, :], c == 0, c == 7, [xmB[XW]], [psB_])
        act(B, lo["tw"][:, :], ps_[:, :], AF.Tanh, [psB_], [loB["tw"]])
        ps_, psB_ = bank(B)
        for c in range(8):
            mm(B, ps_[:, :], a1c[:, c, :], xm[XA][:, c, :], c == 0, c == 7, [xmB[XA]], [psB_])
        act(B, lo["ta"][:, :], ps_[:, :], AF.Copy, [psB_], [loB["ta"]])
        ps_, psB_ = bank(B)
        for c in range(8):
            mm(B, ps_[:, :], g1a[:, c, :], xm[XG][:, c, :], c == 0, c == 7, [xmB[XG]], [psB_])
        act(B, lo["tg0"][:, :], ps_[:, :], AF.Sigmoid, [psB_], [loB["tg0"]])
        ps_, psB_ = bank(B)
        for c in range(8):
            mm(B, ps_[0:32, :], g1b[:, c, :], xm[XG][:, c, :], c == 0, c == 7, [xmB[XG]], [psB_])
        act(B, lo["tg1"][0:32, :], ps_[0:32, :], AF.Sigmoid, [psB_], [loB["tg1"]])
        if has_vres:
            ps_, psB_ = bank(B)
            for c in range(8):
                mm(B, ps_[0:32, :], v1s[:, c, :], xm[XV][:, c, :], c == 0, c == 7, [xmB[XV]], [psB_])
            act(B, lo["tv"][0:32, :], ps_[0:32, :], AF.Copy, [psB_], [loB["tv"]])
        def oc_unit(oc, tm, tmB, sqk, sqkB, rk, rkB, vt, vtB):
            osl = slice(oc * 128, (oc + 1) * 128)
            for mI, nm in enumerate("rkv"):
                ps_, psB_ = bank(B)
                for c in range(8):
                    mm(B, ps_[:, :], wrkv[mI][:, c, osl], xm[mI][:, c, :], c == 0, c == 7, [xmB[mI]], [psB_])
                if mI == 1:
                    p.add("dve", lambda e, ps_=ps_: e.tensor_copy(out=tm["k"][:, :], in_=ps_[:, :]),
                          [psB_], [tmB["k"]])
                else:
                    act(B, tm[nm][:, :], ps_[:, :], AF.Copy, [psB_], [tmB[nm]])
            yield
            for dI in range(2):
                dsl = slice(64 * dI, 64 * dI + 64)
                ps_, psB_ = bank(B)
                mm(B, ps_[:, :], w2c[dsl, osl], lo["tw"][dsl, :], True, True, [loB["tw"]], [psB_])
                w0c = V["rw_w0"][j][dI] + oc
                act(B, tm[f"lw{dI}"][:, :], ps_[:, :], AF.Sigmoid, [psB_], [tmB[f"lw{dI}"]],
                    bias=vs[:, w0c:w0c + 1], scale=1.0)
                ts(B, "pool", tm[f"lw{dI}"][:, :], tm[f"lw{dI}"][:, :], -0.6065306597126334, None, ALU.mult, None,
                   [tmB[f"lw{dI}"]], [tmB[f"lw{dI}"]])
                ps_, psB_ = bank(B)
                mm(B, ps_[:, :], a2c[dsl, osl], lo["ta"][dsl, :], True, True, [loB["ta"]], [psB_])
                a0c = V["rw_a0"][j][dI] + oc
                act(B, tm[f"ic{dI}"][:, :], ps_[:, :], AF.Sigmoid, [psB_], [tmB[f"ic{dI}"]],
                    bias=vs[:, a0c:a0c + 1], scale=1.0)
            yield
            ps_, psB_ = bank(B)
            mm(B, ps_[:, :], g2a[:, osl], lo["tg0"][:, :], True, False, [loB["tg0"]], [psB_])
            mm(B, ps_[:, :], g2b[0:32, osl], lo["tg1"][0:32, :], False, True, [loB["tg1"]], [psB_])
            act(B, tm["g"][:, :], ps_[:, :], AF.Copy, [psB_], [tmB["g"]])
            if has_vres:
                ps_, psB_ = bank(B)
                mm(B, ps_[:, :], v2s[0:32, osl], lo["tv"][0:32, :], True, True, [loB["tv"]], [psB_])
                v0c = V["rw_v0"][j - 1] + oc
                act(B, tm["vg"][:, :], ps_[:, :], AF.Sigmoid, [psB_], [tmB["vg"]], bias=vs[:, v0c:v0c + 1],
                    scale=1.0)
                B.dma("sp", tm["vf"][:, :], S["vfirst"][osl, t0:t0 + TT], writes=[tmB["vf"]])
                tt(B, "pool", tm["vf"][:, :], tm["vf"][:, :], tm["v"][:, :], ALU.subtract,
                   [tmB["vf"], tmB["v"]], [tmB["vf"]])
                tt(B, "pool", tm["vf"][:, :], tm["vf"][:, :], tm["vg"][:, :], ALU.mult,
                   [tmB["vf"], tmB["vg"]], [tmB["vf"]])
                tt(B, "pool", tm["v"][:, :], tm["v"][:, :], tm["vf"][:, :], ALU.add,
                   [tmB["vf"], tmB["v"]], [tmB["v"]])
            else:
                B.dma("sp", S["vfirst"][osl, t0:t0 + TT], tm["v"][:, :], reads=[tmB["v"]])
            yield
            kkc = V["rw_kk"][j] + oc
            ts(B, "pool", tm["kk"][:, :], tm["k"][:, :], vs[:, kkc:kkc + 1], None, ALU.mult, None,
               [tmB["k"]], [tmB["kk"]])
            act(B, sqk[:, :], tm["kk"][:, :], AF.Square, [tmB["kk"]], [sqkB])
            ps_, psB_ = bank(B)
            mm(B, ps_[:, :], bones[:, :], sqk[:, :], True, True, [sqkB, B.bonesB], [psB_])
            act(B, tm["rn"][:, :], ps_[:, :], AF.Sqrt, [psB_], [tmB["rn"]])
            ts(B, "dve", tm["rn"][:, :], tm["rn"][:, :], 1e-12, None, ALU.max, None, [tmB["rn"]], [tmB["rn"]])
            p.add("dve", lambda e: e.reciprocal(out=tm["rn"][:, :], in_=tm["rn"][:, :]), [tmB["rn"]], [tmB["rn"]])
            tt(B, "dve", tm["kk"][:, :], tm["kk"][:, :], tm["rn"][:, :], ALU.mult, [tmB["kk"], tmB["rn"]],
               [tmB["kk"]])
            yield
            kac = V["rw_ka"][j] + oc
            for dI in range(2):
                ic, kd, bb = tm[f"ic{dI}"], tm[f"kd{dI}"], tm[f"b{dI}"]
                icB, kdB, bbB = tmB[f"ic{dI}"], tmB[f"kd{dI}"], tmB[f"b{dI}"]
                ts(B, "pool", tm["t1"][:, :], ic[:, :], 1.0, vs[:, kac:kac + 1], ALU.subtract, ALU.mult,
                   [icB], [tmB["t1"]])
                stt(B, kd[:, :], tm["t1"][:, :], 1.0, tm["k"][:, :], ALU.add, ALU.mult, [tmB["t1"], tmB["k"]], [kdB])
                tt(B, "pool", bb[:, :], tm["kk"][:, :], ic[:, :], ALU.mult, [tmB["kk"], icB], [bbB])
            yield
            tt(B, "pool", tm["t1"][:, :], tm["kd0"][:, :], tm["kd1"][:, :], ALU.add, [tmB["kd0"], tmB["kd1"]],
               [tmB["t1"]])
            rkc = V["rw_rk"][j] + oc
            stt(B, rk[:, :], tm["t1"][:, :], vs[:, rkc:rkc + 1], tm["r"][:, :], ALU.mult, ALU.mult,
                [tmB["t1"], tmB["r"]], [rkB])
            ps_, psB_ = bank(B)
            mm(B, ps_[:, :], bones[:, :], rk[:, :], True, True, [rkB, B.bonesB], [psB_])
            tt(B, "dve", tm["bv"][:, :], ps_[:, :], tm["v"][:, :], ALU.mult, [psB_, tmB["v"]], [tmB["bv"]])
            yield
            ps_, psB_ = bank(B)
            for s4 in range(4):
                p.add("pe", lambda e, ps_=ps_, s4=s4: e.transpose(ps_[:, s4 * 128:(s4 + 1) * 128],
                                                                 tm["v"][:, s4 * 128:(s4 + 1) * 128], ident),
                      [tmB["v"]], [psB_])
            act(B, vt[:, :, :], ps_[:, :].rearrange("p (s c) -> p s c", c=128), AF.Copy, [psB_], [vtB])
            B.dma("sp", S["vtok"][t0:t0 + TT, osl].rearrange("(s p) c -> p s c", p=128), vt[:, :, :], reads=[vtB])
            yield
            for nm in ("r", "kk", "g", "bv", "lw0", "lw1", "kd0", "kd1", "b0", "b1"):
                B.dma("sp", S[nm][osl, t0:t0 + TT], tm[nm][:, :], reads=[tmB[nm]])

        for oc0 in range(0, 8, 2):
            gens = [oc_unit(oc0 + s_, TMs[s_], TMBs[s_], SQKs[s_], SQKBs[s_], RKs[s_], RKBs[s_], VTs[s_], VTBs[s_])
                    for s_ in range(2)]
            live = [True, True]
            while any(live):
                for gi, g_ in enumerate(gens):
                    if live[gi]:
                        try:
                            next(g_)
                        except StopIteration:
                            live[gi] = False
    p.barrier()
    sb.release(m0)


def rwkv_r2(B, S):
    nc, p, sb = B.nc, B.p, B.sb
    T = B.T
    TT, L, NQ = 512, 64, 8
    NT = T // TT
    m0 = sb.mark()
    cst = B.consts_sb
    C = B.C
    MASK = {k: cst[:, C[k]:C[k] + 128] for k in ("LT", "LE", "GT", "GE")}
    identbf = B.ident_bf
    blk_sb = sb.alloc([128, 768], F32, "blk")
    B.dma("sp", blk_sb[:, :], B.blkm_dram[:, :], writes=[Buf()])
    BLK = [blk_sb[:, 128 * li:128 * li + 128] for li in range(6)]
    onesf = sb.alloc([128, L], F32, "onesf")
    p.add("pool", lambda e: e.memset(onesf[:, :], 1.0))
    nseg = T // SEG

    class CS:
        pass

    def mk(tag):
        c_ = CS()
        inn = ["r", "kk", "lw", "kd", "b"]
        c_.inp = {n: sb.alloc([128, NQ, L], F32, n + tag) for n in inn}
        c_.inpB = {n: Buf() for n in inn}
        c_.Vf = sb.alloc([128, NQ, L], F32, "Vf" + tag)
        c_.VfB = Buf()
        c_.Vs = sb.alloc([128, NQ, L], BF16, "Vs" + tag)
        c_.VsB = Buf()
        f32n = ["P", "E", "Sx", "Si", "epos", "eneg", "egm", "er"]
        c_.ft = {n: sb.alloc([128, NQ, L], F32, n + tag) for n in f32n}
        c_.ftB = {n: Buf() for n in f32n}
        c_.wl = sb.alloc([128, NQ], F32, "wl" + tag)
        c_.wlB = Buf()
        bdn = ["bdr", "bda", "bdb", "bdk", "bdbw", "bdkw"]
        c_.bd = {n: sb.alloc([128, NQ, 128], BF16, n + tag) for n in bdn}
        c_.bdB = {n: Buf() for n in bdn}
        for n in bdn:
            p.add("pool", lambda e, t_=c_.bd[n]: e.memset(t_[:, :, :], 0.0), writes=[c_.bdB[n]])
        c_.X0 = sb.alloc([128, NQ, 128], BF16, "X0" + tag)
        c_.Y0 = sb.alloc([128, NQ, 128], BF16, "Y0" + tag)
        c_.X0B, c_.Y0B = Buf(), Buf()
        c_.xo = [sb.alloc([128, NQ, 128], BF16, f"xo{i}" + tag) for i in range(2)]
        c_.ao = [sb.alloc([128, NQ, 128], BF16, f"ao{i}" + tag) for i in range(2)]
        c_.xoB = [Buf(), Buf()]
        c_.aoB = [Buf(), Buf()]
        c_.Et = [sb.alloc([128, NQ, 128], BF16, f"E{i}" + tag) for i in range(2)]
        c_.Dt = [sb.alloc([128, NQ, 128], BF16, f"D{i}" + tag) for i in range(2)]
        c_.EtB = [Buf(), Buf()]
        c_.DtB = [Buf(), Buf()]
        c_.Qt = sb.alloc([128, NQ, 128], BF16, "Qt" + tag)
        c_.Rt = sb.alloc([128, NQ, 128], BF16, "Rt" + tag)
        c_.QtB, c_.RtB = Buf(), Buf()
        amn = ["ArbT", "AakT", "ArkT", "bWT", "kWT"]
        c_.am = {n: sb.alloc([128, NQ, 128], BF16, n + tag) for n in amn}
        c_.amB = {n: Buf() for n in amn}
        c_.ST = sb.alloc([128, L], F32, "ST" + tag)
        c_.STb = sb.alloc([128, L], BF16, "STb" + tag)
        c_.STB, c_.STbB = Buf(), Buf()
        c_.RHS = sb.alloc([128, L], BF16, "RHS" + tag)
        c_.U = sb.alloc([128, L], BF16, "U" + tag)
        c_.RHSB, c_.UB = Buf(), Buf()
        c_.yt = sb.alloc([128, NQ, L], F32, "yt" + tag)
        c_.ytB = Buf()
        return c_

    chains = [mk("f"), mk("b")]
    p.barrier()

    def unit(cs, d, c, t):
        fwd = (d == 0)
        M_strict, M_incl, M_strictT = (MASK["LT"], MASK["LE"], MASK["GT"]) if fwd else \
            (MASK["GT"], MASK["GE"], MASK["LT"])
        csl = slice(c * 128, (c + 1) * 128)
        t0 = t * TT
        seg = t0 // SEG
        I_, IB, ft, ftB, bd, bdB, am, amB = cs.inp, cs.inpB, cs.ft, cs.ftB, cs.bd, cs.bdB, cs.am, cs.amB
        ST, STb, STB, STbB = cs.ST, cs.STb, cs.STB, cs.STbB
        fl = None
        if fwd and t0 % SEG == 0 and seg > 0:
            fl = seg
        if (not fwd) and (t0 + TT) % SEG == 0 and seg < nseg - 1:
            fl = seg + 1
        if fl is not None:
            ts(B, "dve", ST[:, :], ST[:, :], B.flags_sb[:, fl:fl + 1], None, ALU.mult, None, [STB], [STB])
            ts(B, "dve", STb[:, :], STb[:, :], B.flags_sb[:, fl:fl + 1], None, ALU.mult, None, [STbB], [STbB])
        for n, key in (("r", "r"), ("kk", "kk"), ("lw", f"lw{d}"), ("kd", f"kd{d}"), ("b", f"b{d}")):
            B.dma("sp", I_[n][:, :, :].rearrange("p q l -> p (q l)"), S[key][csl, t0:t0 + TT], writes=[IB[n]])
        for h in range(2):
            col = (2 * c + h) * 64
            B.dma("sp", cs.Vf[h * 64:(h + 1) * 64, :, :],
                  S["vtok"][t0:t0 + TT, col:col + 64].rearrange("(q j) v -> j q v", j=L), writes=[cs.VfB])
        yield
        act(B, cs.Vs[:, :, :], cs.Vf[:, :, :], AF.Copy, [cs.VfB], [cs.VsB])
        lw = I_["lw"]
        for q in range(NQ):
            p.add("dve", lambda e, q=q: e.tensor_tensor_scan(
                out=ft["P"][:, q, :], data0=onesf[:, :], data1=lw[:, q, :], initial=0.0,
                op0=ALU.mult, op1=ALU.add), [IB["lw"]], [ftB["P"]])
        yield
        tot = ft["P"][:, :, L - 1:L]
        tt(B, "pool", ft["E"][:, :, :], ft["P"][:, :, :], lw[:, :, :], ALU.subtract, [ftB["P"], IB["lw"]], [ftB["E"]])
        tt(B, "dve", ft["Sx"][:, :, :], tot.to_broadcast([128, NQ, L]), ft["P"][:, :, :], ALU.subtract,
           [ftB["P"]], [ftB["Sx"]])
        if fwd:
            G, GB, Gm, GmB, R, RB = ft["P"], ftB["P"], ft["E"], ftB["E"], ft["Sx"], ftB["Sx"]
        else:
            tt(B, "pool", ft["Si"][:, :, :], ft["Sx"][:, :, :], lw[:, :, :], ALU.add, [ftB["Sx"], IB["lw"]],
               [ftB["Si"]])
            G, GB, Gm, GmB, R, RB = ft["Si"], ftB["Si"], ft["Sx"], ftB["Sx"], ft["E"], ftB["E"]
        yield
        act(B, ft["epos"][:, :, :], G[:, :, :], AF.Exp, [GB], [ftB["epos"]])
        act(B, ft["eneg"][:, :, :], G[:, :, :], AF.Exp, [GB], [ftB["eneg"]], scale=-1.0)
        yield
        act(B, ft["egm"][:, :, :], Gm[:, :, :], AF.Exp, [GmB], [ftB["egm"]])
        act(B, ft["er"][:, :, :], R[:, :, :], AF.Exp, [RB], [ftB["er"]])
        act(B, cs.wl[:, :], ft["P"][:, :, L - 1], AF.Exp, [ftB["P"]], [cs.wlB])
        yield
        k2 = 0
        for h in range(2):
            hs = slice(h * 64, (h + 1) * 64)
            stt(B, bd["bda"][hs, :, hs], I_["kk"][hs, :, :], -1.0, ft["egm"][hs, :, :], ALU.mult, ALU.mult,
                [IB["kk"], ftB["egm"]], [bdB["bda"]])
            for dst, a_, e_ in (("bdr", "r", "epos"), ("bdb", "b", "eneg"), ("bdk", "kd", "eneg"),
                                ("bdbw", "b", "er"), ("bdkw", "kd", "er")):
                eng = "dve" if k2 % 2 == 0 else "pool"
                k2 += 1
                tt(B, eng, bd[dst][hs, :, hs], I_[a_][hs, :, :], ft[e_][hs, :, :], ALU.mult,
                   [IB[a_], ftB[e_]], [bdB[dst]])
            yield

        def grp(lhs, lhsB, rhs, rhsB, evac, g):
            ps_, psB_ = bank(B)
            for qq in range(4):
                q = g * 4 + qq
                rr_ = rhs if rhs is identbf else None
                mm(B, ps_[:, qq * 128:(qq + 1) * 128], lhs[:, q, :],
                   (identbf[:, :] if rhs is identbf else rhs[:, q, :]), True, True,
                   [lhsB] + ([] if rhs is identbf else [rhsB]), [psB_])
            evac(g, ps_[:, :].rearrange("p (q c) -> p q c", c=128), psB_)

        def ev_mask(dst, dstB, mask):
            return lambda g, pv, pB: tt(B, "dve", dst[:, g * 4:g * 4 + 4, :], pv,
                                        mask.unsqueeze(1).to_broadcast([128, 4, 128]), ALU.mult, [pB], [dstB])

        def ev_act(dst, dstB):
            return lambda g, pv, pB: act(B, dst[:, g * 4:g * 4 + 4, :], pv, AF.Copy, [pB], [dstB])

        def ev_dve(dst, dstB):
            return lambda g, pv, pB: p.add("dve", lambda e: e.tensor_copy(out=dst[:, g * 4:g * 4 + 4, :], in_=pv),
                                           [pB], [dstB])

        def ev_add(dst, dstB, old, oldB):
            return lambda g, pv, pB: tt(B, "dve", dst[:, g * 4:g * 4 + 4, :], pv, old[:, g * 4:g * 4 + 4, :],
                                        ALU.add, [pB, oldB], [dstB])

        for (lh, rh, mk_, dst, dstB) in (
                ("bdb", "bda", M_strict, cs.X0, cs.X0B), ("bda", "bdb", M_strictT, cs.Y0, cs.Y0B),
                ("bdb", "bdr", M_incl, am["ArbT"], amB["ArbT"]), ("bdk", "bda", M_strict, am["AakT"], amB["AakT"]),
                ("bdk", "bdr", M_incl, am["ArkT"], amB["ArkT"])):
            for g in range(2):
                grp(bd[lh], bdB[lh], bd[rh], bdB[rh], ev_mask(dst, dstB, mk_), g)
                yield
        for nm, src_ in (("bWT", "bdbw"), ("kWT", "bdkw")):
            for g in range(2):
                grp(bd[src_], bdB[src_], identbf, None, ev_act(am[nm], amB[nm]), g)
                yield
        idb = identbf[:, :].unsqueeze(1).to_broadcast([128, NQ, 128])

        def offs(li, slot):
            mk2 = BLK[li].unsqueeze(1).to_broadcast([128, NQ, 128])
            tt(B, "pool", cs.xo[slot][:, :, :], cs.X0[:, :, :], mk2, ALU.mult, [cs.X0B], [cs.xoB[slot]])
            tt(B, "pool", cs.ao[slot][:, :, :], cs.Y0[:, :, :], mk2, ALU.mult, [cs.Y0B], [cs.aoB[slot]])

        offs(0, 0)
        cur = 0
        tt(B, "pool", cs.Et[0][:, :, :], cs.xo[0][:, :, :], idb, ALU.add, [cs.xoB[0]], [cs.EtB[0]])
        tt(B, "pool", cs.Dt[0][:, :, :], cs.ao[0][:, :, :], idb, ALU.add, [cs.aoB[0]], [cs.DtB[0]])
        offs(1, 1)
        yield
        for li in range(1, 6):
            lastl = (li == 5)
            nxt = 1 - cur
            sl_ = li % 2
            xo, xoB, ao, aoB = cs.xo[sl_], cs.xoB[sl_], cs.ao[sl_], cs.aoB[sl_]
            E_, EB_, D_, DB_ = cs.Et[cur], cs.EtB[cur], cs.Dt[cur], cs.DtB[cur]
            for g in range(2):
                grp(ao, aoB, E_, EB_, ev_act(cs.Qt, cs.QtB), g)
                yield
            if not lastl:
                for g in range(2):
                    grp(xo, xoB, D_, DB_, ev_dve(cs.Rt, cs.RtB), g)
                    yield
            for g in range(2):
                grp(D_, DB_, cs.Qt, cs.QtB, ev_add(cs.Et[nxt], cs.EtB[nxt], E_, EB_), g)
                yield
            if not lastl:
                for g in range(2):
                    grp(E_, EB_, cs.Rt, cs.RtB, ev_add(cs.Dt[nxt], cs.DtB[nxt], D_, DB_), g)
                    yield
                offs(li + 1, (li + 1) % 2)
            cur = nxt
        Z, ZB = cs.Et[cur], cs.EtB[cur]
        Vs, VsB, RHS, U, RHSB, UB, wl, wlB = cs.Vs, cs.VsB, cs.RHS, cs.U, cs.RHSB, cs.UB, cs.wl, cs.wlB
        qs = list(range(NQ)) if fwd else list(range(NQ - 1, -1, -1))
        for q in qs:
            ps1, ps1B = bank(B)
            mm(B, ps1[:, 0:L], bd["bda"][:, q, :], STb[:, :], True, False, [bdB["bda"], STbB], [ps1B])
            mm(B, ps1[:, 0:L], am["AakT"][:, q, :], Vs[:, q, :], False, True, [amB["AakT"], VsB], [ps1B])
            act(B, RHS[:, :], ps1[:, 0:L], AF.Copy, [ps1B], [RHSB])
            yield
            ps2, ps2B = bank(B)
            mm(B, ps2[:, 0:L], Z[:, q, :], RHS[:, :], True, True, [ZB, RHSB], [ps2B])
            p.add("dve", lambda e, ps2=ps2: e.tensor_copy(out=U[:, :], in_=ps2[:, 0:L]), [ps2B], [UB])
            yield
            ps3, ps3B = bank(B)
            mm(B, ps3[:, 0:L], bd["bdr"][:, q, :], STb[:, :], True, False, [bdB["bdr"], STbB], [ps3B])
            mm(B, ps3[:, 0:L], am["ArbT"][:, q, :], U[:, :], False, False, [amB["ArbT"], UB], [ps3B])
            mm(B, ps3[:, 0:L], am["ArkT"][:, q, :], Vs[:, q, :], False, True, [amB["ArkT"], VsB], [ps3B])
            act(B, cs.yt[:, q, :], ps3[:, 0:L], AF.Copy, [ps3B], [cs.ytB])
            ps4, ps4B = bank(B)
            mm(B, ps4[:, 0:L], am["bWT"][:, q, :], U[:, :], True, False, [amB["bWT"], UB], [ps4B])
            mm(B, ps4[:, 0:L], am["kWT"][:, q, :], Vs[:, q, :], False, True, [amB["kWT"], VsB], [ps4B])
            stt(B, STb[:, :], ST[:, :], wl[:, q:q + 1], ps4[:, 0:L], ALU.mult, ALU.add, [STB, wlB, ps4B], [STbB])
            stt(B, ST[:, :], ST[:, :], wl[:, q:q + 1], ps4[:, 0:L], ALU.mult, ALU.add, [STB, wlB, ps4B], [STB])
            yield
        for h in range(2):
            col = (2 * c + h) * 64
            B.dma("pool", S[f"ytok{d}"][t0:t0 + TT, col:col + 64].rearrange("(q t) v -> t q v", t=L),
                  cs.yt[h * 64:(h + 1) * 64, :, :], reads=[cs.ytB])
        yield

    for c in range(8):
        for cs in chains:
            p.add("pool", lambda e, cs=cs: e.memset(cs.ST[:, :], 0.0), writes=[cs.STB])
            p.add("pool", lambda e, cs=cs: e.memset(cs.STb[:, :], 0.0), writes=[cs.STbB])
        for k in range(NT):
            gens = [unit(chains[0], 0, c, k), unit(chains[1], 1, c, NT - 1 - k)]
            live = [True, True]
            while any(live):
                for gi, g_ in enumerate(gens):
                    if live[gi]:
                        try:
                            next(g_)
                        except StopIteration:
                            live[gi] = False
    p.barrier()
    sb.release(m0)


GN_EPS = 64e-5


def rwkv_r3(B, j, xin, xout, W, V, S):
    nc, p, sb = B.nc, B.p, B.sb
    T = B.T
    TT = 512
    NT = T // TT
    vs = B.vecs_sb
    m0 = sb.mark()
    cst = B.consts_sb
    ident = cst[:, B.C["ident"]:B.C["ident"] + 128]
    wo = sb.alloc([128, 8, D], BF16, "wo")
    woB = Buf()
    stg_t = [sb.alloc([128, 4096], F32, "stg") for _ in range(2)]
    stg = Stager(B, stg_t, [Buf(), Buf()])
    src = W["rw_o"][j].rearrange("(c p) n -> p c n", p=128)
    for hI in range(2):
        stg.load(wo[:, 4 * hI:4 * hI + 4, :], src[:, 4 * hI:4 * hI + 4, :],
                 lambda t: t[:, :].rearrange("p (c n) -> p c n", n=D), woB)
    yin = [[sb.alloc([128, 16, 64], F32, f"y{d}") for d in range(2)] for _ in range(2)]
    yinB = [[Buf(), Buf()] for _ in range(2)]
    ys = sb.alloc([128, 16, 64], F32, "ys")
    ysB = Buf()
    sqc = sb.alloc([128, 16, 64], F32, "sqc")
    sqcB = Buf()
    yn = sb.alloc([128, 16, 64], F32, "yn")
    ynB = Buf()
    st1 = sb.alloc([128, 16], F32, "st1")
    st2 = sb.alloc([128, 16], F32, "st2")
    st1B, st2B = Buf(), Buf()
    gne = sb.alloc([128, 1], F32, "gne")
    gneB = Buf()
    p.add("pool", lambda e: e.memset(gne[:, :], GN_EPS), writes=[gneB])
    zt = sb.alloc([128, 8, TT], F32, "zt")
    ztB = [Buf() for _ in range(8)]
    zb = sb.alloc([128, 8, TT], BF16, "zb")
    zbB = [Buf() for _ in range(8)]
    bvt = [sb.alloc([128, TT], F32, "bvt") for _ in range(2)]
    gt = [sb.alloc([128, TT], F32, "gt") for _ in range(2)]
    bvB = [Buf(), Buf()]
    gB = [Buf(), Buf()]
    xt = sb.alloc([128, 8, TT], F32, "xt")
    xtB = Buf()
    p.barrier()
    xiv = xin.rearrange("(c p) t -> p c t", p=128)
    xov = xout.rearrange("(c p) t -> p c t", p=128)
    lg, lb = V["rw_lnx_g"][j], V["rw_lnx_b"][j]
    k = 0
    for t in range(NT):
        t0 = t * TT
        B.dma("sp", xt[:, :, :], xiv[:, :, t0:t0 + TT], writes=[xtB])
        for s4 in range(4):
            bi = k % 2
            k += 1
            r0 = t0 + s4 * 128
            for d in range(2):
                B.dma("sp", yin[bi][d][:, :, :].rearrange("p h v -> p (h v)"), S[f"ytok{d}"][r0:r0 + 128, :],
                      writes=[yinB[bi][d]])
            tt(B, "pool", ys[:, :, :], yin[bi][0][:, :, :], yin[bi][1][:, :, :], ALU.add,
               [yinB[bi][0], yinB[bi][1]], [ysB])
            p.add("dve", lambda e: e.tensor_reduce(out=st1[:, :], in_=ys[:, :, :], axis=AX.X, op=ALU.add),
                  [ysB], [st1B])
            ts(B, "pool", st1[:, :], st1[:, :], -1.0 / 64, None, ALU.mult, None, [st1B], [st1B])
            tt(B, "dve", ys[:, :, :], ys[:, :, :], st1[:, :].unsqueeze(2).to_broadcast([128, 16, 64]), ALU.add,
               [ysB, st1B], [ysB])
            act(B, sqc[:, :, :], ys[:, :, :], AF.Square, [ysB], [sqcB])
            p.add("dve", lambda e: e.tensor_reduce(out=st2[:, :], in_=sqc[:, :, :], axis=AX.X, op=ALU.add),
                  [sqcB], [st2B])
            act(B, st2[:, :], st2[:, :], AF.Sqrt, [st2B, gneB], [st2B], bias=gne[:, 0:1], scale=1.0 / 64)
            p.add("dve", lambda e: e.reciprocal(out=st2[:, :], in_=st2[:, :]), [st2B], [st2B])
            tt(B, "dve", yn[:, :, :], ys[:, :, :], st2[:, :].unsqueeze(2).to_broadcast([128, 16, 64]), ALU.mult,
               [ysB, st2B], [ynB])
            ynf = yn[:, :, :].rearrange("p h v -> p (h v)")
            for g2 in range(2):
                ps_, psB_ = bank(B)
                for o4 in range(4):
                    oc = g2 * 4 + o4
                    p.add("pe", lambda e, ps_=ps_, o4=o4, oc=oc: e.transpose(
                        ps_[:, o4 * 128:(o4 + 1) * 128], ynf[:, oc * 128:(oc + 1) * 128], ident), [ynB], [psB_])
                for o4 in range(4):
                    oc = g2 * 4 + o4
                    act(B, zt[:, oc, s4 * 128:(s4 + 1) * 128], ps_[:, o4 * 128:(o4 + 1) * 128], AF.Identity,
                        [psB_], [ztB[oc]], bias=vs[:, lb + oc:lb + oc + 1], scale=vs[:, lg + oc:lg + oc + 1])
        for oc in range(8):
            osl = slice(oc * 128, (oc + 1) * 128)
            bi = oc % 2
            B.dma("sp", bvt[bi][:, :], S["bv"][osl, t0:t0 + TT], writes=[bvB[bi]])
            B.dma("sp", gt[bi][:, :], S["g"][osl, t0:t0 + TT], writes=[gB[bi]])
            tt(B, "pool", zt[:, oc, :], zt[:, oc, :], bvt[bi][:, :], ALU.add, [ztB[oc], bvB[bi]], [ztB[oc]])
            tt(B, "dve", zb[:, oc, :], zt[:, oc, :], gt[bi][:, :], ALU.mult, [ztB[oc], gB[bi]], [zbB[oc]])
        for oc in range(8):
            ps_, psB_ = bank(B)
            for c in range(8):
                mm(B, ps_[:, :], wo[:, c, oc * 128:(oc + 1) * 128], zb[:, c, :], c == 0, c == 7, [woB, zbB[c]],
                   [psB_])
            tt(B, "dve", xt[:, oc, :], xt[:, oc, :], ps_[:, :], ALU.add, [xtB, psB_], [xtB])
        B.dma("pool", xov[:, :, t0:t0 + TT], xt[:, :, :], reads=[xtB])
    p.barrier()
    sb.release(m0)


CONST_LAYOUT = {"ident": 0, "LT": 128, "LE": 256, "GT": 384, "GE": 512, "bones": 640}
NCONST = 768


def host_consts():
    pp = np.arange(128)[:, None]
    ff = np.arange(128)[None, :]
    out = np.zeros((128, NCONST), np.float32)
    out[:, 0:128] = (pp == ff)
    out[:, 128:256] = (pp % 64 < ff % 64)
    out[:, 256:384] = (pp % 64 <= ff % 64)
    out[:, 384:512] = (pp % 64 > ff % 64)
    out[:, 512:640] = (pp % 64 >= ff % 64)
    out[:, 640:768] = (pp // 64 == ff // 64)
    return out


def host_blk_masks():
    pp = np.arange(128)[:, None]
    ff = np.arange(128)[None, :]
    out = np.zeros((128, 768), np.float32)
    for li, s in enumerate((1, 2, 4, 8, 16, 32)):
        out[:, 128 * li:128 * li + 128] = ((pp // 64 == ff // 64) & (pp // (2 * s) == ff // (2 * s))
                                           & (pp // s != ff // s))
    return out


def setup_common(B, ncols):
    nc, p, sb = B.nc, B.p, B.sb
    B.bank_i = 0
    B.C = CONST_LAYOUT
    consts = B.din("consts", [128, NCONST]).ap()
    flags = B.din("flags", [128, 8]).ap()
    B.blkm_dram = B.din("blkm", [128, 768]).ap()
    vecs = B.din("vecs", [128, ncols]).ap()
    B.consts_sb = sb.alloc([128, NCONST], F32, "consts")
    B.flags_sb = sb.alloc([128, 8], F32, "flags")
    B.vecs_sb = sb.alloc([128, ncols], F32, "vecs")
    B.ident_bf = sb.alloc([128, 128], BF16, "identbf")
    B.bones_bf = sb.alloc([128, 128], BF16, "bonesbf")
    B.bonesB = Buf()
    cB = Buf()
    B.dma("sp", B.consts_sb[:, :], consts[:, :], writes=[cB])
    B.dma("sp", B.flags_sb[:, :], flags[:, :], writes=[Buf()])
    B.dma("sp", B.vecs_sb[:, :], vecs[:, :], writes=[Buf()])
    p.add("dve", lambda e: e.tensor_copy(out=B.ident_bf[:, :], in_=B.consts_sb[:, 0:128]), [cB], [Buf()])
    p.add("dve", lambda e: e.tensor_copy(out=B.bones_bf[:, :], in_=B.consts_sb[:, 640:768]), [cB], [B.bonesB])
    p.barrier()


RW_SCRATCH_F = ["r", "kk", "g", "bv", "lw0", "lw1", "kd0", "kd1", "b0", "b1", "vfirst"]


def alloc_rwkv_scratch(B):
    T = B.T
    S = {n: B.dscr("s_" + n, [D, T]).ap() for n in RW_SCRATCH_F}
    for n in ("vtok", "ytok0", "ytok1"):
        S[n] = B.dscr("s_" + n, [T, D]).ap()
    return S


def pack_vectors(inp):
    vp = VecPack()
    V = {}
    V["norm_mix_g"] = [vp.add(f"nmg{l}", inp["norm_mix_g"][l]) for l in range(inp["norm_mix_g"].shape[0])]
    V["norm_mlp_g"] = [vp.add(f"nlg{l}", inp["norm_mlp_g"][l]) for l in range(inp["norm_mlp_g"].shape[0])]
    nrw = inp["rw_mix"].shape[0]
    V["rw_mix"] = [vp.add(f"mix{j}", inp["rw_mix"][j]) for j in range(nrw)]
    V["rw_w0"] = [[vp.add(f"w0{j}{d}", inp["rw_w0"][j, d]) for d in range(2)] for j in range(nrw)]
    V["rw_a0"] = [[vp.add(f"a0{j}{d}", inp["rw_a0"][j, d]) for d in range(2)] for j in range(nrw)]
    V["rw_v0"] = [vp.add(f"v0{j}", inp["rw_v0"][j]) for j in range(inp["rw_v0"].shape[0])]
    for nm in ("rw_kk", "rw_ka", "rw_rk", "rw_lnx_g", "rw_lnx_b"):
        V[nm] = [vp.add(f"{nm}{j}", inp[nm][j]) for j in range(nrw)]
    if "na_q_g" in inp:
        nna = inp["na_q_g"].shape[0]
        V["na_q_g"] = [vp.add(f"qg{j}", np.tile(inp["na_q_g"][j], 2)) for j in range(nna)]
        V["na_k_g"] = [vp.add(f"kg{j}", np.tile(inp["na_k_g"][j], 2)) for j in range(nna)]
    return vp, V


RW_WEIGHTS = ["rw_rkv", "rw_w1", "rw_w2", "rw_a1", "rw_a2", "rw_v1", "rw_v2", "rw_g1", "rw_g2", "rw_o"]


def build_rwkv_probe(T, ncols, V, shapes, j, layer):
    B = Builder(T)
    nc = B.nc
    setup_common(B, ncols)
    xT = B.din("xT", [D, T]).ap()
    W = {n: B.din(n, list(shapes[n])).ap() for n in RW_WEIGHTS}
    yT = B.dout("yT", [D, T]).ap()
    S = alloc_rwkv_scratch(B)
    if j > 0:
        vf_in = B.din("vfirst_in", [D, T]).ap()
        S["vfirst"] = vf_in
    rwkv_r1(B, j, layer, xT, W, V, S)
    rwkv_r2(B, S)
    rwkv_r3(B, j, xT, yT, W, V, S)
    with ExitStack() as st:
        B.p.emit(nc, st)
    return B


GRID_W = 64
ROWS_SEG = SEG // GRID_W
NEG = -30000.0


def na_window(i, kind, nseg_sample=4):
    seg = i // ROWS_SEG
    if kind == "S" and seg < nseg_sample:
        rows = nseg_sample * ROWS_SEG
        return int(np.clip(i - 4, 0, rows - 8))
    li = i % ROWS_SEG
    return seg * ROWS_SEG + int(np.clip(li - 4, 0, ROWS_SEG - 8))


def na_slots(T):
    nrows = T // GRID_W
    nseg = T // SEG
    nss = min(4, nseg)
    out = []
    for i in range(nrows):
        lo = min(na_window(i, "P"), na_window(i, "S", nss))
        hi = max(na_window(i, "P"), na_window(i, "S", nss)) + 8
        out.append(list(range(lo // 2, (hi - 1) // 2 + 1)))
    return out


def host_na_nbias(T, kind):
    slots = na_slots(T)
    nss = min(4, T // SEG)
    cols = []
    for i, ms in enumerate(slots):
        lo = na_window(i, kind, nss)
        for m in ms:
            col = np.zeros(128, np.float32)
            for hf in range(2):
                r = 2 * m + hf
                if not (lo <= r < lo + 8):
                    col[hf * 64:(hf + 1) * 64] = NEG
            cols.append(col)
    return np.ascontiguousarray(np.stack(cols, axis=1))


def host_na_bias_table(rpb):
    qc = np.arange(64)
    kc = np.arange(64)
    ws = np.clip(qc - 8, 0, 48)
    cm = (kc[:, None] >= ws[None, :]) & (kc[:, None] < ws[None, :] + 16)
    dc = np.clip(kc[:, None] - qc[None, :] + 15, 0, 30)
    out = np.full((16, 128, 16, 64), NEG, np.float32)
    for e in range(16):
        for hf in range(2):
            dr = e - 8 + hf
            if abs(dr) > 7:
                continue
            g = rpb[:, dr + 7, :][:, dc]
            g = np.where(cm[None], g, np.float32(NEG))
            out[e, hf * 64:(hf + 1) * 64] = np.transpose(g, (1, 0, 2))
    return out


def na_n1(B, jn, layer, xin, W, V, S):
    nc, p, sb = B.nc, B.p, B.sb
    T = B.T
    TT = 512
    NT = T // TT
    vs = B.vecs_sb
    m0 = sb.mark()
    wq = sb.alloc([128, 8, 3 * D], BF16, "wqkv")
    wqB = Buf()
    stg_t = [sb.alloc([128, 4096], F32, "stg") for _ in range(2)]
    stg = Stager(B, stg_t, [Buf(), Buf()])
    src = W["na_qkv"][jn].rearrange("(c p) n -> p c n", p=128)
    for c in range(8):
        for h3 in range(3):
            if h3 < 2:
                stg.load(wq[:, c, h3 * 1024:(h3 + 1) * 1024], src[:, c, h3 * 1024:(h3 + 1) * 1024],
                         lambda t: t[:, 0:1024], wqB)
            else:
                stg.load(wq[:, c, 2048:3072], src[:, c, 2048:3072], lambda t: t[:, 0:1024], wqB)
    xs = [sb.alloc([128, 8, TT], F32, "x") for _ in range(2)]
    xB = [Buf(), Buf()]
    sq = sb.alloc([128, 8, TT], BF16, "sq")
    sqB = Buf()
    hn = sb.alloc([128, 8, TT], BF16, "hn")
    hnB = Buf()
    rs = sb.alloc([128, TT], F32, "rs")
    rstd = sb.alloc([128, TT], F32, "rstd")
    rsB, rstdB = Buf(), Buf()
    epsb = sb.alloc([128, 2], F32, "eps")
    epsB = Buf()
    p.add("pool", lambda e: e.memset(epsb[:, 0:1], RMS_EPS), writes=[epsB])
    p.add("pool", lambda e: e.memset(epsb[:, 1:2], 64 * RMS_EPS), writes=[epsB])
    tq = [sb.alloc([128, TT], F32, "tq") for _ in range(2)]
    tqB = [Buf(), Buf()]
    sqq = [sb.alloc([128, TT], BF16, "sqq") for _ in range(2)]
    sqqB = [Buf(), Buf()]
    rq = [sb.alloc([128, TT], F32, "rq") for _ in range(2)]
    rqB = [Buf(), Buf()]
    qo = [sb.alloc([128, TT], BF16, "qo") for _ in range(2)]
    qoB = [Buf(), Buf()]
    vtl = [sb.alloc([128, D], BF16, "vtl") for _ in range(2)]
    vtlB = [Buf(), Buf()]
    p.barrier()
    xiv = xin.rearrange("(c p) t -> p c t", p=128)
    gc = V["norm_mix_g"][layer]
    k2 = 0
    for t in range(NT):
        t0 = t * TT
        b = t % 2
        x = xs[b]
        B.dma("sp", x[:, :, :], xiv[:, :, t0:t0 + TT], writes=[xB[b]])
        for hI in range(2):
            act(B, sq[:, 4 * hI:4 * hI + 4, :], x[:, 4 * hI:4 * hI + 4, :], AF.Square, [xB[b]], [sqB])
        psA, psAB = bank(B)
        for c in range(8):
            mm(B, psA[:, :], B.ones_bf[:, :], sq[:, c, :], c == 0, c == 7, [B.onesB, sqB], [psAB])
        act(B, rs[:, :], psA[:, :], AF.Sqrt, [psAB, epsB], [rsB], bias=epsb[:, 0:1], scale=1.0 / D)
        p.add("dve", lambda e: e.reciprocal(out=rstd[:, :], in_=rs[:, :]), [rsB], [rstdB])
        for c in range(8):
            stt(B, hn[:, c, :], x[:, c, :], vs[:, gc + c:gc + c + 1], rstd[:, :], ALU.mult, ALU.mult,
                [xB[b], rstdB], [hnB])
        for mI in range(2):
            gcol = V["na_q_g"][jn] if mI == 0 else V["na_k_g"][jn]
            for oc in range(8):
                bi = k2 % 2
                k2 += 1
                ps_, psB_ = bank(B)
                for c in range(8):
                    mm(B, ps_[:, :], wq[:, c, mI * 1024 + oc * 128:mI * 1024 + (oc + 1) * 128], hn[:, c, :],
                       c == 0, c == 7, [hnB], [psB_])
                act(B, tq[bi][:, :], ps_[:, :], AF.Copy, [psB_], [tqB[bi]])
                tt(B, "pool", sqq[bi][:, :], tq[bi][:, :], tq[bi][:, :], ALU.mult, [tqB[bi]], [sqqB[bi]])
                ps2, ps2B = bank(B)
                mm(B, ps2[:, :], B.bones_bf[:, :], sqq[bi][:, :], True, True, [sqqB[bi], B.bonesB], [ps2B])
                if mI == 0:
                    act(B, rq[bi][:, :], ps2[:, :], AF.Sqrt, [ps2B, epsB], [rqB[bi]], bias=epsb[:, 1:2], scale=1.0)
                else:
                    act(B, rq[bi][:, :], ps2[:, :], AF.Sqrt, [ps2B, epsB], [rqB[bi]], bias=epsb[:, 0:1],
                        scale=1.0 / 64)
                p.add("dve", lambda e, bi=bi: e.reciprocal(out=rq[bi][:, :], in_=rq[bi][:, :]), [rqB[bi]], [rqB[bi]])
                stt(B, qo[bi][:, :], tq[bi][:, :], vs[:, gcol:gcol + 1], rq[bi][:, :], ALU.mult, ALU.mult,
                    [tqB[bi], rqB[bi]], [qoB[bi]])
                dst = S["qT"] if mI == 0 else S["kT"]
                B.dma("sp", dst[oc * 128:(oc + 1) * 128, t0:t0 + TT], qo[bi][:, :], reads=[qoB[bi]])
        for tb in range(4):
            bi = tb % 2
            for hf in range(2):
                ps_, psB_ = bank(B)
                for c in range(8):
                    mm(B, ps_[:, :], hn[:, c, tb * 128:(tb + 1) * 128], wq[:, c, 2048 + hf * 512:2048 + (hf + 1) * 512],
                       c == 0, c == 7, [hnB], [psB_])
                if hf == 0:
                    act(B, vtl[bi][:, 0:512], ps_[:, :], AF.Copy, [psB_], [vtlB[bi]])
                else:
                    p.add("dve", lambda e, ps_=ps_, bi=bi: e.tensor_copy(out=vtl[bi][:, 512:1024], in_=ps_[:, :]),
                          [psB_], [vtlB[bi]])
            B.dma("sp", S["vtokb"][t0 + tb * 128:t0 + (tb + 1) * 128, :], vtl[bi][:, :], reads=[vtlB[bi]])
    p.barrier()
    sb.release(m0)


def na_n2(B, jn, xin, xout, W, S, nbias_dram, btab_dram, dbg=9):
    nc, p, sb = B.nc, B.p, B.sb
    T = B.T
    TT = 512
    NT = T // TT
    nrows = T // GRID_W
    slots = na_slots(T)
    nslot_tot = sum(len(s) for s in slots)
    m0 = sb.mark()
    cst = B.consts_sb
    ident = cst[:, B.C["ident"]:B.C["ident"] + 128]
    wo = sb.alloc([128, 8, D], BF16, "wo")
    woB = Buf()
    btab = sb.alloc([128, 16, 16, 64], BF16, "btab")
    btB = Buf()
    nb = sb.alloc([128, nslot_tot], F32, "nbias")
    m1 = sb.mark()
    stg_t = [sb.alloc([128, 4096], F32, "stg") for _ in range(2)]
    stgB = [Buf(), Buf()]
    stg = Stager(B, stg_t, stgB)
    src = W["na_o"][jn].rearrange("(c p) n -> p c n", p=128)
    for hI in range(2):
        stg.load(wo[:, 4 * hI:4 * hI + 4, :], src[:, 4 * hI:4 * hI + 4, :],
                 lambda t: t[:, :].rearrange("p (c n) -> p c n", n=D), woB)
    for e in range(16):
        stg.load(btab[:, e, :, :], btab_dram[e].rearrange("p (h q) -> p h q", q=64),
                 lambda t: t[:, 0:1024].rearrange("p (h q) -> p h q", q=64), btB)
    B.dma("sp", nb[:, :], nbias_dram[:, :], writes=[Buf()])
    p.barrier()
    sb.release(m1)
    NKR = 24
    KT = sb.alloc([128, 8, NKR * 64], BF16, "KT")
    KTB = Buf()
    QT = sb.alloc([128, 8, TT], BF16, "QT")
    QTB = Buf()
    Vraw = sb.alloc([128, NKR // 2, D], BF16, "Vraw")
    VrawB = Buf()
    Vaug = sb.alloc([128, NKR // 2, 16, 68], BF16, "Vaug")
    VaugB = Buf()
    p.add("pool", lambda e: e.memset(Vaug[:, :, :, 64:68], 0.0), writes=[VaugB])
    p.add("pool", lambda e: e.memset(Vaug[:, :, :, 64:65], 1.0), writes=[VaugB])
    NPT = 8
    PT = [sb.alloc([128, 16, 64], BF16, f"PT{i}") for i in range(NPT)]
    PTB = [Buf() for _ in range(NPT)]
    tmp = [sb.alloc([128, 8, 64], F32, "tmp") for _ in range(2)]
    tmpB = [Buf(), Buf()]
    rc = sb.alloc([64, 16], F32, "rc")
    rcB = Buf()
    o = sb.alloc([64, 16, 64], F32, "o")
    oB = Buf()
    oT = sb.alloc([128, 8, TT], BF16, "oT")
    oTB = [Buf() for _ in range(8)]
    xt = sb.alloc([128, 8, TT], F32, "xt")
    xtB = Buf()
    p.barrier()
    xiv = xin.rearrange("(c p) t -> p c t", p=128)
    xov = xout.rearrange("(c p) t -> p c t", p=128)
    qv = S["qT"].rearrange("(c p) t -> p c t", p=128)
    kv = S["kT"].rearrange("(c p) t -> p c t", p=128)
    scol = 0
    k2 = 0
    for t in range(NT):
        t0 = t * TT
        i0 = t0 // GRID_W
        klo = max(0, i0 - 8)
        khi = min(nrows, i0 + 16)
        nk = khi - klo
        B.dma("sp", xt[:, :, :], xiv[:, :, t0:t0 + TT], writes=[xtB])
        B.dma("sp", QT[:, :, :], qv[:, :, t0:t0 + TT], writes=[QTB])
        B.dma("sp", KT[:, :, 0:nk * 64], kv[:, :, klo * 64:khi * 64], writes=[KTB])
        B.dma("sp", Vraw[:, 0:nk // 2, :], S["vtokb"][klo * 64:khi * 64, :].rearrange("(m p) c -> p m c", p=128),
              writes=[VrawB])
        p.add("pool", lambda e, nk=nk: e.tensor_copy(
            out=Vaug[:, 0:nk // 2, :, 0:64], in_=Vraw[:, 0:nk // 2, :].rearrange("p m (h v) -> p m h v", v=64)),
            [VrawB], [VaugB])
        for rr in range(8):
            i = i0 + rr
            ms = slots[i]
            assert len(ms) <= NPT
            for si, m in enumerate(ms if dbg >= 2 else []):
                pl = m - klo // 2
                e_ = 2 * m - i + 8
                assert 0 <= pl < nk // 2 and 0 <= e_ < 16, (i, m, pl, e_)
                pss = [bank(B), bank(B)]
                for h in range(16):
                    hs = slice((h % 2) * 64, (h % 2) * 64 + 64)
                    mm(B, pss[h % 2][0][:, (h // 2) * 64:(h // 2 + 1) * 64], KT[hs, h // 2, pl * 128:(pl + 1) * 128],
                       QT[hs, h // 2, rr * 64:(rr + 1) * 64], True, True, [KTB, QTB], [pss[h % 2][1]])
                for g in range(2):
                    ps_, psB_ = pss[g]
                    tb_ = k2 % 2
                    k2 += 1
                    import os
                    sub = int(os.environ.get("NA_SUB", "9"))
                    if sub >= 2:
                        tt(B, "dve", tmp[tb_][:, :, :], ps_[:, :].rearrange("p (h q) -> p h q", q=64),
                           btab[:, e_, g:16:2, :], ALU.add, [psB_, btB], [tmpB[tb_]])
                    if sub >= 3:
                        act(B, PT[si][:, g:16:2, :], tmp[tb_][:, :, :], AF.Exp, [tmpB[tb_]], [PTB[si]],
                            bias=nb[:, scol:scol + 1], scale=1.0)
                scol += 1
            if dbg < 2:
                scol += len(ms)
            pvb = [bank(B) for _ in range(4)]
            for h in range(16 if dbg >= 3 else 0):
                ps_, psB_ = pvb[h // 4]
                for si, m in enumerate(ms):
                    pl = m - klo // 2
                    mm(B, ps_[0:64, (h % 4) * 128:(h % 4) * 128 + 66], PT[si][:, h, :], Vaug[:, pl, h, 0:66],
                       si == 0, si == len(ms) - 1, [PTB[si], VaugB], [psB_])
            for b4 in range(4 if dbg >= 4 else 0):
                ps_, psB_ = pvb[b4]
                pv3 = ps_[0:64, :].rearrange("p (h c) -> p h c", c=128)
                p.add("dve", lambda e, pv3=pv3, b4=b4: e.reciprocal(out=rc[:, b4 * 4:(b4 + 1) * 4], in_=pv3[:, :, 64]),
                      [psB_], [rcB])
                tt(B, "dve", o[:, b4 * 4:(b4 + 1) * 4, :], pv3[:, :, 0:64],
                   rc[:, b4 * 4:(b4 + 1) * 4].unsqueeze(2).to_broadcast([64, 4, 64]), ALU.mult, [psB_, rcB], [oB])
            of = o[:, :, :].rearrange("p h v -> p (h v)")
            ps_, psB_ = bank(B)
            for oc in range(8 if dbg >= 5 else 0):
                p.add("pe", lambda e, ps_=ps_, oc=oc: e.transpose(ps_[:, oc * 64:(oc + 1) * 64],
                                                                 of[:, oc * 128:(oc + 1) * 128], ident[0:64, 0:64]),
                      [oB], [psB_])
            if dbg >= 5:
                act(B, oT[:, :, rr * 64:(rr + 1) * 64], ps_[:, :].rearrange("p (c q) -> p c q", q=64), AF.Copy,
                    [psB_], oTB)
        for oc in range(8 if dbg >= 6 else 0):
            ps_, psB_ = bank(B)
            for c in range(8):
                mm(B, ps_[:, :], wo[:, c, oc * 128:(oc + 1) * 128], oT[:, c, :], c == 0, c == 7, [woB] + oTB, [psB_])
            tt(B, "dve", xt[:, oc, :], xt[:, oc, :], ps_[:, :], ALU.add, [xtB, psB_], [xtB])
        B.dma("pool", xov[:, :, t0:t0 + TT], xt[:, :, :], reads=[xtB])
    assert scol == nslot_tot
    p.barrier()
    sb.release(m0)


def alloc_na_scratch(B):
    T = B.T
    S = {"qT": B.dscr("s_qT", [D, T], BF16).ap(), "kT": B.dscr("s_kT", [D, T], BF16).ap(),
         "vtokb": B.dscr("s_vtokb", [T, D], BF16).ap()}
    return S


def build_na_probe(T, ncols, V, shapes, jn, layer, mode="full"):
    B = Builder(T)
    nc = B.nc
    setup_common(B, ncols)
    xT = B.din("xT", [D, T]).ap()
    W = {n: B.din(n, list(shapes[n])).ap() for n in ("na_qkv", "na_o")}
    nslot_tot = sum(len(s) for s in na_slots(T))
    nbias = B.din("nbias", [128, nslot_tot]).ap()
    btab = B.din("btab", [16, 128, 1024]).ap()
    yT = B.dout("yT", [D, T]).ap()
    S = alloc_na_scratch(B)
    na_n1(B, jn, layer, xT, W, V, S)
    if mode == "n1":
        B.dma("sp", yT[:, :], xT[:, :])
        B.p.barrier()
    else:
        na_n2(B, jn, xT, yT, W, S, nbias, btab, dbg=int(mode) if mode.isdigit() else 9)
    with ExitStack() as st:
        B.p.emit(nc, st)
    return B


T_CORE = NSEG * SEG
DEPTH = 4
W_NAMES = ["w_up", "w_down", "rw_rkv", "rw_w1", "rw_w2", "rw_a1", "rw_a2", "rw_v1", "rw_v2", "rw_g1", "rw_g2",
           "rw_o", "na_qkv", "na_o"]
_CACHE = {}


def build_full(ncols, V, shapes, T=None, depth=DEPTH, skip_last_mlp=False):
    T = T or T_CORE
    B = Builder(T)
    nc = B.nc
    setup_common(B, ncols)
    xT = B.din("xT", [D, T]).ap()
    W = {n: B.din(n, list(shapes[n])).ap() for n in W_NAMES}
    nslot_tot = sum(len(s) for s in na_slots(T))
    nbias = B.din("nbias", [128, nslot_tot]).ap()
    btabs = [B.din(f"btab{j}", [16, 128, 1024]).ap() for j in range(2)]
    yT = B.dout("yT", [D, T]).ap()
    xA = B.dscr("xA", [D, T]).ap()
    xB = B.dscr("xB", [D, T]).ap()
    SR = alloc_rwkv_scratch(B)
    SN = alloc_na_scratch(B)
    cur = xT
    for layer in range(depth):
        j = layer // 2
        lastl = (layer == depth - 1)
        mdst = yT if (lastl and skip_last_mlp) else xA
        if layer % 2 == 0:
            rwkv_r1(B, j, layer, cur, W, V, SR)
            rwkv_r2(B, SR)
            rwkv_r3(B, j, cur, mdst, W, V, SR)
        else:
            na_n1(B, j, layer, cur, W, V, SN)
            na_n2(B, j, cur, mdst, W, SN, nbias, btabs[j])
        if lastl and skip_last_mlp:
            break
        dst = yT if lastl else xB
        mlp_stage(B, xA, dst, W["w_up"][layer], W["w_down"][layer], B.vecs_sb, V["norm_mlp_g"][layer])
        cur = xB
    with ExitStack() as st:
        B.p.emit(nc, st)
    return B


def kernel(**inputs):
    inp = {k: np.asarray(v) for k, v in inputs.items()}
    xp = inp["x_prompt"]
    xs = inp["x_sample"]
    vp, V = pack_vectors(inp)
    vecs = vp.array()
    shapes = {n: inp[n].shape for n in W_NAMES}
    key = (vp.n,)
    if key not in _CACHE:
        _CACHE[key] = build_full(vp.n, V, shapes)
    B = _CACHE[key]
    consts = host_consts()
    blkm = host_blk_masks()
    btabs = [np.ascontiguousarray(host_na_bias_table(inp["na_rpb"][j]).reshape(16, 128, 1024)) for j in range(2)]
    nb = {"S": host_na_nbias(T_CORE, "S"), "P": host_na_nbias(T_CORE, "P")}
    prompt_ids = []
    in_maps = []
    for c in range(NCORES):
        if c < 4:
            ids = [2 * c, 2 * c + 1]
            xc = np.concatenate([xs[c], xp[ids[0]], xp[ids[1]]], axis=0)
        else:
            ids = list(range(8 + 6 * (c - 4), 8 + 6 * (c - 4) + 6))
            xc = np.concatenate([xp[i] for i in ids], axis=0)
        prompt_ids.append(ids)
        fl = np.zeros((128, 8), np.float32)
        if c < 4:
            fl[:, 1:4] = 1.0
        m = {"xT": np.ascontiguousarray(xc.T), "vecs": vecs, "consts": consts, "flags": fl, "blkm": blkm,
             "nbias": nb["S" if c < 4 else "P"], "btab0": btabs[0], "btab1": btabs[1]}
        for n in W_NAMES:
            m[n] = inp[n]
        in_maps.append(m)
    res = run_bass_kernel_spmd(B.nc, in_maps, core_ids=list(range(NCORES)))
    y_prompt = np.empty_like(xp)
    y_sample = np.empty_like(xs)
    for c in range(NCORES):
        y = res.results[c]["yT"].T
        if c < 4:
            y_sample[c] = y[0:8192]
            for k, i in enumerate(prompt_ids[c]):
                y_prompt[i] = y[8192 + k * SEG:8192 + (k + 1) * SEG]
        else:
            for k, i in enumerate(prompt_ids[c]):
                y_prompt[i] = y[k * SEG:(k + 1) * SEG]
    return (y_prompt, y_sample)
```

```python
from contextlib import ExitStack
import numpy as np
import concourse.bass as bass
import concourse.mybir as mybir
from concourse.bass_utils import run_bass_kernel_spmd

F32 = mybir.dt.float32
BF16 = mybir.dt.bfloat16
ALU = mybir.AluOpType
AF = mybir.ActivationFunctionType
AX = mybir.AxisListType

D = 1024
NCH = 8
DFF = 4096
NSEG = 6
SEG = 2048
NCORES = 8
RMS_EPS = 1e-6

NSLOT = 12
EPOCH = 15000
NEPOCH = 16


class Buf:
    __slots__ = ("w", "r", "parent", "kids")

    def __init__(self, parent=None):
        self.w = None
        self.r = []
        self.parent = parent
        self.kids = []
        if parent is not None:
            parent.kids.append(self)


class Op:
    __slots__ = ("eng", "seq", "fn", "waits", "dma", "sigidx", "slot", "slotval", "slotprev")


class Prog:
    ENGS = ("pe", "act", "dve", "pool", "sp")
    COMPUTE = ("pe", "act", "dve", "pool")

    def __init__(self):
        self.ops = {e: [] for e in self.ENGS}
        self.known = {f: {e: -1 for e in self.ENGS} for f in self.ENGS}
        self.known_dma = {f: set() for f in self.ENGS}
        self.last_compute = {e: -1 for e in self.ENGS}
        self.slot_uses = {q: [0] * NSLOT for q in ("sp", "pool", "act")}
        self.slot_last = {q: [None] * NSLOT for q in ("sp", "pool", "act")}
        self.dma_n = {q: 0 for q in ("sp", "pool", "act")}
        self.fence = {e: -1 for e in self.ENGS}

    def add(self, eng, fn, reads=(), writes=(), dma=False):
        ops = self.ops[eng]
        seq = len(ops)
        deps = set()
        for b in reads:
            if b.w is not None:
                deps.add(b.w)
            if b.parent is not None and b.parent.w is not None:
                deps.add(b.parent.w)
            for kb in b.kids:
                if kb.w is not None:
                    deps.add(kb.w)
        for b in writes:
            if b.w is not None:
                deps.add(b.w)
            deps.update(b.r)
            if b.parent is not None:
                if b.parent.w is not None:
                    deps.add(b.parent.w)
                deps.update(b.parent.r)
            for kb in b.kids:
                if kb.w is not None:
                    deps.add(kb.w)
                deps.update(kb.r)
        waits = []
        kn = self.known[eng]
        kd = self.known_dma[eng]
        for d in sorted(deps):
            E, s, isdma = d
            if s <= self.fence[E]:
                continue
            if isdma:
                if (E, s) in kd:
                    continue
                kd.add((E, s))
                waits.append(d)
            else:
                if E == eng and not dma:
                    if eng == "pe":
                        continue
                    if seq - s > 3:
                        continue
                if kn[E] >= s:
                    continue
                kn[E] = s
                waits.append(d)
        op = Op()
        op.eng, op.seq, op.fn, op.waits, op.dma, op.sigidx = eng, seq, fn, waits, dma, 0
        op.slot = op.slotval = op.slotprev = None
        if dma:
            n = self.dma_n[eng]
            self.dma_n[eng] = n + 1
            sl = n % NSLOT
            op.slot = sl
            op.slotprev = self.slot_last[eng][sl]
            self.slot_uses[eng][sl] += 1
            op.slotval = 16 * self.slot_uses[eng][sl]
            self.slot_last[eng][sl] = (eng, seq, True)
            if op.slotprev is not None:
                kd.add(op.slotprev[:2])
        else:
            self.last_compute[eng] = seq
        tok = (eng, seq, dma)
        for b in reads:
            b.r.append(tok)
        for b in writes:
            b.w = tok
            b.r = []
        ops.append(op)
        return op

    def barrier(self):
        lasts = dict(self.last_compute)
        dmas = []
        for q in self.slot_last:
            for t in self.slot_last[q]:
                if t is not None:
                    dmas.append(t)
        for F in self.ENGS:
            waits = []
            for E in self.COMPUTE:
                if E != F and lasts[E] >= 0 and self.known[F][E] < lasts[E]:
                    waits.append((E, lasts[E], False))
                    self.known[F][E] = lasts[E]
            for t in dmas:
                if t[:2] not in self.known_dma[F]:
                    self.known_dma[F].add(t[:2])
                    waits.append(t)
            op = Op()
            op.eng, op.seq, op.fn, op.waits, op.dma, op.sigidx = F, len(self.ops[F]), None, waits, False, 0
            op.slot = op.slotval = op.slotprev = None
            self.ops[F].append(op)
        for E in self.ENGS:
            self.fence[E] = len(self.ops[E]) - 1

    def emit(self, nc, st):
        sig = {e: set() for e in self.ENGS}
        for F in self.ENGS:
            for op in self.ops[F]:
                for (E, s, isdma) in op.waits:
                    if not isdma:
                        sig[E].add(s)
        nsig = {}
        for E in self.COMPUTE:
            c = 0
            for op in self.ops[E]:
                if op.seq in sig[E]:
                    c += 1
                    op.sigidx = c
            nsig[E] = c
            assert c <= EPOCH * NEPOCH, (E, c)
        csem = {E: [st.enter_context(nc.semaphore(f"c_{E}_{k}")) for k in range((nsig[E] + EPOCH - 1) // EPOCH)]
                for E in self.COMPUTE}
        dsem = {q: [st.enter_context(nc.semaphore(f"d_{q}_{k}")) for k in range(NSLOT)]
                for q in self.slot_last if self.dma_n[q] > 0}
        allops = self.ops

        def run(F, eng):
            for op in allops[F]:
                for (E, s, isdma) in op.waits:
                    t = allops[E][s]
                    if isdma:
                        eng.wait_ge(dsem[E][t.slot], t.slotval)
                    else:
                        i = t.sigidx - 1
                        eng.wait_ge(csem[E][i // EPOCH], i % EPOCH + 1)
                if op.dma and op.slotprev is not None:
                    t = allops[op.slotprev[0]][op.slotprev[1]]
                    eng.wait_ge(dsem[F][t.slot], t.slotval)
                if op.fn is None:
                    continue
                ins = op.fn(eng)
                if op.dma:
                    ins.then_inc(dsem[F][op.slot], 16)
                elif op.sigidx:
                    i = op.sigidx - 1
                    ins.then_inc(csem[F][i // EPOCH], 1)

        block = st.enter_context(nc.Block())

        @block.tensor
        def _(eng):
            run("pe", eng)

        @block.scalar
        def _(eng):
            run("act", eng)

        @block.vector
        def _(eng):
            run("dve", eng)

        @block.gpsimd
        def _(eng):
            run("pool", eng)

        @block.sync
        def _(eng):
            run("sp", eng)


class SB:
    def __init__(self, nc):
        self.nc = nc
        self.base = nc.sbuf_base + 64
        self.top = nc.sbuf_top
        self.ptr = self.base
        self.n = 0

    def alloc(self, shape, dtype, name="t"):
        esz = 2 if dtype == BF16 else 4
        per = esz
        for s in shape[1:]:
            per *= s
        off = (self.ptr + 63) // 64 * 64
        assert off + per <= self.top, (name, off, per, self.top)
        self.ptr = off + per
        self.n += 1
        return self.nc.alloc_sbuf_tensor_at(f"{name}_{self.n}", list(shape), dtype, offset=off)

    def mark(self):
        return self.ptr

    def release(self, m):
        self.ptr = m


class Builder:
    def __init__(self, T):
        self.T = T
        self.nc = bass.Bass("TRN2", target_bir_lowering=False)
        self.p = Prog()
        self.sb = SB(self.nc)
        nc = self.nc
        self.ps = [nc.alloc_psum_tensor(f"psb{i}", [128, 512], F32) for i in range(8)]
        self.psB = [Buf() for _ in range(8)]
        self.ones_bf = self.sb.alloc([128, 128], BF16, "ones")
        self.onesB = Buf()
        self.p.add("pool", lambda e: e.memset(self.ones_bf[:, :], 1.0), writes=[self.onesB])
        self.perm_mark = None

    def din(self, name, shape, dtype=F32):
        return self.nc.dram_tensor(name, list(shape), dtype, kind="ExternalInput")

    def dout(self, name, shape, dtype=F32):
        return self.nc.dram_tensor(name, list(shape), dtype, kind="ExternalOutput")

    def dscr(self, name, shape, dtype=F32):
        return self.nc.dram_tensor(name, list(shape), dtype)

    def dma(self, q, out, in_, reads=(), writes=(), slow=False):
        if slow:
            self.p.add(q, lambda e, o=out, i=in_: e.dma_start(out=o, in_=i, allow_slow_non_contiguous=True),
                       reads, writes, dma=True)
        else:
            self.p.add(q, lambda e, o=out, i=in_: e.dma_start(out=o, in_=i), reads, writes, dma=True)

    def cast(self, k, out, in_, reads, writes):
        eng = ("dve", "pool", "act")[k % 3]
        if eng == "act":
            self.p.add("act", lambda e, o=out, i=in_: e.activation(out=o, in_=i, func=AF.Copy), reads, writes)
        else:
            self.p.add(eng, lambda e, o=out, i=in_: e.tensor_copy(out=o, in_=i), reads, writes)


def mlp_stage(B, xin, xout, w_up, w_down, vecs_sb, gcol):
    nc, p, sb = B.nc, B.p, B.sb
    T = B.T
    TT = 512
    NT = T // TT
    m = sb.mark()
    wup = sb.alloc([128, 8, DFF], BF16, "wup")
    wdn = sb.alloc([128, 32, D], BF16, "wdn")
    wupB = [Buf() for _ in range(8)]
    wdnB = [Buf() for _ in range(8)]
    xs = [sb.alloc([128, 8, TT], F32, "x") for _ in range(2)]
    xoff = []
    xB = [Buf() for _ in range(2)]
    hn = sb.alloc([128, 8, TT], BF16, "hn")
    hnB = Buf()
    a = sb.alloc([128, 16, TT], BF16, "a")
    aB = [Buf() for _ in range(16)]
    r = [sb.alloc([128, TT], F32, "r") for _ in range(2)]
    rB = [Buf() for _ in range(2)]
    rs = sb.alloc([128, TT], F32, "rs")
    rsB = Buf()
    rstd = sb.alloc([128, TT], F32, "rstd")
    rstdB = Buf()
    epsb = sb.alloc([128, 1], F32, "eps")
    epsB = Buf()
    p.add("pool", lambda e: e.memset(epsb[:, :], RMS_EPS), writes=[epsB])
    stg = [xs[0], xs[1]]
    k = 0
    for kc in range(8):
        s = stg[k % 2]
        sv = s[:, :, :].rearrange("p a b -> p (a b)")
        B.dma("sp", sv, w_up[kc * 128:(kc + 1) * 128, :], writes=[xB[k % 2]])
        for h in range(2):
            B.cast(2 * k + h, wup[:, kc, h * 2048:(h + 1) * 2048], sv[:, h * 2048:(h + 1) * 2048],
                   [xB[k % 2]], [wupB[kc]])
        k += 1
    wdv = w_down.rearrange("(c p) n -> p c n", p=128)
    for g4 in range(8):
        s = stg[k % 2]
        sv = s[:, :, :].rearrange("p a b -> p (a b)").rearrange("p (g c) -> p g c", c=D)
        B.dma("sp", sv, wdv[:, g4 * 4:(g4 + 1) * 4, :], writes=[xB[k % 2]])
        for h in range(2):
            B.cast(2 * k + h, wdn[:, g4 * 4 + 2 * h:g4 * 4 + 2 * h + 2, :], sv[:, 2 * h:2 * h + 2, :],
                   [xB[k % 2]], [wdnB[g4]])
        k += 1
    xiv = xin.rearrange("(c p) t -> p c t", p=128)
    xov = xout.rearrange("(c p) t -> p c t", p=128)
    PS_SS, PS_UP, PS_DN = 0, (1, 2), (3, 4, 5, 6)
    dn_i = 0
    for t in range(NT):
        t0 = t * TT
        b = t % 2
        x = xs[b]
        B.dma("sp", x[:, :, :], xiv[:, :, t0:t0 + TT], writes=[xB[b]])
        sq = a
        for h in range(2):
            p.add("act", lambda e, o=sq[:, 4 * h:4 * h + 4, :], i=x[:, 4 * h:4 * h + 4, :]:
                  e.activation(out=o, in_=i, func=AF.Square), [xB[b]], aB[4 * h:4 * h + 4])
        for c in range(8):
            p.add("pe", lambda e, c=c: e.matmul(B.ps[PS_SS][:, :], B.ones_bf[:, :], sq[:, c, :],
                                                 start=(c == 0), stop=(c == 7)),
                  [B.onesB, aB[c]], [B.psB[PS_SS]])
        p.add("act", lambda e: e.activation(out=rs[:, :], in_=B.ps[PS_SS][:, :], func=AF.Sqrt,
                                            bias=epsb[:, 0:1], scale=1.0 / D),
              [B.psB[PS_SS], epsB], [rsB])
        p.add("dve", lambda e: e.reciprocal(out=rstd[:, :], in_=rs[:, :]), [rsB], [rstdB])
        for c in range(8):
            p.add("dve", lambda e, c=c, x=x: e.scalar_tensor_tensor(
                out=hn[:, c, :], in0=x[:, c, :], scalar=vecs_sb[:, gcol + c:gcol + c + 1], in1=rstd[:, :],
                op0=ALU.mult, op1=ALU.mult), [xB[b], rstdB], [hnB])
        for half in range(2):
            for jj in range(16):
                j = half * 16 + jj
                pu = PS_UP[j % 2]
                for kc in range(8):
                    p.add("pe", lambda e, kc=kc, j=j, pu=pu: e.matmul(
                        B.ps[pu][:, :], wup[:, kc, j * 128:(j + 1) * 128], hn[:, kc, :],
                        start=(kc == 0), stop=(kc == 7)), [wupB[kc], hnB], [B.psB[pu]])
                p.add("act", lambda e, j=j, pu=pu: e.activation(out=r[j % 2][:, :], in_=B.ps[pu][:, :],
                                                                 func=AF.Relu),
                      [B.psB[pu]], [rB[j % 2]])
                p.add("pool", lambda e, j=j, jj=jj: e.tensor_tensor(out=a[:, jj, :], in0=r[j % 2][:, :],
                                                                     in1=r[j % 2][:, :], op=ALU.mult),
                      [rB[j % 2]], [aB[jj]])
            for o in range(8):
                pd = PS_DN[dn_i % 4]
                dn_i += 1
                for jj in range(16):
                    j = half * 16 + jj
                    p.add("pe", lambda e, o=o, j=j, jj=jj, pd=pd: e.matmul(
                        B.ps[pd][:, :], wdn[:, j, o * 128:(o + 1) * 128], a[:, jj, :],
                        start=(jj == 0), stop=(jj == 15)), [wdnB[j // 4], aB[jj]], [B.psB[pd]])
                p.add("dve", lambda e, o=o, pd=pd, x=x: e.tensor_tensor(
                    out=x[:, o, :], in0=x[:, o, :], in1=B.ps[pd][:, :], op=ALU.add),
                    [xB[b], B.psB[pd]], [xB[b]])
        B.dma("pool", xov[:, :, t0:t0 + TT], x[:, :, :], reads=[xB[b]])
    p.barrier()
    sb.release(m)


class VecPack:
    def __init__(self):
        self.cols = {}
        self.n = 0
        self.data = []

    def add(self, name, v):
        v = np.asarray(v, np.float32).reshape(-1)
        assert v.size % 128 == 0
        nc_ = v.size // 128
        self.cols[name] = self.n
        self.n += nc_
        self.data.append(np.ascontiguousarray(v.reshape(nc_, 128).T))
        return self.cols[name]

    def array(self):
        return np.ascontiguousarray(np.concatenate(self.data, axis=1))


def build_mlp_only(T, ncols, gcol):
    B = Builder(T)
    nc = B.nc
    xT = B.din("xT", [D, T]).ap()
    vecs = B.din("vecs", [128, ncols]).ap()
    w_up = B.din("w_up", [D, DFF]).ap()
    w_down = B.din("w_down", [DFF, D]).ap()
    yT = B.dout("yT", [D, T]).ap()
    vecs_sb = B.sb.alloc([128, ncols], F32, "vecs")
    vB = Buf()
    B.dma("sp", vecs_sb[:, :], vecs[:, :], writes=[vB])
    B.p.barrier()
    mlp_stage(B, xT, yT, w_up, w_down, vecs_sb, gcol)
    with ExitStack() as st:
        B.p.emit(nc, st)
    return B


class Stager:
    def __init__(self, B, tiles, bufs):
        self.B, self.tiles, self.bufs, self.k = B, tiles, bufs, 0

    def load(self, dst_ap, src_ap, stage_view, dstB):
        i = self.k % len(self.tiles)
        sv = stage_view(self.tiles[i])
        self.B.dma("sp", sv, src_ap, writes=[self.bufs[i]])
        self.B.cast(self.k, dst_ap, sv, [self.bufs[i]], [dstB])
        self.k += 1


def bank(B):
    i = B.bank_i % 8
    B.bank_i += 1
    return B.ps[i], B.psB[i]


def act(B, out, in_, func, reads, writes, bias=None, scale=None):
    kw = {}
    if bias is not None:
        kw["bias"] = bias
    if scale is not None:
        kw["scale"] = scale
    B.p.add("act", lambda e: e.activation(out=out, in_=in_, func=func, **kw), reads, writes)


def tt(B, eng, out, in0, in1, op, reads, writes):
    B.p.add(eng, lambda e: e.tensor_tensor(out=out, in0=in0, in1=in1, op=op), reads, writes)


def ts(B, eng, out, in0, s1, s2, op0, op1, reads, writes):
    if op1 is None:
        B.p.add(eng, lambda e: e.tensor_scalar(out=out, in0=in0, scalar1=s1, scalar2=None, op0=op0), reads, writes)
    else:
        B.p.add(eng, lambda e: e.tensor_scalar(out=out, in0=in0, scalar1=s1, scalar2=s2, op0=op0, op1=op1),
                reads, writes)


def stt(B, out, in0, scalar, in1, op0, op1, reads, writes):
    B.p.add("dve", lambda e: e.scalar_tensor_tensor(out=out, in0=in0, scalar=scalar, in1=in1, op0=op0, op1=op1),
            reads, writes)


def mm(B, out, lhsT, rhs, start, stop, reads, writes):
    B.p.add("pe", lambda e: e.matmul(out, lhsT, rhs, start=start, stop=stop), reads, writes)


def rwkv_r1(B, j, layer, xin, W, V, S):
    nc, p, sb = B.nc, B.p, B.sb
    T = B.T
    TT = 512
    NT = T // TT
    vs = B.vecs_sb
    has_vres = j > 0
    m0 = sb.mark()
    wrkv = [sb.alloc([128, 8, D], BF16, f"w{n}") for n in "rkv"]
    wrkvB = [[Buf() for _ in range(2)] for _ in range(3)]
    w1c = sb.alloc([128, 8, 128], BF16, "w1c")
    a1c = sb.alloc([128, 8, 128], BF16, "a1c")
    g1a = sb.alloc([128, 8, 128], BF16, "g1a")
    g1b = sb.alloc([128, 8, 32], BF16, "g1b")
    w2c = sb.alloc([128, D], BF16, "w2c")
    a2c = sb.alloc([128, D], BF16, "a2c")
    g2a = sb.alloc([128, D], BF16, "g2a")
    g2b = sb.alloc([32, D], BF16, "g2b")
    if has_vres:
        v1s = sb.alloc([128, 8, 32], BF16, "v1s")
        v2s = sb.alloc([32, D], BF16, "v2s")
    wsB = Buf()
    B.sb_off = {}
    B.sb_off["xh"] = (sb.ptr + 63) // 64 * 64
    xh = sb.alloc([128, 8, TT + 2], F32, "xh")
    xhB = Buf()
    B.sb_off["xx"] = (sb.ptr + 63) // 64 * 64
    xx = sb.alloc([128, 8, TT], F32, "xx")
    xxB = Buf()
    rs = sb.alloc([128, TT + 2], F32, "rs")
    rsB = Buf()
    rstd = sb.alloc([128, TT + 2], F32, "rstd")
    rstdB = Buf()
    epsb = sb.alloc([128, 1], F32, "eps")
    epsB = Buf()
    p.add("pool", lambda e: e.memset(epsb[:, :], RMS_EPS), writes=[epsB])
    xm_off = (sb.ptr + 63) // 64 * 64
    xm = [sb.alloc([128, 8, TT], BF16, f"xm{i}") for i in range(6)]
    xmB = [Buf() for _ in range(6)]
    sq = nc.alloc_sbuf_tensor_at(f"r1sq{j}", [128, 8, TT + 2], BF16, offset=xm_off + 4 * 8192)
    sqB = Buf()
    SQW = [sqB, xmB[4], xmB[5]]
    stg_t = [nc.alloc_sbuf_tensor_at(f"r1stg{j}_{i}", [128, 4096], F32, offset=xm_off + i * 16384) for i in range(2)]
    stgB = [Buf(), Buf()]
    stg = Stager(B, stg_t, stgB)
    names = ["tw", "ta", "tg0", "tg1", "tv"]
    lo = {n: sb.alloc([128, TT], BF16, n) for n in names}
    loB = {n: Buf() for n in names}
    tmpn = ["r", "k", "v", "g", "kk", "rn", "lw0", "lw1", "ic0", "ic1", "t1", "kd0", "kd1", "b0", "b1", "bv",
            "vg", "vf"]
    TMs = [{n: sb.alloc([128, TT], F32, n) for n in tmpn}]
    TMBs = [{n: Buf() for n in tmpn}]
    SQKs = [sb.alloc([128, TT], BF16, "sqk")]
    RKs = [sb.alloc([128, TT], BF16, "rk")]
    VTs = [sb.alloc([128, 4, 128], F32, "vt")]
    SQKBs, RKBs, VTBs = [Buf()], [Buf()], [Buf()]
    xh_off = B.sb_off["xh"]
    xx_off = B.sb_off["xx"]
    slots = [(xh_off + i * 2048, xhB) for i in range(8)] + [(xx_off + i * 2048, xxB) for i in range(8)]
    tm1, tmB1 = {}, {}
    si_ = 0
    for n in tmpn:
        if si_ < 16 and n not in ("t1", "rn"):
            off_, par_ = slots[si_]
            si_ += 1
            tm1[n] = nc.alloc_sbuf_tensor_at(f"r1t1{j}_{n}", [128, TT], F32, offset=off_)
            tmB1[n] = Buf(parent=par_)
        else:
            tm1[n] = sb.alloc([128, TT], F32, n + "1")
            tmB1[n] = Buf()
    TMs.append(tm1)
    TMBs.append(tmB1)
    SQKs.append(sb.alloc([128, TT], BF16, "sqk1"))
    RKs.append(sb.alloc([128, TT], BF16, "rk1"))
    VTs.append(sb.alloc([128, 4, 128], F32, "vt1"))
    SQKBs.append(Buf())
    RKBs.append(Buf())
    VTBs.append(Buf())

    rkv = W["rw_rkv"]
    for mI in range(3):
        src = rkv[j, mI].rearrange("(c p) n -> p c n", p=128)
        for hI in range(2):
            stg.load(wrkv[mI][:, 4 * hI:4 * hI + 4, :], src[:, 4 * hI:4 * hI + 4, :],
                     lambda t: t[:, :].rearrange("p (c n) -> p c n", n=D), wsB)
    for dI in range(2):
        stg.load(w1c[:, :, 64 * dI:64 * dI + 64], W["rw_w1"][j, dI].rearrange("(c p) n -> p c n", p=128),
                 lambda t: t[:, 0:512].rearrange("p (c n) -> p c n", n=64), wsB)
        stg.load(a1c[:, :, 64 * dI:64 * dI + 64], W["rw_a1"][j, dI].rearrange("(c p) n -> p c n", p=128),
                 lambda t: t[:, 0:512].rearrange("p (c n) -> p c n", n=64), wsB)
        stg.load(w2c[64 * dI:64 * dI + 64, :], W["rw_w2"][j, dI], lambda t, dI=dI: t[64 * dI:64 * dI + 64, 0:D], wsB)
        stg.load(a2c[64 * dI:64 * dI + 64, :], W["rw_a2"][j, dI], lambda t, dI=dI: t[64 * dI:64 * dI + 64, 0:D], wsB)
    g1v = W["rw_g1"][j].rearrange("(c p) n -> p c n", p=128)
    stg.load(g1a[:, :, :], g1v[:, :, 0:128], lambda t: t[:, 0:1024].rearrange("p (c n) -> p c n", n=128), wsB)
    stg.load(g1b[:, :, :], g1v[:, :, 128:160], lambda t: t[:, 0:256].rearrange("p (c n) -> p c n", n=32), wsB)
    stg.load(g2a[:, :], W["rw_g2"][j, 0:128, :], lambda t: t[:, 0:D], wsB)
    stg.load(g2b[:, :], W["rw_g2"][j, 128:160, :], lambda t: t[0:32, 0:D], wsB)
    if has_vres:
        stg.load(v1s[:, :, :], W["rw_v1"][j - 1].rearrange("(c p) n -> p c n", p=128),
                 lambda t: t[:, 0:256].rearrange("p (c n) -> p c n", n=32), wsB)
        stg.load(v2s[:, :], W["rw_v2"][j - 1], lambda t: t[0:32, 0:D], wsB)
    p.barrier()

    xiv = xin.rearrange("(c p) t -> p c t", p=128)
    cst = B.consts_sb
    ident = cst[:, B.C["ident"]:B.C["ident"] + 128]
    bones = B.bones_bf
    for t in range(NT):
        t0 = t * TT
        seg = t0 // SEG
        first = (t0 % SEG == 0)
        last = ((t0 + TT) % SEG == 0)
        B.dma("sp", xh[:, :, 1:TT + 1], xiv[:, :, t0:t0 + TT], writes=[xhB])
        if t0 > 0:
            B.dma("sp", xh[:, :, 0:1], xiv[:, :, t0 - 1:t0], writes=[xhB], slow=True)
        else:
            p.add("pool", lambda e: e.memset(xh[:, :, 0:1], 0.0), writes=[xhB])
        if t0 + TT < T:
            B.dma("sp", xh[:, :, TT + 1:TT + 2], xiv[:, :, t0 + TT:t0 + TT + 1], writes=[xhB], slow=True)
        else:
            p.add("pool", lambda e: e.memset(xh[:, :, TT + 1:TT + 2], 0.0), writes=[xhB])
        for hI in range(2):
            act(B, sq[:, 4 * hI:4 * hI + 4, :], xh[:, 4 * hI:4 * hI + 4, :], AF.Square, [xhB], SQW)
        psA, psAB = bank(B)
        for c in range(8):
            mm(B, psA[:, :], B.ones_bf[:, :], sq[:, c, 0:TT], c == 0, c == 7, [B.onesB] + SQW, [psAB])
        psH, psHB = bank(B)
        for c in range(8):
            mm(B, psH[:, 0:2], B.ones_bf[:, :], sq[:, c, TT:TT + 2], c == 0, c == 7, [B.onesB] + SQW, [psHB])
        act(B, rs[:, 0:TT], psA[:, :], AF.Sqrt, [psAB, epsB], [rsB], bias=epsb[:, 0:1], scale=1.0 / D)
        act(B, rs[:, TT:TT + 2], psH[:, 0:2], AF.Sqrt, [psHB, epsB], [rsB], bias=epsb[:, 0:1], scale=1.0 / D)
        p.add("dve", lambda e: e.reciprocal(out=rstd[:, :], in_=rs[:, :]), [rsB], [rstdB])
        gc = V["norm_mix_g"][layer]
        for c in range(8):
            stt(B, xh[:, c, :], xh[:, c, :], vs[:, gc + c:gc + c + 1], rstd[:, :], ALU.mult, ALU.mult,
                [xhB, rstdB], [xhB])
        if first and seg > 0:
            ts(B, "dve", xh[:, :, 0:1], xh[:, :, 0:1], B.flags_sb[:, seg:seg + 1], None, ALU.mult, None,
               [xhB], [xhB])
        if last and seg < (T // SEG) - 1:
            ts(B, "dve", xh[:, :, TT + 1:TT + 2], xh[:, :, TT + 1:TT + 2], B.flags_sb[:, seg + 1:seg + 2], None,
               ALU.mult, None, [xhB], [xhB])
        tt(B, "pool", xx[:, :, :], xh[:, :, 0:TT], xh[:, :, 2:TT + 2], ALU.add, [xhB], [xxB])
        stt(B, xx[:, :, :], xx[:, :, :], 0.5, xh[:, :, 1:TT + 1], ALU.mult, ALU.subtract, [xxB, xhB], [xxB])
        mc = V["rw_mix"][j]
        for mI in range(6):
            for c in range(8):
                stt(B, xm[mI][:, c, :], xx[:, c, :], vs[:, mc + 8 * mI + c:mc + 8 * mI + c + 1], xh[:, c, 1:TT + 1],
                    ALU.mult, ALU.add, [xxB, xhB], [xmB[mI]])
        XR, XK, XV, XW, XA, XG = range(6)
        ps_, psB_ = bank(B)
        for c in range(8):
            mm(B, ps_[:, :], w1c[:, c, :], xm[XW][:, c, :], c == 0, c == 7, [xmB[XW]], [psB_])
        act(B, lo["tw"][:, :], ps_[:, :], AF.Tanh, [psB_], [loB["tw"]])
        ps_, psB_ = bank(B)
        for c in range(8):
            mm(B, ps_[:, :], a1c[:, c, :], xm[XA][:, c, :], c == 0, c == 7, [xmB[XA]], [psB_])
        act(B, lo["ta"][:, :], ps_[:, :], AF.Copy, [psB_], [loB["ta"]])
        ps_, psB_ = bank(B)
        for c in range(8):
            mm(B, ps_[:, :], g1a[:, c, :], xm[XG][:, c, :], c == 0, c == 7, [xmB[XG]], [psB_])
        act(B, lo["tg0"][:, :], ps_[:, :], AF.Sigmoid, [psB_], [loB["tg0"]])
        ps_, psB_ = bank(B)
        for c in range(8):
            mm(B, ps_[0:32, :], g1b[:, c, :], xm[XG][:, c, :], c == 0, c == 7, [xmB[XG]], [psB_])
        act(B, lo["tg1"][0:32, :], ps_[0:32, :], AF.Sigmoid, [psB_], [loB["tg1"]])
        if has_vres:
            ps_, psB_ = bank(B)
            for c in range(8):
                mm(B, ps_[0:32, :], v1s[:, c, :], xm[XV][:, c, :], c == 0, c == 7, [xmB[XV]], [psB_])
            act(B, lo["tv"][0:32, :], ps_[0:32, :], AF.Copy, [psB_], [loB["tv"]])
        def oc_unit(oc, tm, tmB, sqk, sqkB, rk, rkB, vt, vtB):
            osl = slice(oc * 128, (oc + 1) * 128)
            for mI, nm in enumerate("rkv"):
                ps_, psB_ = bank(B)
                for c in range(8):
                    mm(B, ps_[:, :], wrkv[mI][:, c, osl], xm[mI][:, c, :], c == 0, c == 7, [xmB[mI]], [psB_])
                if mI == 1:
                    p.add("dve", lambda e, ps_=ps_: e.tensor_copy(out=tm["k"][:, :], in_=ps_[:, :]),
                          [psB_], [tmB["k"]])
                else:
                    act(B, tm[nm][:, :], ps_[:, :], AF.Copy, [psB_], [tmB[nm]])
            yield
            for dI in range(2):
                dsl = slice(64 * dI, 64 * dI + 64)
                ps_, psB_ = bank(B)
                mm(B, ps_[:, :], w2c[dsl, osl], lo["tw"][dsl, :], True, True, [loB["tw"]], [psB_])
                w0c = V["rw_w0"][j][dI] + oc
                act(B, tm[f"lw{dI}"][:, :], ps_[:, :], AF.Sigmoid, [psB_], [tmB[f"lw{dI}"]],
                    bias=vs[:, w0c:w0c + 1], scale=1.0)
                act(B, tm[f"lw{dI}"][:, :], tm[f"lw{dI}"][:, :], AF.Identity, [tmB[f"lw{dI}"]], [tmB[f"lw{dI}"]],
                    scale=-0.6065306597126334)
                ps_, psB_ = bank(B)
                mm(B, ps_[:, :], a2c[dsl, osl], lo["ta"][dsl, :], True, True, [loB["ta"]], [psB_])
                a0c = V["rw_a0"][j][dI] + oc
                act(B, tm[f"ic{dI}"][:, :], ps_[:, :], AF.Sigmoid, [psB_], [tmB[f"ic{dI}"]],
                    bias=vs[:, a0c:a0c + 1], scale=1.0)
            yield
            ps_, psB_ = bank(B)
            mm(B, ps_[:, :], g2a[:, osl], lo["tg0"][:, :], True, False, [loB["tg0"]], [psB_])
            mm(B, ps_[:, :], g2b[0:32, osl], lo["tg1"][0:32, :], False, True, [loB["tg1"]], [psB_])
            act(B, tm["g"][:, :], ps_[:, :], AF.Copy, [psB_], [tmB["g"]])
            if has_vres:
                ps_, psB_ = bank(B)
                mm(B, ps_[:, :], v2s[0:32, osl], lo["tv"][0:32, :], True, True, [loB["tv"]], [psB_])
                v0c = V["rw_v0"][j - 1] + oc
                act(B, tm["vg"][:, :], ps_[:, :], AF.Sigmoid, [psB_], [tmB["vg"]], bias=vs[:, v0c:v0c + 1],
                    scale=1.0)
                B.dma("sp", tm["vf"][:, :], S["vfirst"][osl, t0:t0 + TT], writes=[tmB["vf"]])
                tt(B, "pool", tm["vf"][:, :], tm["vf"][:, :], tm["v"][:, :], ALU.subtract,
                   [tmB["vf"], tmB["v"]], [tmB["vf"]])
                tt(B, "pool", tm["vf"][:, :], tm["vf"][:, :], tm["vg"][:, :], ALU.mult,
                   [tmB["vf"], tmB["vg"]], [tmB["vf"]])
                tt(B, "pool", tm["v"][:, :], tm["v"][:, :], tm["vf"][:, :], ALU.add,
                   [tmB["vf"], tmB["v"]], [tmB["v"]])
            else:
                B.dma("sp", S["vfirst"][osl, t0:t0 + TT], tm["v"][:, :], reads=[tmB["v"]])
            yield
            kkc = V["rw_kk"][j] + oc
            act(B, tm["kk"][:, :], tm["k"][:, :], AF.Identity, [tmB["k"]], [tmB["kk"]], scale=vs[:, kkc:kkc + 1])
            act(B, sqk[:, :], tm["kk"][:, :], AF.Square, [tmB["kk"]], [sqkB])
            ps_, psB_ = bank(B)
            mm(B, ps_[:, :], bones[:, :], sqk[:, :], True, True, [sqkB, B.bonesB], [psB_])
            act(B, tm["rn"][:, :], ps_[:, :], AF.Sqrt, [psB_], [tmB["rn"]])
            ts(B, "dve", tm["rn"][:, :], tm["rn"][:, :], 1e-12, None, ALU.max, None, [tmB["rn"]], [tmB["rn"]])
            p.add("dve", lambda e: e.reciprocal(out=tm["rn"][:, :], in_=tm["rn"][:, :]), [tmB["rn"]], [tmB["rn"]])
            tt(B, "dve", tm["kk"][:, :], tm["kk"][:, :], tm["rn"][:, :], ALU.mult, [tmB["kk"], tmB["rn"]],
               [tmB["kk"]])
            yield
            kac = V["rw_ka"][j] + oc
            for dI in range(2):
                ic, kd, bb = tm[f"ic{dI}"], tm[f"kd{dI}"], tm[f"b{dI}"]
                icB, kdB, bbB = tmB[f"ic{dI}"], tmB[f"kd{dI}"], tmB[f"b{dI}"]
                ts(B, "dve", tm["t1"][:, :], ic[:, :], 1.0, vs[:, kac:kac + 1], ALU.subtract, ALU.mult,
                   [icB], [tmB["t1"]])
                stt(B, kd[:, :], tm["t1"][:, :], 1.0, tm["k"][:, :], ALU.add, ALU.mult, [tmB["t1"], tmB["k"]], [kdB])
                tt(B, "pool", bb[:, :], tm["kk"][:, :], ic[:, :], ALU.mult, [tmB["kk"], icB], [bbB])
            yield
            tt(B, "pool", tm["t1"][:, :], tm["kd0"][:, :], tm["kd1"][:, :], ALU.add, [tmB["kd0"], tmB["kd1"]],
               [tmB["t1"]])
            rkc = V["rw_rk"][j] + oc
            stt(B, rk[:, :], tm["t1"][:, :], vs[:, rkc:rkc + 1], tm["r"][:, :], ALU.mult, ALU.mult,
                [tmB["t1"], tmB["r"]], [rkB])
            ps_, psB_ = bank(B)
            mm(B, ps_[:, :], bones[:, :], rk[:, :], True, True, [rkB, B.bonesB], [psB_])
            tt(B, "dve", tm["bv"][:, :], ps_[:, :], tm["v"][:, :], ALU.mult, [psB_, tmB["v"]], [tmB["bv"]])
            yield
            ps_, psB_ = bank(B)
            for s4 in range(4):
                p.add("pe", lambda e, ps_=ps_, s4=s4: e.transpose(ps_[:, s4 * 128:(s4 + 1) * 128],
                                                                 tm["v"][:, s4 * 128:(s4 + 1) * 128], ident),
                      [tmB["v"]], [psB_])
            act(B, vt[:, :, :], ps_[:, :].rearrange("p (s c) -> p s c", c=128), AF.Copy, [psB_], [vtB])
            B.dma("sp", S["vtok"][t0:t0 + TT, osl].rearrange("(s p) c -> p s c", p=128), vt[:, :, :], reads=[vtB])
            yield
            for nm in ("r", "kk", "g", "bv", "lw0", "lw1", "kd0", "kd1", "b0", "b1"):
                B.dma("sp", S[nm][osl, t0:t0 + TT], tm[nm][:, :], reads=[tmB[nm]])

        for oc0 in range(0, 8, 2):
            gens = [oc_unit(oc0 + s_, TMs[s_], TMBs[s_], SQKs[s_], SQKBs[s_], RKs[s_], RKBs[s_], VTs[s_], VTBs[s_])
                    for s_ in range(2)]
            live = [True, True]
            while any(live):
                for gi, g_ in enumerate(gens):
                    if live[gi]:
                        try:
                            next(g_)
                        except StopIteration:
                            live[gi] = False
    p.barrier()
    sb.release(m0)


def rwkv_r2(B, S):
    nc, p, sb = B.nc, B.p, B.sb
    T = B.T
    TT, L, NQ = 512, 64, 8
    NT = T // TT
    m0 = sb.mark()
    cst = B.consts_sb
    C = B.C
    MASK = {k: cst[:, C[k]:C[k] + 128] for k in ("LT", "LE", "GT", "GE")}
    identbf = B.ident_bf
    blk_sb = sb.alloc([128, 768], F32, "blk")
    B.dma("sp", blk_sb[:, :], B.blkm_dram[:, :], writes=[Buf()])
    BLK = [blk_sb[:, 128 * li:128 * li + 128] for li in range(6)]
    onesf = sb.alloc([128, L], F32, "onesf")
    p.add("pool", lambda e: e.memset(onesf[:, :], 1.0))
    nseg = T // SEG

    class CS:
        pass

    def mk(tag):
        c_ = CS()
        inn = ["r", "kk", "lw", "kd", "b"]
        c_.inp = {n: sb.alloc([128, NQ, L], F32, n + tag) for n in inn}
        c_.inpB = {n: Buf() for n in inn}
        c_.Vf = sb.alloc([128, NQ, L], F32, "Vf" + tag)
        c_.VfB = Buf()
        c_.Vs = sb.alloc([128, NQ, L], BF16, "Vs" + tag)
        c_.VsB = Buf()
        f32n = ["P", "E", "Sx", "Si", "epos", "eneg", "egm", "er"]
        c_.ft = {n: sb.alloc([128, NQ, L], F32, n + tag) for n in f32n}
        c_.ftB = {n: Buf() for n in f32n}
        c_.wl = sb.alloc([128, NQ], F32, "wl" + tag)
        c_.wlB = Buf()
        bdn = ["bdr", "bda", "bdb", "bdk", "bdbw", "bdkw"]
        c_.bd = {n: sb.alloc([128, NQ, 128], BF16, n + tag) for n in bdn}
        c_.bdB = {n: Buf() for n in bdn}
        for n in bdn:
            p.add("pool", lambda e, t_=c_.bd[n]: e.memset(t_[:, :, :], 0.0), writes=[c_.bdB[n]])
        c_.X0 = sb.alloc([128, NQ, 128], BF16, "X0" + tag)
        c_.Y0 = sb.alloc([128, NQ, 128], BF16, "Y0" + tag)
        c_.X0B, c_.Y0B = Buf(), Buf()
        c_.xo = [sb.alloc([128, NQ, 128], BF16, f"xo{i}" + tag) for i in range(2)]
        c_.ao = [sb.alloc([128, NQ, 128], BF16, f"ao{i}" + tag) for i in range(2)]
        c_.xoB = [Buf(), Buf()]
        c_.aoB = [Buf(), Buf()]
        c_.Et = [sb.alloc([128, NQ, 128], BF16, f"E{i}" + tag) for i in range(2)]
        c_.Dt = [sb.alloc([128, NQ, 128], BF16, f"D{i}" + tag) for i in range(2)]
        c_.EtB = [Buf(), Buf()]
        c_.DtB = [Buf(), Buf()]
        c_.Qt = sb.alloc([128, NQ, 128], BF16, "Qt" + tag)
        c_.Rt = sb.alloc([128, NQ, 128], BF16, "Rt" + tag)
        c_.QtB, c_.RtB = Buf(), Buf()
        amn = ["ArbT", "AakT", "ArkT", "bWT", "kWT"]
        c_.am = {n: sb.alloc([128, NQ, 128], BF16, n + tag) for n in amn}
        c_.amB = {n: Buf() for n in amn}
        c_.ST = sb.alloc([128, L], F32, "ST" + tag)
        c_.STb = sb.alloc([128, L], BF16, "STb" + tag)
        c_.STB, c_.STbB = Buf(), Buf()
        c_.RHS = sb.alloc([128, L], BF16, "RHS" + tag)
        c_.U = sb.alloc([128, L], BF16, "U" + tag)
        c_.RHSB, c_.UB = Buf(), Buf()
        c_.yt = sb.alloc([128, NQ, L], F32, "yt" + tag)
        c_.ytB = Buf()
        return c_

    chains = [mk("f"), mk("b")]
    p.barrier()

    def unit(cs, d, c, t):
        fwd = (d == 0)
        M_strict, M_incl, M_strictT = (MASK["LT"], MASK["LE"], MASK["GT"]) if fwd else \
            (MASK["GT"], MASK["GE"], MASK["LT"])
        csl = slice(c * 128, (c + 1) * 128)
        t0 = t * TT
        seg = t0 // SEG
        I_, IB, ft, ftB, bd, bdB, am, amB = cs.inp, cs.inpB, cs.ft, cs.ftB, cs.bd, cs.bdB, cs.am, cs.amB
        ST, STb, STB, STbB = cs.ST, cs.STb, cs.STB, cs.STbB
        fl = None
        if fwd and t0 % SEG == 0 and seg > 0:
            fl = seg
        if (not fwd) and (t0 + TT) % SEG == 0 and seg < nseg - 1:
            fl = seg + 1
        if fl is not None:
            ts(B, "dve", ST[:, :], ST[:, :], B.flags_sb[:, fl:fl + 1], None, ALU.mult, None, [STB], [STB])
            ts(B, "dve", STb[:, :], STb[:, :], B.flags_sb[:, fl:fl + 1], None, ALU.mult, None, [STbB], [STbB])
        for n, key in (("r", "r"), ("kk", "kk"), ("lw", f"lw{d}"), ("kd", f"kd{d}"), ("b", f"b{d}")):
            B.dma("sp", I_[n][:, :, :].rearrange("p q l -> p (q l)"), S[key][csl, t0:t0 + TT], writes=[IB[n]])
        for h in range(2):
            col = (2 * c + h) * 64
            B.dma("sp", cs.Vf[h * 64:(h + 1) * 64, :, :],
                  S["vtok"][t0:t0 + TT, col:col + 64].rearrange("(q j) v -> j q v", j=L), writes=[cs.VfB])
        yield
        act(B, cs.Vs[:, :, :], cs.Vf[:, :, :], AF.Copy, [cs.VfB], [cs.VsB])
        lw = I_["lw"]
        for q in range(NQ):
            p.add("dve", lambda e, q=q: e.tensor_tensor_scan(
                out=ft["P"][:, q, :], data0=onesf[:, :], data1=lw[:, q, :], initial=0.0,
                op0=ALU.mult, op1=ALU.add), [IB["lw"]], [ftB["P"]])
        yield
        tot = ft["P"][:, :, L - 1:L]
        tt(B, "pool", ft["E"][:, :, :], ft["P"][:, :, :], lw[:, :, :], ALU.subtract, [ftB["P"], IB["lw"]], [ftB["E"]])
        tt(B, "dve", ft["Sx"][:, :, :], tot.to_broadcast([128, NQ, L]), ft["P"][:, :, :], ALU.subtract,
           [ftB["P"]], [ftB["Sx"]])
        if fwd:
            G, GB, Gm, GmB, R, RB = ft["P"], ftB["P"], ft["E"], ftB["E"], ft["Sx"], ftB["Sx"]
        else:
            tt(B, "pool", ft["Si"][:, :, :], ft["Sx"][:, :, :], lw[:, :, :], ALU.add, [ftB["Sx"], IB["lw"]],
               [ftB["Si"]])
            G, GB, Gm, GmB, R, RB = ft["Si"], ftB["Si"], ft["Sx"], ftB["Sx"], ft["E"], ftB["E"]
        yield
        act(B, ft["epos"][:, :, :], G[:, :, :], AF.Exp, [GB], [ftB["epos"]])
        act(B, ft["eneg"][:, :, :], G[:, :, :], AF.Exp, [GB], [ftB["eneg"]], scale=-1.0)
        yield
        act(B, ft["egm"][:, :, :], Gm[:, :, :], AF.Exp, [GmB], [ftB["egm"]])
        act(B, ft["er"][:, :, :], R[:, :, :], AF.Exp, [RB], [ftB["er"]])
        act(B, cs.wl[:, :], ft["P"][:, :, L - 1], AF.Exp, [ftB["P"]], [cs.wlB])
        yield
        k2 = 0
        for h in range(2):
            hs = slice(h * 64, (h + 1) * 64)
            stt(B, bd["bda"][hs, :, hs], I_["kk"][hs, :, :], -1.0, ft["egm"][hs, :, :], ALU.mult, ALU.mult,
                [IB["kk"], ftB["egm"]], [bdB["bda"]])
            for dst, a_, e_ in (("bdr", "r", "epos"), ("bdb", "b", "eneg"), ("bdk", "kd", "eneg"),
                                ("bdbw", "b", "er"), ("bdkw", "kd", "er")):
                eng = "dve" if k2 % 2 == 0 else "pool"
                k2 += 1
                tt(B, eng, bd[dst][hs, :, hs], I_[a_][hs, :, :], ft[e_][hs, :, :], ALU.mult,
                   [IB[a_], ftB[e_]], [bdB[dst]])
            yield

        def grp(lhs, lhsB, rhs, rhsB, evac, g):
            ps_, psB_ = bank(B)
            for qq in range(4):
                q = g * 4 + qq
                rr_ = rhs if rhs is identbf else None
                mm(B, ps_[:, qq * 128:(qq + 1) * 128], lhs[:, q, :],
                   (identbf[:, :] if rhs is identbf else rhs[:, q, :]), True, True,
                   [lhsB] + ([] if rhs is identbf else [rhsB]), [psB_])
            evac(g, ps_[:, :].rearrange("p (q c) -> p q c", c=128), psB_)

        def ev_mask(dst, dstB, mask):
            return lambda g, pv, pB: tt(B, "dve", dst[:, g * 4:g * 4 + 4, :], pv,
                                        mask.unsqueeze(1).to_broadcast([128, 4, 128]), ALU.mult, [pB], [dstB])

        def ev_act(dst, dstB):
            return lambda g, pv, pB: act(B, dst[:, g * 4:g * 4 + 4, :], pv, AF.Copy, [pB], [dstB])

        def ev_dve(dst, dstB):
            return lambda g, pv, pB: p.add("dve", lambda e: e.tensor_copy(out=dst[:, g * 4:g * 4 + 4, :], in_=pv),
                                           [pB], [dstB])

        def ev_add(dst, dstB, old, oldB):
            return lambda g, pv, pB: tt(B, "dve", dst[:, g * 4:g * 4 + 4, :], pv, old[:, g * 4:g * 4 + 4, :],
                                        ALU.add, [pB, oldB], [dstB])

        for (lh, rh, mk_, dst, dstB) in (
                ("bdb", "bda", M_strict, cs.X0, cs.X0B), ("bda", "bdb", M_strictT, cs.Y0, cs.Y0B),
                ("bdb", "bdr", M_incl, am["ArbT"], amB["ArbT"]), ("bdk", "bda", M_strict, am["AakT"], amB["AakT"]),
                ("bdk", "bdr", M_incl, am["ArkT"], amB["ArkT"])):
            for g in range(2):
                grp(bd[lh], bdB[lh], bd[rh], bdB[rh], ev_mask(dst, dstB, mk_), g)
                yield
        for nm, src_ in (("bWT", "bdbw"), ("kWT", "bdkw")):
            for g in range(2):
                grp(bd[src_], bdB[src_], identbf, None, ev_act(am[nm], amB[nm]), g)
                yield
        idb = identbf[:, :].unsqueeze(1).to_broadcast([128, NQ, 128])

        def offs(li, slot):
            mk2 = BLK[li].unsqueeze(1).to_broadcast([128, NQ, 128])
            tt(B, "pool", cs.xo[slot][:, :, :], cs.X0[:, :, :], mk2, ALU.mult, [cs.X0B], [cs.xoB[slot]])
            tt(B, "pool", cs.ao[slot][:, :, :], cs.Y0[:, :, :], mk2, ALU.mult, [cs.Y0B], [cs.aoB[slot]])

        offs(0, 0)
        cur = 0
        tt(B, "pool", cs.Et[0][:, :, :], cs.xo[0][:, :, :], idb, ALU.add, [cs.xoB[0]], [cs.EtB[0]])
        tt(B, "pool", cs.Dt[0][:, :, :], cs.ao[0][:, :, :], idb, ALU.add, [cs.aoB[0]], [cs.DtB[0]])
        offs(1, 1)
        yield
        for li in range(1, 6):
            lastl = (li == 5)
            nxt = 1 - cur
            sl_ = li % 2
            xo, xoB, ao, aoB = cs.xo[sl_], cs.xoB[sl_], cs.ao[sl_], cs.aoB[sl_]
            E_, EB_, D_, DB_ = cs.Et[cur], cs.EtB[cur], cs.Dt[cur], cs.DtB[cur]
            for g in range(2):
                grp(ao, aoB, E_, EB_, ev_act(cs.Qt, cs.QtB), g)
                yield
            if not lastl:
                for g in range(2):
                    grp(xo, xoB, D_, DB_, ev_act(cs.Rt, cs.RtB), g)
                    yield
            for g in range(2):
                grp(D_, DB_, cs.Qt, cs.QtB, ev_add(cs.Et[nxt], cs.EtB[nxt], E_, EB_), g)
                yield
            if not lastl:
                for g in range(2):
                    grp(E_, EB_, cs.Rt, cs.RtB, ev_add(cs.Dt[nxt], cs.DtB[nxt], D_, DB_), g)
                    yield
                offs(li + 1, (li + 1) % 2)
            cur = nxt
        Z, ZB = cs.Et[cur], cs.EtB[cur]
        Vs, VsB, RHS, U, RHSB, UB, wl, wlB = cs.Vs, cs.VsB, cs.RHS, cs.U, cs.RHSB, cs.UB, cs.wl, cs.wlB
        qs = list(range(NQ)) if fwd else list(range(NQ - 1, -1, -1))
        for q in qs:
            ps1, ps1B = bank(B)
            mm(B, ps1[:, 0:L], bd["bda"][:, q, :], STb[:, :], True, False, [bdB["bda"], STbB], [ps1B])
            mm(B, ps1[:, 0:L], am["AakT"][:, q, :], Vs[:, q, :], False, True, [amB["AakT"], VsB], [ps1B])
            act(B, RHS[:, :], ps1[:, 0:L], AF.Copy, [ps1B], [RHSB])
            yield
            ps2, ps2B = bank(B)
            mm(B, ps2[:, 0:L], Z[:, q, :], RHS[:, :], True, True, [ZB, RHSB], [ps2B])
            act(B, U[:, :], ps2[:, 0:L], AF.Copy, [ps2B], [UB])
            yield
            ps3, ps3B = bank(B)
            mm(B, ps3[:, 0:L], bd["bdr"][:, q, :], STb[:, :], True, False, [bdB["bdr"], STbB], [ps3B])
            mm(B, ps3[:, 0:L], am["ArbT"][:, q, :], U[:, :], False, False, [amB["ArbT"], UB], [ps3B])
            mm(B, ps3[:, 0:L], am["ArkT"][:, q, :], Vs[:, q, :], False, True, [amB["ArkT"], VsB], [ps3B])
            act(B, cs.yt[:, q, :], ps3[:, 0:L], AF.Copy, [ps3B], [cs.ytB])
            ps4, ps4B = bank(B)
            mm(B, ps4[:, 0:L], am["bWT"][:, q, :], U[:, :], True, False, [amB["bWT"], UB], [ps4B])
            mm(B, ps4[:, 0:L], am["kWT"][:, q, :], Vs[:, q, :], False, True, [amB["kWT"], VsB], [ps4B])
            stt(B, STb[:, :], ST[:, :], wl[:, q:q + 1], ps4[:, 0:L], ALU.mult, ALU.add, [STB, wlB, ps4B], [STbB])
            stt(B, ST[:, :], ST[:, :], wl[:, q:q + 1], ps4[:, 0:L], ALU.mult, ALU.add, [STB, wlB, ps4B], [STB])
            yield
        for h in range(2):
            col = (2 * c + h) * 64
            B.dma("pool", S[f"ytok{d}"][t0:t0 + TT, col:col + 64].rearrange("(q t) v -> t q v", t=L),
                  cs.yt[h * 64:(h + 1) * 64, :, :], reads=[cs.ytB])
        yield

    for c in range(8):
        for cs in chains:
            p.add("pool", lambda e, cs=cs: e.memset(cs.ST[:, :], 0.0), writes=[cs.STB])
            p.add("pool", lambda e, cs=cs: e.memset(cs.STb[:, :], 0.0), writes=[cs.STbB])
        for k in range(NT):
            gens = [unit(chains[0], 0, c, k), unit(chains[1], 1, c, NT - 1 - k)]
            live = [True, True]
            while any(live):
                for gi, g_ in enumerate(gens):
                    if live[gi]:
                        try:
                            next(g_)
                        except StopIteration:
                            live[gi] = False
    p.barrier()
    sb.release(m0)


GN_EPS = 64e-5


def rwkv_r3(B, j, xin, xout, W, V, S):
    nc, p, sb = B.nc, B.p, B.sb
    T = B.T
    TT = 512
    NT = T // TT
    vs = B.vecs_sb
    m0 = sb.mark()
    cst = B.consts_sb
    ident = cst[:, B.C["ident"]:B.C["ident"] + 128]
    wo = sb.alloc([128, 8, D], BF16, "wo")
    woB = Buf()
    stg_t = [sb.alloc([128, 4096], F32, "stg") for _ in range(2)]
    stg = Stager(B, stg_t, [Buf(), Buf()])
    src = W["rw_o"][j].rearrange("(c p) n -> p c n", p=128)
    for hI in range(2):
        stg.load(wo[:, 4 * hI:4 * hI + 4, :], src[:, 4 * hI:4 * hI + 4, :],
                 lambda t: t[:, :].rearrange("p (c n) -> p c n", n=D), woB)
    yin = [[sb.alloc([128, 16, 64], F32, f"y{d}") for d in range(2)] for _ in range(2)]
    yinB = [[Buf(), Buf()] for _ in range(2)]
    ys = sb.alloc([128, 16, 64], F32, "ys")
    ysB = Buf()
    sqc = sb.alloc([128, 16, 64], F32, "sqc")
    sqcB = Buf()
    yn = sb.alloc([128, 16, 64], F32, "yn")
    ynB = Buf()
    st1 = sb.alloc([128, 16], F32, "st1")
    st2 = sb.alloc([128, 16], F32, "st2")
    st1B, st2B = Buf(), Buf()
    gne = sb.alloc([128, 1], F32, "gne")
    gneB = Buf()
    p.add("pool", lambda e: e.memset(gne[:, :], GN_EPS), writes=[gneB])
    zt = sb.alloc([128, 8, TT], F32, "zt")
    ztB = [Buf() for _ in range(8)]
    zb = sb.alloc([128, 8, TT], BF16, "zb")
    zbB = [Buf() for _ in range(8)]
    bvt = [sb.alloc([128, TT], F32, "bvt") for _ in range(2)]
    gt = [sb.alloc([128, TT], F32, "gt") for _ in range(2)]
    bvB = [Buf(), Buf()]
    gB = [Buf(), Buf()]
    xt = sb.alloc([128, 8, TT], F32, "xt")
    xtB = Buf()
    p.barrier()
    xiv = xin.rearrange("(c p) t -> p c t", p=128)
    xov = xout.rearrange("(c p) t -> p c t", p=128)
    lg, lb = V["rw_lnx_g"][j], V["rw_lnx_b"][j]
    k = 0
    for t in range(NT):
        t0 = t * TT
        B.dma("sp", xt[:, :, :], xiv[:, :, t0:t0 + TT], writes=[xtB])
        for s4 in range(4):
            bi = k % 2
            k += 1
            r0 = t0 + s4 * 128
            for d in range(2):
                B.dma("sp", yin[bi][d][:, :, :].rearrange("p h v -> p (h v)"), S[f"ytok{d}"][r0:r0 + 128, :],
                      writes=[yinB[bi][d]])
            tt(B, "pool", ys[:, :, :], yin[bi][0][:, :, :], yin[bi][1][:, :, :], ALU.add,
               [yinB[bi][0], yinB[bi][1]], [ysB])
            p.add("dve", lambda e: e.tensor_reduce(out=st1[:, :], in_=ys[:, :, :], axis=AX.X, op=ALU.add),
                  [ysB], [st1B])
            ts(B, "pool", st1[:, :], st1[:, :], -1.0 / 64, None, ALU.mult, None, [st1B], [st1B])
            tt(B, "dve", ys[:, :, :], ys[:, :, :], st1[:, :].unsqueeze(2).to_broadcast([128, 16, 64]), ALU.add,
               [ysB, st1B], [ysB])
            act(B, sqc[:, :, :], ys[:, :, :], AF.Square, [ysB], [sqcB])
            p.add("dve", lambda e: e.tensor_reduce(out=st2[:, :], in_=sqc[:, :, :], axis=AX.X, op=ALU.add),
                  [sqcB], [st2B])
            act(B, st2[:, :], st2[:, :], AF.Sqrt, [st2B, gneB], [st2B], bias=gne[:, 0:1], scale=1.0 / 64)
            p.add("dve", lambda e: e.reciprocal(out=st2[:, :], in_=st2[:, :]), [st2B], [st2B])
            tt(B, "dve", yn[:, :, :], ys[:, :, :], st2[:, :].unsqueeze(2).to_broadcast([128, 16, 64]), ALU.mult,
               [ysB, st2B], [ynB])
            ynf = yn[:, :, :].rearrange("p h v -> p (h v)")
            for g2 in range(2):
                ps_, psB_ = bank(B)
                for o4 in range(4):
                    oc = g2 * 4 + o4
                    p.add("pe", lambda e, ps_=ps_, o4=o4, oc=oc: e.transpose(
                        ps_[:, o4 * 128:(o4 + 1) * 128], ynf[:, oc * 128:(oc + 1) * 128], ident), [ynB], [psB_])
                for o4 in range(4):
                    oc = g2 * 4 + o4
                    act(B, zt[:, oc, s4 * 128:(s4 + 1) * 128], ps_[:, o4 * 128:(o4 + 1) * 128], AF.Identity,
                        [psB_], [ztB[oc]], bias=vs[:, lb + oc:lb + oc + 1], scale=vs[:, lg + oc:lg + oc + 1])
        for oc in range(8):
            osl = slice(oc * 128, (oc + 1) * 128)
            bi = oc % 2
            B.dma("sp", bvt[bi][:, :], S["bv"][osl, t0:t0 + TT], writes=[bvB[bi]])
            B.dma("sp", gt[bi][:, :], S["g"][osl, t0:t0 + TT], writes=[gB[bi]])
            tt(B, "pool", zt[:, oc, :], zt[:, oc, :], bvt[bi][:, :], ALU.add, [ztB[oc], bvB[bi]], [ztB[oc]])
            tt(B, "dve", zb[:, oc, :], zt[:, oc, :], gt[bi][:, :], ALU.mult, [ztB[oc], gB[bi]], [zbB[oc]])
        for oc in range(8):
            ps_, psB_ = bank(B)
            for c in range(8):
                mm(B, ps_[:, :], wo[:, c, oc * 128:(oc + 1) * 128], zb[:, c, :], c == 0, c == 7, [woB, zbB[c]],
                   [psB_])
            tt(B, "dve", xt[:, oc, :], xt[:, oc, :], ps_[:, :], ALU.add, [xtB, psB_], [xtB])
        B.dma("pool", xov[:, :, t0:t0 + TT], xt[:, :, :], reads=[xtB])
    p.barrier()
    sb.release(m0)


CONST_LAYOUT = {"ident": 0, "LT": 128, "LE": 256, "GT": 384, "GE": 512, "bones": 640}
NCONST = 768


def host_consts():
    pp = np.arange(128)[:, None]
    ff = np.arange(128)[None, :]
    out = np.zeros((128, NCONST), np.float32)
    out[:, 0:128] = (pp == ff)
    out[:, 128:256] = (pp % 64 < ff % 64)
    out[:, 256:384] = (pp % 64 <= ff % 64)
    out[:, 384:512] = (pp % 64 > ff % 64)
    out[:, 512:640] = (pp % 64 >= ff % 64)
    out[:, 640:768] = (pp // 64 == ff // 64)
    return out


def host_blk_masks():
    pp = np.arange(128)[:, None]
    ff = np.arange(128)[None, :]
    out = np.zeros((128, 768), np.float32)
    for li, s in enumerate((1, 2, 4, 8, 16, 32)):
        out[:, 128 * li:128 * li + 128] = ((pp // 64 == ff // 64) & (pp // (2 * s) == ff // (2 * s))
                                           & (pp // s != ff // s))
    return out


def setup_common(B, ncols):
    nc, p, sb = B.nc, B.p, B.sb
    B.bank_i = 0
    B.C = CONST_LAYOUT
    consts = B.din("consts", [128, NCONST]).ap()
    flags = B.din("flags", [128, 8]).ap()
    B.blkm_dram = B.din("blkm", [128, 768]).ap()
    vecs = B.din("vecs", [128, ncols]).ap()
    B.consts_sb = sb.alloc([128, NCONST], F32, "consts")
    B.flags_sb = sb.alloc([128, 8], F32, "flags")
    B.vecs_sb = sb.alloc([128, ncols], F32, "vecs")
    B.ident_bf = sb.alloc([128, 128], BF16, "identbf")
    B.bones_bf = sb.alloc([128, 128], BF16, "bonesbf")
    B.bonesB = Buf()
    cB = Buf()
    B.dma("sp", B.consts_sb[:, :], consts[:, :], writes=[cB])
    B.dma("sp", B.flags_sb[:, :], flags[:, :], writes=[Buf()])
    B.dma("sp", B.vecs_sb[:, :], vecs[:, :], writes=[Buf()])
    p.add("dve", lambda e: e.tensor_copy(out=B.ident_bf[:, :], in_=B.consts_sb[:, 0:128]), [cB], [Buf()])
    p.add("dve", lambda e: e.tensor_copy(out=B.bones_bf[:, :], in_=B.consts_sb[:, 640:768]), [cB], [B.bonesB])
    p.barrier()


RW_SCRATCH_F = ["r", "kk", "g", "bv", "lw0", "lw1", "kd0", "kd1", "b0", "b1", "vfirst"]


def alloc_rwkv_scratch(B):
    T = B.T
    S = {n: B.dscr("s_" + n, [D, T]).ap() for n in RW_SCRATCH_F}
    for n in ("vtok", "ytok0", "ytok1"):
        S[n] = B.dscr("s_" + n, [T, D]).ap()
    return S


def pack_vectors(inp):
    vp = VecPack()
    V = {}
    V["norm_mix_g"] = [vp.add(f"nmg{l}", inp["norm_mix_g"][l]) for l in range(inp["norm_mix_g"].shape[0])]
    V["norm_mlp_g"] = [vp.add(f"nlg{l}", inp["norm_mlp_g"][l]) for l in range(inp["norm_mlp_g"].shape[0])]
    nrw = inp["rw_mix"].shape[0]
    V["rw_mix"] = [vp.add(f"mix{j}", inp["rw_mix"][j]) for j in range(nrw)]
    V["rw_w0"] = [[vp.add(f"w0{j}{d}", inp["rw_w0"][j, d]) for d in range(2)] for j in range(nrw)]
    V["rw_a0"] = [[vp.add(f"a0{j}{d}", inp["rw_a0"][j, d]) for d in range(2)] for j in range(nrw)]
    V["rw_v0"] = [vp.add(f"v0{j}", inp["rw_v0"][j]) for j in range(inp["rw_v0"].shape[0])]
    for nm in ("rw_kk", "rw_ka", "rw_rk", "rw_lnx_g", "rw_lnx_b"):
        V[nm] = [vp.add(f"{nm}{j}", inp[nm][j]) for j in range(nrw)]
    if "na_q_g" in inp:
        nna = inp["na_q_g"].shape[0]
        V["na_q_g"] = [vp.add(f"qg{j}", np.tile(inp["na_q_g"][j], 2)) for j in range(nna)]
        V["na_k_g"] = [vp.add(f"kg{j}", np.tile(inp["na_k_g"][j], 2)) for j in range(nna)]
    return vp, V


RW_WEIGHTS = ["rw_rkv", "rw_w1", "rw_w2", "rw_a1", "rw_a2", "rw_v1", "rw_v2", "rw_g1", "rw_g2", "rw_o"]


def build_rwkv_probe(T, ncols, V, shapes, j, layer):
    B = Builder(T)
    nc = B.nc
    setup_common(B, ncols)
    xT = B.din("xT", [D, T]).ap()
    W = {n: B.din(n, list(shapes[n])).ap() for n in RW_WEIGHTS}
    yT = B.dout("yT", [D, T]).ap()
    S = alloc_rwkv_scratch(B)
    if j > 0:
        vf_in = B.din("vfirst_in", [D, T]).ap()
        S["vfirst"] = vf_in
    rwkv_r1(B, j, layer, xT, W, V, S)
    rwkv_r2(B, S)
    rwkv_r3(B, j, xT, yT, W, V, S)
    with ExitStack() as st:
        B.p.emit(nc, st)
    return B


GRID_W = 64
ROWS_SEG = SEG // GRID_W
NEG = -30000.0


def na_window(i, kind, nseg_sample=4):
    seg = i // ROWS_SEG
    if kind == "S" and seg < nseg_sample:
        rows = nseg_sample * ROWS_SEG
        return int(np.clip(i - 4, 0, rows - 8))
    li = i % ROWS_SEG
    return seg * ROWS_SEG + int(np.clip(li - 4, 0, ROWS_SEG - 8))


def na_slots(T):
    nrows = T // GRID_W
    nseg = T // SEG
    nss = min(4, nseg)
    out = []
    for i in range(nrows):
        lo = min(na_window(i, "P"), na_window(i, "S", nss))
        hi = max(na_window(i, "P"), na_window(i, "S", nss)) + 8
        out.append(list(range(lo // 2, (hi - 1) // 2 + 1)))
    return out


def host_na_nbias(T, kind):
    slots = na_slots(T)
    nss = min(4, T // SEG)
    cols = []
    for i, ms in enumerate(slots):
        lo = na_window(i, kind, nss)
        for m in ms:
            col = np.zeros(128, np.float32)
            for hf in range(2):
                r = 2 * m + hf
                if not (lo <= r < lo + 8):
                    col[hf * 64:(hf + 1) * 64] = NEG
            cols.append(col)
    return np.ascontiguousarray(np.stack(cols, axis=1))


def host_na_bias_table(rpb):
    qc = np.arange(64)
    kc = np.arange(64)
    ws = np.clip(qc - 8, 0, 48)
    cm = (kc[:, None] >= ws[None, :]) & (kc[:, None] < ws[None, :] + 16)
    dc = np.clip(kc[:, None] - qc[None, :] + 15, 0, 30)
    out = np.full((16, 128, 16, 64), NEG, np.float32)
    for e in range(16):
        for hf in range(2):
            dr = e - 8 + hf
            if abs(dr) > 7:
                continue
            g = rpb[:, dr + 7, :][:, dc]
            g = np.where(cm[None], g, np.float32(NEG))
            out[e, hf * 64:(hf + 1) * 64] = np.transpose(g, (1, 0, 2))
    return out


def na_n1(B, jn, layer, xin, W, V, S):
    nc, p, sb = B.nc, B.p, B.sb
    T = B.T
    TT = 512
    NT = T // TT
    vs = B.vecs_sb
    m0 = sb.mark()
    wq = sb.alloc([128, 8, 3 * D], BF16, "wqkv")
    wqB = Buf()
    stg_t = [sb.alloc([128, 4096], F32, "stg") for _ in range(2)]
    stg = Stager(B, stg_t, [Buf(), Buf()])
    src = W["na_qkv"][jn].rearrange("(c p) n -> p c n", p=128)
    for c in range(8):
        for h3 in range(3):
            if h3 < 2:
                stg.load(wq[:, c, h3 * 1024:(h3 + 1) * 1024], src[:, c, h3 * 1024:(h3 + 1) * 1024],
                         lambda t: t[:, 0:1024], wqB)
            else:
                stg.load(wq[:, c, 2048:3072], src[:, c, 2048:3072], lambda t: t[:, 0:1024], wqB)
    xs = [sb.alloc([128, 8, TT], F32, "x") for _ in range(2)]
    xB = [Buf(), Buf()]
    sq = sb.alloc([128, 8, TT], BF16, "sq")
    sqB = Buf()
    hn = sb.alloc([128, 8, TT], BF16, "hn")
    hnB = Buf()
    rs = sb.alloc([128, TT], F32, "rs")
    rstd = sb.alloc([128, TT], F32, "rstd")
    rsB, rstdB = Buf(), Buf()
    epsb = sb.alloc([128, 2], F32, "eps")
    epsB = Buf()
    p.add("pool", lambda e: e.memset(epsb[:, 0:1], RMS_EPS), writes=[epsB])
    p.add("pool", lambda e: e.memset(epsb[:, 1:2], 64 * RMS_EPS), writes=[epsB])
    tq = [sb.alloc([128, TT], F32, "tq") for _ in range(2)]
    tqB = [Buf(), Buf()]
    sqq = [sb.alloc([128, TT], BF16, "sqq") for _ in range(2)]
    sqqB = [Buf(), Buf()]
    rq = [sb.alloc([128, TT], F32, "rq") for _ in range(2)]
    rqB = [Buf(), Buf()]
    qo = [sb.alloc([128, TT], BF16, "qo") for _ in range(2)]
    qoB = [Buf(), Buf()]
    vtl = [sb.alloc([128, D], BF16, "vtl") for _ in range(2)]
    vtlB = [Buf(), Buf()]
    p.barrier()
    xiv = xin.rearrange("(c p) t -> p c t", p=128)
    gc = V["norm_mix_g"][layer]
    k2 = 0
    for t in range(NT):
        t0 = t * TT
        b = t % 2
        x = xs[b]
        B.dma("sp", x[:, :, :], xiv[:, :, t0:t0 + TT], writes=[xB[b]])
        for hI in range(2):
            act(B, sq[:, 4 * hI:4 * hI + 4, :], x[:, 4 * hI:4 * hI + 4, :], AF.Square, [xB[b]], [sqB])
        psA, psAB = bank(B)
        for c in range(8):
            mm(B, psA[:, :], B.ones_bf[:, :], sq[:, c, :], c == 0, c == 7, [B.onesB, sqB], [psAB])
        act(B, rs[:, :], psA[:, :], AF.Sqrt, [psAB, epsB], [rsB], bias=epsb[:, 0:1], scale=1.0 / D)
        p.add("dve", lambda e: e.reciprocal(out=rstd[:, :], in_=rs[:, :]), [rsB], [rstdB])
        for c in range(8):
            stt(B, hn[:, c, :], x[:, c, :], vs[:, gc + c:gc + c + 1], rstd[:, :], ALU.mult, ALU.mult,
                [xB[b], rstdB], [hnB])
        for mI in range(2):
            gcol = V["na_q_g"][jn] if mI == 0 else V["na_k_g"][jn]
            for oc in range(8):
                bi = k2 % 2
                k2 += 1
                ps_, psB_ = bank(B)
                for c in range(8):
                    mm(B, ps_[:, :], wq[:, c, mI * 1024 + oc * 128:mI * 1024 + (oc + 1) * 128], hn[:, c, :],
                       c == 0, c == 7, [hnB], [psB_])
                act(B, tq[bi][:, :], ps_[:, :], AF.Copy, [psB_], [tqB[bi]])
                tt(B, "pool", sqq[bi][:, :], tq[bi][:, :], tq[bi][:, :], ALU.mult, [tqB[bi]], [sqqB[bi]])
                ps2, ps2B = bank(B)
                mm(B, ps2[:, :], B.bones_bf[:, :], sqq[bi][:, :], True, True, [sqqB[bi], B.bonesB], [ps2B])
                if mI == 0:
                    act(B, rq[bi][:, :], ps2[:, :], AF.Sqrt, [ps2B, epsB], [rqB[bi]], bias=epsb[:, 1:2], scale=1.0)
                else:
                    act(B, rq[bi][:, :], ps2[:, :], AF.Sqrt, [ps2B, epsB], [rqB[bi]], bias=epsb[:, 0:1],
                        scale=1.0 / 64)
                p.add("dve", lambda e, bi=bi: e.reciprocal(out=rq[bi][:, :], in_=rq[bi][:, :]), [rqB[bi]], [rqB[bi]])
                stt(B, qo[bi][:, :], tq[bi][:, :], vs[:, gcol:gcol + 1], rq[bi][:, :], ALU.mult, ALU.mult,
                    [tqB[bi], rqB[bi]], [qoB[bi]])
                dst = S["qT"] if mI == 0 else S["kT"]
                B.dma("sp", dst[oc * 128:(oc + 1) * 128, t0:t0 + TT], qo[bi][:, :], reads=[qoB[bi]])
        for tb in range(4):
            bi = tb % 2
            for hf in range(2):
                ps_, psB_ = bank(B)
                for c in range(8):
                    mm(B, ps_[:, :], hn[:, c, tb * 128:(tb + 1) * 128], wq[:, c, 2048 + hf * 512:2048 + (hf + 1) * 512],
                       c == 0, c == 7, [hnB], [psB_])
                if hf == 0:
                    act(B, vtl[bi][:, 0:512], ps_[:, :], AF.Copy, [psB_], [vtlB[bi]])
                else:
                    p.add("dve", lambda e, ps_=ps_, bi=bi: e.tensor_copy(out=vtl[bi][:, 512:1024], in_=ps_[:, :]),
                          [psB_], [vtlB[bi]])
            B.dma("sp", S["vtokb"][t0 + tb * 128:t0 + (tb + 1) * 128, :], vtl[bi][:, :], reads=[vtlB[bi]])
    p.barrier()
    sb.release(m0)


def na_n2(B, jn, xin, xout, W, S, nbias_dram, btab_dram, dbg=9):
    nc, p, sb = B.nc, B.p, B.sb
    T = B.T
    TT = 512
    NT = T // TT
    nrows = T // GRID_W
    slots = na_slots(T)
    nslot_tot = sum(len(s) for s in slots)
    m0 = sb.mark()
    cst = B.consts_sb
    ident = cst[:, B.C["ident"]:B.C["ident"] + 128]
    wo = sb.alloc([128, 8, D], BF16, "wo")
    woB = Buf()
    btab = sb.alloc([128, 16, 16, 64], BF16, "btab")
    btB = Buf()
    nb = sb.alloc([128, nslot_tot], F32, "nbias")
    m1 = sb.mark()
    stg_t = [sb.alloc([128, 4096], F32, "stg") for _ in range(2)]
    stgB = [Buf(), Buf()]
    stg = Stager(B, stg_t, stgB)
    src = W["na_o"][jn].rearrange("(c p) n -> p c n", p=128)
    for hI in range(2):
        stg.load(wo[:, 4 * hI:4 * hI + 4, :], src[:, 4 * hI:4 * hI + 4, :],
                 lambda t: t[:, :].rearrange("p (c n) -> p c n", n=D), woB)
    for e in range(16):
        stg.load(btab[:, e, :, :], btab_dram[e].rearrange("p (h q) -> p h q", q=64),
                 lambda t: t[:, 0:1024].rearrange("p (h q) -> p h q", q=64), btB)
    B.dma("sp", nb[:, :], nbias_dram[:, :], writes=[Buf()])
    p.barrier()
    sb.release(m1)
    NKR = 24
    KT = sb.alloc([128, 8, NKR * 64], BF16, "KT")
    KTB = Buf()
    QT = sb.alloc([128, 8, TT], BF16, "QT")
    QTB = Buf()
    Vraw = sb.alloc([128, NKR // 2, D], BF16, "Vraw")
    VrawB = Buf()
    Vaug = sb.alloc([128, NKR // 2, 16, 68], BF16, "Vaug")
    VaugB = Buf()
    p.add("pool", lambda e: e.memset(Vaug[:, :, :, 64:68], 0.0), writes=[VaugB])
    p.add("pool", lambda e: e.memset(Vaug[:, :, :, 64:65], 1.0), writes=[VaugB])
    NPT = 8
    PT = [sb.alloc([128, 16, 64], BF16, f"PT{i}") for i in range(NPT)]
    PTB = [Buf() for _ in range(NPT)]
    tmp = [sb.alloc([128, 8, 64], F32, "tmp") for _ in range(2)]
    tmpB = [Buf(), Buf()]
    rc = sb.alloc([64, 16], F32, "rc")
    rcB = Buf()
    o = sb.alloc([64, 16, 64], F32, "o")
    oB = Buf()
    oT = sb.alloc([128, 8, TT], BF16, "oT")
    oTB = [Buf() for _ in range(8)]
    xt = sb.alloc([128, 8, TT], F32, "xt")
    xtB = Buf()
    p.barrier()
    xiv = xin.rearrange("(c p) t -> p c t", p=128)
    xov = xout.rearrange("(c p) t -> p c t", p=128)
    qv = S["qT"].rearrange("(c p) t -> p c t", p=128)
    kv = S["kT"].rearrange("(c p) t -> p c t", p=128)
    scol = 0
    k2 = 0
    for t in range(NT):
        t0 = t * TT
        i0 = t0 // GRID_W
        klo = max(0, i0 - 8)
        khi = min(nrows, i0 + 16)
        nk = khi - klo
        B.dma("sp", xt[:, :, :], xiv[:, :, t0:t0 + TT], writes=[xtB])
        B.dma("sp", QT[:, :, :], qv[:, :, t0:t0 + TT], writes=[QTB])
        B.dma("sp", KT[:, :, 0:nk * 64], kv[:, :, klo * 64:khi * 64], writes=[KTB])
        B.dma("sp", Vraw[:, 0:nk // 2, :], S["vtokb"][klo * 64:khi * 64, :].rearrange("(m p) c -> p m c", p=128),
              writes=[VrawB])
        p.add("pool", lambda e, nk=nk: e.tensor_copy(
            out=Vaug[:, 0:nk // 2, :, 0:64], in_=Vraw[:, 0:nk // 2, :].rearrange("p m (h v) -> p m h v", v=64)),
            [VrawB], [VaugB])
        for rr in range(8):
            i = i0 + rr
            ms = slots[i]
            assert len(ms) <= NPT
            for si, m in enumerate(ms if dbg >= 2 else []):
                pl = m - klo // 2
                e_ = 2 * m - i + 8
                assert 0 <= pl < nk // 2 and 0 <= e_ < 16, (i, m, pl, e_)
                pss = [bank(B), bank(B)]
                for h in range(16):
                    hs = slice((h % 2) * 64, (h % 2) * 64 + 64)
                    mm(B, pss[h % 2][0][:, (h // 2) * 64:(h // 2 + 1) * 64], KT[hs, h // 2, pl * 128:(pl + 1) * 128],
                       QT[hs, h // 2, rr * 64:(rr + 1) * 64], True, True, [KTB, QTB], [pss[h % 2][1]])
                for g in range(2):
                    ps_, psB_ = pss[g]
                    tb_ = k2 % 2
                    k2 += 1
                    import os
                    sub = int(os.environ.get("NA_SUB", "9"))
                    if sub >= 2:
                        tt(B, "dve", tmp[tb_][:, :, :], ps_[:, :].rearrange("p (h q) -> p h q", q=64),
                           btab[:, e_, g:16:2, :], ALU.add, [psB_, btB], [tmpB[tb_]])
                    if sub >= 3:
                        act(B, PT[si][:, g:16:2, :], tmp[tb_][:, :, :], AF.Exp, [tmpB[tb_]], [PTB[si]],
                            bias=nb[:, scol:scol + 1], scale=1.0)
                scol += 1
            if dbg < 2:
                scol += len(ms)
            pvb = [bank(B) for _ in range(4)]
            for h in range(16 if dbg >= 3 else 0):
                ps_, psB_ = pvb[h // 4]
                for si, m in enumerate(ms):
                    pl = m - klo // 2
                    mm(B, ps_[0:64, (h % 4) * 128:(h % 4) * 128 + 66], PT[si][:, h, :], Vaug[:, pl, h, 0:66],
                       si == 0, si == len(ms) - 1, [PTB[si], VaugB], [psB_])
            for b4 in range(4 if dbg >= 4 else 0):
                ps_, psB_ = pvb[b4]
                pv3 = ps_[0:64, :].rearrange("p (h c) -> p h c", c=128)
                p.add("dve", lambda e, pv3=pv3, b4=b4: e.reciprocal(out=rc[:, b4 * 4:(b4 + 1) * 4], in_=pv3[:, :, 64]),
                      [psB_], [rcB])
                tt(B, "dve", o[:, b4 * 4:(b4 + 1) * 4, :], pv3[:, :, 0:64],
                   rc[:, b4 * 4:(b4 + 1) * 4].unsqueeze(2).to_broadcast([64, 4, 64]), ALU.mult, [psB_, rcB], [oB])
            of = o[:, :, :].rearrange("p h v -> p (h v)")
            ps_, psB_ = bank(B)
            for oc in range(8 if dbg >= 5 else 0):
                p.add("pe", lambda e, ps_=ps_, oc=oc: e.transpose(ps_[:, oc * 64:(oc + 1) * 64],
                                                                 of[:, oc * 128:(oc + 1) * 128], ident[0:64, 0:64]),
                      [oB], [psB_])
            if dbg >= 5:
                act(B, oT[:, :, rr * 64:(rr + 1) * 64], ps_[:, :].rearrange("p (c q) -> p c q", q=64), AF.Copy,
                    [psB_], oTB)
        for oc in range(8 if dbg >= 6 else 0):
            ps_, psB_ = bank(B)
            for c in range(8):
                mm(B, ps_[:, :], wo[:, c, oc * 128:(oc + 1) * 128], oT[:, c, :], c == 0, c == 7, [woB] + oTB, [psB_])
            tt(B, "dve", xt[:, oc, :], xt[:, oc, :], ps_[:, :], ALU.add, [xtB, psB_], [xtB])
        B.dma("pool", xov[:, :, t0:t0 + TT], xt[:, :, :], reads=[xtB])
    assert scol == nslot_tot
    p.barrier()
    sb.release(m0)


def alloc_na_scratch(B):
    T = B.T
    S = {"qT": B.dscr("s_qT", [D, T], BF16).ap(), "kT": B.dscr("s_kT", [D, T], BF16).ap(),
         "vtokb": B.dscr("s_vtokb", [T, D], BF16).ap()}
    return S


def build_na_probe(T, ncols, V, shapes, jn, layer, mode="full"):
    B = Builder(T)
    nc = B.nc
    setup_common(B, ncols)
    xT = B.din("xT", [D, T]).ap()
    W = {n: B.din(n, list(shapes[n])).ap() for n in ("na_qkv", "na_o")}
    nslot_tot = sum(len(s) for s in na_slots(T))
    nbias = B.din("nbias", [128, nslot_tot]).ap()
    btab = B.din("btab", [16, 128, 1024]).ap()
    yT = B.dout("yT", [D, T]).ap()
    S = alloc_na_scratch(B)
    na_n1(B, jn, layer, xT, W, V, S)
    if mode == "n1":
        B.dma("sp", yT[:, :], xT[:, :])
        B.p.barrier()
    else:
        na_n2(B, jn, xT, yT, W, S, nbias, btab, dbg=int(mode) if mode.isdigit() else 9)
    with ExitStack() as st:
        B.p.emit(nc, st)
    return B


T_CORE = NSEG * SEG
DEPTH = 4
W_NAMES = ["w_up", "w_down", "rw_rkv", "rw_w1", "rw_w2", "rw_a1", "rw_a2", "rw_v1", "rw_v2", "rw_g1", "rw_g2",
           "rw_o", "na_qkv", "na_o"]
_CACHE = {}


def build_full(ncols, V, shapes, T=None, depth=DEPTH, skip_last_mlp=False):
    T = T or T_CORE
    B = Builder(T)
    nc = B.nc
    setup_common(B, ncols)
    xT = B.din("xT", [D, T]).ap()
    W = {n: B.din(n, list(shapes[n])).ap() for n in W_NAMES}
    nslot_tot = sum(len(s) for s in na_slots(T))
    nbias = B.din("nbias", [128, nslot_tot]).ap()
    btabs = [B.din(f"btab{j}", [16, 128, 1024]).ap() for j in range(2)]
    yT = B.dout("yT", [D, T]).ap()
    xA = B.dscr("xA", [D, T]).ap()
    xB = B.dscr("xB", [D, T]).ap()
    SR = alloc_rwkv_scratch(B)
    SN = alloc_na_scratch(B)
    cur = xT
    for layer in range(depth):
        j = layer // 2
        lastl = (layer == depth - 1)
        mdst = yT if (lastl and skip_last_mlp) else xA
        if layer % 2 == 0:
            rwkv_r1(B, j, layer, cur, W, V, SR)
            rwkv_r2(B, SR)
            rwkv_r3(B, j, cur, mdst, W, V, SR)
        else:
            na_n1(B, j, layer, cur, W, V, SN)
            na_n2(B, j, cur, mdst, W, SN, nbias, btabs[j])
        if lastl and skip_last_mlp:
            break
        dst = yT if lastl else xB
        mlp_stage(B, xA, dst, W["w_up"][layer], W["w_down"][layer], B.vecs_sb, V["norm_mlp_g"][layer])
        cur = xB
    with ExitStack() as st:
        B.p.emit(nc, st)
    return B


def kernel(**inputs):
    inp = {k: np.asarray(v) for k, v in inputs.items()}
    xp = inp["x_prompt"]
    xs = inp["x_sample"]
    vp, V = pack_vectors(inp)
    vecs = vp.array()
    shapes = {n: inp[n].shape for n in W_NAMES}
    key = (vp.n,)
    if key not in _CACHE:
        _CACHE[key] = build_full(vp.n, V, shapes)
    B = _CACHE[key]
    consts = host_consts()
    blkm = host_blk_masks()
    btabs = [np.ascontiguousarray(host_na_bias_table(inp["na_rpb"][j]).reshape(16, 128, 1024)) for j in range(2)]
    nb = {"S": host_na_nbias(T_CORE, "S"), "P": host_na_nbias(T_CORE, "P")}
    prompt_ids = []
    in_maps = []
    for c in range(NCORES):
        if c < 4:
            ids = [2 * c, 2 * c + 1]
            xc = np.concatenate([xs[c], xp[ids[0]], xp[ids[1]]], axis=0)
        else:
            ids = list(range(8 + 6 * (c - 4), 8 + 6 * (c - 4) + 6))
            xc = np.concatenate([xp[i] for i in ids], axis=0)
        prompt_ids.append(ids)
        fl = np.zeros((128, 8), np.float32)
        if c < 4:
            fl[:, 1:4] = 1.0
        m = {"xT": np.ascontiguousarray(xc.T), "vecs": vecs, "consts": consts, "flags": fl, "blkm": blkm,
             "nbias": nb["S" if c < 4 else "P"], "btab0": btabs[0], "btab1": btabs[1]}
        for n in W_NAMES:
            m[n] = inp[n]
        in_maps.append(m)
    res = run_bass_kernel_spmd(B.nc, in_maps, core_ids=list(range(NCORES)))
    y_prompt = np.empty_like(xp)
    y_sample = np.empty_like(xs)
    for c in range(NCORES):
        y = res.results[c]["yT"].T
        if c < 4:
            y_sample[c] = y[0:8192]
            for k, i in enumerate(prompt_ids[c]):
                y_prompt[i] = y[8192 + k * SEG:8192 + (k + 1) * SEG]
        else:
            for k, i in enumerate(prompt_ids[c]):
                y_prompt[i] = y[k * SEG:(k + 1) * SEG]
    return (y_prompt, y_sample)
```

```python
from contextlib import ExitStack
import numpy as np
import concourse.bass as bass
import concourse.mybir as mybir
from concourse.bass_utils import run_bass_kernel_spmd

F32 = mybir.dt.float32
BF16 = mybir.dt.bfloat16
ALU = mybir.AluOpType
AF = mybir.ActivationFunctionType
AX = mybir.AxisListType

D = 1024
NCH = 8
DFF = 4096
NSEG = 6
SEG = 2048
NCORES = 8
RMS_EPS = 1e-6

NSLOT = 12
EPOCH = 15000
NEPOCH = 16


class Buf:
    __slots__ = ("w", "r", "parent", "kids")

    def __init__(self, parent=None):
        self.w = None
        self.r = []
        self.parent = parent
        self.kids = []
        if parent is not None:
            parent.kids.append(self)


class Op:
    __slots__ = ("eng", "seq", "fn", "waits", "dma", "sigidx", "slot", "slotval", "slotprev")


class Prog:
    ENGS = ("pe", "act", "dve", "pool", "sp")
    COMPUTE = ("pe", "act", "dve", "pool")

    def __init__(self):
        self.ops = {e: [] for e in self.ENGS}
        self.known = {f: {e: -1 for e in self.ENGS} for f in self.ENGS}
        self.known_dma = {f: set() for f in self.ENGS}
        self.last_compute = {e: -1 for e in self.ENGS}
        self.slot_uses = {q: [0] * NSLOT for q in ("sp", "pool", "act")}
        self.slot_last = {q: [None] * NSLOT for q in ("sp", "pool", "act")}
        self.dma_n = {q: 0 for q in ("sp", "pool", "act")}
        self.fence = {e: -1 for e in self.ENGS}

    def add(self, eng, fn, reads=(), writes=(), dma=False):
        ops = self.ops[eng]
        seq = len(ops)
        deps = set()
        for b in reads:
            if b.w is not None:
                deps.add(b.w)
            if b.parent is not None and b.parent.w is not None:
                deps.add(b.parent.w)
            for kb in b.kids:
                if kb.w is not None:
                    deps.add(kb.w)
        for b in writes:
            if b.w is not None:
                deps.add(b.w)
            deps.update(b.r)
            if b.parent is not None:
                if b.parent.w is not None:
                    deps.add(b.parent.w)
                deps.update(b.parent.r)
            for kb in b.kids:
                if kb.w is not None:
                    deps.add(kb.w)
                deps.update(kb.r)
        waits = []
        kn = self.known[eng]
        kd = self.known_dma[eng]
        for d in sorted(deps):
            E, s, isdma = d
            if s <= self.fence[E]:
                continue
            if isdma:
                if (E, s) in kd:
                    continue
                kd.add((E, s))
                waits.append(d)
            else:
                if E == eng and not dma:
                    if eng == "pe":
                        continue
                    if seq - s > 3:
                        continue
                if kn[E] >= s:
                    continue
                kn[E] = s
                waits.append(d)
        op = Op()
        op.eng, op.seq, op.fn, op.waits, op.dma, op.sigidx = eng, seq, fn, waits, dma, 0
        op.slot = op.slotval = op.slotprev = None
        if dma:
            n = self.dma_n[eng]
            self.dma_n[eng] = n + 1
            sl = n % NSLOT
            op.slot = sl
            op.slotprev = self.slot_last[eng][sl]
            self.slot_uses[eng][sl] += 1
            op.slotval = 16 * self.slot_uses[eng][sl]
            self.slot_last[eng][sl] = (eng, seq, True)
            if op.slotprev is not None:
                kd.add(op.slotprev[:2])
        else:
            self.last_compute[eng] = seq
        tok = (eng, seq, dma)
        for b in reads:
            b.r.append(tok)
        for b in writes:
            b.w = tok
            b.r = []
        ops.append(op)
        return op

    def barrier(self):
        lasts = dict(self.last_compute)
        dmas = []
        for q in self.slot_last:
            for t in self.slot_last[q]:
                if t is not None:
                    dmas.append(t)
        for F in self.ENGS:
            waits = []
            for E in self.COMPUTE:
                if E != F and lasts[E] >= 0 and self.known[F][E] < lasts[E]:
                    waits.append((E, lasts[E], False))
                    self.known[F][E] = lasts[E]
            for t in dmas:
                if t[:2] not in self.known_dma[F]:
                    self.known_dma[F].add(t[:2])
                    waits.append(t)
            op = Op()
            op.eng, op.seq, op.fn, op.waits, op.dma, op.sigidx = F, len(self.ops[F]), None, waits, False, 0
            op.slot = op.slotval = op.slotprev = None
            self.ops[F].append(op)
        for E in self.ENGS:
            self.fence[E] = len(self.ops[E]) - 1

    def emit(self, nc, st):
        sig = {e: set() for e in self.ENGS}
        for F in self.ENGS:
            for op in self.ops[F]:
                for (E, s, isdma) in op.waits:
                    if not isdma:
                        sig[E].add(s)
        nsig = {}
        for E in self.COMPUTE:
            c = 0
            for op in self.ops[E]:
                if op.seq in sig[E]:
                    c += 1
                    op.sigidx = c
            nsig[E] = c
            assert c <= EPOCH * NEPOCH, (E, c)
        csem = {E: [st.enter_context(nc.semaphore(f"c_{E}_{k}")) for k in range((nsig[E] + EPOCH - 1) // EPOCH)]
                for E in self.COMPUTE}
        dsem = {q: [st.enter_context(nc.semaphore(f"d_{q}_{k}")) for k in range(NSLOT)]
                for q in self.slot_last if self.dma_n[q] > 0}
        allops = self.ops

        def run(F, eng):
            for op in allops[F]:
                for (E, s, isdma) in op.waits:
                    t = allops[E][s]
                    if isdma:
                        eng.wait_ge(dsem[E][t.slot], t.slotval)
                    else:
                        i = t.sigidx - 1
                        eng.wait_ge(csem[E][i // EPOCH], i % EPOCH + 1)
                if op.dma and op.slotprev is not None:
                    t = allops[op.slotprev[0]][op.slotprev[1]]
                    eng.wait_ge(dsem[F][t.slot], t.slotval)
                if op.fn is None:
                    continue
                ins = op.fn(eng)
                if op.dma:
                    ins.then_inc(dsem[F][op.slot], 16)
                elif op.sigidx:
                    i = op.sigidx - 1
                    ins.then_inc(csem[F][i // EPOCH], 1)

        block = st.enter_context(nc.Block())

        @block.tensor
        def _(eng):
            run("pe", eng)

        @block.scalar
        def _(eng):
            run("act", eng)

        @block.vector
        def _(eng):
            run("dve", eng)

        @block.gpsimd
        def _(eng):
            run("pool", eng)

        @block.sync
        def _(eng):
            run("sp", eng)


class SB:
    def __init__(self, nc):
        self.nc = nc
        self.base = nc.sbuf_base + 64
        self.top = nc.sbuf_top
        self.ptr = self.base
        self.n = 0

    def alloc(self, shape, dtype, name="t"):
        esz = 2 if dtype == BF16 else 4
        per = esz
        for s in shape[1:]:
            per *= s
        off = (self.ptr + 63) // 64 * 64
        assert off + per <= self.top, (name, off, per, self.top)
        self.ptr = off + per
        self.n += 1
        return self.nc.alloc_sbuf_tensor_at(f"{name}_{self.n}", list(shape), dtype, offset=off)

    def mark(self):
        return self.ptr

    def release(self, m):
        self.ptr = m


class Builder:
    def __init__(self, T):
        self.T = T
        self.nc = bass.Bass("TRN2", target_bir_lowering=False)
        self.p = Prog()
        self.sb = SB(self.nc)
        nc = self.nc
        self.ps = [nc.alloc_psum_tensor(f"psb{i}", [128, 512], F32) for i in range(8)]
        self.psB = [Buf() for _ in range(8)]
        self.ones_bf = self.sb.alloc([128, 128], BF16, "ones")
        self.onesB = Buf()
        self.p.add("pool", lambda e: e.memset(self.ones_bf[:, :], 1.0), writes=[self.onesB])
        self.perm_mark = None

    def din(self, name, shape, dtype=F32):
        return self.nc.dram_tensor(name, list(shape), dtype, kind="ExternalInput")

    def dout(self, name, shape, dtype=F32):
        return self.nc.dram_tensor(name, list(shape), dtype, kind="ExternalOutput")

    def dscr(self, name, shape, dtype=F32):
        return self.nc.dram_tensor(name, list(shape), dtype)

    def dma(self, q, out, in_, reads=(), writes=(), slow=False):
        if slow:
            self.p.add(q, lambda e, o=out, i=in_: e.dma_start(out=o, in_=i, allow_slow_non_contiguous=True),
                       reads, writes, dma=True)
        else:
            self.p.add(q, lambda e, o=out, i=in_: e.dma_start(out=o, in_=i), reads, writes, dma=True)

    def cast(self, k, out, in_, reads, writes):
        eng = ("dve", "pool", "act")[k % 3]
        if eng == "act":
            self.p.add("act", lambda e, o=out, i=in_: e.activation(out=o, in_=i, func=AF.Copy), reads, writes)
        else:
            self.p.add(eng, lambda e, o=out, i=in_: e.tensor_copy(out=o, in_=i), reads, writes)


def mlp_stage(B, xin, xout, w_up, w_down, vecs_sb, gcol):
    nc, p, sb = B.nc, B.p, B.sb
    T = B.T
    TT = 512
    NT = T // TT
    m = sb.mark()
    wup = sb.alloc([128, 8, DFF], BF16, "wup")
    wdn = sb.alloc([128, 32, D], BF16, "wdn")
    wupB = [Buf() for _ in range(8)]
    wdnB = [Buf() for _ in range(8)]
    xs = [sb.alloc([128, 8, TT], F32, "x") for _ in range(2)]
    xoff = []
    xB = [Buf() for _ in range(2)]
    hn = sb.alloc([128, 8, TT], BF16, "hn")
    hnB = Buf()
    a = sb.alloc([128, 16, TT], BF16, "a")
    aB = [Buf() for _ in range(16)]
    r = [sb.alloc([128, TT], F32, "r") for _ in range(2)]
    rB = [Buf() for _ in range(2)]
    rs = sb.alloc([128, TT], F32, "rs")
    rsB = Buf()
    rstd = sb.alloc([128, TT], F32, "rstd")
    rstdB = Buf()
    epsb = sb.alloc([128, 1], F32, "eps")
    epsB = Buf()
    p.add("pool", lambda e: e.memset(epsb[:, :], RMS_EPS), writes=[epsB])
    stg = [xs[0], xs[1]]
    k = 0
    for kc in range(8):
        s = stg[k % 2]
        sv = s[:, :, :].rearrange("p a b -> p (a b)")
        B.dma("sp", sv, w_up[kc * 128:(kc + 1) * 128, :], writes=[xB[k % 2]])
        for h in range(2):
            B.cast(2 * k + h, wup[:, kc, h * 2048:(h + 1) * 2048], sv[:, h * 2048:(h + 1) * 2048],
                   [xB[k % 2]], [wupB[kc]])
        k += 1
    wdv = w_down.rearrange("(c p) n -> p c n", p=128)
    for g4 in range(8):
        s = stg[k % 2]
        sv = s[:, :, :].rearrange("p a b -> p (a b)").rearrange("p (g c) -> p g c", c=D)
        B.dma("sp", sv, wdv[:, g4 * 4:(g4 + 1) * 4, :], writes=[xB[k % 2]])
        for h in range(2):
            B.cast(2 * k + h, wdn[:, g4 * 4 + 2 * h:g4 * 4 + 2 * h + 2, :], sv[:, 2 * h:2 * h + 2, :],
                   [xB[k % 2]], [wdnB[g4]])
        k += 1
    xiv = xin.rearrange("(c p) t -> p c t", p=128)
    xov = xout.rearrange("(c p) t -> p c t", p=128)
    PS_SS, PS_UP, PS_DN = 0, (1, 2), (3, 4, 5, 6)
    dn_i = 0
    for t in range(NT):
        t0 = t * TT
        b = t % 2
        x = xs[b]
        B.dma("sp", x[:, :, :], xiv[:, :, t0:t0 + TT], writes=[xB[b]])
        sq = a
        for h in range(2):
            p.add("act", lambda e, o=sq[:, 4 * h:4 * h + 4, :], i=x[:, 4 * h:4 * h + 4, :]:
                  e.activation(out=o, in_=i, func=AF.Square), [xB[b]], aB[4 * h:4 * h + 4])
        for c in range(8):
            p.add("pe", lambda e, c=c: e.matmul(B.ps[PS_SS][:, :], B.ones_bf[:, :], sq[:, c, :],
                                                 start=(c == 0), stop=(c == 7)),
                  [B.onesB, aB[c]], [B.psB[PS_SS]])
        p.add("act", lambda e: e.activation(out=rs[:, :], in_=B.ps[PS_SS][:, :], func=AF.Sqrt,
                                            bias=epsb[:, 0:1], scale=1.0 / D),
              [B.psB[PS_SS], epsB], [rsB])
        p.add("dve", lambda e: e.reciprocal(out=rstd[:, :], in_=rs[:, :]), [rsB], [rstdB])
        for c in range(8):
            p.add("dve", lambda e, c=c, x=x: e.scalar_tensor_tensor(
                out=hn[:, c, :], in0=x[:, c, :], scalar=vecs_sb[:, gcol + c:gcol + c + 1], in1=rstd[:, :],
                op0=ALU.mult, op1=ALU.mult), [xB[b], rstdB], [hnB])
        for half in range(2):
            for jj in range(16):
                j = half * 16 + jj
                pu = PS_UP[j % 2]
                for kc in range(8):
                    p.add("pe", lambda e, kc=kc, j=j, pu=pu: e.matmul(
                        B.ps[pu][:, :], wup[:, kc, j * 128:(j + 1) * 128], hn[:, kc, :],
                        start=(kc == 0), stop=(kc == 7)), [wupB[kc], hnB], [B.psB[pu]])
                p.add("act", lambda e, j=j, pu=pu: e.activation(out=r[j % 2][:, :], in_=B.ps[pu][:, :],
                                                                 func=AF.Relu),
                      [B.psB[pu]], [rB[j % 2]])
                p.add("pool", lambda e, j=j, jj=jj: e.tensor_tensor(out=a[:, jj, :], in0=r[j % 2][:, :],
                                                                     in1=r[j % 2][:, :], op=ALU.mult),
                      [rB[j % 2]], [aB[jj]])
            for o in range(8):
                pd = PS_DN[dn_i % 4]
                dn_i += 1
                for jj in range(16):
                    j = half * 16 + jj
                    p.add("pe", lambda e, o=o, j=j, jj=jj, pd=pd: e.matmul(
                        B.ps[pd][:, :], wdn[:, j, o * 128:(o + 1) * 128], a[:, jj, :],
                        start=(jj == 0), stop=(jj == 15)), [wdnB[j // 4], aB[jj]], [B.psB[pd]])
                p.add("dve", lambda e, o=o, pd=pd, x=x: e.tensor_tensor(
                    out=x[:, o, :], in0=x[:, o, :], in1=B.ps[pd][:, :], op=ALU.add),
                    [xB[b], B.psB[pd]], [xB[b]])
        B.dma("pool", xov[:, :, t0:t0 + TT], x[:, :, :], reads=[xB[b]])
    p.barrier()
    sb.release(m)


class VecPack:
    def __init__(self):
        self.cols = {}
        self.n = 0
        self.data = []

    def add(self, name, v):
        v = np.asarray(v, np.float32).reshape(-1)
        assert v.size % 128 == 0
        nc_ = v.size // 128
        self.cols[name] = self.n
        self.n += nc_
        self.data.append(np.ascontiguousarray(v.reshape(nc_, 128).T))
        return self.cols[name]

    def array(self):
        return np.ascontiguousarray(np.concatenate(self.data, axis=1))


def build_mlp_only(T, ncols, gcol):
    B = Builder(T)
    nc = B.nc
    xT = B.din("xT", [D, T]).ap()
    vecs = B.din("vecs", [128, ncols]).ap()
    w_up = B.din("w_up", [D, DFF]).ap()
    w_down = B.din("w_down", [DFF, D]).ap()
    yT = B.dout("yT", [D, T]).ap()
    vecs_sb = B.sb.alloc([128, ncols], F32, "vecs")
    vB = Buf()
    B.dma("sp", vecs_sb[:, :], vecs[:, :], writes=[vB])
    B.p.barrier()
    mlp_stage(B, xT, yT, w_up, w_down, vecs_sb, gcol)
    with ExitStack() as st:
        B.p.emit(nc, st)
    return B


class Stager:
    def __init__(self, B, tiles, bufs):
        self.B, self.tiles, self.bufs, self.k = B, tiles, bufs, 0

    def load(self, dst_ap, src_ap, stage_view, dstB):
        i = self.k % len(self.tiles)
        sv = stage_view(self.tiles[i])
        self.B.dma("sp", sv, src_ap, writes=[self.bufs[i]])
        self.B.cast(self.k, dst_ap, sv, [self.bufs[i]], [dstB])
        self.k += 1


def bank(B):
    i = B.bank_i % 8
    B.bank_i += 1
    return B.ps[i], B.psB[i]


def act(B, out, in_, func, reads, writes, bias=None, scale=None):
    kw = {}
    if bias is not None:
        kw["bias"] = bias
    if scale is not None:
        kw["scale"] = scale
    B.p.add("act", lambda e: e.activation(out=out, in_=in_, func=func, **kw), reads, writes)


def tt(B, eng, out, in0, in1, op, reads, writes):
    B.p.add(eng, lambda e: e.tensor_tensor(out=out, in0=in0, in1=in1, op=op), reads, writes)


def ts(B, eng, out, in0, s1, s2, op0, op1, reads, writes):
    if op1 is None:
        B.p.add(eng, lambda e: e.tensor_scalar(out=out, in0=in0, scalar1=s1, scalar2=None, op0=op0), reads, writes)
    else:
        B.p.add(eng, lambda e: e.tensor_scalar(out=out, in0=in0, scalar1=s1, scalar2=s2, op0=op0, op1=op1),
                reads, writes)


def stt(B, out, in0, scalar, in1, op0, op1, reads, writes):
    B.p.add("dve", lambda e: e.scalar_tensor_tensor(out=out, in0=in0, scalar=scalar, in1=in1, op0=op0, op1=op1),
            reads, writes)


def mm(B, out, lhsT, rhs, start, stop, reads, writes):
    B.p.add("pe", lambda e: e.matmul(out, lhsT, rhs, start=start, stop=stop), reads, writes)


def rwkv_r1(B, j, layer, xin, W, V, S):
    nc, p, sb = B.nc, B.p, B.sb
    T = B.T
    TT = 512
    NT = T // TT
    vs = B.vecs_sb
    has_vres = j > 0
    m0 = sb.mark()
    wrkv = [sb.alloc([128, 8, D], BF16, f"w{n}") for n in "rkv"]
    wrkvB = [[Buf() for _ in range(2)] for _ in range(3)]
    w1c = sb.alloc([128, 8, 128], BF16, "w1c")
    a1c = sb.alloc([128, 8, 128], BF16, "a1c")
    g1a = sb.alloc([128, 8, 128], BF16, "g1a")
    g1b = sb.alloc([128, 8, 32], BF16, "g1b")
    w2c = sb.alloc([128, D], BF16, "w2c")
    a2c = sb.alloc([128, D], BF16, "a2c")
    g2a = sb.alloc([128, D], BF16, "g2a")
    g2b = sb.alloc([32, D], BF16, "g2b")
    if has_vres:
        v1s = sb.alloc([128, 8, 32], BF16, "v1s")
        v2s = sb.alloc([32, D], BF16, "v2s")
    wsB = Buf()
    B.sb_off = {}
    B.sb_off["xh"] = (sb.ptr + 63) // 64 * 64
    xh = sb.alloc([128, 8, TT + 2], F32, "xh")
    xhB = Buf()
    B.sb_off["xx"] = (sb.ptr + 63) // 64 * 64
    xx = sb.alloc([128, 8, TT], F32, "xx")
    xxB = Buf()
    rs = sb.alloc([128, TT + 2], F32, "rs")
    rsB = Buf()
    rstd = sb.alloc([128, TT + 2], F32, "rstd")
    rstdB = Buf()
    epsb = sb.alloc([128, 1], F32, "eps")
    epsB = Buf()
    p.add("pool", lambda e: e.memset(epsb[:, :], RMS_EPS), writes=[epsB])
    xm_off = (sb.ptr + 63) // 64 * 64
    xm = [sb.alloc([128, 8, TT], BF16, f"xm{i}") for i in range(6)]
    xmB = [Buf() for _ in range(6)]
    sq = nc.alloc_sbuf_tensor_at(f"r1sq{j}", [128, 8, TT + 2], BF16, offset=xm_off + 4 * 8192)
    sqB = Buf()
    SQW = [sqB, xmB[4], xmB[5]]
    stg_t = [nc.alloc_sbuf_tensor_at(f"r1stg{j}_{i}", [128, 4096], F32, offset=xm_off + i * 16384) for i in range(2)]
    stgB = [Buf(), Buf()]
    stg = Stager(B, stg_t, stgB)
    names = ["tw", "ta", "tg0", "tg1", "tv"]
    lo = {n: sb.alloc([128, TT], BF16, n) for n in names}
    loB = {n: Buf() for n in names}
    tmpn = ["r", "k", "v", "g", "kk", "rn", "lw0", "lw1", "ic0", "ic1", "t1", "kd0", "kd1", "b0", "b1", "bv",
            "vg", "vf"]
    TMs = [{n: sb.alloc([128, TT], F32, n) for n in tmpn}]
    TMBs = [{n: Buf() for n in tmpn}]
    SQKs = [sb.alloc([128, TT], BF16, "sqk")]
    RKs = [sb.alloc([128, TT], BF16, "rk")]
    VTs = [sb.alloc([128, 4, 128], F32, "vt")]
    SQKBs, RKBs, VTBs = [Buf()], [Buf()], [Buf()]
    xh_off = B.sb_off["xh"]
    xx_off = B.sb_off["xx"]
    slots = [(xh_off + i * 2048, xhB) for i in range(8)] + [(xx_off + i * 2048, xxB) for i in range(8)]
    tm1, tmB1 = {}, {}
    si_ = 0
    for n in tmpn:
        if si_ < 16 and n not in ("t1", "rn"):
            off_, par_ = slots[si_]
            si_ += 1
            tm1[n] = nc.alloc_sbuf_tensor_at(f"r1t1{j}_{n}", [128, TT], F32, offset=off_)
            tmB1[n] = Buf(parent=par_)
        else:
            tm1[n] = sb.alloc([128, TT], F32, n + "1")
            tmB1[n] = Buf()
    TMs.append(tm1)
    TMBs.append(tmB1)
    SQKs.append(sb.alloc([128, TT], BF16, "sqk1"))
    RKs.append(sb.alloc([128, TT], BF16, "rk1"))
    VTs.append(sb.alloc([128, 4, 128], F32, "vt1"))
    SQKBs.append(Buf())
    RKBs.append(Buf())
    VTBs.append(Buf())

    rkv = W["rw_rkv"]
    for mI in range(3):
        src = rkv[j, mI].rearrange("(c p) n -> p c n", p=128)
        for hI in range(2):
            stg.load(wrkv[mI][:, 4 * hI:4 * hI + 4, :], src[:, 4 * hI:4 * hI + 4, :],
                     lambda t: t[:, :].rearrange("p (c n) -> p c n", n=D), wsB)
    for dI in range(2):
        stg.load(w1c[:, :, 64 * dI:64 * dI + 64], W["rw_w1"][j, dI].rearrange("(c p) n -> p c n", p=128),
                 lambda t: t[:, 0:512].rearrange("p (c n) -> p c n", n=64), wsB)
        stg.load(a1c[:, :, 64 * dI:64 * dI + 64], W["rw_a1"][j, dI].rearrange("(c p) n -> p c n", p=128),
                 lambda t: t[:, 0:512].rearrange("p (c n) -> p c n", n=64), wsB)
        stg.load(w2c[64 * dI:64 * dI + 64, :], W["rw_w2"][j, dI], lambda t, dI=dI: t[64 * dI:64 * dI + 64, 0:D], wsB)
        stg.load(a2c[64 * dI:64 * dI + 64, :], W["rw_a2"][j, dI], lambda t, dI=dI: t[64 * dI:64 * dI + 64, 0:D], wsB)
    g1v = W["rw_g1"][j].rearrange("(c p) n -> p c n", p=128)
    stg.load(g1a[:, :, :], g1v[:, :, 0:128], lambda t: t[:, 0:1024].rearrange("p (c n) -> p c n", n=128), wsB)
    stg.load(g1b[:, :, :], g1v[:, :, 128:160], lambda t: t[:, 0:256].rearrange("p (c n) -> p c n", n=32), wsB)
    stg.load(g2a[:, :], W["rw_g2"][j, 0:128, :], lambda t: t[:, 0:D], wsB)
    stg.load(g2b[:, :], W["rw_g2"][j, 128:160, :], lambda t: t[0:32, 0:D], wsB)
    if has_vres:
        stg.load(v1s[:, :, :], W["rw_v1"][j - 1].rearrange("(c p) n -> p c n", p=128),
                 lambda t: t[:, 0:256].rearrange("p (c n) -> p c n", n=32), wsB)
        stg.load(v2s[:, :], W["rw_v2"][j - 1], lambda t: t[0:32, 0:D], wsB)
    p.barrier()

    xiv = xin.rearrange("(c p) t -> p c t", p=128)
    cst = B.consts_sb
    ident = cst[:, B.C["ident"]:B.C["ident"] + 128]
    bones = B.bones_bf
    for t in range(NT):
        t0 = t * TT
        seg = t0 // SEG
        first = (t0 % SEG == 0)
        last = ((t0 + TT) % SEG == 0)
        B.dma("sp", xh[:, :, 1:TT + 1], xiv[:, :, t0:t0 + TT], writes=[xhB])
        if t0 > 0:
            B.dma("sp", xh[:, :, 0:1], xiv[:, :, t0 - 1:t0], writes=[xhB], slow=True)
        else:
            p.add("pool", lambda e: e.memset(xh[:, :, 0:1], 0.0), writes=[xhB])
        if t0 + TT < T:
            B.dma("sp", xh[:, :, TT + 1:TT + 2], xiv[:, :, t0 + TT:t0 + TT + 1], writes=[xhB], slow=True)
        else:
            p.add("pool", lambda e: e.memset(xh[:, :, TT + 1:TT + 2], 0.0), writes=[xhB])
        for hI in range(2):
            act(B, sq[:, 4 * hI:4 * hI + 4, :], xh[:, 4 * hI:4 * hI + 4, :], AF.Square, [xhB], SQW)
        psA, psAB = bank(B)
        for c in range(8):
            mm(B, psA[:, :], B.ones_bf[:, :], sq[:, c, 0:TT], c == 0, c == 7, [B.onesB] + SQW, [psAB])
        psH, psHB = bank(B)
        for c in range(8):
            mm(B, psH[:, 0:2], B.ones_bf[:, :], sq[:, c, TT:TT + 2], c == 0, c == 7, [B.onesB] + SQW, [psHB])
        act(B, rs[:, 0:TT], psA[:, :], AF.Sqrt, [psAB, epsB], [rsB], bias=epsb[:, 0:1], scale=1.0 / D)
        act(B, rs[:, TT:TT + 2], psH[:, 0:2], AF.Sqrt, [psHB, epsB], [rsB], bias=epsb[:, 0:1], scale=1.0 / D)
        p.add("dve", lambda e: e.reciprocal(out=rstd[:, :], in_=rs[:, :]), [rsB], [rstdB])
        gc = V["norm_mix_g"][layer]
        for c in range(8):
            stt(B, xh[:, c, :], xh[:, c, :], vs[:, gc + c:gc + c + 1], rstd[:, :], ALU.mult, ALU.mult,
                [xhB, rstdB], [xhB])
        if first and seg > 0:
            ts(B, "dve", xh[:, :, 0:1], xh[:, :, 0:1], B.flags_sb[:, seg:seg + 1], None, ALU.mult, None,
               [xhB], [xhB])
        if last and seg < (T // SEG) - 1:
            ts(B, "dve", xh[:, :, TT + 1:TT + 2], xh[:, :, TT + 1:TT + 2], B.flags_sb[:, seg + 1:seg + 2], None,
               ALU.mult, None, [xhB], [xhB])
        tt(B, "pool", xx[:, :, :], xh[:, :, 0:TT], xh[:, :, 2:TT + 2], ALU.add, [xhB], [xxB])
        stt(B, xx[:, :, :], xx[:, :, :], 0.5, xh[:, :, 1:TT + 1], ALU.mult, ALU.subtract, [xxB, xhB], [xxB])
        mc = V["rw_mix"][j]
        for mI in range(6):
            for c in range(8):
                stt(B, xm[mI][:, c, :], xx[:, c, :], vs[:, mc + 8 * mI + c:mc + 8 * mI + c + 1], xh[:, c, 1:TT + 1],
                    ALU.mult, ALU.add, [xxB, xhB], [xmB[mI]])
        XR, XK, XV, XW, XA, XG = range(6)
        ps_, psB_ = bank(B)
        for c in range(8):
            mm(B, ps_[:, :], w1c[:, c, :], xm[XW][:, c, :], c == 0, c == 7, [xmB[XW]], [psB_])
        act(B, lo["tw"][:, :], ps_[:, :], AF.Tanh, [psB_], [loB["tw"]])
        ps_, psB_ = bank(B)
        for c in range(8):
            mm(B, ps_[:, :], a1c[:, c, :], xm[XA][:, c, :], c == 0, c == 7, [xmB[XA]], [psB_])
        act(B, lo["ta"][:, :], ps_[:, :], AF.Copy, [psB_], [loB["ta"]])
        ps_, psB_ = bank(B)
        for c in range(8):
            mm(B, ps_[:, :], g1a[:, c, :], xm[XG][:, c, :], c == 0, c == 7, [xmB[XG]], [psB_])
        act(B, lo["tg0"][:, :], ps_[:, :], AF.Sigmoid, [psB_], [loB["tg0"]])
        ps_, psB_ = bank(B)
        for c in range(8):
            mm(B, ps_[0:32, :], g1b[:, c, :], xm[XG][:, c, :], c == 0, c == 7, [xmB[XG]], [psB_])
        act(B, lo["tg1"][0:32, :], ps_[0:32, :], AF.Sigmoid, [psB_], [loB["tg1"]])
        if has_vres:
            ps_, psB_ = bank(B)
            for c in range(8):
                mm(B, ps_[0:32, :], v1s[:, c, :], xm[XV][:, c, :], c == 0, c == 7, [xmB[XV]], [psB_])
            act(B, lo["tv"][0:32, :], ps_[0:32, :], AF.Copy, [psB_], [loB["tv"]])
        def oc_unit(oc, tm, tmB, sqk, sqkB, rk, rkB, vt, vtB):
            osl = slice(oc * 128, (oc + 1) * 128)
            for mI, nm in enumerate("rkv"):
                ps_, psB_ = bank(B)
                for c in range(8):
                    mm(B, ps_[:, :], wrkv[mI][:, c, osl], xm[mI][:, c, :], c == 0, c == 7, [xmB[mI]], [psB_])
                if mI == 1:
                    p.add("dve", lambda e, ps_=ps_: e.tensor_copy(out=tm["k"][:, :], in_=ps_[:, :]),
                          [psB_], [tmB["k"]])
                else:
                    act(B, tm[nm][:, :], ps_[:, :], AF.Copy, [psB_], [tmB[nm]])
            yield
            for dI in range(2):
                dsl = slice(64 * dI, 64 * dI + 64)
                ps_, psB_ = bank(B)
                mm(B, ps_[:, :], w2c[dsl, osl], lo["tw"][dsl, :], True, True, [loB["tw"]], [psB_])
                w0c = V["rw_w0"][j][dI] + oc
                act(B, tm[f"lw{dI}"][:, :], ps_[:, :], AF.Sigmoid, [psB_], [tmB[f"lw{dI}"]],
                    bias=vs[:, w0c:w0c + 1], scale=1.0)
                act(B, tm[f"lw{dI}"][:, :], tm[f"lw{dI}"][:, :], AF.Identity, [tmB[f"lw{dI}"]], [tmB[f"lw{dI}"]],
                    scale=-0.6065306597126334)
                ps_, psB_ = bank(B)
                mm(B, ps_[:, :], a2c[dsl, osl], lo["ta"][dsl, :], True, True, [loB["ta"]], [psB_])
                a0c = V["rw_a0"][j][dI] + oc
                act(B, tm[f"ic{dI}"][:, :], ps_[:, :], AF.Sigmoid, [psB_], [tmB[f"ic{dI}"]],
                    bias=vs[:, a0c:a0c + 1], scale=1.0)
            yield
            ps_, psB_ = bank(B)
            mm(B, ps_[:, :], g2a[:, osl], lo["tg0"][:, :], True, False, [loB["tg0"]], [psB_])
            mm(B, ps_[:, :], g2b[0:32, osl], lo["tg1"][0:32, :], False, True, [loB["tg1"]], [psB_])
            act(B, tm["g"][:, :], ps_[:, :], AF.Copy, [psB_], [tmB["g"]])
            if has_vres:
                ps_, psB_ = bank(B)
                mm(B, ps_[:, :], v2s[0:32, osl], lo["tv"][0:32, :], True, True, [loB["tv"]], [psB_])
                v0c = V["rw_v0"][j - 1] + oc
                act(B, tm["vg"][:, :], ps_[:, :], AF.Sigmoid, [psB_], [tmB["vg"]], bias=vs[:, v0c:v0c + 1],
                    scale=1.0)
                B.dma("sp", tm["vf"][:, :], S["vfirst"][osl, t0:t0 + TT], writes=[tmB["vf"]])
                tt(B, "pool", tm["vf"][:, :], tm["vf"][:, :], tm["v"][:, :], ALU.subtract,
                   [tmB["vf"], tmB["v"]], [tmB["vf"]])
                tt(B, "pool", tm["vf"][:, :], tm["vf"][:, :], tm["vg"][:, :], ALU.mult,
                   [tmB["vf"], tmB["vg"]], [tmB["vf"]])
                tt(B, "pool", tm["v"][:, :], tm["v"][:, :], tm["vf"][:, :], ALU.add,
                   [tmB["vf"], tmB["v"]], [tmB["v"]])
            else:
                B.dma("sp", S["vfirst"][osl, t0:t0 + TT], tm["v"][:, :], reads=[tmB["v"]])
            yield
            kkc = V["rw_kk"][j] + oc
            act(B, tm["kk"][:, :], tm["k"][:, :], AF.Identity, [tmB["k"]], [tmB["kk"]], scale=vs[:, kkc:kkc + 1])
            act(B, sqk[:, :], tm["kk"][:, :], AF.Square, [tmB["kk"]], [sqkB])
            ps_, psB_ = bank(B)
            mm(B, ps_[:, :], bones[:, :], sqk[:, :], True, True, [sqkB, B.bonesB], [psB_])
            act(B, tm["rn"][:, :], ps_[:, :], AF.Sqrt, [psB_], [tmB["rn"]])
            ts(B, "dve", tm["rn"][:, :], tm["rn"][:, :], 1e-12, None, ALU.max, None, [tmB["rn"]], [tmB["rn"]])
            p.add("dve", lambda e: e.reciprocal(out=tm["rn"][:, :], in_=tm["rn"][:, :]), [tmB["rn"]], [tmB["rn"]])
            tt(B, "dve", tm["kk"][:, :], tm["kk"][:, :], tm["rn"][:, :], ALU.mult, [tmB["kk"], tmB["rn"]],
               [tmB["kk"]])
            yield
            kac = V["rw_ka"][j] + oc
            for dI in range(2):
                ic, kd, bb = tm[f"ic{dI}"], tm[f"kd{dI}"], tm[f"b{dI}"]
                icB, kdB, bbB = tmB[f"ic{dI}"], tmB[f"kd{dI}"], tmB[f"b{dI}"]
                ts(B, "dve", tm["t1"][:, :], ic[:, :], 1.0, vs[:, kac:kac + 1], ALU.subtract, ALU.mult,
                   [icB], [tmB["t1"]])
                stt(B, kd[:, :], tm["t1"][:, :], 1.0, tm["k"][:, :], ALU.add, ALU.mult, [tmB["t1"], tmB["k"]], [kdB])
                tt(B, "pool", bb[:, :], tm["kk"][:, :], ic[:, :], ALU.mult, [tmB["kk"], icB], [bbB])
            yield
            tt(B, "pool", tm["t1"][:, :], tm["kd0"][:, :], tm["kd1"][:, :], ALU.add, [tmB["kd0"], tmB["kd1"]],
               [tmB["t1"]])
            rkc = V["rw_rk"][j] + oc
            stt(B, rk[:, :], tm["t1"][:, :], vs[:, rkc:rkc + 1], tm["r"][:, :], ALU.mult, ALU.mult,
                [tmB["t1"], tmB["r"]], [rkB])
            ps_, psB_ = bank(B)
            mm(B, ps_[:, :], bones[:, :], rk[:, :], True, True, [rkB, B.bonesB], [psB_])
            tt(B, "dve", tm["bv"][:, :], ps_[:, :], tm["v"][:, :], ALU.mult, [psB_, tmB["v"]], [tmB["bv"]])
            yield
            ps_, psB_ = bank(B)
            for s4 in range(4):
                p.add("pe", lambda e, ps_=ps_, s4=s4: e.transpose(ps_[:, s4 * 128:(s4 + 1) * 128],
                                                                 tm["v"][:, s4 * 128:(s4 + 1) * 128], ident),
                      [tmB["v"]], [psB_])
            act(B, vt[:, :, :], ps_[:, :].rearrange("p (s c) -> p s c", c=128), AF.Copy, [psB_], [vtB])
            B.dma("sp", S["vtok"][t0:t0 + TT, osl].rearrange("(s p) c -> p s c", p=128), vt[:, :, :], reads=[vtB])
            yield
            for nm in ("r", "kk", "g", "bv", "lw0", "lw1", "kd0", "kd1", "b0", "b1"):
                B.dma("sp", S[nm][osl, t0:t0 + TT], tm[nm][:, :], reads=[tmB[nm]])

        for oc0 in range(0, 8, 2):
            gens = [oc_unit(oc0 + s_, TMs[s_], TMBs[s_], SQKs[s_], SQKBs[s_], RKs[s_], RKBs[s_], VTs[s_], VTBs[s_])
                    for s_ in range(2)]
            live = [True, True]
            while any(live):
                for gi, g_ in enumerate(gens):
                    if live[gi]:
                        try:
                            next(g_)
                        except StopIteration:
                            live[gi] = False
    p.barrier()
    sb.release(m0)


def rwkv_r2(B, S):
    nc, p, sb = B.nc, B.p, B.sb
    T = B.T
    TT, L, NQ = 512, 64, 8
    NT = T // TT
    m0 = sb.mark()
    cst = B.consts_sb
    C = B.C
    MASK = {k: cst[:, C[k]:C[k] + 128] for k in ("LT", "LE", "GT", "GE")}
    identbf = B.ident_bf
    blk_sb = sb.alloc([128, 768], F32, "blk")
    B.dma("sp", blk_sb[:, :], B.blkm_dram[:, :], writes=[Buf()])
    BLK = [blk_sb[:, 128 * li:128 * li + 128] for li in range(6)]
    onesf = sb.alloc([128, L], F32, "onesf")
    p.add("pool", lambda e: e.memset(onesf[:, :], 1.0))
    nseg = T // SEG

    class CS:
        pass

    def mk(tag):
        c_ = CS()
        inn = ["r", "kk", "lw", "kd", "b"]
        c_.inp = {n: sb.alloc([128, NQ, L], F32, n + tag) for n in inn}
        c_.inpB = {n: Buf() for n in inn}
        c_.Vf = sb.alloc([128, NQ, L], F32, "Vf" + tag)
        c_.VfB = Buf()
        c_.Vs = sb.alloc([128, NQ, L], BF16, "Vs" + tag)
        c_.VsB = Buf()
        f32n = ["P", "E", "Sx", "Si", "epos", "eneg", "egm", "er"]
        c_.ft = {n: sb.alloc([128, NQ, L], F32, n + tag) for n in f32n}
        c_.ftB = {n: Buf() for n in f32n}
        c_.wl = sb.alloc([128, NQ], F32, "wl" + tag)
        c_.wlB = Buf()
        bdn = ["bdr", "bda", "bdb", "bdk", "bdbw", "bdkw"]
        c_.bd = {n: sb.alloc([128, NQ, 128], BF16, n + tag) for n in bdn}
        c_.bdB = {n: Buf() for n in bdn}
        for n in bdn:
            p.add("pool", lambda e, t_=c_.bd[n]: e.memset(t_[:, :, :], 0.0), writes=[c_.bdB[n]])
        c_.X0 = sb.alloc([128, NQ, 128], BF16, "X0" + tag)
        c_.Y0 = sb.alloc([128, NQ, 128], BF16, "Y0" + tag)
        c_.X0B, c_.Y0B = Buf(), Buf()
        c_.xo = [sb.alloc([128, NQ, 128], BF16, f"xo{i}" + tag) for i in range(2)]
        c_.ao = [sb.alloc([128, NQ, 128], BF16, f"ao{i}" + tag) for i in range(2)]
        c_.xoB = [Buf(), Buf()]
        c_.aoB = [Buf(), Buf()]
        c_.Et = [sb.alloc([128, NQ, 128], BF16, f"E{i}" + tag) for i in range(2)]
        c_.Dt = [sb.alloc([128, NQ, 128], BF16, f"D{i}" + tag) for i in range(2)]
        c_.EtB = [Buf(), Buf()]
        c_.DtB = [Buf(), Buf()]
        c_.Qt = sb.alloc([128, NQ, 128], BF16, "Qt" + tag)
        c_.Rt = sb.alloc([128, NQ, 128], BF16, "Rt" + tag)
        c_.QtB, c_.RtB = Buf(), Buf()
        amn = ["ArbT", "AakT", "ArkT", "bWT", "kWT"]
        c_.am = {n: sb.alloc([128, NQ, 128], BF16, n + tag) for n in amn}
        c_.amB = {n: Buf() for n in amn}
        c_.ST = sb.alloc([128, L], F32, "ST" + tag)
        c_.STb = sb.alloc([128, L], BF16, "STb" + tag)
        c_.STB, c_.STbB = Buf(), Buf()
        c_.RHS = sb.alloc([128, L], BF16, "RHS" + tag)
        c_.U = sb.alloc([128, L], BF16, "U" + tag)
        c_.RHSB, c_.UB = Buf(), Buf()
        c_.yt = sb.alloc([128, NQ, L], F32, "yt" + tag)
        c_.ytB = Buf()
        return c_

    chains = [mk("f"), mk("b")]
    p.barrier()

    def unit(cs, d, c, t):
        fwd = (d == 0)
        M_strict, M_incl, M_strictT = (MASK["LT"], MASK["LE"], MASK["GT"]) if fwd else \
            (MASK["GT"], MASK["GE"], MASK["LT"])
        csl = slice(c * 128, (c + 1) * 128)
        t0 = t * TT
        seg = t0 // SEG
        I_, IB, ft, ftB, bd, bdB, am, amB = cs.inp, cs.inpB, cs.ft, cs.ftB, cs.bd, cs.bdB, cs.am, cs.amB
        ST, STb, STB, STbB = cs.ST, cs.STb, cs.STB, cs.STbB
        fl = None
        if fwd and t0 % SEG == 0 and seg > 0:
            fl = seg
        if (not fwd) and (t0 + TT) % SEG == 0 and seg < nseg - 1:
            fl = seg + 1
        if fl is not None:
            ts(B, "dve", ST[:, :], ST[:, :], B.flags_sb[:, fl:fl + 1], None, ALU.mult, None, [STB], [STB])
            ts(B, "dve", STb[:, :], STb[:, :], B.flags_sb[:, fl:fl + 1], None, ALU.mult, None, [STbB], [STbB])
        for n, key in (("r", "r"), ("kk", "kk"), ("lw", f"lw{d}"), ("kd", f"kd{d}"), ("b", f"b{d}")):
            B.dma("sp", I_[n][:, :, :].rearrange("p q l -> p (q l)"), S[key][csl, t0:t0 + TT], writes=[IB[n]])
        for h in range(2):
            col = (2 * c + h) * 64
            B.dma("sp", cs.Vf[h * 64:(h + 1) * 64, :, :],
                  S["vtok"][t0:t0 + TT, col:col + 64].rearrange("(q j) v -> j q v", j=L), writes=[cs.VfB])
        yield
        act(B, cs.Vs[:, :, :], cs.Vf[:, :, :], AF.Copy, [cs.VfB], [cs.VsB])
        lw = I_["lw"]
        for q in range(NQ):
            p.add("dve", lambda e, q=q: e.tensor_tensor_scan(
                out=ft["P"][:, q, :], data0=onesf[:, :], data1=lw[:, q, :], initial=0.0,
                op0=ALU.mult, op1=ALU.add), [IB["lw"]], [ftB["P"]])
        yield
        tot = ft["P"][:, :, L - 1:L]
        tt(B, "pool", ft["E"][:, :, :], ft["P"][:, :, :], lw[:, :, :], ALU.subtract, [ftB["P"], IB["lw"]], [ftB["E"]])
        tt(B, "dve", ft["Sx"][:, :, :], tot.to_broadcast([128, NQ, L]), ft["P"][:, :, :], ALU.subtract,
           [ftB["P"]], [ftB["Sx"]])
        if fwd:
            G, GB, Gm, GmB, R, RB = ft["P"], ftB["P"], ft["E"], ftB["E"], ft["Sx"], ftB["Sx"]
        else:
            tt(B, "pool", ft["Si"][:, :, :], ft["Sx"][:, :, :], lw[:, :, :], ALU.add, [ftB["Sx"], IB["lw"]],
               [ftB["Si"]])
            G, GB, Gm, GmB, R, RB = ft["Si"], ftB["Si"], ft["Sx"], ftB["Sx"], ft["E"], ftB["E"]
        yield
        act(B, ft["epos"][:, :, :], G[:, :, :], AF.Exp, [GB], [ftB["epos"]])
        act(B, ft["eneg"][:, :, :], G[:, :, :], AF.Exp, [GB], [ftB["eneg"]], scale=-1.0)
        yield
        act(B, ft["egm"][:, :, :], Gm[:, :, :], AF.Exp, [GmB], [ftB["egm"]])
        act(B, ft["er"][:, :, :], R[:, :, :], AF.Exp, [RB], [ftB["er"]])
        act(B, cs.wl[:, :], ft["P"][:, :, L - 1], AF.Exp, [ftB["P"]], [cs.wlB])
        yield
        k2 = 0
        for h in range(2):
            hs = slice(h * 64, (h + 1) * 64)
            stt(B, bd["bda"][hs, :, hs], I_["kk"][hs, :, :], -1.0, ft["egm"][hs, :, :], ALU.mult, ALU.mult,
                [IB["kk"], ftB["egm"]], [bdB["bda"]])
            for dst, a_, e_ in (("bdr", "r", "epos"), ("bdb", "b", "eneg"), ("bdk", "kd", "eneg"),
                                ("bdbw", "b", "er"), ("bdkw", "kd", "er")):
                eng = "dve" if k2 % 2 == 0 else "pool"
                k2 += 1
                tt(B, eng, bd[dst][hs, :, hs], I_[a_][hs, :, :], ft[e_][hs, :, :], ALU.mult,
                   [IB[a_], ftB[e_]], [bdB[dst]])
            yield

        def grp(lhs, lhsB, rhs, rhsB, evac, g):
            ps_, psB_ = bank(B)
            for qq in range(4):
                q = g * 4 + qq
                rr_ = rhs if rhs is identbf else None
                mm(B, ps_[:, qq * 128:(qq + 1) * 128], lhs[:, q, :],
                   (identbf[:, :] if rhs is identbf else rhs[:, q, :]), True, True,
                   [lhsB] + ([] if rhs is identbf else [rhsB]), [psB_])
            evac(g, ps_[:, :].rearrange("p (q c) -> p q c", c=128), psB_)

        def ev_mask(dst, dstB, mask):
            return lambda g, pv, pB: tt(B, "dve", dst[:, g * 4:g * 4 + 4, :], pv,
                                        mask.unsqueeze(1).to_broadcast([128, 4, 128]), ALU.mult, [pB], [dstB])

        def ev_act(dst, dstB):
            return lambda g, pv, pB: act(B, dst[:, g * 4:g * 4 + 4, :], pv, AF.Copy, [pB], [dstB])

        def ev_dve(dst, dstB):
            return lambda g, pv, pB: p.add("dve", lambda e: e.tensor_copy(out=dst[:, g * 4:g * 4 + 4, :], in_=pv),
                                           [pB], [dstB])

        def ev_add(dst, dstB, old, oldB):
            return lambda g, pv, pB: tt(B, "dve", dst[:, g * 4:g * 4 + 4, :], pv, old[:, g * 4:g * 4 + 4, :],
                                        ALU.add, [pB, oldB], [dstB])

        for (lh, rh, mk_, dst, dstB) in (
                ("bdb", "bda", M_strict, cs.X0, cs.X0B), ("bda", "bdb", M_strictT, cs.Y0, cs.Y0B),
                ("bdb", "bdr", M_incl, am["ArbT"], amB["ArbT"]), ("bdk", "bda", M_strict, am["AakT"], amB["AakT"]),
                ("bdk", "bdr", M_incl, am["ArkT"], amB["ArkT"])):
            for g in range(2):
                grp(bd[lh], bdB[lh], bd[rh], bdB[rh], ev_mask(dst, dstB, mk_), g)
                yield
        for nm, src_ in (("bWT", "bdbw"), ("kWT", "bdkw")):
            for g in range(2):
                grp(bd[src_], bdB[src_], identbf, None, ev_act(am[nm], amB[nm]), g)
                yield
        idb = identbf[:, :].unsqueeze(1).to_broadcast([128, NQ, 128])

        def offs(li, slot):
            mk2 = BLK[li].unsqueeze(1).to_broadcast([128, NQ, 128])
            tt(B, "pool", cs.xo[slot][:, :, :], cs.X0[:, :, :], mk2, ALU.mult, [cs.X0B], [cs.xoB[slot]])
            tt(B, "pool", cs.ao[slot][:, :, :], cs.Y0[:, :, :], mk2, ALU.mult, [cs.Y0B], [cs.aoB[slot]])

        offs(0, 0)
        cur = 0
        tt(B, "pool", cs.Et[0][:, :, :], cs.xo[0][:, :, :], idb, ALU.add, [cs.xoB[0]], [cs.EtB[0]])
        tt(B, "pool", cs.Dt[0][:, :, :], cs.ao[0][:, :, :], idb, ALU.add, [cs.aoB[0]], [cs.DtB[0]])
        offs(1, 1)
        yield
        for li in range(1, 6):
            lastl = (li == 5)
            nxt = 1 - cur
            sl_ = li % 2
            xo, xoB, ao, aoB = cs.xo[sl_], cs.xoB[sl_], cs.ao[sl_], cs.aoB[sl_]
            E_, EB_, D_, DB_ = cs.Et[cur], cs.EtB[cur], cs.Dt[cur], cs.DtB[cur]
            for g in range(2):
                grp(ao, aoB, E_, EB_, ev_act(cs.Qt, cs.QtB), g)
                yield
            if not lastl:
                for g in range(2):
                    grp(xo, xoB, D_, DB_, ev_act(cs.Rt, cs.RtB), g)
                    yield
            for g in range(2):
                grp(D_, DB_, cs.Qt, cs.QtB, ev_add(cs.Et[nxt], cs.EtB[nxt], E_, EB_), g)
                yield
            if not lastl:
                for g in range(2):
                    grp(E_, EB_, cs.Rt, cs.RtB, ev_add(cs.Dt[nxt], cs.DtB[nxt], D_, DB_), g)
                    yield
                offs(li + 1, (li + 1) % 2)
            cur = nxt
        Z, ZB = cs.Et[cur], cs.EtB[cur]
        Vs, VsB, RHS, U, RHSB, UB, wl, wlB = cs.Vs, cs.VsB, cs.RHS, cs.U, cs.RHSB, cs.UB, cs.wl, cs.wlB
        qs = list(range(NQ)) if fwd else list(range(NQ - 1, -1, -1))
        for q in qs:
            ps1, ps1B = bank(B)
            mm(B, ps1[:, 0:L], bd["bda"][:, q, :], STb[:, :], True, False, [bdB["bda"], STbB], [ps1B])
            mm(B, ps1[:, 0:L], am["AakT"][:, q, :], Vs[:, q, :], False, True, [amB["AakT"], VsB], [ps1B])
            act(B, RHS[:, :], ps1[:, 0:L], AF.Copy, [ps1B], [RHSB])
            yield
            ps2, ps2B = bank(B)
            mm(B, ps2[:, 0:L], Z[:, q, :], RHS[:, :], True, True, [ZB, RHSB], [ps2B])
            act(B, U[:, :], ps2[:, 0:L], AF.Copy, [ps2B], [UB])
            yield
            ps3, ps3B = bank(B)
            mm(B, ps3[:, 0:L], bd["bdr"][:, q, :], STb[:, :], True, False, [bdB["bdr"], STbB], [ps3B])
            mm(B, ps3[:, 0:L], am["ArbT"][:, q, :], U[:, :], False, False, [amB["ArbT"], UB], [ps3B])
            mm(B, ps3[:, 0:L], am["ArkT"][:, q, :], Vs[:, q, :], False, True, [amB["ArkT"], VsB], [ps3B])
            act(B, cs.yt[:, q, :], ps3[:, 0:L], AF.Copy, [ps3B], [cs.ytB])
            ps4, ps4B = bank(B)
            mm(B, ps4[:, 0:L], am["bWT"][:, q, :], U[:, :], True, False, [amB["bWT"], UB], [ps4B])
            mm(B, ps4[:, 0:L], am["kWT"][:, q, :], Vs[:, q, :], False, True, [amB["kWT"], VsB], [ps4B])
            stt(B, STb[:, :], ST[:, :], wl[:, q:q + 1], ps4[:, 0:L], ALU.mult, ALU.add, [STB, wlB, ps4B], [STbB])
            stt(B, ST[:, :], ST[:, :], wl[:, q:q + 1], ps4[:, 0:L], ALU.mult, ALU.add, [STB, wlB, ps4B], [STB])
            yield
        for h in range(2):
            col = (2 * c + h) * 64
            B.dma("pool", S[f"ytok{d}"][t0:t0 + TT, col:col + 64].rearrange("(q t) v -> t q v", t=L),
                  cs.yt[h * 64:(h + 1) * 64, :, :], reads=[cs.ytB])
        yield

    for c in range(8):
        for cs in chains:
            p.add("pool", lambda e, cs=cs: e.memset(cs.ST[:, :], 0.0), writes=[cs.STB])
            p.add("pool", lambda e, cs=cs: e.memset(cs.STb[:, :], 0.0), writes=[cs.STbB])
        for k in range(NT):
            gens = [unit(chains[0], 0, c, k), unit(chains[1], 1, c, NT - 1 - k)]
            live = [True, True]
            while any(live):
                for gi, g_ in enumerate(gens):
                    if live[gi]:
                        try:
                            next(g_)
                        except StopIteration:
                            live[gi] = False
    p.barrier()
    sb.release(m0)


GN_EPS = 64e-5


def rwkv_r3(B, j, xin, xout, W, V, S):
    nc, p, sb = B.nc, B.p, B.sb
    T = B.T
    TT = 512
    NT = T // TT
    vs = B.vecs_sb
    m0 = sb.mark()
    cst = B.consts_sb
    ident = cst[:, B.C["ident"]:B.C["ident"] + 128]
    wo = sb.alloc([128, 8, D], BF16, "wo")
    woB = Buf()
    stg_t = [sb.alloc([128, 4096], F32, "stg") for _ in range(2)]
    stg = Stager(B, stg_t, [Buf(), Buf()])
    src = W["rw_o"][j].rearrange("(c p) n -> p c n", p=128)
    for hI in range(2):
        stg.load(wo[:, 4 * hI:4 * hI + 4, :], src[:, 4 * hI:4 * hI + 4, :],
                 lambda t: t[:, :].rearrange("p (c n) -> p c n", n=D), woB)
    yin = [[sb.alloc([128, 16, 64], F32, f"y{d}") for d in range(2)] for _ in range(2)]
    yinB = [[Buf(), Buf()] for _ in range(2)]
    ys = sb.alloc([128, 16, 64], F32, "ys")
    ysB = Buf()
    sqc = sb.alloc([128, 16, 64], F32, "sqc")
    sqcB = Buf()
    yn = sb.alloc([128, 16, 64], F32, "yn")
    ynB = Buf()
    st1 = sb.alloc([128, 16], F32, "st1")
    st2 = sb.alloc([128, 16], F32, "st2")
    st1B, st2B = Buf(), Buf()
    gne = sb.alloc([128, 1], F32, "gne")
    gneB = Buf()
    p.add("pool", lambda e: e.memset(gne[:, :], GN_EPS), writes=[gneB])
    zt = sb.alloc([128, 8, TT], F32, "zt")
    ztB = [Buf() for _ in range(8)]
    zb = sb.alloc([128, 8, TT], BF16, "zb")
    zbB = [Buf() for _ in range(8)]
    bvt = [sb.alloc([128, TT], F32, "bvt") for _ in range(2)]
    gt = [sb.alloc([128, TT], F32, "gt") for _ in range(2)]
    bvB = [Buf(), Buf()]
    gB = [Buf(), Buf()]
    xt = sb.alloc([128, 8, TT], F32, "xt")
    xtB = Buf()
    p.barrier()
    xiv = xin.rearrange("(c p) t -> p c t", p=128)
    xov = xout.rearrange("(c p) t -> p c t", p=128)
    lg, lb = V["rw_lnx_g"][j], V["rw_lnx_b"][j]
    k = 0
    for t in range(NT):
        t0 = t * TT
        B.dma("sp", xt[:, :, :], xiv[:, :, t0:t0 + TT], writes=[xtB])
        for s4 in range(4):
            bi = k % 2
            k += 1
            r0 = t0 + s4 * 128
            for d in range(2):
                B.dma("sp", yin[bi][d][:, :, :].rearrange("p h v -> p (h v)"), S[f"ytok{d}"][r0:r0 + 128, :],
                      writes=[yinB[bi][d]])
            tt(B, "pool", ys[:, :, :], yin[bi][0][:, :, :], yin[bi][1][:, :, :], ALU.add,
               [yinB[bi][0], yinB[bi][1]], [ysB])
            p.add("dve", lambda e: e.tensor_reduce(out=st1[:, :], in_=ys[:, :, :], axis=AX.X, op=ALU.add),
                  [ysB], [st1B])
            ts(B, "pool", st1[:, :], st1[:, :], -1.0 / 64, None, ALU.mult, None, [st1B], [st1B])
            tt(B, "dve", ys[:, :, :], ys[:, :, :], st1[:, :].unsqueeze(2).to_broadcast([128, 16, 64]), ALU.add,
               [ysB, st1B], [ysB])
            act(B, sqc[:, :, :], ys[:, :, :], AF.Square, [ysB], [sqcB])
            p.add("dve", lambda e: e.tensor_reduce(out=st2[:, :], in_=sqc[:, :, :], axis=AX.X, op=ALU.add),
                  [sqcB], [st2B])
            act(B, st2[:, :], st2[:, :], AF.Sqrt, [st2B, gneB], [st2B], bias=gne[:, 0:1], scale=1.0 / 64)
            p.add("dve", lambda e: e.reciprocal(out=st2[:, :], in_=st2[:, :]), [st2B], [st2B])
            tt(B, "dve", yn[:, :, :], ys[:, :, :], st2[:, :].unsqueeze(2).to_broadcast([128, 16, 64]), ALU.mult,
               [ysB, st2B], [ynB])
            ynf = yn[:, :, :].rearrange("p h v -> p (h v)")
            for g2 in range(2):
                ps_, psB_ = bank(B)
                for o4 in range(4):
                    oc = g2 * 4 + o4
                    p.add("pe", lambda e, ps_=ps_, o4=o4, oc=oc: e.transpose(
                        ps_[:, o4 * 128:(o4 + 1) * 128], ynf[:, oc * 128:(oc + 1) * 128], ident), [ynB], [psB_])
                for o4 in range(4):
                    oc = g2 * 4 + o4
                    act(B, zt[:, oc, s4 * 128:(s4 + 1) * 128], ps_[:, o4 * 128:(o4 + 1) * 128], AF.Identity,
                        [psB_], [ztB[oc]], bias=vs[:, lb + oc:lb + oc + 1], scale=vs[:, lg + oc:lg + oc + 1])
        for oc in range(8):
            osl = slice(oc * 128, (oc + 1) * 128)
            bi = oc % 2
            B.dma("sp", bvt[bi][:, :], S["bv"][osl, t0:t0 + TT], writes=[bvB[bi]])
            B.dma("sp", gt[bi][:, :], S["g"][osl, t0:t0 + TT], writes=[gB[bi]])
            tt(B, "pool", zt[:, oc, :], zt[:, oc, :], bvt[bi][:, :], ALU.add, [ztB[oc], bvB[bi]], [ztB[oc]])
            tt(B, "dve", zb[:, oc, :], zt[:, oc, :], gt[bi][:, :], ALU.mult, [ztB[oc], gB[bi]], [zbB[oc]])
        for oc in range(8):
            ps_, psB_ = bank(B)
            for c in range(8):
                mm(B, ps_[:, :], wo[:, c, oc * 128:(oc + 1) * 128], zb[:, c, :], c == 0, c == 7, [woB, zbB[c]],
                   [psB_])
            tt(B, "dve", xt[:, oc, :], xt[:, oc, :], ps_[:, :], ALU.add, [xtB, psB_], [xtB])
        B.dma("pool", xov[:, :, t0:t0 + TT], xt[:, :, :], reads=[xtB])
    p.barrier()
    sb.release(m0)


CONST_LAYOUT = {"ident": 0, "LT": 128, "LE": 256, "GT": 384, "GE": 512, "bones": 640}
NCONST = 768


def host_consts():
    pp = np.arange(128)[:, None]
    ff = np.arange(128)[None, :]
    out = np.zeros((128, NCONST), np.float32)
    out[:, 0:128] = (pp == ff)
    out[:, 128:256] = (pp % 64 < ff % 64)
    out[:, 256:384] = (pp % 64 <= ff % 64)
    out[:, 384:512] = (pp % 64 > ff % 64)
    out[:, 512:640] = (pp % 64 >= ff % 64)
    out[:, 640:768] = (pp // 64 == ff // 64)
    return out


def host_blk_masks():
    pp = np.arange(128)[:, None]
    ff = np.arange(128)[None, :]
    out = np.zeros((128, 768), np.float32)
    for li, s in enumerate((1, 2, 4, 8, 16, 32)):
        out[:, 128 * li:128 * li + 128] = ((pp // 64 == ff // 64) & (pp // (2 * s) == ff // (2 * s))
                                           & (pp // s != ff // s))
    return out


def setup_common(B, ncols):
    nc, p, sb = B.nc, B.p, B.sb
    B.bank_i = 0
    B.C = CONST_LAYOUT
    consts = B.din("consts", [128, NCONST]).ap()
    flags = B.din("flags", [128, 8]).ap()
    B.blkm_dram = B.din("blkm", [128, 768]).ap()
    vecs = B.din("vecs", [128, ncols]).ap()
    B.consts_sb = sb.alloc([128, NCONST], F32, "consts")
    B.flags_sb = sb.alloc([128, 8], F32, "flags")
    B.vecs_sb = sb.alloc([128, ncols], F32, "vecs")
    B.ident_bf = sb.alloc([128, 128], BF16, "identbf")
    B.bones_bf = sb.alloc([128, 128], BF16, "bonesbf")
    B.bonesB = Buf()
    cB = Buf()
    B.dma("sp", B.consts_sb[:, :], consts[:, :], writes=[cB])
    B.dma("sp", B.flags_sb[:, :], flags[:, :], writes=[Buf()])
    B.dma("sp", B.vecs_sb[:, :], vecs[:, :], writes=[Buf()])
    p.add("dve", lambda e: e.tensor_copy(out=B.ident_bf[:, :], in_=B.consts_sb[:, 0:128]), [cB], [Buf()])
    p.add("dve", lambda e: e.tensor_copy(out=B.bones_bf[:, :], in_=B.consts_sb[:, 640:768]), [cB], [B.bonesB])
    p.barrier()


RW_SCRATCH_F = ["r", "kk", "g", "bv", "lw0", "lw1", "kd0", "kd1", "b0", "b1", "vfirst"]


def alloc_rwkv_scratch(B):
    T = B.T
    S = {n: B.dscr("s_" + n, [D, T]).ap() for n in RW_SCRATCH_F}
    for n in ("vtok", "ytok0", "ytok1"):
        S[n] = B.dscr("s_" + n, [T, D]).ap()
    return S


def pack_vectors(inp):
    vp = VecPack()
    V = {}
    V["norm_mix_g"] = [vp.add(f"nmg{l}", inp["norm_mix_g"][l]) for l in range(inp["norm_mix_g"].shape[0])]
    V["norm_mlp_g"] = [vp.add(f"nlg{l}", inp["norm_mlp_g"][l]) for l in range(inp["norm_mlp_g"].shape[0])]
    nrw = inp["rw_mix"].shape[0]
    V["rw_mix"] = [vp.add(f"mix{j}", inp["rw_mix"][j]) for j in range(nrw)]
    V["rw_w0"] = [[vp.add(f"w0{j}{d}", inp["rw_w0"][j, d]) for d in range(2)] for j in range(nrw)]
    V["rw_a0"] = [[vp.add(f"a0{j}{d}", inp["rw_a0"][j, d]) for d in range(2)] for j in range(nrw)]
    V["rw_v0"] = [vp.add(f"v0{j}", inp["rw_v0"][j]) for j in range(inp["rw_v0"].shape[0])]
    for nm in ("rw_kk", "rw_ka", "rw_rk", "rw_lnx_g", "rw_lnx_b"):
        V[nm] = [vp.add(f"{nm}{j}", inp[nm][j]) for j in range(nrw)]
    if "na_q_g" in inp:
        nna = inp["na_q_g"].shape[0]
        V["na_q_g"] = [vp.add(f"qg{j}", np.tile(inp["na_q_g"][j], 2)) for j in range(nna)]
        V["na_k_g"] = [vp.add(f"kg{j}", np.tile(inp["na_k_g"][j], 2)) for j in range(nna)]
    return vp, V


RW_WEIGHTS = ["rw_rkv", "rw_w1", "rw_w2", "rw_a1", "rw_a2", "rw_v1", "rw_v2", "rw_g1", "rw_g2", "rw_o"]


def build_rwkv_probe(T, ncols, V, shapes, j, layer):
    B = Builder(T)
    nc = B.nc
    setup_common(B, ncols)
    xT = B.din("xT", [D, T]).ap()
    W = {n: B.din(n, list(shapes[n])).ap() for n in RW_WEIGHTS}
    yT = B.dout("yT", [D, T]).ap()
    S = alloc_rwkv_scratch(B)
    if j > 0:
        vf_in = B.din("vfirst_in", [D, T]).ap()
        S["vfirst"] = vf_in
    rwkv_r1(B, j, layer, xT, W, V, S)
    rwkv_r2(B, S)
    rwkv_r3(B, j, xT, yT, W, V, S)
    with ExitStack() as st:
        B.p.emit(nc, st)
    return B


GRID_W = 64
ROWS_SEG = SEG // GRID_W
NEG = -30000.0


def na_window(i, kind, nseg_sample=4):
    seg = i // ROWS_SEG
    if kind == "S" and seg < nseg_sample:
        rows = nseg_sample * ROWS_SEG
        return int(np.clip(i - 4, 0, rows - 8))
    li = i % ROWS_SEG
    return seg * ROWS_SEG + int(np.clip(li - 4, 0, ROWS_SEG - 8))


def na_slots(T):
    nrows = T // GRID_W
    nseg = T // SEG
    nss = min(4, nseg)
    out = []
    for i in range(nrows):
        lo = min(na_window(i, "P"), na_window(i, "S", nss))
        hi = max(na_window(i, "P"), na_window(i, "S", nss)) + 8
        out.append(list(range(lo // 2, (hi - 1) // 2 + 1)))
    return out


def host_na_nbias(T, kind):
    slots = na_slots(T)
    nss = min(4, T // SEG)
    cols = []
    for i, ms in enumerate(slots):
        lo = na_window(i, kind, nss)
        for m in ms:
            col = np.zeros(128, np.float32)
            for hf in range(2):
                r = 2 * m + hf
                if not (lo <= r < lo + 8):
                    col[hf * 64:(hf + 1) * 64] = NEG
            cols.append(col)
    return np.ascontiguousarray(np.stack(cols, axis=1))


def host_na_bias_table(rpb):
    qc = np.arange(64)
    kc = np.arange(64)
    ws = np.clip(qc - 8, 0, 48)
    cm = (kc[:, None] >= ws[None, :]) & (kc[:, None] < ws[None, :] + 16)
    dc = np.clip(kc[:, None] - qc[None, :] + 15, 0, 30)
    out = np.full((16, 128, 16, 64), NEG, np.float32)
    for e in range(16):
        for hf in range(2):
            dr = e - 8 + hf
            if abs(dr) > 7:
                continue
            g = rpb[:, dr + 7, :][:, dc]
            g = np.where(cm[None], g, np.float32(NEG))
            out[e, hf * 64:(hf + 1) * 64] = np.transpose(g, (1, 0, 2))
    return out


def na_n1(B, jn, layer, xin, W, V, S):
    nc, p, sb = B.nc, B.p, B.sb
    T = B.T
    TT = 512
    NT = T // TT
    vs = B.vecs_sb
    m0 = sb.mark()
    wq = sb.alloc([128, 8, 3 * D], BF16, "wqkv")
    wqB = Buf()
    stg_t = [sb.alloc([128, 4096], F32, "stg") for _ in range(2)]
    stg = Stager(B, stg_t, [Buf(), Buf()])
    src = W["na_qkv"][jn].rearrange("(c p) n -> p c n", p=128)
    for c in range(8):
        for h3 in range(3):
            if h3 < 2:
                stg.load(wq[:, c, h3 * 1024:(h3 + 1) * 1024], src[:, c, h3 * 1024:(h3 + 1) * 1024],
                         lambda t: t[:, 0:1024], wqB)
            else:
                stg.load(wq[:, c, 2048:3072], src[:, c, 2048:3072], lambda t: t[:, 0:1024], wqB)
    xs = [sb.alloc([128, 8, TT], F32, "x") for _ in range(2)]
    xB = [Buf(), Buf()]
    sq = sb.alloc([128, 8, TT], BF16, "sq")
    sqB = Buf()
    hn = sb.alloc([128, 8, TT], BF16, "hn")
    hnB = Buf()
    rs = sb.alloc([128, TT], F32, "rs")
    rstd = sb.alloc([128, TT], F32, "rstd")
    rsB, rstdB = Buf(), Buf()
    epsb = sb.alloc([128, 2], F32, "eps")
    epsB = Buf()
    p.add("pool", lambda e: e.memset(epsb[:, 0:1], RMS_EPS), writes=[epsB])
    p.add("pool", lambda e: e.memset(epsb[:, 1:2], 64 * RMS_EPS), writes=[epsB])
    tq = [sb.alloc([128, TT], F32, "tq") for _ in range(2)]
    tqB = [Buf(), Buf()]
    sqq = [sb.alloc([128, TT], BF16, "sqq") for _ in range(2)]
    sqqB = [Buf(), Buf()]
    rq = [sb.alloc([128, TT], F32, "rq") for _ in range(2)]
    rqB = [Buf(), Buf()]
    qo = [sb.alloc([128, TT], BF16, "qo") for _ in range(2)]
    qoB = [Buf(), Buf()]
    vtl = [sb.alloc([128, D], BF16, "vtl") for _ in range(2)]
    vtlB = [Buf(), Buf()]
    p.barrier()
    xiv = xin.rearrange("(c p) t -> p c t", p=128)
    gc = V["norm_mix_g"][layer]
    k2 = 0
    for t in range(NT):
        t0 = t * TT
        b = t % 2
        x = xs[b]
        B.dma("sp", x[:, :, :], xiv[:, :, t0:t0 + TT], writes=[xB[b]])
        for hI in range(2):
            act(B, sq[:, 4 * hI:4 * hI + 4, :], x[:, 4 * hI:4 * hI + 4, :], AF.Square, [xB[b]], [sqB])
        psA, psAB = bank(B)
        for c in range(8):
            mm(B, psA[:, :], B.ones_bf[:, :], sq[:, c, :], c == 0, c == 7, [B.onesB, sqB], [psAB])
        act(B, rs[:, :], psA[:, :], AF.Sqrt, [psAB, epsB], [rsB], bias=epsb[:, 0:1], scale=1.0 / D)
        p.add("dve", lambda e: e.reciprocal(out=rstd[:, :], in_=rs[:, :]), [rsB], [rstdB])
        for c in range(8):
            stt(B, hn[:, c, :], x[:, c, :], vs[:, gc + c:gc + c + 1], rstd[:, :], ALU.mult, ALU.mult,
                [xB[b], rstdB], [hnB])
        for mI in range(2):
            gcol = V["na_q_g"][jn] if mI == 0 else V["na_k_g"][jn]
            for oc in range(8):
                bi = k2 % 2
                k2 += 1
                ps_, psB_ = bank(B)
                for c in range(8):
                    mm(B, ps_[:, :], wq[:, c, mI * 1024 + oc * 128:mI * 1024 + (oc + 1) * 128], hn[:, c, :],
                       c == 0, c == 7, [hnB], [psB_])
                act(B, tq[bi][:, :], ps_[:, :], AF.Copy, [psB_], [tqB[bi]])
                tt(B, "pool", sqq[bi][:, :], tq[bi][:, :], tq[bi][:, :], ALU.mult, [tqB[bi]], [sqqB[bi]])
                ps2, ps2B = bank(B)
                mm(B, ps2[:, :], B.bones_bf[:, :], sqq[bi][:, :], True, True, [sqqB[bi], B.bonesB], [ps2B])
                if mI == 0:
                    act(B, rq[bi][:, :], ps2[:, :], AF.Sqrt, [ps2B, epsB], [rqB[bi]], bias=epsb[:, 1:2], scale=1.0)
                else:
                    act(B, rq[bi][:, :], ps2[:, :], AF.Sqrt, [ps2B, epsB], [rqB[bi]], bias=epsb[:, 0:1],
                        scale=1.0 / 64)
                p.add("dve", lambda e, bi=bi: e.reciprocal(out=rq[bi][:, :], in_=rq[bi][:, :]), [rqB[bi]], [rqB[bi]])
                stt(B, qo[bi][:, :], tq[bi][:, :], vs[:, gcol:gcol + 1], rq[bi][:, :], ALU.mult, ALU.mult,
                    [tqB[bi], rqB[bi]], [qoB[bi]])
                dst = S["qT"] if mI == 0 else S["kT"]
                B.dma("sp", dst[oc * 128:(oc + 1) * 128, t0:t0 + TT], qo[bi][:, :], reads=[qoB[bi]])
        for tb in range(4):
            bi = tb % 2
            for hf in range(2):
                ps_, psB_ = bank(B)
                for c in range(8):
                    mm(B, ps_[:, :], hn[:, c, tb * 128:(tb + 1) * 128], wq[:, c, 2048 + hf * 512:2048 + (hf + 1) * 512],
                       c == 0, c == 7, [hnB], [psB_])
                if hf == 0:
                    act(B, vtl[bi][:, 0:512], ps_[:, :], AF.Copy, [psB_], [vtlB[bi]])
                else:
                    p.add("dve", lambda e, ps_=ps_, bi=bi: e.tensor_copy(out=vtl[bi][:, 512:1024], in_=ps_[:, :]),
                          [psB_], [vtlB[bi]])
            B.dma("sp", S["vtokb"][t0 + tb * 128:t0 + (tb + 1) * 128, :], vtl[bi][:, :], reads=[vtlB[bi]])
    p.barrier()
    sb.release(m0)


def na_n2(B, jn, xin, xout, W, S, nbias_dram, btab_dram, dbg=9):
    nc, p, sb = B.nc, B.p, B.sb
    T = B.T
    TT = 512
    NT = T // TT
    nrows = T // GRID_W
    slots = na_slots(T)
    nslot_tot = sum(len(s) for s in slots)
    m0 = sb.mark()
    cst = B.consts_sb
    ident = cst[:, B.C["ident"]:B.C["ident"] + 128]
    wo = sb.alloc([128, 8, D], BF16, "wo")
    woB = Buf()
    btab = sb.alloc([128, 16, 16, 64], BF16, "btab")
    btB = Buf()
    nb = sb.alloc([128, nslot_tot], F32, "nbias")
    m1 = sb.mark()
    stg_t = [sb.alloc([128, 4096], F32, "stg") for _ in range(2)]
    stgB = [Buf(), Buf()]
    stg = Stager(B, stg_t, stgB)
    src = W["na_o"][jn].rearrange("(c p) n -> p c n", p=128)
    for hI in range(2):
        stg.load(wo[:, 4 * hI:4 * hI + 4, :], src[:, 4 * hI:4 * hI + 4, :],
                 lambda t: t[:, :].rearrange("p (c n) -> p c n", n=D), woB)
    for e in range(16):
        stg.load(btab[:, e, :, :], btab_dram[e].rearrange("p (h q) -> p h q", q=64),
                 lambda t: t[:, 0:1024].rearrange("p (h q) -> p h q", q=64), btB)
    B.dma("sp", nb[:, :], nbias_dram[:, :], writes=[Buf()])
    p.barrier()
    sb.release(m1)
    NKR = 24
    KT = sb.alloc([128, 8, NKR * 64], BF16, "KT")
    KTB = Buf()
    QT = sb.alloc([128, 8, TT], BF16, "QT")
    QTB = Buf()
    Vraw = sb.alloc([128, NKR // 2, D], BF16, "Vraw")
    VrawB = Buf()
    Vaug = sb.alloc([128, NKR // 2, 16, 68], BF16, "Vaug")
    VaugB = Buf()
    p.add("pool", lambda e: e.memset(Vaug[:, :, :, 64:68], 0.0), writes=[VaugB])
    p.add("pool", lambda e: e.memset(Vaug[:, :, :, 64:65], 1.0), writes=[VaugB])
    NPT = 6
    NCH = 2
    PTs = [[sb.alloc([128, 16, 64], BF16, f"PT{i}_{k}") for i in range(NPT)] for k in range(NCH)]
    PTBs = [[Buf() for _ in range(NPT)] for k in range(NCH)]
    tmps = [[sb.alloc([128, 8, 64], F32, f"tmp{k}") for _ in range(2)] for k in range(NCH)]
    tmpBs = [[Buf(), Buf()] for k in range(NCH)]
    rcs = [sb.alloc([64, 16], F32, f"rc{k}") for k in range(NCH)]
    rcBs = [Buf() for k in range(NCH)]
    os_ = [sb.alloc([64, 16, 64], F32, f"o{k}") for k in range(NCH)]
    oBs = [Buf() for k in range(NCH)]
    oT = sb.alloc([128, 8, TT], BF16, "oT")
    oTB = [Buf() for _ in range(8)]
    xt = sb.alloc([128, 8, TT], F32, "xt")
    xtB = Buf()
    p.barrier()
    xiv = xin.rearrange("(c p) t -> p c t", p=128)
    xov = xout.rearrange("(c p) t -> p c t", p=128)
    qv = S["qT"].rearrange("(c p) t -> p c t", p=128)
    kv = S["kT"].rearrange("(c p) t -> p c t", p=128)
    scol_base = np.concatenate([[0], np.cumsum([len(s_) for s_ in slots])]).astype(int)
    for t in range(NT):
        t0 = t * TT
        i0 = t0 // GRID_W
        klo = max(0, i0 - 8)
        khi = min(nrows, i0 + 16)
        nk = khi - klo
        B.dma("sp", xt[:, :, :], xiv[:, :, t0:t0 + TT], writes=[xtB])
        B.dma("sp", QT[:, :, :], qv[:, :, t0:t0 + TT], writes=[QTB])
        B.dma("sp", KT[:, :, 0:nk * 64], kv[:, :, klo * 64:khi * 64], writes=[KTB])
        B.dma("sp", Vraw[:, 0:nk // 2, :], S["vtokb"][klo * 64:khi * 64, :].rearrange("(m p) c -> p m c", p=128),
              writes=[VrawB])
        p.add("pool", lambda e, nk=nk: e.tensor_copy(
            out=Vaug[:, 0:nk // 2, :, 0:64], in_=Vraw[:, 0:nk // 2, :].rearrange("p m (h v) -> p m h v", v=64)),
            [VrawB], [VaugB])
        def row_unit(rr, PT, PTB, tmp, tmpB, rc, rcB, o, oB):
            i = i0 + rr
            ms = slots[i]
            scol = scol_base[i]
            k2 = 0
            assert len(ms) <= NPT
            for si, m in enumerate(ms if dbg >= 2 else []):
                pl = m - klo // 2
                e_ = 2 * m - i + 8
                assert 0 <= pl < nk // 2 and 0 <= e_ < 16, (i, m, pl, e_)
                pss = [bank(B), bank(B)]
                for h in range(16):
                    hs = slice((h % 2) * 64, (h % 2) * 64 + 64)
                    mm(B, pss[h % 2][0][:, (h // 2) * 64:(h // 2 + 1) * 64], KT[hs, h // 2, pl * 128:(pl + 1) * 128],
                       QT[hs, h // 2, rr * 64:(rr + 1) * 64], True, True, [KTB, QTB], [pss[h % 2][1]])
                for g in range(2):
                    ps_, psB_ = pss[g]
                    tb_ = k2 % 2
                    k2 += 1
                    import os
                    sub = int(os.environ.get("NA_SUB", "9"))
                    if sub >= 2:
                        tt(B, "dve", tmp[tb_][:, :, :], ps_[:, :].rearrange("p (h q) -> p h q", q=64),
                           btab[:, e_, g:16:2, :], ALU.add, [psB_, btB], [tmpB[tb_]])
                    if sub >= 3:
                        act(B, PT[si][:, g:16:2, :], tmp[tb_][:, :, :], AF.Exp, [tmpB[tb_]], [PTB[si]],
                            bias=nb[:, scol:scol + 1], scale=1.0)
                scol += 1
                yield
            pvb = [bank(B) for _ in range(4)]
            for h in range(16 if dbg >= 3 else 0):
                ps_, psB_ = pvb[h // 4]
                for si, m in enumerate(ms):
                    pl = m - klo // 2
                    mm(B, ps_[0:64, (h % 4) * 128:(h % 4) * 128 + 66], PT[si][:, h, :], Vaug[:, pl, h, 0:66],
                       si == 0, si == len(ms) - 1, [PTB[si], VaugB], [psB_])
            yield
            for b4 in range(4 if dbg >= 4 else 0):
                ps_, psB_ = pvb[b4]
                pv3 = ps_[0:64, :].rearrange("p (h c) -> p h c", c=128)
                p.add("dve", lambda e, pv3=pv3, b4=b4: e.reciprocal(out=rc[:, b4 * 4:(b4 + 1) * 4], in_=pv3[:, :, 64]),
                      [psB_], [rcB])
                tt(B, "dve", o[:, b4 * 4:(b4 + 1) * 4, :], pv3[:, :, 0:64],
                   rc[:, b4 * 4:(b4 + 1) * 4].unsqueeze(2).to_broadcast([64, 4, 64]), ALU.mult, [psB_, rcB], [oB])
            yield
            of = o[:, :, :].rearrange("p h v -> p (h v)")
            ps_, psB_ = bank(B)
            for oc in range(8 if dbg >= 5 else 0):
                p.add("pe", lambda e, ps_=ps_, oc=oc: e.transpose(ps_[:, oc * 64:(oc + 1) * 64],
                                                                 of[:, oc * 128:(oc + 1) * 128], ident[0:64, 0:64]),
                      [oB], [psB_])
            if dbg >= 5:
                act(B, oT[:, :, rr * 64:(rr + 1) * 64], ps_[:, :].rearrange("p (c q) -> p c q", q=64), AF.Copy,
                    [psB_], [oTB[rr]])
        for rr0 in range(0, 8, NCH):
            gens = [row_unit(rr0 + k_, PTs[k_], PTBs[k_], tmps[k_], tmpBs[k_], rcs[k_], rcBs[k_], os_[k_], oBs[k_])
                    for k_ in range(NCH)]
            live = [True] * NCH
            while any(live):
                for gi, g_ in enumerate(gens):
                    if live[gi]:
                        try:
                            next(g_)
                        except StopIteration:
                            live[gi] = False
        for oc in range(8 if dbg >= 6 else 0):
            ps_, psB_ = bank(B)
            for c in range(8):
                mm(B, ps_[:, :], wo[:, c, oc * 128:(oc + 1) * 128], oT[:, c, :], c == 0, c == 7, [woB] + oTB, [psB_])
            tt(B, "dve", xt[:, oc, :], xt[:, oc, :], ps_[:, :], ALU.add, [xtB, psB_], [xtB])
        B.dma("pool", xov[:, :, t0:t0 + TT], xt[:, :, :], reads=[xtB])
    p.barrier()
    sb.release(m0)


def alloc_na_scratch(B):
    T = B.T
    S = {"qT": B.dscr("s_qT", [D, T], BF16).ap(), "kT": B.dscr("s_kT", [D, T], BF16).ap(),
         "vtokb": B.dscr("s_vtokb", [T, D], BF16).ap()}
    return S


def build_na_probe(T, ncols, V, shapes, jn, layer, mode="full"):
    B = Builder(T)
    nc = B.nc
    setup_common(B, ncols)
    xT = B.din("xT", [D, T]).ap()
    W = {n: B.din(n, list(shapes[n])).ap() for n in ("na_qkv", "na_o")}
    nslot_tot = sum(len(s) for s in na_slots(T))
    nbias = B.din("nbias", [128, nslot_tot]).ap()
    btab = B.din("btab", [16, 128, 1024]).ap()
    yT = B.dout("yT", [D, T]).ap()
    S = alloc_na_scratch(B)
    na_n1(B, jn, layer, xT, W, V, S)
    if mode == "n1":
        B.dma("sp", yT[:, :], xT[:, :])
        B.p.barrier()
    else:
        na_n2(B, jn, xT, yT, W, S, nbias, btab, dbg=int(mode) if mode.isdigit() else 9)
    with ExitStack() as st:
        B.p.emit(nc, st)
    return B


T_CORE = NSEG * SEG
DEPTH = 4
W_NAMES = ["w_up", "w_down", "rw_rkv", "rw_w1", "rw_w2", "rw_a1", "rw_a2", "rw_v1", "rw_v2", "rw_g1", "rw_g2",
           "rw_o", "na_qkv", "na_o"]
_CACHE = {}


def build_full(ncols, V, shapes, T=None, depth=DEPTH, skip_last_mlp=False):
    T = T or T_CORE
    B = Builder(T)
    nc = B.nc
    setup_common(B, ncols)
    xT = B.din("xT", [D, T]).ap()
    W = {n: B.din(n, list(shapes[n])).ap() for n in W_NAMES}
    nslot_tot = sum(len(s) for s in na_slots(T))
    nbias = B.din("nbias", [128, nslot_tot]).ap()
    btabs = [B.din(f"btab{j}", [16, 128, 1024]).ap() for j in range(2)]
    yT = B.dout("yT", [D, T]).ap()
    xA = B.dscr("xA", [D, T]).ap()
    xB = B.dscr("xB", [D, T]).ap()
    SR = alloc_rwkv_scratch(B)
    SN = alloc_na_scratch(B)
    cur = xT
    for layer in range(depth):
        j = layer // 2
        lastl = (layer == depth - 1)
        mdst = yT if (lastl and skip_last_mlp) else xA
        if layer % 2 == 0:
            rwkv_r1(B, j, layer, cur, W, V, SR)
            rwkv_r2(B, SR)
            rwkv_r3(B, j, cur, mdst, W, V, SR)
        else:
            na_n1(B, j, layer, cur, W, V, SN)
            na_n2(B, j, cur, mdst, W, SN, nbias, btabs[j])
        if lastl and skip_last_mlp:
            break
        dst = yT if lastl else xB
        mlp_stage(B, xA, dst, W["w_up"][layer], W["w_down"][layer], B.vecs_sb, V["norm_mlp_g"][layer])
        cur = xB
    with ExitStack() as st:
        B.p.emit(nc, st)
    return B


def kernel(**inputs):
    inp = {k: np.asarray(v) for k, v in inputs.items()}
    xp = inp["x_prompt"]
    xs = inp["x_sample"]
    vp, V = pack_vectors(inp)
    vecs = vp.array()
    shapes = {n: inp[n].shape for n in W_NAMES}
    key = (vp.n,)
    if key not in _CACHE:
        _CACHE[key] = build_full(vp.n, V, shapes)
    B = _CACHE[key]
    consts = host_consts()
    blkm = host_blk_masks()
    btabs = [np.ascontiguousarray(host_na_bias_table(inp["na_rpb"][j]).reshape(16, 128, 1024)) for j in range(2)]
    nb = {"S": host_na_nbias(T_CORE, "S"), "P": host_na_nbias(T_CORE, "P")}
    prompt_ids = []
    in_maps = []
    for c in range(NCORES):
        if c < 4:
            ids = [2 * c, 2 * c + 1]
            xc = np.concatenate([xs[c], xp[ids[0]], xp[ids[1]]], axis=0)
        else:
            ids = list(range(8 + 6 * (c - 4), 8 + 6 * (c - 4) + 6))
            xc = np.concatenate([xp[i] for i in ids], axis=0)
        prompt_ids.append(ids)
        fl = np.zeros((128, 8), np.float32)
        if c < 4:
            fl[:, 1:4] = 1.0
        m = {"xT": np.ascontiguousarray(xc.T), "vecs": vecs, "consts": consts, "flags": fl, "blkm": blkm,
             "nbias": nb["S" if c < 4 else "P"], "btab0": btabs[0], "btab1": btabs[1]}
        for n in W_NAMES:
            m[n] = inp[n]
        in_maps.append(m)
    res = run_bass_kernel_spmd(B.nc, in_maps, core_ids=list(range(NCORES)))
    y_prompt = np.empty_like(xp)
    y_sample = np.empty_like(xs)
    for c in range(NCORES):
        y = res.results[c]["yT"].T
        if c < 4:
            y_sample[c] = y[0:8192]
            for k, i in enumerate(prompt_ids[c]):
                y_prompt[i] = y[8192 + k * SEG:8192 + (k + 1) * SEG]
        else:
            for k, i in enumerate(prompt_ids[c]):
                y_prompt[i] = y[k * SEG:(k + 1) * SEG]
    return (y_prompt, y_sample)
```

```python
from contextlib import ExitStack
import numpy as np
import concourse.bass as bass
import concourse.mybir as mybir
from concourse.bass_utils import run_bass_kernel_spmd

F32 = mybir.dt.float32
BF16 = mybir.dt.bfloat16
ALU = mybir.AluOpType
AF = mybir.ActivationFunctionType
AX = mybir.AxisListType

D = 1024
NCH = 8
DFF = 4096
NSEG = 6
SEG = 2048
NCORES = 8
RMS_EPS = 1e-6

NSLOT = 12
EPOCH = 15000
NEPOCH = 16


class Buf:
    __slots__ = ("w", "r", "parent", "kids")

    def __init__(self, parent=None):
        self.w = None
        self.r = []
        self.parent = parent
        self.kids = []
        if parent is not None:
            parent.kids.append(self)


class Op:
    __slots__ = ("eng", "seq", "fn", "waits", "dma", "sigidx", "slot", "slotval", "slotprev")


class Prog:
    ENGS = ("pe", "act", "dve", "pool", "sp")
    COMPUTE = ("pe", "act", "dve", "pool")

    def __init__(self):
        self.ops = {e: [] for e in self.ENGS}
        self.known = {f: {e: -1 for e in self.ENGS} for f in self.ENGS}
        self.known_dma = {f: set() for f in self.ENGS}
        self.last_compute = {e: -1 for e in self.ENGS}
        self.slot_uses = {q: [0] * NSLOT for q in ("sp", "pool", "act")}
        self.slot_last = {q: [None] * NSLOT for q in ("sp", "pool", "act")}
        self.dma_n = {q: 0 for q in ("sp", "pool", "act")}
        self.fence = {e: -1 for e in self.ENGS}

    def add(self, eng, fn, reads=(), writes=(), dma=False):
        ops = self.ops[eng]
        seq = len(ops)
        deps = set()
        for b in reads:
            if b.w is not None:
                deps.add(b.w)
            if b.parent is not None and b.parent.w is not None:
                deps.add(b.parent.w)
            for kb in b.kids:
                if kb.w is not None:
                    deps.add(kb.w)
        for b in writes:
            if b.w is not None:
                deps.add(b.w)
            deps.update(b.r)
            if b.parent is not None:
                if b.parent.w is not None:
                    deps.add(b.parent.w)
                deps.update(b.parent.r)
            for kb in b.kids:
                if kb.w is not None:
                    deps.add(kb.w)
                deps.update(kb.r)
        waits = []
        kn = self.known[eng]
        kd = self.known_dma[eng]
        for d in sorted(deps):
            E, s, isdma = d
            if s <= self.fence[E]:
                continue
            if isdma:
                if (E, s) in kd:
                    continue
                kd.add((E, s))
                waits.append(d)
            else:
                if E == eng and not dma:
                    if eng == "pe":
                        continue
                    if seq - s > 3:
                        continue
                if kn[E] >= s:
                    continue
                kn[E] = s
                waits.append(d)
        op = Op()
        op.eng, op.seq, op.fn, op.waits, op.dma, op.sigidx = eng, seq, fn, waits, dma, 0
        op.slot = op.slotval = op.slotprev = None
        if dma:
            n = self.dma_n[eng]
            self.dma_n[eng] = n + 1
            sl = n % NSLOT
            op.slot = sl
            op.slotprev = self.slot_last[eng][sl]
            self.slot_uses[eng][sl] += 1
            op.slotval = 16 * self.slot_uses[eng][sl]
            self.slot_last[eng][sl] = (eng, seq, True)
            if op.slotprev is not None:
                kd.add(op.slotprev[:2])
        else:
            self.last_compute[eng] = seq
        tok = (eng, seq, dma)
        for b in reads:
            b.r.append(tok)
        for b in writes:
            b.w = tok
            b.r = []
        ops.append(op)
        return op

    def barrier(self):
        lasts = dict(self.last_compute)
        dmas = []
        for q in self.slot_last:
            for t in self.slot_last[q]:
                if t is not None:
                    dmas.append(t)
        for F in self.ENGS:
            waits = []
            for E in self.COMPUTE:
                if E != F and lasts[E] >= 0 and self.known[F][E] < lasts[E]:
                    waits.append((E, lasts[E], False))
                    self.known[F][E] = lasts[E]
            for t in dmas:
                if t[:2] not in self.known_dma[F]:
                    self.known_dma[F].add(t[:2])
                    waits.append(t)
            op = Op()
            op.eng, op.seq, op.fn, op.waits, op.dma, op.sigidx = F, len(self.ops[F]), None, waits, False, 0
            op.slot = op.slotval = op.slotprev = None
            self.ops[F].append(op)
        for E in self.ENGS:
            self.fence[E] = len(self.ops[E]) - 1

    def emit(self, nc, st):
        sig = {e: set() for e in self.ENGS}
        for F in self.ENGS:
            for op in self.ops[F]:
                for (E, s, isdma) in op.waits:
                    if not isdma:
                        sig[E].add(s)
        nsig = {}
        for E in self.COMPUTE:
            c = 0
            for op in self.ops[E]:
                if op.seq in sig[E]:
                    c += 1
                    op.sigidx = c
            nsig[E] = c
            assert c <= EPOCH * NEPOCH, (E, c)
        csem = {E: [st.enter_context(nc.semaphore(f"c_{E}_{k}")) for k in range((nsig[E] + EPOCH - 1) // EPOCH)]
                for E in self.COMPUTE}
        dsem = {q: [st.enter_context(nc.semaphore(f"d_{q}_{k}")) for k in range(NSLOT)]
                for q in self.slot_last if self.dma_n[q] > 0}
        allops = self.ops

        def run(F, eng):
            for op in allops[F]:
                for (E, s, isdma) in op.waits:
                    t = allops[E][s]
                    if isdma:
                        eng.wait_ge(dsem[E][t.slot], t.slotval)
                    else:
                        i = t.sigidx - 1
                        eng.wait_ge(csem[E][i // EPOCH], i % EPOCH + 1)
                if op.dma and op.slotprev is not None:
                    t = allops[op.slotprev[0]][op.slotprev[1]]
                    eng.wait_ge(dsem[F][t.slot], t.slotval)
                if op.fn is None:
                    continue
                ins = op.fn(eng)
                if op.dma:
                    ins.then_inc(dsem[F][op.slot], 16)
                elif op.sigidx:
                    i = op.sigidx - 1
                    ins.then_inc(csem[F][i // EPOCH], 1)

        block = st.enter_context(nc.Block())

        @block.tensor
        def _(eng):
            run("pe", eng)

        @block.scalar
        def _(eng):
            run("act", eng)

        @block.vector
        def _(eng):
            run("dve", eng)

        @block.gpsimd
        def _(eng):
            run("pool", eng)

        @block.sync
        def _(eng):
            run("sp", eng)


class SB:
    def __init__(self, nc):
        self.nc = nc
        self.base = nc.sbuf_base + 64
        self.top = nc.sbuf_top
        self.ptr = self.base
        self.n = 0

    def alloc(self, shape, dtype, name="t"):
        esz = 2 if dtype == BF16 else 4
        per = esz
        for s in shape[1:]:
            per *= s
        off = (self.ptr + 63) // 64 * 64
        assert off + per <= self.top, (name, off, per, self.top)
        self.ptr = off + per
        self.n += 1
        return self.nc.alloc_sbuf_tensor_at(f"{name}_{self.n}", list(shape), dtype, offset=off)

    def mark(self):
        return self.ptr

    def release(self, m):
        self.ptr = m


class Builder:
    def __init__(self, T):
        self.T = T
        self.nc = bass.Bass("TRN2", target_bir_lowering=False)
        self.p = Prog()
        self.sb = SB(self.nc)
        nc = self.nc
        self.ps = [nc.alloc_psum_tensor(f"psb{i}", [128, 512], F32) for i in range(8)]
        self.psB = [Buf() for _ in range(8)]
        self.ones_bf = self.sb.alloc([128, 128], BF16, "ones")
        self.onesB = Buf()
        self.p.add("pool", lambda e: e.memset(self.ones_bf[:, :], 1.0), writes=[self.onesB])
        self.perm_mark = None

    def din(self, name, shape, dtype=F32):
        return self.nc.dram_tensor(name, list(shape), dtype, kind="ExternalInput")

    def dout(self, name, shape, dtype=F32):
        return self.nc.dram_tensor(name, list(shape), dtype, kind="ExternalOutput")

    def dscr(self, name, shape, dtype=F32):
        return self.nc.dram_tensor(name, list(shape), dtype)

    def dma(self, q, out, in_, reads=(), writes=(), slow=False):
        if slow:
            self.p.add(q, lambda e, o=out, i=in_: e.dma_start(out=o, in_=i, allow_slow_non_contiguous=True),
                       reads, writes, dma=True)
        else:
            self.p.add(q, lambda e, o=out, i=in_: e.dma_start(out=o, in_=i), reads, writes, dma=True)

    def cast(self, k, out, in_, reads, writes):
        eng = ("dve", "pool", "act")[k % 3]
        if eng == "act":
            self.p.add("act", lambda e, o=out, i=in_: e.activation(out=o, in_=i, func=AF.Copy), reads, writes)
        else:
            self.p.add(eng, lambda e, o=out, i=in_: e.tensor_copy(out=o, in_=i), reads, writes)


def mlp_stage(B, xin, xout, w_up, w_down, vecs_sb, gcol):
    nc, p, sb = B.nc, B.p, B.sb
    T = B.T
    TT = 512
    NT = T // TT
    m = sb.mark()
    wup = sb.alloc([128, 8, DFF], BF16, "wup")
    wdn = sb.alloc([128, 32, D], BF16, "wdn")
    wupB = [Buf() for _ in range(8)]
    wdnB = [Buf() for _ in range(8)]
    xs = [sb.alloc([128, 8, TT], F32, "x") for _ in range(2)]
    xoff = []
    xB = [Buf() for _ in range(2)]
    hn = sb.alloc([128, 8, TT], BF16, "hn")
    hnB = Buf()
    a = sb.alloc([128, 16, TT], BF16, "a")
    aB = [Buf() for _ in range(16)]
    r = [sb.alloc([128, TT], F32, "r") for _ in range(2)]
    rB = [Buf() for _ in range(2)]
    rs = sb.alloc([128, TT], F32, "rs")
    rsB = Buf()
    rstd = sb.alloc([128, TT], F32, "rstd")
    rstdB = Buf()
    epsb = sb.alloc([128, 1], F32, "eps")
    epsB = Buf()
    p.add("pool", lambda e: e.memset(epsb[:, :], RMS_EPS), writes=[epsB])
    stg = [xs[0], xs[1]]
    k = 0
    for kc in range(8):
        s = stg[k % 2]
        sv = s[:, :, :].rearrange("p a b -> p (a b)")
        B.dma("sp", sv, w_up[kc * 128:(kc + 1) * 128, :], writes=[xB[k % 2]])
        for h in range(2):
            B.cast(2 * k + h, wup[:, kc, h * 2048:(h + 1) * 2048], sv[:, h * 2048:(h + 1) * 2048],
                   [xB[k % 2]], [wupB[kc]])
        k += 1
    wdv = w_down.rearrange("(c p) n -> p c n", p=128)
    for g4 in range(8):
        s = stg[k % 2]
        sv = s[:, :, :].rearrange("p a b -> p (a b)").rearrange("p (g c) -> p g c", c=D)
        B.dma("sp", sv, wdv[:, g4 * 4:(g4 + 1) * 4, :], writes=[xB[k % 2]])
        for h in range(2):
            B.cast(2 * k + h, wdn[:, g4 * 4 + 2 * h:g4 * 4 + 2 * h + 2, :], sv[:, 2 * h:2 * h + 2, :],
                   [xB[k % 2]], [wdnB[g4]])
        k += 1
    xiv = xin.rearrange("(c p) t -> p c t", p=128)
    xov = xout.rearrange("(c p) t -> p c t", p=128)
    PS_SS, PS_UP, PS_DN = 0, (1, 2), (3, 4, 5, 6)
    dn_i = 0
    for t in range(NT):
        t0 = t * TT
        b = t % 2
        x = xs[b]
        B.dma("sp", x[:, :, :], xiv[:, :, t0:t0 + TT], writes=[xB[b]])
        sq = a
        for h in range(2):
            p.add("act", lambda e, o=sq[:, 4 * h:4 * h + 4, :], i=x[:, 4 * h:4 * h + 4, :]:
                  e.activation(out=o, in_=i, func=AF.Square), [xB[b]], aB[4 * h:4 * h + 4])
        for c in range(8):
            p.add("pe", lambda e, c=c: e.matmul(B.ps[PS_SS][:, :], B.ones_bf[:, :], sq[:, c, :],
                                                 start=(c == 0), stop=(c == 7)),
                  [B.onesB, aB[c]], [B.psB[PS_SS]])
        p.add("act", lambda e: e.activation(out=rs[:, :], in_=B.ps[PS_SS][:, :], func=AF.Sqrt,
                                            bias=epsb[:, 0:1], scale=1.0 / D),
              [B.psB[PS_SS], epsB], [rsB])
        p.add("dve", lambda e: e.reciprocal(out=rstd[:, :], in_=rs[:, :]), [rsB], [rstdB])
        for c in range(8):
            p.add("dve", lambda e, c=c, x=x: e.scalar_tensor_tensor(
                out=hn[:, c, :], in0=x[:, c, :], scalar=vecs_sb[:, gcol + c:gcol + c + 1], in1=rstd[:, :],
                op0=ALU.mult, op1=ALU.mult), [xB[b], rstdB], [hnB])
        for half in range(2):
            for jj in range(16):
                j = half * 16 + jj
                pu = PS_UP[j % 2]
                for kc in range(8):
                    p.add("pe", lambda e, kc=kc, j=j, pu=pu: e.matmul(
                        B.ps[pu][:, :], wup[:, kc, j * 128:(j + 1) * 128], hn[:, kc, :],
                        start=(kc == 0), stop=(kc == 7)), [wupB[kc], hnB], [B.psB[pu]])
                p.add("act", lambda e, j=j, pu=pu: e.activation(out=r[j % 2][:, :], in_=B.ps[pu][:, :],
                                                                 func=AF.Relu),
                      [B.psB[pu]], [rB[j % 2]])
                p.add("pool", lambda e, j=j, jj=jj: e.tensor_tensor(out=a[:, jj, :], in0=r[j % 2][:, :],
                                                                     in1=r[j % 2][:, :], op=ALU.mult),
                      [rB[j % 2]], [aB[jj]])
            for o in range(8):
                pd = PS_DN[dn_i % 4]
                dn_i += 1
                for jj in range(16):
                    j = half * 16 + jj
                    p.add("pe", lambda e, o=o, j=j, jj=jj, pd=pd: e.matmul(
                        B.ps[pd][:, :], wdn[:, j, o * 128:(o + 1) * 128], a[:, jj, :],
                        start=(jj == 0), stop=(jj == 15)), [wdnB[j // 4], aB[jj]], [B.psB[pd]])
                p.add("dve", lambda e, o=o, pd=pd, x=x: e.tensor_tensor(
                    out=x[:, o, :], in0=x[:, o, :], in1=B.ps[pd][:, :], op=ALU.add),
                    [xB[b], B.psB[pd]], [xB[b]])
        B.dma("pool", xov[:, :, t0:t0 + TT], x[:, :, :], reads=[xB[b]])
    p.barrier()
    sb.release(m)


class VecPack:
    def __init__(self):
        self.cols = {}
        self.n = 0
        self.data = []

    def add(self, name, v):
        v = np.asarray(v, np.float32).reshape(-1)
        assert v.size % 128 == 0
        nc_ = v.size // 128
        self.cols[name] = self.n
        self.n += nc_
        self.data.append(np.ascontiguousarray(v.reshape(nc_, 128).T))
        return self.cols[name]

    def array(self):
        return np.ascontiguousarray(np.concatenate(self.data, axis=1))


def build_mlp_only(T, ncols, gcol):
    B = Builder(T)
    nc = B.nc
    xT = B.din("xT", [D, T]).ap()
    vecs = B.din("vecs", [128, ncols]).ap()
    w_up = B.din("w_up", [D, DFF]).ap()
    w_down = B.din("w_down", [DFF, D]).ap()
    yT = B.dout("yT", [D, T]).ap()
    vecs_sb = B.sb.alloc([128, ncols], F32, "vecs")
    vB = Buf()
    B.dma("sp", vecs_sb[:, :], vecs[:, :], writes=[vB])
    B.p.barrier()
    mlp_stage(B, xT, yT, w_up, w_down, vecs_sb, gcol)
    with ExitStack() as st:
        B.p.emit(nc, st)
    return B


class Stager:
    def __init__(self, B, tiles, bufs):
        self.B, self.tiles, self.bufs, self.k = B, tiles, bufs, 0

    def load(self, dst_ap, src_ap, stage_view, dstB):
        i = self.k % len(self.tiles)
        sv = stage_view(self.tiles[i])
        self.B.dma("sp", sv, src_ap, writes=[self.bufs[i]])
        self.B.cast(self.k, dst_ap, sv, [self.bufs[i]], [dstB])
        self.k += 1


def bank(B):
    i = B.bank_i % 8
    B.bank_i += 1
    return B.ps[i], B.psB[i]


def act(B, out, in_, func, reads, writes, bias=None, scale=None):
    kw = {}
    if bias is not None:
        kw["bias"] = bias
    if scale is not None:
        kw["scale"] = scale
    B.p.add("act", lambda e: e.activation(out=out, in_=in_, func=func, **kw), reads, writes)


def tt(B, eng, out, in0, in1, op, reads, writes):
    B.p.add(eng, lambda e: e.tensor_tensor(out=out, in0=in0, in1=in1, op=op), reads, writes)


def ts(B, eng, out, in0, s1, s2, op0, op1, reads, writes):
    if op1 is None:
        B.p.add(eng, lambda e: e.tensor_scalar(out=out, in0=in0, scalar1=s1, scalar2=None, op0=op0), reads, writes)
    else:
        B.p.add(eng, lambda e: e.tensor_scalar(out=out, in0=in0, scalar1=s1, scalar2=s2, op0=op0, op1=op1),
                reads, writes)


def stt(B, out, in0, scalar, in1, op0, op1, reads, writes):
    B.p.add("dve", lambda e: e.scalar_tensor_tensor(out=out, in0=in0, scalar=scalar, in1=in1, op0=op0, op1=op1),
            reads, writes)


def mm(B, out, lhsT, rhs, start, stop, reads, writes):
    B.p.add("pe", lambda e: e.matmul(out, lhsT, rhs, start=start, stop=stop), reads, writes)


def rwkv_r1(B, j, layer, xin, W, V, S):
    nc, p, sb = B.nc, B.p, B.sb
    T = B.T
    TT = 512
    NT = T // TT
    vs = B.vecs_sb
    has_vres = j > 0
    m0 = sb.mark()
    wrkv = [sb.alloc([128, 8, D], BF16, f"w{n}") for n in "rkv"]
    wrkvB = [[Buf() for _ in range(2)] for _ in range(3)]
    w1c = sb.alloc([128, 8, 128], BF16, "w1c")
    a1c = sb.alloc([128, 8, 128], BF16, "a1c")
    g1a = sb.alloc([128, 8, 128], BF16, "g1a")
    g1b = sb.alloc([128, 8, 32], BF16, "g1b")
    w2c = sb.alloc([128, D], BF16, "w2c")
    a2c = sb.alloc([128, D], BF16, "a2c")
    g2a = sb.alloc([128, D], BF16, "g2a")
    g2b = sb.alloc([32, D], BF16, "g2b")
    if has_vres:
        v1s = sb.alloc([128, 8, 32], BF16, "v1s")
        v2s = sb.alloc([32, D], BF16, "v2s")
    wsB = Buf()
    B.sb_off = {}
    B.sb_off["xh"] = (sb.ptr + 63) // 64 * 64
    xh = sb.alloc([128, 8, TT + 2], F32, "xh")
    xhB = Buf()
    B.sb_off["xx"] = (sb.ptr + 63) // 64 * 64
    xx = sb.alloc([128, 8, TT], F32, "xx")
    xxB = Buf()
    rs = sb.alloc([128, TT + 2], F32, "rs")
    rsB = Buf()
    rstd = sb.alloc([128, TT + 2], F32, "rstd")
    rstdB = Buf()
    epsb = sb.alloc([128, 1], F32, "eps")
    epsB = Buf()
    p.add("pool", lambda e: e.memset(epsb[:, :], RMS_EPS), writes=[epsB])
    xm_off = (sb.ptr + 63) // 64 * 64
    xm = [sb.alloc([128, 8, TT], BF16, f"xm{i}") for i in range(6)]
    xmB = [Buf() for _ in range(6)]
    sq = nc.alloc_sbuf_tensor_at(f"r1sq{j}", [128, 8, TT + 2], BF16, offset=xm_off + 4 * 8192)
    sqB = Buf()
    SQW = [sqB, xmB[4], xmB[5]]
    stg_t = [nc.alloc_sbuf_tensor_at(f"r1stg{j}_{i}", [128, 4096], F32, offset=xm_off + i * 16384) for i in range(2)]
    stgB = [Buf(), Buf()]
    stg = Stager(B, stg_t, stgB)
    names = ["tw", "ta", "tg0", "tg1", "tv"]
    lo = {n: sb.alloc([128, TT], BF16, n) for n in names}
    loB = {n: Buf() for n in names}
    tmpn = ["r", "k", "v", "g", "kk", "rn", "lw0", "lw1", "ic0", "ic1", "t1", "kd0", "kd1", "b0", "b1", "bv",
            "vg", "vf"]
    TMs = [{n: sb.alloc([128, TT], F32, n) for n in tmpn}]
    TMBs = [{n: Buf() for n in tmpn}]
    SQKs = [sb.alloc([128, TT], BF16, "sqk")]
    RKs = [sb.alloc([128, TT], BF16, "rk")]
    VTs = [sb.alloc([128, 4, 128], F32, "vt")]
    SQKBs, RKBs, VTBs = [Buf()], [Buf()], [Buf()]
    xh_off = B.sb_off["xh"]
    xx_off = B.sb_off["xx"]
    slots = [(xh_off + i * 2048, xhB) for i in range(8)] + [(xx_off + i * 2048, xxB) for i in range(8)]
    tm1, tmB1 = {}, {}
    si_ = 0
    for n in tmpn:
        if si_ < 16 and n not in ("t1", "rn"):
            off_, par_ = slots[si_]
            si_ += 1
            tm1[n] = nc.alloc_sbuf_tensor_at(f"r1t1{j}_{n}", [128, TT], F32, offset=off_)
            tmB1[n] = Buf(parent=par_)
        else:
            tm1[n] = sb.alloc([128, TT], F32, n + "1")
            tmB1[n] = Buf()
    TMs.append(tm1)
    TMBs.append(tmB1)
    SQKs.append(sb.alloc([128, TT], BF16, "sqk1"))
    RKs.append(sb.alloc([128, TT], BF16, "rk1"))
    VTs.append(sb.alloc([128, 4, 128], F32, "vt1"))
    SQKBs.append(Buf())
    RKBs.append(Buf())
    VTBs.append(Buf())

    rkv = W["rw_rkv"]
    for mI in range(3):
        src = rkv[j, mI].rearrange("(c p) n -> p c n", p=128)
        for hI in range(2):
            stg.load(wrkv[mI][:, 4 * hI:4 * hI + 4, :], src[:, 4 * hI:4 * hI + 4, :],
                     lambda t: t[:, :].rearrange("p (c n) -> p c n", n=D), wsB)
    for dI in range(2):
        stg.load(w1c[:, :, 64 * dI:64 * dI + 64], W["rw_w1"][j, dI].rearrange("(c p) n -> p c n", p=128),
                 lambda t: t[:, 0:512].rearrange("p (c n) -> p c n", n=64), wsB)
        stg.load(a1c[:, :, 64 * dI:64 * dI + 64], W["rw_a1"][j, dI].rearrange("(c p) n -> p c n", p=128),
                 lambda t: t[:, 0:512].rearrange("p (c n) -> p c n", n=64), wsB)
        stg.load(w2c[64 * dI:64 * dI + 64, :], W["rw_w2"][j, dI], lambda t, dI=dI: t[64 * dI:64 * dI + 64, 0:D], wsB)
        stg.load(a2c[64 * dI:64 * dI + 64, :], W["rw_a2"][j, dI], lambda t, dI=dI: t[64 * dI:64 * dI + 64, 0:D], wsB)
    g1v = W["rw_g1"][j].rearrange("(c p) n -> p c n", p=128)
    stg.load(g1a[:, :, :], g1v[:, :, 0:128], lambda t: t[:, 0:1024].rearrange("p (c n) -> p c n", n=128), wsB)
    stg.load(g1b[:, :, :], g1v[:, :, 128:160], lambda t: t[:, 0:256].rearrange("p (c n) -> p c n", n=32), wsB)
    stg.load(g2a[:, :], W["rw_g2"][j, 0:128, :], lambda t: t[:, 0:D], wsB)
    stg.load(g2b[:, :], W["rw_g2"][j, 128:160, :], lambda t: t[0:32, 0:D], wsB)
    if has_vres:
        stg.load(v1s[:, :, :], W["rw_v1"][j - 1].rearrange("(c p) n -> p c n", p=128),
                 lambda t: t[:, 0:256].rearrange("p (c n) -> p c n", n=32), wsB)
        stg.load(v2s[:, :], W["rw_v2"][j - 1], lambda t: t[0:32, 0:D], wsB)
    p.barrier()

    xiv = xin.rearrange("(c p) t -> p c t", p=128)
    cst = B.consts_sb
    ident = cst[:, B.C["ident"]:B.C["ident"] + 128]
    bones = B.bones_bf
    for t in range(NT):
        t0 = t * TT
        seg = t0 // SEG
        first = (t0 % SEG == 0)
        last = ((t0 + TT) % SEG == 0)
        B.dma("sp", xh[:, :, 1:TT + 1], xiv[:, :, t0:t0 + TT], writes=[xhB])
        if t0 > 0:
            B.dma("sp", xh[:, :, 0:1], xiv[:, :, t0 - 1:t0], writes=[xhB], slow=True)
        else:
            p.add("pool", lambda e: e.memset(xh[:, :, 0:1], 0.0), writes=[xhB])
        if t0 + TT < T:
            B.dma("sp", xh[:, :, TT + 1:TT + 2], xiv[:, :, t0 + TT:t0 + TT + 1], writes=[xhB], slow=True)
        else:
            p.add("pool", lambda e: e.memset(xh[:, :, TT + 1:TT + 2], 0.0), writes=[xhB])
        for hI in range(2):
            act(B, sq[:, 4 * hI:4 * hI + 4, :], xh[:, 4 * hI:4 * hI + 4, :], AF.Square, [xhB], SQW)
        psA, psAB = bank(B)
        for c in range(8):
            mm(B, psA[:, :], B.ones_bf[:, :], sq[:, c, 0:TT], c == 0, c == 7, [B.onesB] + SQW, [psAB])
        psH, psHB = bank(B)
        for c in range(8):
            mm(B, psH[:, 0:2], B.ones_bf[:, :], sq[:, c, TT:TT + 2], c == 0, c == 7, [B.onesB] + SQW, [psHB])
        act(B, rs[:, 0:TT], psA[:, :], AF.Sqrt, [psAB, epsB], [rsB], bias=epsb[:, 0:1], scale=1.0 / D)
        act(B, rs[:, TT:TT + 2], psH[:, 0:2], AF.Sqrt, [psHB, epsB], [rsB], bias=epsb[:, 0:1], scale=1.0 / D)
        p.add("dve", lambda e: e.reciprocal(out=rstd[:, :], in_=rs[:, :]), [rsB], [rstdB])
        gc = V["norm_mix_g"][layer]
        for c in range(8):
            stt(B, xh[:, c, :], xh[:, c, :], vs[:, gc + c:gc + c + 1], rstd[:, :], ALU.mult, ALU.mult,
                [xhB, rstdB], [xhB])
        if first and seg > 0:
            ts(B, "dve", xh[:, :, 0:1], xh[:, :, 0:1], B.flags_sb[:, seg:seg + 1], None, ALU.mult, None,
               [xhB], [xhB])
        if last and seg < (T // SEG) - 1:
            ts(B, "dve", xh[:, :, TT + 1:TT + 2], xh[:, :, TT + 1:TT + 2], B.flags_sb[:, seg + 1:seg + 2], None,
               ALU.mult, None, [xhB], [xhB])
        tt(B, "pool", xx[:, :, :], xh[:, :, 0:TT], xh[:, :, 2:TT + 2], ALU.add, [xhB], [xxB])
        stt(B, xx[:, :, :], xx[:, :, :], 0.5, xh[:, :, 1:TT + 1], ALU.mult, ALU.subtract, [xxB, xhB], [xxB])
        mc = V["rw_mix"][j]
        for mI in range(6):
            for c in range(8):
                stt(B, xm[mI][:, c, :], xx[:, c, :], vs[:, mc + 8 * mI + c:mc + 8 * mI + c + 1], xh[:, c, 1:TT + 1],
                    ALU.mult, ALU.add, [xxB, xhB], [xmB[mI]])
        XR, XK, XV, XW, XA, XG = range(6)
        ps_, psB_ = bank(B)
        for c in range(8):
            mm(B, ps_[:, :], w1c[:, c, :], xm[XW][:, c, :], c == 0, c == 7, [xmB[XW]], [psB_])
        act(B, lo["tw"][:, :], ps_[:, :], AF.Tanh, [psB_], [loB["tw"]])
        ps_, psB_ = bank(B)
        for c in range(8):
            mm(B, ps_[:, :], a1c[:, c, :], xm[XA][:, c, :], c == 0, c == 7, [xmB[XA]], [psB_])
        act(B, lo["ta"][:, :], ps_[:, :], AF.Copy, [psB_], [loB["ta"]])
        ps_, psB_ = bank(B)
        for c in range(8):
            mm(B, ps_[:, :], g1a[:, c, :], xm[XG][:, c, :], c == 0, c == 7, [xmB[XG]], [psB_])
        act(B, lo["tg0"][:, :], ps_[:, :], AF.Sigmoid, [psB_], [loB["tg0"]])
        ps_, psB_ = bank(B)
        for c in range(8):
            mm(B, ps_[0:32, :], g1b[:, c, :], xm[XG][:, c, :], c == 0, c == 7, [xmB[XG]], [psB_])
        act(B, lo["tg1"][0:32, :], ps_[0:32, :], AF.Sigmoid, [psB_], [loB["tg1"]])
        if has_vres:
            ps_, psB_ = bank(B)
            for c in range(8):
                mm(B, ps_[0:32, :], v1s[:, c, :], xm[XV][:, c, :], c == 0, c == 7, [xmB[XV]], [psB_])
            act(B, lo["tv"][0:32, :], ps_[0:32, :], AF.Copy, [psB_], [loB["tv"]])
        def oc_unit(oc, tm, tmB, sqk, sqkB, rk, rkB, vt, vtB):
            osl = slice(oc * 128, (oc + 1) * 128)
            for mI, nm in enumerate("rkv"):
                ps_, psB_ = bank(B)
                for c in range(8):
                    mm(B, ps_[:, :], wrkv[mI][:, c, osl], xm[mI][:, c, :], c == 0, c == 7, [xmB[mI]], [psB_])
                if mI == 1:
                    p.add("dve", lambda e, ps_=ps_: e.tensor_copy(out=tm["k"][:, :], in_=ps_[:, :]),
                          [psB_], [tmB["k"]])
                else:
                    act(B, tm[nm][:, :], ps_[:, :], AF.Copy, [psB_], [tmB[nm]])
            yield
            for dI in range(2):
                dsl = slice(64 * dI, 64 * dI + 64)
                ps_, psB_ = bank(B)
                mm(B, ps_[:, :], w2c[dsl, osl], lo["tw"][dsl, :], True, True, [loB["tw"]], [psB_])
                w0c = V["rw_w0"][j][dI] + oc
                act(B, tm[f"lw{dI}"][:, :], ps_[:, :], AF.Sigmoid, [psB_], [tmB[f"lw{dI}"]],
                    bias=vs[:, w0c:w0c + 1], scale=1.0)
                act(B, tm[f"lw{dI}"][:, :], tm[f"lw{dI}"][:, :], AF.Identity, [tmB[f"lw{dI}"]], [tmB[f"lw{dI}"]],
                    scale=-0.6065306597126334)
                ps_, psB_ = bank(B)
                mm(B, ps_[:, :], a2c[dsl, osl], lo["ta"][dsl, :], True, True, [loB["ta"]], [psB_])
                a0c = V["rw_a0"][j][dI] + oc
                act(B, tm[f"ic{dI}"][:, :], ps_[:, :], AF.Sigmoid, [psB_], [tmB[f"ic{dI}"]],
                    bias=vs[:, a0c:a0c + 1], scale=1.0)
            yield
            ps_, psB_ = bank(B)
            mm(B, ps_[:, :], g2a[:, osl], lo["tg0"][:, :], True, False, [loB["tg0"]], [psB_])
            mm(B, ps_[:, :], g2b[0:32, osl], lo["tg1"][0:32, :], False, True, [loB["tg1"]], [psB_])
            act(B, tm["g"][:, :], ps_[:, :], AF.Copy, [psB_], [tmB["g"]])
            if has_vres:
                ps_, psB_ = bank(B)
                mm(B, ps_[:, :], v2s[0:32, osl], lo["tv"][0:32, :], True, True, [loB["tv"]], [psB_])
                v0c = V["rw_v0"][j - 1] + oc
                act(B, tm["vg"][:, :], ps_[:, :], AF.Sigmoid, [psB_], [tmB["vg"]], bias=vs[:, v0c:v0c + 1],
                    scale=1.0)
                B.dma("sp", tm["vf"][:, :], S["vfirst"][osl, t0:t0 + TT], writes=[tmB["vf"]])
                tt(B, "pool", tm["vf"][:, :], tm["vf"][:, :], tm["v"][:, :], ALU.subtract,
                   [tmB["vf"], tmB["v"]], [tmB["vf"]])
                tt(B, "pool", tm["vf"][:, :], tm["vf"][:, :], tm["vg"][:, :], ALU.mult,
                   [tmB["vf"], tmB["vg"]], [tmB["vf"]])
                tt(B, "pool", tm["v"][:, :], tm["v"][:, :], tm["vf"][:, :], ALU.add,
                   [tmB["vf"], tmB["v"]], [tmB["v"]])
            else:
                B.dma("sp", S["vfirst"][osl, t0:t0 + TT], tm["v"][:, :], reads=[tmB["v"]])
            yield
            kkc = V["rw_kk"][j] + oc
            act(B, tm["kk"][:, :], tm["k"][:, :], AF.Identity, [tmB["k"]], [tmB["kk"]], scale=vs[:, kkc:kkc + 1])
            act(B, sqk[:, :], tm["kk"][:, :], AF.Square, [tmB["kk"]], [sqkB])
            ps_, psB_ = bank(B)
            mm(B, ps_[:, :], bones[:, :], sqk[:, :], True, True, [sqkB, B.bonesB], [psB_])
            act(B, tm["rn"][:, :], ps_[:, :], AF.Sqrt, [psB_], [tmB["rn"]])
            ts(B, "dve", tm["rn"][:, :], tm["rn"][:, :], 1e-12, None, ALU.max, None, [tmB["rn"]], [tmB["rn"]])
            p.add("dve", lambda e: e.reciprocal(out=tm["rn"][:, :], in_=tm["rn"][:, :]), [tmB["rn"]], [tmB["rn"]])
            tt(B, "dve", tm["kk"][:, :], tm["kk"][:, :], tm["rn"][:, :], ALU.mult, [tmB["kk"], tmB["rn"]],
               [tmB["kk"]])
            yield
            kac = V["rw_ka"][j] + oc
            for dI in range(2):
                ic, kd, bb = tm[f"ic{dI}"], tm[f"kd{dI}"], tm[f"b{dI}"]
                icB, kdB, bbB = tmB[f"ic{dI}"], tmB[f"kd{dI}"], tmB[f"b{dI}"]
                ts(B, "dve", tm["t1"][:, :], ic[:, :], 1.0, vs[:, kac:kac + 1], ALU.subtract, ALU.mult,
                   [icB], [tmB["t1"]])
                stt(B, kd[:, :], tm["t1"][:, :], 1.0, tm["k"][:, :], ALU.add, ALU.mult, [tmB["t1"], tmB["k"]], [kdB])
                tt(B, "pool", bb[:, :], tm["kk"][:, :], ic[:, :], ALU.mult, [tmB["kk"], icB], [bbB])
            yield
            tt(B, "pool", tm["t1"][:, :], tm["kd0"][:, :], tm["kd1"][:, :], ALU.add, [tmB["kd0"], tmB["kd1"]],
               [tmB["t1"]])
            rkc = V["rw_rk"][j] + oc
            stt(B, rk[:, :], tm["t1"][:, :], vs[:, rkc:rkc + 1], tm["r"][:, :], ALU.mult, ALU.mult,
                [tmB["t1"], tmB["r"]], [rkB])
            ps_, psB_ = bank(B)
            mm(B, ps_[:, :], bones[:, :], rk[:, :], True, True, [rkB, B.bonesB], [psB_])
            tt(B, "dve", tm["bv"][:, :], ps_[:, :], tm["v"][:, :], ALU.mult, [psB_, tmB["v"]], [tmB["bv"]])
            yield
            ps_, psB_ = bank(B)
            for s4 in range(4):
                p.add("pe", lambda e, ps_=ps_, s4=s4: e.transpose(ps_[:, s4 * 128:(s4 + 1) * 128],
                                                                 tm["v"][:, s4 * 128:(s4 + 1) * 128], ident),
                      [tmB["v"]], [psB_])
            act(B, vt[:, :, :], ps_[:, :].rearrange("p (s c) -> p s c", c=128), AF.Copy, [psB_], [vtB])
            B.dma("sp", S["vtok"][t0:t0 + TT, osl].rearrange("(s p) c -> p s c", p=128), vt[:, :, :], reads=[vtB])
            yield
            for nm in ("r", "kk", "g", "bv", "lw0", "lw1", "kd0", "kd1", "b0", "b1"):
                B.dma("sp", S[nm][osl, t0:t0 + TT], tm[nm][:, :], reads=[tmB[nm]])

        for oc0 in range(0, 8, 2):
            gens = [oc_unit(oc0 + s_, TMs[s_], TMBs[s_], SQKs[s_], SQKBs[s_], RKs[s_], RKBs[s_], VTs[s_], VTBs[s_])
                    for s_ in range(2)]
            live = [True, True]
            while any(live):
                for gi, g_ in enumerate(gens):
                    if live[gi]:
                        try:
                            next(g_)
                        except StopIteration:
                            live[gi] = False
    p.barrier()
    sb.release(m0)


def rwkv_r2(B, S):
    nc, p, sb = B.nc, B.p, B.sb
    T = B.T
    TT, L, NQ = 512, 64, 8
    NT = T // TT
    m0 = sb.mark()
    cst = B.consts_sb
    C = B.C
    MASK = {k: cst[:, C[k]:C[k] + 128] for k in ("LT", "LE", "GT", "GE")}
    identbf = B.ident_bf
    blk_sb = sb.alloc([128, 768], F32, "blk")
    B.dma("sp", blk_sb[:, :], B.blkm_dram[:, :], writes=[Buf()])
    BLK = [blk_sb[:, 128 * li:128 * li + 128] for li in range(6)]
    onesf = sb.alloc([128, L], F32, "onesf")
    p.add("pool", lambda e: e.memset(onesf[:, :], 1.0))
    nseg = T // SEG

    class CS:
        pass

    def mk(tag):
        c_ = CS()
        inn = ["r", "kk", "lw", "kd", "b"]
        c_.inp = {n: sb.alloc([128, NQ, L], F32, n + tag) for n in inn}
        c_.inpB = {n: Buf() for n in inn}
        c_.Vf = sb.alloc([128, NQ, L], F32, "Vf" + tag)
        c_.VfB = Buf()
        c_.Vs = sb.alloc([128, NQ, L], BF16, "Vs" + tag)
        c_.VsB = Buf()
        f32n = ["P", "E", "Sx", "Si", "epos", "eneg", "egm", "er"]
        c_.ft = {n: sb.alloc([128, NQ, L], F32, n + tag) for n in f32n}
        c_.ftB = {n: Buf() for n in f32n}
        c_.wl = sb.alloc([128, NQ], F32, "wl" + tag)
        c_.wlB = Buf()
        bdn = ["bdr", "bda", "bdb", "bdk", "bdbw", "bdkw"]
        c_.bd = {n: sb.alloc([128, NQ, 128], BF16, n + tag) for n in bdn}
        c_.bdB = {n: Buf() for n in bdn}
        for n in bdn:
            p.add("pool", lambda e, t_=c_.bd[n]: e.memset(t_[:, :, :], 0.0), writes=[c_.bdB[n]])
        c_.X0 = sb.alloc([128, NQ, 128], BF16, "X0" + tag)
        c_.Y0 = sb.alloc([128, NQ, 128], BF16, "Y0" + tag)
        c_.X0B, c_.Y0B = Buf(), Buf()
        c_.xo = [sb.alloc([128, NQ, 128], BF16, f"xo{i}" + tag) for i in range(2)]
        c_.ao = [sb.alloc([128, NQ, 128], BF16, f"ao{i}" + tag) for i in range(2)]
        c_.xoB = [Buf(), Buf()]
        c_.aoB = [Buf(), Buf()]
        c_.Et = [sb.alloc([128, NQ, 128], BF16, f"E{i}" + tag) for i in range(2)]
        c_.Dt = [sb.alloc([128, NQ, 128], BF16, f"D{i}" + tag) for i in range(2)]
        c_.EtB = [Buf(), Buf()]
        c_.DtB = [Buf(), Buf()]
        c_.Qt = sb.alloc([128, NQ, 128], BF16, "Qt" + tag)
        c_.Rt = sb.alloc([128, NQ, 128], BF16, "Rt" + tag)
        c_.QtB, c_.RtB = Buf(), Buf()
        amn = ["ArbT", "AakT", "ArkT", "bWT", "kWT"]
        c_.am = {n: sb.alloc([128, NQ, 128], BF16, n + tag) for n in amn}
        c_.amB = {n: Buf() for n in amn}
        c_.ST = sb.alloc([128, L], F32, "ST" + tag)
        c_.STb = sb.alloc([128, L], BF16, "STb" + tag)
        c_.STB, c_.STbB = Buf(), Buf()
        c_.RHS = sb.alloc([128, L], BF16, "RHS" + tag)
        c_.U = sb.alloc([128, L], BF16, "U" + tag)
        c_.RHSB, c_.UB = Buf(), Buf()
        c_.yt = sb.alloc([128, NQ, L], F32, "yt" + tag)
        c_.ytB = Buf()
        return c_

    chains = [mk("f"), mk("b")]
    p.barrier()

    def unit(cs, d, c, t):
        fwd = (d == 0)
        M_strict, M_incl, M_strictT = (MASK["LT"], MASK["LE"], MASK["GT"]) if fwd else \
            (MASK["GT"], MASK["GE"], MASK["LT"])
        csl = slice(c * 128, (c + 1) * 128)
        t0 = t * TT
        seg = t0 // SEG
        I_, IB, ft, ftB, bd, bdB, am, amB = cs.inp, cs.inpB, cs.ft, cs.ftB, cs.bd, cs.bdB, cs.am, cs.amB
        ST, STb, STB, STbB = cs.ST, cs.STb, cs.STB, cs.STbB
        fl = None
        if fwd and t0 % SEG == 0 and seg > 0:
            fl = seg
        if (not fwd) and (t0 + TT) % SEG == 0 and seg < nseg - 1:
            fl = seg + 1
        if fl is not None:
            ts(B, "dve", ST[:, :], ST[:, :], B.flags_sb[:, fl:fl + 1], None, ALU.mult, None, [STB], [STB])
            ts(B, "dve", STb[:, :], STb[:, :], B.flags_sb[:, fl:fl + 1], None, ALU.mult, None, [STbB], [STbB])
        for n, key in (("r", "r"), ("kk", "kk"), ("lw", f"lw{d}"), ("kd", f"kd{d}"), ("b", f"b{d}")):
            B.dma("sp", I_[n][:, :, :].rearrange("p q l -> p (q l)"), S[key][csl, t0:t0 + TT], writes=[IB[n]])
        for h in range(2):
            col = (2 * c + h) * 64
            B.dma("sp", cs.Vf[h * 64:(h + 1) * 64, :, :],
                  S["vtok"][t0:t0 + TT, col:col + 64].rearrange("(q j) v -> j q v", j=L), writes=[cs.VfB])
        yield
        act(B, cs.Vs[:, :, :], cs.Vf[:, :, :], AF.Copy, [cs.VfB], [cs.VsB])
        lw = I_["lw"]
        for q in range(NQ):
            p.add("dve", lambda e, q=q: e.tensor_tensor_scan(
                out=ft["P"][:, q, :], data0=onesf[:, :], data1=lw[:, q, :], initial=0.0,
                op0=ALU.mult, op1=ALU.add), [IB["lw"]], [ftB["P"]])
        yield
        tot = ft["P"][:, :, L - 1:L]
        tt(B, "pool", ft["E"][:, :, :], ft["P"][:, :, :], lw[:, :, :], ALU.subtract, [ftB["P"], IB["lw"]], [ftB["E"]])
        tt(B, "dve", ft["Sx"][:, :, :], tot.to_broadcast([128, NQ, L]), ft["P"][:, :, :], ALU.subtract,
           [ftB["P"]], [ftB["Sx"]])
        if fwd:
            G, GB, Gm, GmB, R, RB = ft["P"], ftB["P"], ft["E"], ftB["E"], ft["Sx"], ftB["Sx"]
        else:
            tt(B, "pool", ft["Si"][:, :, :], ft["Sx"][:, :, :], lw[:, :, :], ALU.add, [ftB["Sx"], IB["lw"]],
               [ftB["Si"]])
            G, GB, Gm, GmB, R, RB = ft["Si"], ftB["Si"], ft["Sx"], ftB["Sx"], ft["E"], ftB["E"]
        yield
        act(B, ft["epos"][:, :, :], G[:, :, :], AF.Exp, [GB], [ftB["epos"]])
        act(B, ft["eneg"][:, :, :], G[:, :, :], AF.Exp, [GB], [ftB["eneg"]], scale=-1.0)
        yield
        act(B, ft["egm"][:, :, :], Gm[:, :, :], AF.Exp, [GmB], [ftB["egm"]])
        act(B, ft["er"][:, :, :], R[:, :, :], AF.Exp, [RB], [ftB["er"]])
        act(B, cs.wl[:, :], ft["P"][:, :, L - 1], AF.Exp, [ftB["P"]], [cs.wlB])
        yield
        k2 = 0
        for h in range(2):
            hs = slice(h * 64, (h + 1) * 64)
            stt(B, bd["bda"][hs, :, hs], I_["kk"][hs, :, :], -1.0, ft["egm"][hs, :, :], ALU.mult, ALU.mult,
                [IB["kk"], ftB["egm"]], [bdB["bda"]])
            for dst, a_, e_ in (("bdr", "r", "epos"), ("bdb", "b", "eneg"), ("bdk", "kd", "eneg"),
                                ("bdbw", "b", "er"), ("bdkw", "kd", "er")):
                eng = "dve" if k2 % 2 == 0 else "pool"
                k2 += 1
                tt(B, eng, bd[dst][hs, :, hs], I_[a_][hs, :, :], ft[e_][hs, :, :], ALU.mult,
                   [IB[a_], ftB[e_]], [bdB[dst]])
            yield

        def grp(lhs, lhsB, rhs, rhsB, evac, g):
            ps_, psB_ = bank(B)
            for qq in range(4):
                q = g * 4 + qq
                rr_ = rhs if rhs is identbf else None
                mm(B, ps_[:, qq * 128:(qq + 1) * 128], lhs[:, q, :],
                   (identbf[:, :] if rhs is identbf else rhs[:, q, :]), True, True,
                   [lhsB] + ([] if rhs is identbf else [rhsB]), [psB_])
            evac(g, ps_[:, :].rearrange("p (q c) -> p q c", c=128), psB_)

        def ev_mask(dst, dstB, mask):
            return lambda g, pv, pB: tt(B, "dve", dst[:, g * 4:g * 4 + 4, :], pv,
                                        mask.unsqueeze(1).to_broadcast([128, 4, 128]), ALU.mult, [pB], [dstB])

        def ev_act(dst, dstB):
            return lambda g, pv, pB: act(B, dst[:, g * 4:g * 4 + 4, :], pv, AF.Copy, [pB], [dstB])

        def ev_dve(dst, dstB):
            return lambda g, pv, pB: p.add("dve", lambda e: e.tensor_copy(out=dst[:, g * 4:g * 4 + 4, :], in_=pv),
                                           [pB], [dstB])

        def ev_add(dst, dstB, old, oldB):
            return lambda g, pv, pB: tt(B, "dve", dst[:, g * 4:g * 4 + 4, :], pv, old[:, g * 4:g * 4 + 4, :],
                                        ALU.add, [pB, oldB], [dstB])

        for (lh, rh, mk_, dst, dstB) in (
                ("bdb", "bda", M_strict, cs.X0, cs.X0B), ("bda", "bdb", M_strictT, cs.Y0, cs.Y0B),
                ("bdb", "bdr", M_incl, am["ArbT"], amB["ArbT"]), ("bdk", "bda", M_strict, am["AakT"], amB["AakT"]),
                ("bdk", "bdr", M_incl, am["ArkT"], amB["ArkT"])):
            for g in range(2):
                grp(bd[lh], bdB[lh], bd[rh], bdB[rh], ev_mask(dst, dstB, mk_), g)
                yield
        for nm, src_ in (("bWT", "bdbw"), ("kWT", "bdkw")):
            for g in range(2):
                grp(bd[src_], bdB[src_], identbf, None, ev_act(am[nm], amB[nm]), g)
                yield
        idb = identbf[:, :].unsqueeze(1).to_broadcast([128, NQ, 128])

        def offs(li, slot):
            mk2 = BLK[li].unsqueeze(1).to_broadcast([128, NQ, 128])
            tt(B, "pool", cs.xo[slot][:, :, :], cs.X0[:, :, :], mk2, ALU.mult, [cs.X0B], [cs.xoB[slot]])
            tt(B, "pool", cs.ao[slot][:, :, :], cs.Y0[:, :, :], mk2, ALU.mult, [cs.Y0B], [cs.aoB[slot]])

        offs(0, 0)
        cur = 0
        tt(B, "pool", cs.Et[0][:, :, :], cs.xo[0][:, :, :], idb, ALU.add, [cs.xoB[0]], [cs.EtB[0]])
        tt(B, "pool", cs.Dt[0][:, :, :], cs.ao[0][:, :, :], idb, ALU.add, [cs.aoB[0]], [cs.DtB[0]])
        offs(1, 1)
        yield
        for li in range(1, 6):
            lastl = (li == 5)
            nxt = 1 - cur
            sl_ = li % 2
            xo, xoB, ao, aoB = cs.xo[sl_], cs.xoB[sl_], cs.ao[sl_], cs.aoB[sl_]
            E_, EB_, D_, DB_ = cs.Et[cur], cs.EtB[cur], cs.Dt[cur], cs.DtB[cur]
            for g in range(2):
                grp(ao, aoB, E_, EB_, ev_act(cs.Qt, cs.QtB), g)
                yield
            if not lastl:
                for g in range(2):
                    grp(xo, xoB, D_, DB_, ev_act(cs.Rt, cs.RtB), g)
                    yield
            for g in range(2):
                grp(D_, DB_, cs.Qt, cs.QtB, ev_add(cs.Et[nxt], cs.EtB[nxt], E_, EB_), g)
                yield
            if not lastl:
                for g in range(2):
                    grp(E_, EB_, cs.Rt, cs.RtB, ev_add(cs.Dt[nxt], cs.DtB[nxt], D_, DB_), g)
                    yield
                offs(li + 1, (li + 1) % 2)
            cur = nxt
        Z, ZB = cs.Et[cur], cs.EtB[cur]
        Vs, VsB, RHS, U, RHSB, UB, wl, wlB = cs.Vs, cs.VsB, cs.RHS, cs.U, cs.RHSB, cs.UB, cs.wl, cs.wlB
        qs = list(range(NQ)) if fwd else list(range(NQ - 1, -1, -1))
        for q in qs:
            ps1, ps1B = bank(B)
            mm(B, ps1[:, 0:L], bd["bda"][:, q, :], STb[:, :], True, False, [bdB["bda"], STbB], [ps1B])
            mm(B, ps1[:, 0:L], am["AakT"][:, q, :], Vs[:, q, :], False, True, [amB["AakT"], VsB], [ps1B])
            act(B, RHS[:, :], ps1[:, 0:L], AF.Copy, [ps1B], [RHSB])
            yield
            ps2, ps2B = bank(B)
            mm(B, ps2[:, 0:L], Z[:, q, :], RHS[:, :], True, True, [ZB, RHSB], [ps2B])
            act(B, U[:, :], ps2[:, 0:L], AF.Copy, [ps2B], [UB])
            yield
            ps3, ps3B = bank(B)
            mm(B, ps3[:, 0:L], bd["bdr"][:, q, :], STb[:, :], True, False, [bdB["bdr"], STbB], [ps3B])
            mm(B, ps3[:, 0:L], am["ArbT"][:, q, :], U[:, :], False, False, [amB["ArbT"], UB], [ps3B])
            mm(B, ps3[:, 0:L], am["ArkT"][:, q, :], Vs[:, q, :], False, True, [amB["ArkT"], VsB], [ps3B])
            act(B, cs.yt[:, q, :], ps3[:, 0:L], AF.Copy, [ps3B], [cs.ytB])
            ps4, ps4B = bank(B)
            mm(B, ps4[:, 0:L], am["bWT"][:, q, :], U[:, :], True, False, [amB["bWT"], UB], [ps4B])
            mm(B, ps4[:, 0:L], am["kWT"][:, q, :], Vs[:, q, :], False, True, [amB["kWT"], VsB], [ps4B])
            stt(B, STb[:, :], ST[:, :], wl[:, q:q + 1], ps4[:, 0:L], ALU.mult, ALU.add, [STB, wlB, ps4B], [STbB])
            stt(B, ST[:, :], ST[:, :], wl[:, q:q + 1], ps4[:, 0:L], ALU.mult, ALU.add, [STB, wlB, ps4B], [STB])
            yield
        for h in range(2):
            col = (2 * c + h) * 64
            B.dma("pool", S[f"ytok{d}"][t0:t0 + TT, col:col + 64].rearrange("(q t) v -> t q v", t=L),
                  cs.yt[h * 64:(h + 1) * 64, :, :], reads=[cs.ytB])
        yield

    for c in range(8):
        for cs in chains:
            p.add("pool", lambda e, cs=cs: e.memset(cs.ST[:, :], 0.0), writes=[cs.STB])
            p.add("pool", lambda e, cs=cs: e.memset(cs.STb[:, :], 0.0), writes=[cs.STbB])
        for k in range(NT):
            gens = [unit(chains[0], 0, c, k), unit(chains[1], 1, c, NT - 1 - k)]
            live = [True, True]
            while any(live):
                for gi, g_ in enumerate(gens):
                    if live[gi]:
                        try:
                            next(g_)
                        except StopIteration:
                            live[gi] = False
    p.barrier()
    sb.release(m0)


GN_EPS = 64e-5


def rwkv_r3(B, j, xin, xout, W, V, S):
    nc, p, sb = B.nc, B.p, B.sb
    T = B.T
    TT = 512
    NT = T // TT
    vs = B.vecs_sb
    m0 = sb.mark()
    cst = B.consts_sb
    ident = cst[:, B.C["ident"]:B.C["ident"] + 128]
    wo = sb.alloc([128, 8, D], BF16, "wo")
    woB = Buf()
    stg_t = [sb.alloc([128, 4096], F32, "stg") for _ in range(2)]
    stg = Stager(B, stg_t, [Buf(), Buf()])
    src = W["rw_o"][j].rearrange("(c p) n -> p c n", p=128)
    for hI in range(2):
        stg.load(wo[:, 4 * hI:4 * hI + 4, :], src[:, 4 * hI:4 * hI + 4, :],
                 lambda t: t[:, :].rearrange("p (c n) -> p c n", n=D), woB)
    yin = [[sb.alloc([128, 16, 64], F32, f"y{d}") for d in range(2)] for _ in range(2)]
    yinB = [[Buf(), Buf()] for _ in range(2)]
    SETS = []
    for k_ in range(2):
        SETS.append(dict(ys=sb.alloc([128, 16, 64], F32, f"ys{k_}"), ysB=Buf(),
                         sqc=sb.alloc([128, 16, 64], F32, f"sqc{k_}"), sqcB=Buf(),
                         yn=sb.alloc([128, 16, 64], F32, f"yn{k_}"), ynB=Buf(),
                         st1=sb.alloc([128, 16], F32, f"st1{k_}"), st2=sb.alloc([128, 16], F32, f"st2{k_}"),
                         st1B=Buf(), st2B=Buf()))
    gne = sb.alloc([128, 1], F32, "gne")
    gneB = Buf()
    p.add("pool", lambda e: e.memset(gne[:, :], GN_EPS), writes=[gneB])
    zt = sb.alloc([128, 8, TT], F32, "zt")
    ztB = [Buf() for _ in range(8)]
    zb = sb.alloc([128, 8, TT], BF16, "zb")
    zbB = [Buf() for _ in range(8)]
    bvt = [sb.alloc([128, TT], F32, "bvt") for _ in range(2)]
    gt = [sb.alloc([128, TT], F32, "gt") for _ in range(2)]
    bvB = [Buf(), Buf()]
    gB = [Buf(), Buf()]
    xt = sb.alloc([128, 8, TT], F32, "xt")
    xtB = Buf()
    p.barrier()
    xiv = xin.rearrange("(c p) t -> p c t", p=128)
    xov = xout.rearrange("(c p) t -> p c t", p=128)
    lg, lb = V["rw_lnx_g"][j], V["rw_lnx_b"][j]
    k = 0
    for t in range(NT):
        t0 = t * TT
        B.dma("sp", xt[:, :, :], xiv[:, :, t0:t0 + TT], writes=[xtB])
        def sub_unit(s4, bi, st_):
            ys, ysB, sqc, sqcB, yn, ynB = st_['ys'], st_['ysB'], st_['sqc'], st_['sqcB'], st_['yn'], st_['ynB']
            st1, st2, st1B, st2B = st_['st1'], st_['st2'], st_['st1B'], st_['st2B']
            r0 = t0 + s4 * 128
            for d in range(2):
                B.dma("sp", yin[bi][d][:, :, :].rearrange("p h v -> p (h v)"), S[f"ytok{d}"][r0:r0 + 128, :],
                      writes=[yinB[bi][d]])
            yield
            tt(B, "pool", ys[:, :, :], yin[bi][0][:, :, :], yin[bi][1][:, :, :], ALU.add,
               [yinB[bi][0], yinB[bi][1]], [ysB])
            p.add("dve", lambda e: e.tensor_reduce(out=st1[:, :], in_=ys[:, :, :], axis=AX.X, op=ALU.add),
                  [ysB], [st1B])
            ts(B, "pool", st1[:, :], st1[:, :], -1.0 / 64, None, ALU.mult, None, [st1B], [st1B])
            tt(B, "dve", ys[:, :, :], ys[:, :, :], st1[:, :].unsqueeze(2).to_broadcast([128, 16, 64]), ALU.add,
               [ysB, st1B], [ysB])
            yield
            act(B, sqc[:, :, :], ys[:, :, :], AF.Square, [ysB], [sqcB])
            p.add("dve", lambda e: e.tensor_reduce(out=st2[:, :], in_=sqc[:, :, :], axis=AX.X, op=ALU.add),
                  [sqcB], [st2B])
            act(B, st2[:, :], st2[:, :], AF.Sqrt, [st2B, gneB], [st2B], bias=gne[:, 0:1], scale=1.0 / 64)
            p.add("dve", lambda e: e.reciprocal(out=st2[:, :], in_=st2[:, :]), [st2B], [st2B])
            tt(B, "dve", yn[:, :, :], ys[:, :, :], st2[:, :].unsqueeze(2).to_broadcast([128, 16, 64]), ALU.mult,
               [ysB, st2B], [ynB])
            yield
            ynf = yn[:, :, :].rearrange("p h v -> p (h v)")
            for g2 in range(2):
                ps_, psB_ = bank(B)
                for o4 in range(4):
                    oc = g2 * 4 + o4
                    p.add("pe", lambda e, ps_=ps_, o4=o4, oc=oc: e.transpose(
                        ps_[:, o4 * 128:(o4 + 1) * 128], ynf[:, oc * 128:(oc + 1) * 128], ident), [ynB], [psB_])
                for o4 in range(4):
                    oc = g2 * 4 + o4
                    act(B, zt[:, oc, s4 * 128:(s4 + 1) * 128], ps_[:, o4 * 128:(o4 + 1) * 128], AF.Identity,
                        [psB_], [ztB[oc]], bias=vs[:, lb + oc:lb + oc + 1], scale=vs[:, lg + oc:lg + oc + 1])
        for s0 in range(0, 4, 2):
            gens = [sub_unit(s0 + k_, k_, SETS[k_]) for k_ in range(2)]
            live = [True, True]
            while any(live):
                for gi, g_ in enumerate(gens):
                    if live[gi]:
                        try:
                            next(g_)
                        except StopIteration:
                            live[gi] = False
        for oc in range(8):
            osl = slice(oc * 128, (oc + 1) * 128)
            bi = oc % 2
            B.dma("sp", bvt[bi][:, :], S["bv"][osl, t0:t0 + TT], writes=[bvB[bi]])
            B.dma("sp", gt[bi][:, :], S["g"][osl, t0:t0 + TT], writes=[gB[bi]])
            tt(B, "pool", zt[:, oc, :], zt[:, oc, :], bvt[bi][:, :], ALU.add, [ztB[oc], bvB[bi]], [ztB[oc]])
            tt(B, "dve", zb[:, oc, :], zt[:, oc, :], gt[bi][:, :], ALU.mult, [ztB[oc], gB[bi]], [zbB[oc]])
        for oc in range(8):
            ps_, psB_ = bank(B)
            for c in range(8):
                mm(B, ps_[:, :], wo[:, c, oc * 128:(oc + 1) * 128], zb[:, c, :], c == 0, c == 7, [woB, zbB[c]],
                   [psB_])
            tt(B, "dve", xt[:, oc, :], xt[:, oc, :], ps_[:, :], ALU.add, [xtB, psB_], [xtB])
        B.dma("pool", xov[:, :, t0:t0 + TT], xt[:, :, :], reads=[xtB])
    p.barrier()
    sb.release(m0)


CONST_LAYOUT = {"ident": 0, "LT": 128, "LE": 256, "GT": 384, "GE": 512, "bones": 640}
NCONST = 768


def host_consts():
    pp = np.arange(128)[:, None]
    ff = np.arange(128)[None, :]
    out = np.zeros((128, NCONST), np.float32)
    out[:, 0:128] = (pp == ff)
    out[:, 128:256] = (pp % 64 < ff % 64)
    out[:, 256:384] = (pp % 64 <= ff % 64)
    out[:, 384:512] = (pp % 64 > ff % 64)
    out[:, 512:640] = (pp % 64 >= ff % 64)
    out[:, 640:768] = (pp // 64 == ff // 64)
    return out


def host_blk_masks():
    pp = np.arange(128)[:, None]
    ff = np.arange(128)[None, :]
    out = np.zeros((128, 768), np.float32)
    for li, s in enumerate((1, 2, 4, 8, 16, 32)):
        out[:, 128 * li:128 * li + 128] = ((pp // 64 == ff // 64) & (pp // (2 * s) == ff // (2 * s))
                                           & (pp // s != ff // s))
    return out


def setup_common(B, ncols):
    nc, p, sb = B.nc, B.p, B.sb
    B.bank_i = 0
    B.C = CONST_LAYOUT
    consts = B.din("consts", [128, NCONST]).ap()
    flags = B.din("flags", [128, 8]).ap()
    B.blkm_dram = B.din("blkm", [128, 768]).ap()
    vecs = B.din("vecs", [128, ncols]).ap()
    B.consts_sb = sb.alloc([128, NCONST], F32, "consts")
    B.flags_sb = sb.alloc([128, 8], F32, "flags")
    B.vecs_sb = sb.alloc([128, ncols], F32, "vecs")
    B.ident_bf = sb.alloc([128, 128], BF16, "identbf")
    B.bones_bf = sb.alloc([128, 128], BF16, "bonesbf")
    B.bonesB = Buf()
    cB = Buf()
    B.dma("sp", B.consts_sb[:, :], consts[:, :], writes=[cB])
    B.dma("sp", B.flags_sb[:, :], flags[:, :], writes=[Buf()])
    B.dma("sp", B.vecs_sb[:, :], vecs[:, :], writes=[Buf()])
    p.add("dve", lambda e: e.tensor_copy(out=B.ident_bf[:, :], in_=B.consts_sb[:, 0:128]), [cB], [Buf()])
    p.add("dve", lambda e: e.tensor_copy(out=B.bones_bf[:, :], in_=B.consts_sb[:, 640:768]), [cB], [B.bonesB])
    p.barrier()


RW_SCRATCH_F = ["r", "kk", "g", "bv", "lw0", "lw1", "kd0", "kd1", "b0", "b1", "vfirst"]


def alloc_rwkv_scratch(B):
    T = B.T
    S = {n: B.dscr("s_" + n, [D, T]).ap() for n in RW_SCRATCH_F}
    for n in ("vtok", "ytok0", "ytok1"):
        S[n] = B.dscr("s_" + n, [T, D]).ap()
    return S


def pack_vectors(inp):
    vp = VecPack()
    V = {}
    V["norm_mix_g"] = [vp.add(f"nmg{l}", inp["norm_mix_g"][l]) for l in range(inp["norm_mix_g"].shape[0])]
    V["norm_mlp_g"] = [vp.add(f"nlg{l}", inp["norm_mlp_g"][l]) for l in range(inp["norm_mlp_g"].shape[0])]
    nrw = inp["rw_mix"].shape[0]
    V["rw_mix"] = [vp.add(f"mix{j}", inp["rw_mix"][j]) for j in range(nrw)]
    V["rw_w0"] = [[vp.add(f"w0{j}{d}", inp["rw_w0"][j, d]) for d in range(2)] for j in range(nrw)]
    V["rw_a0"] = [[vp.add(f"a0{j}{d}", inp["rw_a0"][j, d]) for d in range(2)] for j in range(nrw)]
    V["rw_v0"] = [vp.add(f"v0{j}", inp["rw_v0"][j]) for j in range(inp["rw_v0"].shape[0])]
    for nm in ("rw_kk", "rw_ka", "rw_rk", "rw_lnx_g", "rw_lnx_b"):
        V[nm] = [vp.add(f"{nm}{j}", inp[nm][j]) for j in range(nrw)]
    if "na_q_g" in inp:
        nna = inp["na_q_g"].shape[0]
        V["na_q_g"] = [vp.add(f"qg{j}", np.tile(inp["na_q_g"][j], 2)) for j in range(nna)]
        V["na_k_g"] = [vp.add(f"kg{j}", np.tile(inp["na_k_g"][j], 2)) for j in range(nna)]
    return vp, V


RW_WEIGHTS = ["rw_rkv", "rw_w1", "rw_w2", "rw_a1", "rw_a2", "rw_v1", "rw_v2", "rw_g1", "rw_g2", "rw_o"]


def build_rwkv_probe(T, ncols, V, shapes, j, layer):
    B = Builder(T)
    nc = B.nc
    setup_common(B, ncols)
    xT = B.din("xT", [D, T]).ap()
    W = {n: B.din(n, list(shapes[n])).ap() for n in RW_WEIGHTS}
    yT = B.dout("yT", [D, T]).ap()
    S = alloc_rwkv_scratch(B)
    if j > 0:
        vf_in = B.din("vfirst_in", [D, T]).ap()
        S["vfirst"] = vf_in
    rwkv_r1(B, j, layer, xT, W, V, S)
    rwkv_r2(B, S)
    rwkv_r3(B, j, xT, yT, W, V, S)
    with ExitStack() as st:
        B.p.emit(nc, st)
    return B


GRID_W = 64
ROWS_SEG = SEG // GRID_W
NEG = -30000.0


def na_window(i, kind, nseg_sample=4):
    seg = i // ROWS_SEG
    if kind == "S" and seg < nseg_sample:
        rows = nseg_sample * ROWS_SEG
        return int(np.clip(i - 4, 0, rows - 8))
    li = i % ROWS_SEG
    return seg * ROWS_SEG + int(np.clip(li - 4, 0, ROWS_SEG - 8))


def na_slots(T):
    nrows = T // GRID_W
    nseg = T // SEG
    nss = min(4, nseg)
    out = []
    for i in range(nrows):
        lo = min(na_window(i, "P"), na_window(i, "S", nss))
        hi = max(na_window(i, "P"), na_window(i, "S", nss)) + 8
        out.append(list(range(lo // 2, (hi - 1) // 2 + 1)))
    return out


def host_na_nbias(T, kind):
    slots = na_slots(T)
    nss = min(4, T // SEG)
    cols = []
    for i, ms in enumerate(slots):
        lo = na_window(i, kind, nss)
        for m in ms:
            col = np.zeros(128, np.float32)
            for hf in range(2):
                r = 2 * m + hf
                if not (lo <= r < lo + 8):
                    col[hf * 64:(hf + 1) * 64] = NEG
            cols.append(col)
    return np.ascontiguousarray(np.stack(cols, axis=1))


def host_na_bias_table(rpb):
    qc = np.arange(64)
    kc = np.arange(64)
    ws = np.clip(qc - 8, 0, 48)
    cm = (kc[:, None] >= ws[None, :]) & (kc[:, None] < ws[None, :] + 16)
    dc = np.clip(kc[:, None] - qc[None, :] + 15, 0, 30)
    out = np.full((16, 128, 16, 64), NEG, np.float32)
    for e in range(16):
        for hf in range(2):
            dr = e - 8 + hf
            if abs(dr) > 7:
                continue
            g = rpb[:, dr + 7, :][:, dc]
            g = np.where(cm[None], g, np.float32(NEG))
            out[e, hf * 64:(hf + 1) * 64] = np.transpose(g, (1, 0, 2))
    return out


def na_n1(B, jn, layer, xin, W, V, S):
    nc, p, sb = B.nc, B.p, B.sb
    T = B.T
    TT = 512
    NT = T // TT
    vs = B.vecs_sb
    m0 = sb.mark()
    wq = sb.alloc([128, 8, 3 * D], BF16, "wqkv")
    wqB = Buf()
    stg_t = [sb.alloc([128, 4096], F32, "stg") for _ in range(2)]
    stg = Stager(B, stg_t, [Buf(), Buf()])
    src = W["na_qkv"][jn].rearrange("(c p) n -> p c n", p=128)
    for c in range(8):
        for h3 in range(3):
            if h3 < 2:
                stg.load(wq[:, c, h3 * 1024:(h3 + 1) * 1024], src[:, c, h3 * 1024:(h3 + 1) * 1024],
                         lambda t: t[:, 0:1024], wqB)
            else:
                stg.load(wq[:, c, 2048:3072], src[:, c, 2048:3072], lambda t: t[:, 0:1024], wqB)
    xs = [sb.alloc([128, 8, TT], F32, "x") for _ in range(2)]
    xB = [Buf(), Buf()]
    sq = sb.alloc([128, 8, TT], BF16, "sq")
    sqB = Buf()
    hn = sb.alloc([128, 8, TT], BF16, "hn")
    hnB = Buf()
    rs = sb.alloc([128, TT], F32, "rs")
    rstd = sb.alloc([128, TT], F32, "rstd")
    rsB, rstdB = Buf(), Buf()
    epsb = sb.alloc([128, 2], F32, "eps")
    epsB = Buf()
    p.add("pool", lambda e: e.memset(epsb[:, 0:1], RMS_EPS), writes=[epsB])
    p.add("pool", lambda e: e.memset(epsb[:, 1:2], 64 * RMS_EPS), writes=[epsB])
    tq = [sb.alloc([128, TT], F32, "tq") for _ in range(2)]
    tqB = [Buf(), Buf()]
    sqq = [sb.alloc([128, TT], BF16, "sqq") for _ in range(2)]
    sqqB = [Buf(), Buf()]
    rq = [sb.alloc([128, TT], F32, "rq") for _ in range(2)]
    rqB = [Buf(), Buf()]
    qo = [sb.alloc([128, TT], BF16, "qo") for _ in range(2)]
    qoB = [Buf(), Buf()]
    vtl = [sb.alloc([128, D], BF16, "vtl") for _ in range(2)]
    vtlB = [Buf(), Buf()]
    p.barrier()
    xiv = xin.rearrange("(c p) t -> p c t", p=128)
    gc = V["norm_mix_g"][layer]
    k2 = 0
    for t in range(NT):
        t0 = t * TT
        b = t % 2
        x = xs[b]
        B.dma("sp", x[:, :, :], xiv[:, :, t0:t0 + TT], writes=[xB[b]])
        for hI in range(2):
            act(B, sq[:, 4 * hI:4 * hI + 4, :], x[:, 4 * hI:4 * hI + 4, :], AF.Square, [xB[b]], [sqB])
        psA, psAB = bank(B)
        for c in range(8):
            mm(B, psA[:, :], B.ones_bf[:, :], sq[:, c, :], c == 0, c == 7, [B.onesB, sqB], [psAB])
        act(B, rs[:, :], psA[:, :], AF.Sqrt, [psAB, epsB], [rsB], bias=epsb[:, 0:1], scale=1.0 / D)
        p.add("dve", lambda e: e.reciprocal(out=rstd[:, :], in_=rs[:, :]), [rsB], [rstdB])
        for c in range(8):
            stt(B, hn[:, c, :], x[:, c, :], vs[:, gc + c:gc + c + 1], rstd[:, :], ALU.mult, ALU.mult,
                [xB[b], rstdB], [hnB])
        for mI in range(2):
            gcol = V["na_q_g"][jn] if mI == 0 else V["na_k_g"][jn]
            for oc in range(8):
                bi = k2 % 2
                k2 += 1
                ps_, psB_ = bank(B)
                for c in range(8):
                    mm(B, ps_[:, :], wq[:, c, mI * 1024 + oc * 128:mI * 1024 + (oc + 1) * 128], hn[:, c, :],
                       c == 0, c == 7, [hnB], [psB_])
                act(B, tq[bi][:, :], ps_[:, :], AF.Copy, [psB_], [tqB[bi]])
                tt(B, "pool", sqq[bi][:, :], tq[bi][:, :], tq[bi][:, :], ALU.mult, [tqB[bi]], [sqqB[bi]])
                ps2, ps2B = bank(B)
                mm(B, ps2[:, :], B.bones_bf[:, :], sqq[bi][:, :], True, True, [sqqB[bi], B.bonesB], [ps2B])
                if mI == 0:
                    act(B, rq[bi][:, :], ps2[:, :], AF.Sqrt, [ps2B, epsB], [rqB[bi]], bias=epsb[:, 1:2], scale=1.0)
                else:
                    act(B, rq[bi][:, :], ps2[:, :], AF.Sqrt, [ps2B, epsB], [rqB[bi]], bias=epsb[:, 0:1],
                        scale=1.0 / 64)
                p.add("dve", lambda e, bi=bi: e.reciprocal(out=rq[bi][:, :], in_=rq[bi][:, :]), [rqB[bi]], [rqB[bi]])
                stt(B, qo[bi][:, :], tq[bi][:, :], vs[:, gcol:gcol + 1], rq[bi][:, :], ALU.mult, ALU.mult,
                    [tqB[bi], rqB[bi]], [qoB[bi]])
                dst = S["qT"] if mI == 0 else S["kT"]
                B.dma("sp", dst[oc * 128:(oc + 1) * 128, t0:t0 + TT], qo[bi][:, :], reads=[qoB[bi]])
        for tb in range(4):
            bi = tb % 2
            for hf in range(2):
                ps_, psB_ = bank(B)
                for c in range(8):
                    mm(B, ps_[:, :], hn[:, c, tb * 128:(tb + 1) * 128], wq[:, c, 2048 + hf * 512:2048 + (hf + 1) * 512],
                       c == 0, c == 7, [hnB], [psB_])
                if hf == 0:
                    act(B, vtl[bi][:, 0:512], ps_[:, :], AF.Copy, [psB_], [vtlB[bi]])
                else:
                    p.add("dve", lambda e, ps_=ps_, bi=bi: e.tensor_copy(out=vtl[bi][:, 512:1024], in_=ps_[:, :]),
                          [psB_], [vtlB[bi]])
            B.dma("sp", S["vtokb"][t0 + tb * 128:t0 + (tb + 1) * 128, :], vtl[bi][:, :], reads=[vtlB[bi]])
    p.barrier()
    sb.release(m0)


def na_n2(B, jn, xin, xout, W, S, nbias_dram, btab_dram, dbg=9):
    nc, p, sb = B.nc, B.p, B.sb
    T = B.T
    TT = 512
    NT = T // TT
    nrows = T // GRID_W
    slots = na_slots(T)
    nslot_tot = sum(len(s) for s in slots)
    m0 = sb.mark()
    cst = B.consts_sb
    ident = cst[:, B.C["ident"]:B.C["ident"] + 128]
    wo = sb.alloc([128, 8, D], BF16, "wo")
    woB = Buf()
    btab = sb.alloc([128, 16, 16, 64], BF16, "btab")
    btB = Buf()
    nb = sb.alloc([128, nslot_tot], F32, "nbias")
    m1 = sb.mark()
    stg_t = [sb.alloc([128, 4096], F32, "stg") for _ in range(2)]
    stgB = [Buf(), Buf()]
    stg = Stager(B, stg_t, stgB)
    src = W["na_o"][jn].rearrange("(c p) n -> p c n", p=128)
    for hI in range(2):
        stg.load(wo[:, 4 * hI:4 * hI + 4, :], src[:, 4 * hI:4 * hI + 4, :],
                 lambda t: t[:, :].rearrange("p (c n) -> p c n", n=D), woB)
    for e in range(16):
        stg.load(btab[:, e, :, :], btab_dram[e].rearrange("p (h q) -> p h q", q=64),
                 lambda t: t[:, 0:1024].rearrange("p (h q) -> p h q", q=64), btB)
    B.dma("sp", nb[:, :], nbias_dram[:, :], writes=[Buf()])
    p.barrier()
    sb.release(m1)
    NKR = 24
    KT = sb.alloc([128, 8, NKR * 64], BF16, "KT")
    KTB = Buf()
    QT = sb.alloc([128, 8, TT], BF16, "QT")
    QTB = Buf()
    Vraw = sb.alloc([128, NKR // 2, D], BF16, "Vraw")
    VrawB = Buf()
    Vaug = sb.alloc([128, NKR // 2, 16, 68], BF16, "Vaug")
    VaugB = Buf()
    p.add("pool", lambda e: e.memset(Vaug[:, :, :, 64:68], 0.0), writes=[VaugB])
    p.add("pool", lambda e: e.memset(Vaug[:, :, :, 64:65], 1.0), writes=[VaugB])
    NPT = 6
    NCH = 2
    PTs = [[sb.alloc([128, 16, 64], BF16, f"PT{i}_{k}") for i in range(NPT)] for k in range(NCH)]
    PTBs = [[Buf() for _ in range(NPT)] for k in range(NCH)]
    tmps = [[sb.alloc([128, 8, 64], F32, f"tmp{k}") for _ in range(2)] for k in range(NCH)]
    tmpBs = [[Buf(), Buf()] for k in range(NCH)]
    rcs = [sb.alloc([64, 16], F32, f"rc{k}") for k in range(NCH)]
    rcBs = [Buf() for k in range(NCH)]
    os_ = [sb.alloc([64, 16, 64], F32, f"o{k}") for k in range(NCH)]
    oBs = [Buf() for k in range(NCH)]
    oT = sb.alloc([128, 8, TT], BF16, "oT")
    oTB = [Buf() for _ in range(8)]
    xt = sb.alloc([128, 8, TT], F32, "xt")
    xtB = Buf()
    p.barrier()
    xiv = xin.rearrange("(c p) t -> p c t", p=128)
    xov = xout.rearrange("(c p) t -> p c t", p=128)
    qv = S["qT"].rearrange("(c p) t -> p c t", p=128)
    kv = S["kT"].rearrange("(c p) t -> p c t", p=128)
    scol_base = np.concatenate([[0], np.cumsum([len(s_) for s_ in slots])]).astype(int)
    for t in range(NT):
        t0 = t * TT
        i0 = t0 // GRID_W
        klo = max(0, i0 - 8)
        khi = min(nrows, i0 + 16)
        nk = khi - klo
        B.dma("sp", xt[:, :, :], xiv[:, :, t0:t0 + TT], writes=[xtB])
        B.dma("sp", QT[:, :, :], qv[:, :, t0:t0 + TT], writes=[QTB])
        B.dma("sp", KT[:, :, 0:nk * 64], kv[:, :, klo * 64:khi * 64], writes=[KTB])
        B.dma("sp", Vraw[:, 0:nk // 2, :], S["vtokb"][klo * 64:khi * 64, :].rearrange("(m p) c -> p m c", p=128),
              writes=[VrawB])
        p.add("pool", lambda e, nk=nk: e.tensor_copy(
            out=Vaug[:, 0:nk // 2, :, 0:64], in_=Vraw[:, 0:nk // 2, :].rearrange("p m (h v) -> p m h v", v=64)),
            [VrawB], [VaugB])
        def row_unit(rr, PT, PTB, tmp, tmpB, rc, rcB, o, oB):
            i = i0 + rr
            ms = slots[i]
            scol = scol_base[i]
            k2 = 0
            assert len(ms) <= NPT
            for si, m in enumerate(ms if dbg >= 2 else []):
                pl = m - klo // 2
                e_ = 2 * m - i + 8
                assert 0 <= pl < nk // 2 and 0 <= e_ < 16, (i, m, pl, e_)
                pss = [bank(B), bank(B)]
                for h in range(16):
                    hs = slice((h % 2) * 64, (h % 2) * 64 + 64)
                    mm(B, pss[h % 2][0][:, (h // 2) * 64:(h // 2 + 1) * 64], KT[hs, h // 2, pl * 128:(pl + 1) * 128],
                       QT[hs, h // 2, rr * 64:(rr + 1) * 64], True, True, [KTB, QTB], [pss[h % 2][1]])
                for g in range(2):
                    ps_, psB_ = pss[g]
                    tb_ = k2 % 2
                    k2 += 1
                    import os
                    sub = int(os.environ.get("NA_SUB", "9"))
                    if sub >= 2:
                        tt(B, "dve", tmp[tb_][:, :, :], ps_[:, :].rearrange("p (h q) -> p h q", q=64),
                           btab[:, e_, g:16:2, :], ALU.add, [psB_, btB], [tmpB[tb_]])
                    if sub >= 3:
                        act(B, PT[si][:, g:16:2, :], tmp[tb_][:, :, :], AF.Exp, [tmpB[tb_]], [PTB[si]],
                            bias=nb[:, scol:scol + 1], scale=1.0)
                scol += 1
                yield
            pvb = [bank(B) for _ in range(4)]
            for h in range(16 if dbg >= 3 else 0):
                ps_, psB_ = pvb[h // 4]
                for si, m in enumerate(ms):
                    pl = m - klo // 2
                    mm(B, ps_[0:64, (h % 4) * 128:(h % 4) * 128 + 66], PT[si][:, h, :], Vaug[:, pl, h, 0:66],
                       si == 0, si == len(ms) - 1, [PTB[si], VaugB], [psB_])
            yield
            for b4 in range(4 if dbg >= 4 else 0):
                ps_, psB_ = pvb[b4]
                pv3 = ps_[0:64, :].rearrange("p (h c) -> p h c", c=128)
                p.add("dve", lambda e, pv3=pv3, b4=b4: e.reciprocal(out=rc[:, b4 * 4:(b4 + 1) * 4], in_=pv3[:, :, 64]),
                      [psB_], [rcB])
                tt(B, "dve", o[:, b4 * 4:(b4 + 1) * 4, :], pv3[:, :, 0:64],
                   rc[:, b4 * 4:(b4 + 1) * 4].unsqueeze(2).to_broadcast([64, 4, 64]), ALU.mult, [psB_, rcB], [oB])
            yield
            of = o[:, :, :].rearrange("p h v -> p (h v)")
            ps_, psB_ = bank(B)
            for oc in range(8 if dbg >= 5 else 0):
                p.add("pe", lambda e, ps_=ps_, oc=oc: e.transpose(ps_[:, oc * 64:(oc + 1) * 64],
                                                                 of[:, oc * 128:(oc + 1) * 128], ident[0:64, 0:64]),
                      [oB], [psB_])
            if dbg >= 5:
                act(B, oT[:, :, rr * 64:(rr + 1) * 64], ps_[:, :].rearrange("p (c q) -> p c q", q=64), AF.Copy,
                    [psB_], [oTB[rr]])
        for rr0 in range(0, 8, NCH):
            gens = [row_unit(rr0 + k_, PTs[k_], PTBs[k_], tmps[k_], tmpBs[k_], rcs[k_], rcBs[k_], os_[k_], oBs[k_])
                    for k_ in range(NCH)]
            live = [True] * NCH
            while any(live):
                for gi, g_ in enumerate(gens):
                    if live[gi]:
                        try:
                            next(g_)
                        except StopIteration:
                            live[gi] = False
        for oc in range(8 if dbg >= 6 else 0):
            ps_, psB_ = bank(B)
            for c in range(8):
                mm(B, ps_[:, :], wo[:, c, oc * 128:(oc + 1) * 128], oT[:, c, :], c == 0, c == 7, [woB] + oTB, [psB_])
            tt(B, "dve", xt[:, oc, :], xt[:, oc, :], ps_[:, :], ALU.add, [xtB, psB_], [xtB])
        B.dma("pool", xov[:, :, t0:t0 + TT], xt[:, :, :], reads=[xtB])
    p.barrier()
    sb.release(m0)


def alloc_na_scratch(B):
    T = B.T
    S = {"qT": B.dscr("s_qT", [D, T], BF16).ap(), "kT": B.dscr("s_kT", [D, T], BF16).ap(),
         "vtokb": B.dscr("s_vtokb", [T, D], BF16).ap()}
    return S


def build_na_probe(T, ncols, V, shapes, jn, layer, mode="full"):
    B = Builder(T)
    nc = B.nc
    setup_common(B, ncols)
    xT = B.din("xT", [D, T]).ap()
    W = {n: B.din(n, list(shapes[n])).ap() for n in ("na_qkv", "na_o")}
    nslot_tot = sum(len(s) for s in na_slots(T))
    nbias = B.din("nbias", [128, nslot_tot]).ap()
    btab = B.din("btab", [16, 128, 1024]).ap()
    yT = B.dout("yT", [D, T]).ap()
    S = alloc_na_scratch(B)
    na_n1(B, jn, layer, xT, W, V, S)
    if mode == "n1":
        B.dma("sp", yT[:, :], xT[:, :])
        B.p.barrier()
    else:
        na_n2(B, jn, xT, yT, W, S, nbias, btab, dbg=int(mode) if mode.isdigit() else 9)
    with ExitStack() as st:
        B.p.emit(nc, st)
    return B


T_CORE = NSEG * SEG
DEPTH = 4
W_NAMES = ["w_up", "w_down", "rw_rkv", "rw_w1", "rw_w2", "rw_a1", "rw_a2", "rw_v1", "rw_v2", "rw_g1", "rw_g2",
           "rw_o", "na_qkv", "na_o"]
_CACHE = {}


def build_full(ncols, V, shapes, T=None, depth=DEPTH, skip_last_mlp=False):
    T = T or T_CORE
    B = Builder(T)
    nc = B.nc
    setup_common(B, ncols)
    xT = B.din("xT", [D, T]).ap()
    W = {n: B.din(n, list(shapes[n])).ap() for n in W_NAMES}
    nslot_tot = sum(len(s) for s in na_slots(T))
    nbias = B.din("nbias", [128, nslot_tot]).ap()
    btabs = [B.din(f"btab{j}", [16, 128, 1024]).ap() for j in range(2)]
    yT = B.dout("yT", [D, T]).ap()
    xA = B.dscr("xA", [D, T]).ap()
    xB = B.dscr("xB", [D, T]).ap()
    SR = alloc_rwkv_scratch(B)
    SN = alloc_na_scratch(B)
    cur = xT
    for layer in range(depth):
        j = layer // 2
        lastl = (layer == depth - 1)
        mdst = yT if (lastl and skip_last_mlp) else xA
        if layer % 2 == 0:
            rwkv_r1(B, j, layer, cur, W, V, SR)
            rwkv_r2(B, SR)
            rwkv_r3(B, j, cur, mdst, W, V, SR)
        else:
            na_n1(B, j, layer, cur, W, V, SN)
            na_n2(B, j, cur, mdst, W, SN, nbias, btabs[j])
        if lastl and skip_last_mlp:
            break
        dst = yT if lastl else xB
        mlp_stage(B, xA, dst, W["w_up"][layer], W["w_down"][layer], B.vecs_sb, V["norm_mlp_g"][layer])
        cur = xB
    with ExitStack() as st:
        B.p.emit(nc, st)
    return B


def kernel(**inputs):
    inp = {k: np.asarray(v) for k, v in inputs.items()}
    xp = inp["x_prompt"]
    xs = inp["x_sample"]
    vp, V = pack_vectors(inp)
    vecs = vp.array()
    shapes = {n: inp[n].shape for n in W_NAMES}
    key = (vp.n,)
    if key not in _CACHE:
        _CACHE[key] = build_full(vp.n, V, shapes)
    B = _CACHE[key]
    consts = host_consts()
    blkm = host_blk_masks()
    btabs = [np.ascontiguousarray(host_na_bias_table(inp["na_rpb"][j]).reshape(16, 128, 1024)) for j in range(2)]
    nb = {"S": host_na_nbias(T_CORE, "S"), "P": host_na_nbias(T_CORE, "P")}
    prompt_ids = []
    in_maps = []
    for c in range(NCORES):
        if c < 4:
            ids = [2 * c, 2 * c + 1]
            xc = np.concatenate([xs[c], xp[ids[0]], xp[ids[1]]], axis=0)
        else:
            ids = list(range(8 + 6 * (c - 4), 8 + 6 * (c - 4) + 6))
            xc = np.concatenate([xp[i] for i in ids], axis=0)
        prompt_ids.append(ids)
        fl = np.zeros((128, 8), np.float32)
        if c < 4:
            fl[:, 1:4] = 1.0
        m = {"xT": np.ascontiguousarray(xc.T), "vecs": vecs, "consts": consts, "flags": fl, "blkm": blkm,
             "nbias": nb["S" if c < 4 else "P"], "btab0": btabs[0], "btab1": btabs[1]}
        for n in W_NAMES:
            m[n] = inp[n]
        in_maps.append(m)
    res = run_bass_kernel_spmd(B.nc, in_maps, core_ids=list(range(NCORES)))
    y_prompt = np.empty_like(xp)
    y_sample = np.empty_like(xs)
    for c in range(NCORES):
        y = res.results[c]["yT"].T
        if c < 4:
            y_sample[c] = y[0:8192]
            for k, i in enumerate(prompt_ids[c]):
                y_prompt[i] = y[8192 + k * SEG:8192 + (k + 1) * SEG]
        else:
            for k, i in enumerate(prompt_ids[c]):
                y_prompt[i] = y[k * SEG:(k + 1) * SEG]
    return (y_prompt, y_sample)
```
